# Optimizing a Trainium2 kernel written in Bass

```python
import numpy as np
import jax
import jax.numpy as jnp
from jax import lax

D_MODEL = 1024
BATCH = 8
SEQ = 2048
DEPTH = 1
DEC_BATCH = 32
DEC_SEQ = 4
PAST_LEN = 16384
PAGE_SIZE = 128

MLA_HEADS = 8
MLA_Q_RANK = 384
MLA_KV_RANK = 256
MLA_D_NOPE = 64
MLA_D_ROPE = 32
MLA_D_V = 64
ROPE_THETA = 10000.0
MLA_SCALE = (MLA_D_NOPE + MLA_D_ROPE) ** -0.5
NSA_HEADS = 8
NSA_GROUPS = 2
NSA_HPG = NSA_HEADS // NSA_GROUPS
NSA_DH = 64
NSA_SCALE = NSA_DH ** -0.5
CMP_BLOCK = 32
CMP_STRIDE = 16
CMP_HID = 64
SLC_BLOCK = 64
SLC_TOP_N = 16
WINDOW = 512
D_FF = 2816
CONV_W = 3
Q_BLOCK = 128
SLC_Q_BLOCK = 64
EPS = 1e-6
NEG = -1e30
FORCE = 1e9
IN_TOTAL = (MLA_Q_RANK + MLA_KV_RANK + MLA_D_ROPE + NSA_HEADS * NSA_DH
            + 3 * (2 * NSA_GROUPS * NSA_DH) + 3 * NSA_HEADS + 2 * D_MODEL)

kernel_name = 'hybrid_mla_nsa_convffn_step'


def rms_norm(x, g):
    xf = x.astype(jnp.float32)
    y = xf * lax.rsqrt(jnp.mean(xf * xf, axis=-1, keepdims=True) + EPS)
    return (y * g.astype(jnp.float32)).astype(x.dtype)


def rope(x, pos):
    d = x.shape[-1]
    inv = ROPE_THETA ** (-jnp.arange(0, d, 2, dtype=jnp.float32) / d)
    ang = pos.astype(jnp.float32)[:, None] * inv[None, :]
    cos = jnp.cos(ang)[None, :, None, :]
    sin = jnp.sin(ang)[None, :, None, :]
    x1, x2 = jnp.split(x.astype(jnp.float32), 2, axis=-1)
    return jnp.concatenate([x1 * cos - x2 * sin, x1 * sin + x2 * cos], axis=-1).astype(x.dtype)


def alibi_slopes():
    h = jnp.arange(1, NSA_HEADS + 1, dtype=jnp.float32)
    return (2.0 ** (-8.0 * h / NSA_HEADS)).reshape(NSA_GROUPS, NSA_HPG)


def masked_softmax(s, mask, axis=-1):
    p = jax.nn.softmax(jnp.where(mask, s, NEG), axis=axis)
    return jnp.where(mask, p, 0.0)


def to_blocks(x, blk):
    b, t = x.shape[:2]
    return jnp.moveaxis(x.reshape((b, t // blk, blk) + x.shape[2:]), 1, 0)


def from_blocks(x):
    n, b, blk = x.shape[:3]
    return jnp.moveaxis(x, 0, 1).reshape((b, n * blk) + x.shape[3:])


def mixer_inputs(h, pos, lp):
    b, t, _ = h.shape
    sizes = [MLA_Q_RANK, MLA_KV_RANK, MLA_D_ROPE, NSA_HEADS * NSA_DH,
             2 * NSA_GROUPS * NSA_DH, 2 * NSA_GROUPS * NSA_DH, 2 * NSA_GROUPS * NSA_DH,
             3 * NSA_HEADS, D_MODEL, D_MODEL]
    cuts = np.cumsum(sizes)[:-1].tolist()
    cq, ckv, kr, qn, kvc, kvs, kvw, gn, ga, gb = jnp.split(h @ lp['w_in'], cuts, axis=-1)
    cq = rms_norm(cq, lp['q_norm_g'])
    q = jnp.einsum('btr,rhd->bthd', cq, lp['w_uq'])
    q_nope = q[..., :MLA_D_NOPE]
    q_rope = rope(q[..., MLA_D_NOPE:], pos)
    q_abs = jnp.einsum('bthn,rhn->bthr', q_nope, lp['w_uk'])
    ckv = rms_norm(ckv, lp['kv_norm_g'])
    krope = rope(kr[:, :, None, :], pos)[:, :, 0, :]
    qn = qn.reshape(b, t, NSA_HEADS, NSA_DH)

    def kv(z):
        z = z.reshape(b, t, 2, NSA_GROUPS, NSA_DH)
        return z[:, :, 0], z[:, :, 1]

    kc, vc = kv(kvc)
    ks, vs = kv(kvs)
    kw, vw = kv(kvw)
    gn = jax.nn.sigmoid(gn.reshape(b, t, NSA_HEADS, 3))
    return (q_abs, q_rope, ckv, krope, qn, kc, vc, ks, vs, kw, vw, gn,
            jax.nn.sigmoid(ga), jax.nn.sigmoid(gb))


def mla_core(q_abs, q_rope, ckv, krope, q_pos, k_pos):
    s = (jnp.einsum('bqhr,bkr->bhqk', q_abs, ckv).astype(jnp.float32)
         + jnp.einsum('bqhd,bkd->bhqk', q_rope, krope).astype(jnp.float32)) * MLA_SCALE
    mask = (k_pos[None, :] <= q_pos[:, None])[None, None]
    p = masked_softmax(s, mask)
    return jnp.einsum('bhqk,bkr->bqhr', p.astype(ckv.dtype), ckv)


def compress(k, pos_emb, w1, w2):
    b, t = k.shape[:2]
    n_chunks = t // CMP_STRIDE
    ratio = CMP_BLOCK // CMP_STRIDE
    n_cmp = n_chunks - ratio + 1
    kc = k[:, :n_chunks * CMP_STRIDE].reshape(b, n_chunks, CMP_STRIDE, NSA_GROUPS, NSA_DH)
    blocks = jnp.concatenate([kc[:, j:j + n_cmp] for j in range(ratio)], axis=2)
    blocks = blocks + pos_emb[None, None, :, None, :]
    flat = jnp.moveaxis(blocks, 3, 2).reshape(b, n_cmp, NSA_GROUPS, CMP_BLOCK * NSA_DH)
    return jax.nn.gelu(flat @ w1) @ w2


def cmp_attend(q, kc, vc, q_pos, slopes):
    b, tq = q.shape[:2]
    n_cmp = kc.shape[1]
    end = jnp.arange(n_cmp) * CMP_STRIDE + CMP_BLOCK - 1
    qg = q.reshape(b, tq, NSA_GROUPS, NSA_HPG, NSA_DH)
    dist = (q_pos[:, None] - end[None, :]).astype(jnp.float32)
    s = jnp.einsum('bqgpd,bcgd->bgpqc', qg, kc).astype(jnp.float32) * NSA_SCALE
    s = s - slopes[None, :, :, None, None] * dist
    p = masked_softmax(s, dist >= 0)
    o = jnp.einsum('bgpqc,bcgd->bqgpd', p.astype(vc.dtype), vc).reshape(b, tq, NSA_HEADS, NSA_DH)
    return o, p.sum(axis=2)


def select_blocks(imp, q_pos, n_slc):
    n_cmp = imp.shape[-1]
    i = jnp.arange(n_cmp)[:, None]
    j = jnp.arange(n_slc)[None, :]
    overlap = ((i * CMP_STRIDE < (j + 1) * SLC_BLOCK)
               & (i * CMP_STRIDE + CMP_BLOCK > j * SLC_BLOCK)).astype(jnp.float32)
    score = jnp.einsum('bgqc,cs->bqgs', imp, overlap)
    cur = (q_pos // SLC_BLOCK)[:, None, None]
    jj = jnp.arange(n_slc)[None, None, :]
    forced = (jj == 0) | (jj == cur) | (jj == cur - 1)
    score = jnp.where(forced, FORCE, score)
    score = jnp.where(jj <= cur, score, NEG)
    _, idx = lax.top_k(score, min(SLC_TOP_N, n_slc))
    valid = idx <= (q_pos // SLC_BLOCK)[None, :, None, None]
    return idx, valid


def to_slc_blocks(k, n_slc):
    b, t = k.shape[:2]
    kp = jnp.pad(k, ((0, 0), (0, n_slc * SLC_BLOCK - t), (0, 0), (0, 0)))
    return jnp.transpose(kp.reshape(b, n_slc, SLC_BLOCK, NSA_GROUPS, NSA_DH), (0, 3, 1, 2, 4))


def slc_attend(q, kb, vb, idx, valid, q_pos, slopes):
    b, tq = q.shape[:2]
    bi = jnp.arange(b)[:, None, None, None]
    gi = jnp.arange(NSA_GROUPS)[None, None, :, None]
    ks = kb[bi, gi, idx]
    vs = vb[bi, gi, idx]
    kpos = idx[..., None] * SLC_BLOCK + jnp.arange(SLC_BLOCK)
    dist = (q_pos[None, :, None, None, None] - kpos).astype(jnp.float32)
    qg = q.reshape(b, tq, NSA_GROUPS, NSA_HPG, NSA_DH)
    s = jnp.einsum('bqgpd,bqgnsd->bqgpns', qg, ks).astype(jnp.float32) * NSA_SCALE
    s = s - slopes[None, None, :, :, None, None] * dist[:, :, :, None]
    mask = (valid[..., None] & (dist >= 0))[:, :, :, None]
    p = masked_softmax(s, mask, axis=(-2, -1))
    return jnp.einsum('bqgpns,bqgnsd->bqgpd', p.astype(vs.dtype), vs).reshape(b, tq, NSA_HEADS, NSA_DH)


def band_attend(qb, kb, vb, qpos, kpos, slopes):
    b, n, tq = qb.shape[:3]
    qg = qb.reshape(b, n, tq, NSA_GROUPS, NSA_HPG, NSA_DH)
    dist = (qpos[:, :, None] - kpos[:, None, :]).astype(jnp.float32)
    s = jnp.einsum('bnqgpd,bnkgd->bgpnqk', qg, kb).astype(jnp.float32) * NSA_SCALE
    s = s - slopes[None, :, :, None, None, None] * dist
    mask = (dist >= 0) & (dist < WINDOW) & (kpos[:, None, :] >= 0)
    p = masked_softmax(s, mask)
    return jnp.einsum('bgpnqk,bnkgd->bnqgpd', p.astype(vb.dtype), vb).reshape(b, n, tq, NSA_HEADS, NSA_DH)


def finish(x, o_lat, o_cmp, o_slc, o_win, gn, ga, gb, conv_prev, lp):
    b, t, _ = x.shape
    o_mla = jnp.einsum('bthr,rhv->bthv', o_lat, lp['w_uv']).reshape(b, t, MLA_HEADS * MLA_D_V)
    o_nsa = (gn[..., 0:1] * o_cmp + gn[..., 1:2] * o_slc + gn[..., 2:3] * o_win).reshape(b, t, NSA_HEADS * NSA_DH)
    merged = ga * (o_mla @ lp['w_proj_mla']) + gb * (o_nsa @ lp['w_proj_nsa'])
    x1 = x + merged @ lp['w_out']
    h2 = rms_norm(x1, lp['norm2_g'])
    g = h2 @ lp['w_gate']
    u = h2 @ lp['w_up']
    gp = jnp.concatenate([conv_prev, g], axis=1)
    conv = lp['conv_b']
    for j in range(CONV_W):
        conv = conv + lp['conv_w'][j] * gp[:, j:j + t]
    x2 = x1 + (jax.nn.silu(conv) * u) @ lp['w_down']
    return x2, gp[:, -(CONV_W - 1):]


def layer_prompt(x, lp, slopes):
    b, t, _ = x.shape
    pos = jnp.arange(t, dtype=jnp.int32)
    h = rms_norm(x, lp['norm1_g'])
    (q_abs, q_rope, ckv, krope, qn, kc, vc, ks, vs, kw, vw, gn, ga, gb) = mixer_inputs(h, pos, lp)
    o_lat = from_blocks(lax.map(
        lambda a: mla_core(a[0], a[1], ckv, krope, a[2], pos),
        (to_blocks(q_abs, Q_BLOCK), to_blocks(q_rope, Q_BLOCK), pos.reshape(-1, Q_BLOCK))))
    kcc = compress(kc, lp['cmp_pos_k'], lp['cmp_w1_k'], lp['cmp_w2_k'])
    vcc = compress(vc, lp['cmp_pos_v'], lp['cmp_w1_v'], lp['cmp_w2_v'])
    o_cmp, imp = cmp_attend(qn, kcc, vcc, pos, slopes)
    n_slc = -(-t // SLC_BLOCK)
    idx, valid = select_blocks(imp, pos, n_slc)
    kb = to_slc_blocks(ks, n_slc)
    vb = to_slc_blocks(vs, n_slc)
    o_slc = from_blocks(lax.map(
        lambda a: slc_attend(a[0], kb, vb, a[1], a[2], a[3], slopes),
        (to_blocks(qn, SLC_Q_BLOCK), to_blocks(idx, SLC_Q_BLOCK), to_blocks(valid, SLC_Q_BLOCK),
         pos.reshape(-1, SLC_Q_BLOCK))))
    nqb = t // Q_BLOCK
    band = WINDOW + Q_BLOCK
    gidx = jnp.arange(nqb)[:, None] * Q_BLOCK + jnp.arange(band)[None, :]
    kp = jnp.pad(kw, ((0, 0), (WINDOW, 0), (0, 0), (0, 0)))
    vp = jnp.pad(vw, ((0, 0), (WINDOW, 0), (0, 0), (0, 0)))
    o_win = band_attend(qn.reshape(b, nqb, Q_BLOCK, NSA_HEADS, NSA_DH), kp[:, gidx], vp[:, gidx],
                        pos.reshape(nqb, Q_BLOCK), gidx - WINDOW, slopes).reshape(b, t, NSA_HEADS, NSA_DH)
    conv0 = jnp.zeros((b, CONV_W - 1, D_FF), x.dtype)
    y, conv_state = finish(x, o_lat, o_cmp, o_slc, o_win, gn, ga, gb, conv0, lp)
    n_keep = min(WINDOW, t)
    return y, (ckv, krope, kc, vc, ks, vs, kw[:, -n_keep:], vw[:, -n_keep:], conv_state)


def layer_sample(x, lc, page_table, lp, slopes):
    (c_ckv, c_krope, c_cmp_k, c_cmp_v, c_slc_k, c_slc_v, s_win_k, s_win_v, s_conv) = lc
    b, t, _ = x.shape
    past = page_table.shape[1] * PAGE_SIZE
    pos = past + jnp.arange(t, dtype=jnp.int32)
    all_pos = jnp.arange(past + t, dtype=jnp.int32)
    h = rms_norm(x, lp['norm1_g'])
    (q_abs, q_rope, ckv, krope, qn, kc, vc, ks, vs, kw, vw, gn, ga, gb) = mixer_inputs(h, pos, lp)

    def gather(pool, new):
        rows = pool[page_table].reshape((b, past) + pool.shape[2:])
        return jnp.concatenate([rows, new], axis=1)

    o_lat = mla_core(q_abs, q_rope, gather(c_ckv, ckv), gather(c_krope, krope), pos, all_pos)
    kc_all = gather(c_cmp_k, kc)
    vc_all = gather(c_cmp_v, vc)
    kcc = compress(kc_all, lp['cmp_pos_k'], lp['cmp_w1_k'], lp['cmp_w2_k'])
    vcc = compress(vc_all, lp['cmp_pos_v'], lp['cmp_w1_v'], lp['cmp_w2_v'])
    o_cmp, imp = cmp_attend(qn, kcc, vcc, pos, slopes)
    n_slc = -(-(past + t) // SLC_BLOCK)
    idx, valid = select_blocks(imp, pos, n_slc)
    o_slc = slc_attend(qn, to_slc_blocks(gather(c_slc_k, ks), n_slc), to_slc_blocks(gather(c_slc_v, vs), n_slc),
                       idx, valid, pos, slopes)
    n_buf = s_win_k.shape[1]
    kw_all = jnp.concatenate([s_win_k, kw], axis=1)
    vw_all = jnp.concatenate([s_win_v, vw], axis=1)
    kpos = past - n_buf + jnp.arange(n_buf + t, dtype=jnp.int32)
    o_win = band_attend(qn[:, None], kw_all[:, None], vw_all[:, None], pos[None], kpos[None], slopes)[:, 0]
    y, conv_state = finish(x, o_lat, o_cmp, o_slc, o_win, gn, ga, gb, s_conv, lp)
    return y, (ckv, krope, kc, vc, ks, vs, kw_all[:, -n_buf:], vw_all[:, -n_buf:], conv_state)


def setup_inputs(seed: int = 0) -> dict:
    key = jax.random.key(seed)
    keys = iter(jax.random.split(key, 64))

    def nrm(shape, scale):
        return jax.random.normal(next(keys), shape, jnp.float32) * scale

    def gain(shape):
        return 1.0 + nrm(shape, 0.02)

    n_pages = PAST_LEN // PAGE_SIZE
    n_pool = (DEC_BATCH * n_pages * 5) // 4
    win_buf = min(WINDOW, PAST_LEN)
    L = DEPTH
    page_table = jax.random.permutation(next(keys), n_pool)[:DEC_BATCH * n_pages]
    page_table = page_table.reshape(DEC_BATCH, n_pages).astype(jnp.int32)
    kvp = (L, n_pool, PAGE_SIZE, NSA_GROUPS, NSA_DH)
    return {
        'x_prompt': nrm((BATCH, SEQ, D_MODEL), 1.0),
        'x_sample': nrm((DEC_BATCH, DEC_SEQ, D_MODEL), 1.0),
        'cache_mla_ckv': nrm((L, n_pool, PAGE_SIZE, MLA_KV_RANK), 1.0),
        'cache_mla_krope': nrm((L, n_pool, PAGE_SIZE, MLA_D_ROPE), 1.0),
        'cache_nsa_cmp_k': nrm(kvp, 1.0),
        'cache_nsa_cmp_v': nrm(kvp, 1.0),
        'cache_nsa_slc_k': nrm(kvp, 1.0),
        'cache_nsa_slc_v': nrm(kvp, 1.0),
        'state_win_k': nrm((L, DEC_BATCH, win_buf, NSA_GROUPS, NSA_DH), 1.0),
        'state_win_v': nrm((L, DEC_BATCH, win_buf, NSA_GROUPS, NSA_DH), 1.0),
        'state_ffn_conv': nrm((L, DEC_BATCH, CONV_W - 1, D_FF), 1.0),
        'page_table': page_table,
        'norm1_g': gain((L, D_MODEL)),
        'w_in': nrm((L, D_MODEL, IN_TOTAL), D_MODEL ** -0.5),
        'q_norm_g': gain((L, MLA_Q_RANK)),
        'kv_norm_g': gain((L, MLA_KV_RANK)),
        'w_uq': nrm((L, MLA_Q_RANK, MLA_HEADS, MLA_D_NOPE + MLA_D_ROPE), MLA_Q_RANK ** -0.5),
        'w_uk': nrm((L, MLA_KV_RANK, MLA_HEADS, MLA_D_NOPE), MLA_KV_RANK ** -0.5),
        'w_uv': nrm((L, MLA_KV_RANK, MLA_HEADS, MLA_D_V), MLA_KV_RANK ** -0.5),
        'cmp_pos_k': nrm((L, CMP_BLOCK, NSA_DH), 0.1),
        'cmp_w1_k': nrm((L, CMP_BLOCK * NSA_DH, CMP_HID), (CMP_BLOCK * NSA_DH) ** -0.5),
        'cmp_w2_k': nrm((L, CMP_HID, NSA_DH), CMP_HID ** -0.5),
        'cmp_pos_v': nrm((L, CMP_BLOCK, NSA_DH), 0.1),
        'cmp_w1_v': nrm((L, CMP_BLOCK * NSA_DH, CMP_HID), (CMP_BLOCK * NSA_DH) ** -0.5),
        'cmp_w2_v': nrm((L, CMP_HID, NSA_DH), CMP_HID ** -0.5),
        'w_proj_mla': nrm((L, MLA_HEADS * MLA_D_V, D_MODEL), (MLA_HEADS * MLA_D_V) ** -0.5),
        'w_proj_nsa': nrm((L, NSA_HEADS * NSA_DH, D_MODEL), (NSA_HEADS * NSA_DH) ** -0.5),
        'w_out': nrm((L, D_MODEL, D_MODEL), D_MODEL ** -0.5),
        'norm2_g': gain((L, D_MODEL)),
        'w_gate': nrm((L, D_MODEL, D_FF), D_MODEL ** -0.5),
        'w_up': nrm((L, D_MODEL, D_FF), D_MODEL ** -0.5),
        'conv_w': nrm((L, CONV_W, D_FF), CONV_W ** -0.5),
        'conv_b': nrm((L, D_FF), 0.02),
        'w_down': nrm((L, D_FF, D_MODEL), D_FF ** -0.5),
        'norm_f_g': gain((D_MODEL,)),
    }


def reference(x_prompt, x_sample, cache_mla_ckv, cache_mla_krope, cache_nsa_cmp_k, cache_nsa_cmp_v,
              cache_nsa_slc_k, cache_nsa_slc_v, state_win_k, state_win_v, state_ffn_conv, page_table,
              norm1_g, w_in, q_norm_g, kv_norm_g, w_uq, w_uk, w_uv,
              cmp_pos_k, cmp_w1_k, cmp_w2_k, cmp_pos_v, cmp_w1_v, cmp_w2_v,
              w_proj_mla, w_proj_nsa, w_out, norm2_g, w_gate, w_up, conv_w, conv_b, w_down, norm_f_g):
    slopes = alibi_slopes()
    yp, ys = x_prompt, x_sample
    p_states, s_states = [], []
    for l in range(DEPTH):
        lp = {'norm1_g': norm1_g[l], 'w_in': w_in[l], 'q_norm_g': q_norm_g[l], 'kv_norm_g': kv_norm_g[l],
              'w_uq': w_uq[l], 'w_uk': w_uk[l], 'w_uv': w_uv[l],
              'cmp_pos_k': cmp_pos_k[l], 'cmp_w1_k': cmp_w1_k[l], 'cmp_w2_k': cmp_w2_k[l],
              'cmp_pos_v': cmp_pos_v[l], 'cmp_w1_v': cmp_w1_v[l], 'cmp_w2_v': cmp_w2_v[l],
              'w_proj_mla': w_proj_mla[l], 'w_proj_nsa': w_proj_nsa[l], 'w_out': w_out[l],
              'norm2_g': norm2_g[l], 'w_gate': w_gate[l], 'w_up': w_up[l],
              'conv_w': conv_w[l], 'conv_b': conv_b[l], 'w_down': w_down[l]}
        lc = (cache_mla_ckv[l], cache_mla_krope[l], cache_nsa_cmp_k[l], cache_nsa_cmp_v[l],
              cache_nsa_slc_k[l], cache_nsa_slc_v[l], state_win_k[l], state_win_v[l], state_ffn_conv[l])
        yp, ps = layer_prompt(yp, lp, slopes)
        ys, ss = layer_sample(ys, lc, page_table, lp, slopes)
        p_states.append(ps)
        s_states.append(ss)
    (p_ckv, p_krope, p_cmp_k, p_cmp_v, p_slc_k, p_slc_v, p_win_k, p_win_v, p_conv) = [
        jnp.stack(s) for s in zip(*p_states)]
    (s_ckv, s_krope, s_cmp_k, s_cmp_v, s_slc_k, s_slc_v, s_win_k, s_win_v, s_conv) = [
        jnp.stack(s) for s in zip(*s_states)]
    y_prompt = rms_norm(yp, norm_f_g)
    y_sample = rms_norm(ys, norm_f_g)
    return (y_prompt, y_sample, p_ckv, s_ckv, p_krope, s_krope, p_cmp_k, s_cmp_k, p_cmp_v, s_cmp_v,
            p_slc_k, s_slc_k, p_slc_v, s_slc_v, p_win_k, s_win_k, p_win_v, s_win_v, p_conv, s_conv)
```

```python
import os
import contextlib
import numpy as np
import concourse.bass as bass
import concourse.mybir as mybir
from concourse.bass_utils import run_bass_kernel_spmd

F32 = mybir.dt.float32
BF16 = mybir.dt.bfloat16
I32 = mybir.dt.int32
AF = mybir.ActivationFunctionType
ALU = mybir.AluOpType
AX = mybir.AxisListType

D = 1024
T = 2048
TS = 16
TT = T + TS
NT = 17
EPS = 1e-6
MLA_SCALE = 96 ** -0.5
NSA_SCALE = 0.125
DFF = 2816
NFC = 22
PAST = 16384
C_CQ, C_CKV, C_KR, C_QN, C_KVC, C_KVS, C_KVW, C_GN, C_GA, C_GB = 0, 384, 640, 672, 1184, 1440, 1696, 1952, 1976, 3000


class Buf:
    __slots__ = ("name", "w", "r", "excl")

    def __init__(self, name="", excl=False):
        self.name = name
        self.excl = excl
        self.w = None
        self.r = {}


class Eng:
    def __init__(self, k, name, eng, is_pe=False):
        self.k = k
        self.name = name
        self.e = eng
        self.is_pe = is_pe
        self.sem = k.nc.alloc_semaphore(f"s_{name}_0")
        self.nsem = 1
        self.cnt = 0
        self.seen = {}

    def wait(self, tok):
        sem, val = tok
        if self.is_pe and sem is self.sem:
            return
        key = id(sem)
        if self.seen.get(key, 0) >= val:
            return
        self.seen[key] = val
        self.e.wait_ge(sem, val)

    def next_tok(self):
        if self.cnt >= 30000:
            self.sem = self.k.nc.alloc_semaphore(f"s_{self.name}_{self.nsem}")
            self.nsem += 1
            self.cnt = 0
        return (self.sem, self.cnt + 1)


class DmaQ:
    def __init__(self, k, name, eng_obj, nsem):
        self.k = k
        self.name = name
        self.E = eng_obj
        self.sems = [k.nc.alloc_semaphore(f"d_{name}_{i}") for i in range(nsem)]
        self.cnts = [0] * nsem
        self.i = 0


class K:
    def __init__(self, nc):
        self.nc = nc
        self.pe = Eng(self, "pe", nc.tensor, is_pe=True)
        self.act = Eng(self, "act", nc.scalar)
        self.dve = Eng(self, "dve", nc.vector)
        self.pool = Eng(self, "pool", nc.gpsimd)
        self.sp = Eng(self, "sp", nc.sync)
        self.q_sp = DmaQ(self, "sp", self.sp, 40)
        self.q_pool = DmaQ(self, "pool", self.pool, 24)
        self.all_dma_toks = []

    def _deps(self, E, r, w):
        for b in r:
            if b.w is not None:
                E.wait(b.w)
        for b in w:
            if b.w is not None:
                E.wait(b.w)
            for t in b.r.values():
                E.wait(t)

    def _mark(self, tok, r, w):
        for b in r:
            key = id(tok[0])
            old = b.r.get(key)
            if old is None or old[1] < tok[1]:
                b.r[key] = tok
        for b in w:
            b.w = tok
            b.r = {}

    def op(self, E, fn, r=(), w=(), inc=True):
        ex = [b for b in r if b.excl]
        if ex:
            r = [b for b in r if not b.excl]
            w = list(w) + ex
        self._deps(E, r, w)
        tok = E.next_tok()
        ins = fn()
        if inc:
            ins.then_inc(tok[0], 1)
            E.cnt += 1
        self._mark(tok, r, w)
        return ins

    def dma(self, q, out, in_, r=(), w=(), **kw):
        E = q.E
        self._deps(E, r, w)
        i = q.i
        q.i = (q.i + 1) % len(q.sems)
        if q.cnts[i] > 0:
            E.wait((q.sems[i], q.cnts[i]))
        if q.cnts[i] >= 30000:
            q.sems[i] = self.nc.alloc_semaphore(f"d_{q.name}_{i}_r{q.cnts[i]}")
            q.cnts[i] = 0
        q.cnts[i] += 16
        tok = (q.sems[i], q.cnts[i])
        E.e.dma_start(out=out, in_=in_, **kw).then_inc(tok[0], 16)
        self._mark(tok, r, w)
        return tok

    def gather(self, out, in_, idx_ap, r=(), w=()):
        q = self.q_pool
        E = q.E
        self._deps(E, r, w)
        i = q.i
        q.i = (q.i + 1) % len(q.sems)
        if q.cnts[i] > 0:
            E.wait((q.sems[i], q.cnts[i]))
        q.cnts[i] += 16
        tok = (q.sems[i], q.cnts[i])
        E.e.indirect_dma_start(out=out, out_offset=None, in_=in_,
                               in_offset=bass.IndirectOffsetOnAxis(ap=idx_ap, axis=0)).then_inc(tok[0], 16)
        self._mark(tok, r, w)
        return tok

    def barrier(self):
        toks = []
        for E in (self.pe, self.act, self.dve, self.pool, self.sp):
            if E.cnt > 0:
                toks.append((E.sem, E.cnt))
        for q in (self.q_sp, self.q_pool):
            for s, c in zip(q.sems, q.cnts):
                if c > 0:
                    toks.append((s, c))
        for E in (self.pe, self.act, self.dve, self.pool, self.sp):
            for t in toks:
                if E.is_pe and t[0] is E.sem:
                    continue
                E.wait(t)

    def finish(self):
        for q in (self.q_sp, self.q_pool):
            for s, c in zip(q.sems, q.cnts):
                if c > 0:
                    self.sp.wait((s, c))
        for E in (self.pe, self.act, self.dve, self.pool):
            if E.cnt > 0:
                self.sp.wait((E.sem, E.cnt))

    def mm(self, out, lhsT, rhs, start, stop, r=(), w=(), sig=None):
        if sig is None:
            sig = stop
        return self.op(self.pe, lambda: self.nc.tensor.matmul(out, lhsT, rhs, start=start, stop=stop),
                       r=r, w=w, inc=sig)

    def tr(self, out, in_, ident, r=(), w=(), sig=True):
        return self.op(self.pe, lambda: self.nc.tensor.transpose(out, in_, ident), r=r, w=w, inc=sig)

    def actf(self, out, in_, func, r=(), w=(), scale=1.0, bias=0.0, accum=None):
        kw = {}
        if accum is not None:
            kw["accum_out"] = accum
        return self.op(self.act, lambda: self.nc.scalar.activation(out=out, in_=in_, func=func, bias=bias,
                                                                   scale=scale, **kw), r=r, w=w)

    def v(self, which, fn, r=(), w=()):
        E = {"dve": self.dve, "pool": self.pool}[which]
        return self.op(E, fn, r=r, w=w)


def rope_tables():
    inv = (10000.0 ** (-np.arange(0, 32, 2, dtype=np.float32) / 32)).astype(np.float32)
    pos = np.concatenate([np.arange(T, dtype=np.float32),
                          np.tile(PAST + np.arange(4, dtype=np.float32), 4)])
    pos = np.concatenate([pos, np.zeros(NT * 128 - TT, np.float32)])
    ang = (pos[:, None] * inv[None, :]).astype(np.float32)
    cs = np.concatenate([np.cos(ang), np.sin(ang)], axis=1).astype(np.float32)
    return cs.reshape(NT, 128, 32).transpose(1, 0, 2).copy()


def tri_masks():
    kk = np.arange(128)[:, None]; qq = np.arange(128)[None, :]
    m = np.zeros((128, 2, 128), np.float32)
    m[:, 0, :] = np.where(kk > qq, -30000.0, 0.0)
    m[:, 1, :] = np.where(qq >= kk, -30000.0, 0.0)
    return m


def nsa_consts():
    c = {}
    q = np.arange(T)
    qrows = np.stack([np.full(T, 8.0), np.full(T, 8.0), -8.0 * (q // 128 * 128), -8.0 * (q % 128)]).astype(np.float32)
    krows = np.stack([(q // 128 * 128), (q % 128), np.ones(T), np.ones(T)]).astype(np.float32)
    e = 16 * np.arange(128) + 31
    kcrows = np.stack([(e // 128 * 128), (e % 128), np.ones(128), np.ones(128)]).astype(np.float32)
    c["qrows"] = qrows; c["krows"] = krows; c["kcrows"] = kcrows
    erows = np.zeros((32, T), np.float32)
    erows[q // 64, q] = 32768.0
    c["erows"] = erows
    cm = np.where(e[:, None] > q[None, :], -30000.0, 0.0).astype(np.float32)
    c["cmpmask"] = cm
    i = np.arange(128)[:, None]; j = np.arange(32)[None, :]
    ov = ((i * 16 < (j + 1) * 64) & (i * 16 + 32 > j * 64)).astype(np.float32)
    ov[127] = 0
    c["overlap"] = ov
    selb = np.zeros((24, 3, 8, 64), np.float32)
    for h in range(8):
        for br in range(3):
            selb[h * 3 + br, br, h, :] = 1.0
    c["selb"] = selb
    mt = np.zeros((128, 16, 32), np.float32); at = np.zeros((128, 16, 32), np.float32)
    for qt in range(16):
        for p in range(128):
            cur = (qt * 128 + p) // 64
            jj = np.arange(32)
            m = np.ones(32, np.float32); a = np.zeros(32, np.float32)
            m[jj > cur] = 0; a[jj > cur] = -1e30
            for f, val in ((0, 1e9), (cur, 2e9), (cur - 1, 4e9)):
                if f >= 0:
                    m[f] = 0; a[f] = val
            mt[p, qt] = m; at[p, qt] = a
    c["topk_m"] = mt; c["topk_a"] = at
    return c


SLOPES = [2.0 ** (-8.0 * (h + 1) / 8) for h in range(8)]


def sample_consts():
    c = {}
    sl = np.array(SLOPES, np.float64)
    hh = np.repeat(np.arange(8), 4); tt = np.tile(np.arange(4), 8)
    kb = np.arange(16) // 4; kt = np.arange(16) % 4
    en = np.zeros((16, 4, 32), np.float64); nm = np.zeros((16, 4, 32), np.float64)
    for b in range(4):
        ok = (kb[:, None] == b) & (kt[:, None] <= tt[None, :])
        en[:, b, :] = np.where(ok, np.exp(-sl[hh][None, :] * (tt[None, :] - kt[:, None])), 0.0)
        nm[:, b, :] = ok
    c["s_en"] = en.astype(np.float32); c["s_nm"] = nm.astype(np.float32)
    i = (np.arange(4)[None, :, None] * 128 + np.arange(128)[:, None, None])
    dist = tt[None, None, :] + 512 - i
    c["s_ebw"] = np.where(dist < 512, np.exp(-sl[hh][None, None, :] * dist), 0.0).astype(np.float32)
    p = np.arange(128)
    c["s_lb"] = np.stack([16384.0 - 128.0 * p, np.ones(128), (p == 127).astype(np.float64)]).astype(np.float32)
    rr = np.arange(128)
    rbs = np.zeros((3, 128, 32), np.float64)
    rbs[0] = -8.0 * sl[hh][None, :]
    rbs[1] = -8.0 * sl[hh][None, :] * (tt[None, :] - rr[:, None])
    c["s_rbs"] = rbs.astype(np.float32)
    cc = np.arange(8)
    rbc = np.zeros((3, 8, 32), np.float64)
    rbc[0] = -8.0 * sl[hh][None, :]
    rbc[1] = -8.0 * sl[hh][None, :] * (tt[None, :] - 16 * cc[:, None] - 31)
    rbc[2, 7, :] = -30000.0
    c["s_rbc"] = rbc.astype(np.float32)
    ii = (8 * p[:, None] + cc[None, :])[:, :, None]; jj = np.arange(264)[None, None, :]
    ov = ((ii * 16 < (jj + 1) * 64) & (ii * 16 + 32 > jj * 64) & (jj < 257) & (ii < 1023)).astype(np.float32)
    c["s_ov"] = ov
    m = np.ones((4, 264), np.float32); a = np.zeros((4, 264), np.float32)
    for f, val in ((0, 1e9), (256, 2e9), (255, 4e9)):
        m[:, f] = 0; a[:, f] = val
    m[:, 257:] = 0; a[:, 257:] = -1e30
    c["s_tkm"] = m; c["s_tka"] = a
    ss = np.zeros((16, 4), np.float32)
    for hl in range(4):
        for t in range(4):
            ss[hl * 4 + t, t] = 1.0
    c["s_ssum"] = ss
    return c


def build(phases=("A",)):
    nc = bass.Bass("TRN2", target_bir_lowering=False)
    k = K(nc)

    def din(name, shape, dt=F32):
        return nc.dram_tensor(name, list(shape), dt, kind="ExternalInput").ap()

    def dout(name, shape, dt=F32):
        return nc.dram_tensor(name, list(shape), dt, kind="ExternalOutput").ap()

    xp = din("xp", [T, D])
    xs = din("xs", [TS, D])
    w_in = din("w_in", [D, 4024])
    norm1_g = din("norm1_g", [D])
    q_norm_g = din("q_norm_g", [384])
    kv_norm_g = din("kv_norm_g", [256])
    w_uq = din("w_uq", [384, 768])
    ropecs = din("ropecs", [128, NT, 32])
    identf_d = din("identf", [128, 128])
    w_uk = din("w_uk", [256, 512])
    w_uv = din("w_uv", [256, 512])
    trimask_d = din("trimask", [128, 2, 128])
    qrows_d = din("qrows", [4, T]); krows_d = din("krows", [4, T]); kcrows_d = din("kcrows", [4, 128])
    erows_d = din("erows", [32, T]); cmpmask_d = din("cmpmask", [128, T]); overlap_d = din("overlap", [128, 32])
    selb_d = din("selb", [24, 3, 8, 64]); topkm_d = din("topk_m", [128, 16, 32]); topka_d = din("topk_a", [128, 16, 32])
    cw1 = {"k": din("cmp_w1_k", [2048, 64]), "v": din("cmp_w1_v", [2048, 64])}
    cw2 = {"k": din("cmp_w2_k", [64, 64]), "v": din("cmp_w2_v", [64, 64])}
    cpos = {"k": din("cmp_pos_k", [32, 64]), "v": din("cmp_pos_v", [32, 64])}
    DBG = bool(int(os.environ.get("KDBG", "0")))
    if DBG:
        dbg_omla = dout("dbg_omla", [128, 4, TT])
        dbg_onsa = dout("dbg_onsa", [128, 4, TT])
    norm_f_g = din("norm_f_g", [D]); norm2_g = din("norm2_g", [D])
    w_pm = din("w_proj_mla", [512, D]); w_pn = din("w_proj_nsa", [512, D]); w_out = din("w_out", [D, D])
    w_gate = din("w_gate", [D, DFF]); w_up = din("w_up", [D, DFF]); w_down = din("w_down", [DFF, D])
    conv_w = din("conv_w", [3, DFF]); conv_b = din("conv_b", [DFF]); ffn_state = din("state_ffn_conv", [8, DFF])
    o_yp = dout("yp", [T, D]); o_ys = dout("ys", [TS, D])
    o_pconv = dout("p_conv", [2, DFF]); o_sconv = dout("s_conv", [4, 2, DFF])
    NPG = 5120
    pt_d = din("page_table", [4, 128], I32)
    c_ckv = din("cache_mla_ckv", [NPG * 128, 256]); c_kr = din("cache_mla_krope", [NPG * 128, 32])
    c_ck = din("cache_nsa_cmp_k", [NPG * 128, 128]); c_cv = din("cache_nsa_cmp_v", [NPG * 128, 128])
    c_sk = din("cache_nsa_slc_k", [NPG * 128, 128]); c_sv = din("cache_nsa_slc_v", [NPG * 128, 128])
    s_en_d = din("s_en", [16, 4, 32]); s_nm_d = din("s_nm", [16, 4, 32]); s_ebw_d = din("s_ebw", [128, 4, 32])
    s_lb_d = din("s_lb", [3, 128]); s_rbs_d = din("s_rbs", [3, 128, 32]); s_rbc_d = din("s_rbc", [3, 8, 32])
    s_ov_d = din("s_ov", [128, 8, 264]); s_tkm_d = din("s_tkm", [4, 264]); s_tka_d = din("s_tka", [4, 264]); s_ssum_d = din("s_ssum", [16, 4])
    win_k_in = din("state_win_k", [4, 512, 128])
    win_v_in = din("state_win_v", [4, 512, 128])

    o_pckv = dout("p_ckv", [T, 256]); o_sckv = dout("s_ckv", [TS, 256])
    o_pkr = dout("p_krope", [T, 32]); o_skr = dout("s_krope", [TS, 32])
    o_kv = {}
    for nm in ("cmp_k", "cmp_v", "slc_k", "slc_v"):
        o_kv["p_" + nm] = dout("p_" + nm, [T, 128])
        o_kv["s_" + nm] = dout("s_" + nm, [TS, 128])
    o_pwk = dout("p_win_k", [512, 128]); o_pwv = dout("p_win_v", [512, 128])
    o_swk = dout("s_win_k", [4, 512, 128]); o_swv = dout("s_win_v", [4, 512, 128])

    sb = nc.alloc_sbuf_tensor

    def scoped(stack):
        def f(name, shape, dt):
            return stack.enter_context(nc.sbuf_tensor(name, shape, dt))
        return f
    identf = sb("identf_sb", [128, 128], F32); b_identf = Buf()
    identb = sb("identb_sb", [128, 128], BF16); b_identb = Buf()
    cs_sb = sb("cs_sb", [128, NT, 32], F32); b_cs = Buf()
    big = sb("big", [128, 16 * TT], BF16)
    hT = big[:, 0:8 * TT].rearrange("p (c t) -> p c t", c=8); b_hT = [Buf() for _ in range(NT)]
    g1col = sb("g1col", [128, 8], F32); b_g1 = Buf()
    gqcol = sb("gqcol", [128, 3], F32); b_gq = Buf()
    gkv_b = sb("gkv_b", [128, 256], F32); b_gkv = Buf()
    o_mlaT = big[:, 8 * TT:12 * TT].rearrange("p (c t) -> p c t", c=4); b_omla = Buf()
    o_nsaT = big[:, 12 * TT:16 * TT].rearrange("p (c t) -> p c t", c=4); b_onsa = Buf()
    x1all = big[:, 0:32768].bitcast(F32).rearrange("p (t d) -> p t d", t=16)
    x1s = sb("x1s", [128, D], F32)
    smp_qm = sb("smp_qm", [96, 8, TS], BF16); smp_ckvT = sb("smp_ckvT", [128, 2, TS], BF16); smp_krT = sb("smp_krT", [32, TS], BF16)
    smp_ckvn = sb("smp_ckvn", [TS, 257], BF16); smp_qn = sb("smp_qn", [64, 8, TS], BF16)
    smp_ksT = sb("smp_ksT", [128, TS], BF16); smp_kwT = sb("smp_kwT", [128, TS], BF16)
    smp_vs = sb("smp_vs", [TS, 129], BF16); smp_vw = sb("smp_vw", [TS, 129], BF16); smp_gn = sb("smp_gn", [24, TS], BF16)
    b_smp = Buf()
    b_x1 = [Buf() for _ in range(NT)]
    gf_b = sb("gf_b", [128, D], F32); b_gf = Buf()
    g2col = sb("g2col", [128, 8], F32); b_g2 = Buf()
    trimask = sb("trimask_sb", [128, 2, 128], BF16); b_tri = Buf()
    stL1 = contextlib.ExitStack(); sbL1 = scoped(stL1)
    vs_nat = sbL1("vs_nat", [128, NT, 2, 128], BF16)
    vw_nat = sbL1("vw_nat", [128, NT, 2, 128], BF16)
    b_vnat = [Buf() for _ in range(NT)]
    KAs = [sbL1(f"KAs{g}", [128, T], BF16) for g in range(2)]; b_KAs = [Buf(), Buf()]
    KAw = [sbL1(f"KAw{g}", [128, T], BF16) for g in range(2)]; b_KAw = [Buf(), Buf()]
    kccA = [sbL1(f"kccA{g}", [128, 128], BF16) for g in range(2)]; b_kccA = [Buf(), Buf()]
    vccA = [sbL1(f"vccA{g}", [128, 128], BF16) for g in range(2)]; b_vccA = [Buf(), Buf()]
    stL2 = contextlib.ExitStack(); sbL2 = scoped(stL2)
    qmT = sbL2("qmT", [96, 8, TT], BF16); b_qmT = [Buf() for _ in range(NT)]
    ckvT = sbL2("ckvT", [128, 2, TT], BF16); b_ckvT = [Buf() for _ in range(NT)]
    kropeT = sbL2("kropeT", [32, TT], BF16); b_krT = [Buf() for _ in range(NT)]

    ps = [nc.alloc_psum_tensor(f"ps{i}", [128, 512], F32) for i in range(8)]
    b_ps = [Buf(f"ps{i}", excl=True) for i in range(8)]
    psb = ps[7].bitcast(BF16)
    psbv2 = [ps[7].bitcast(BF16), ps[6].bitcast(BF16)]

    k.dma(k.q_sp, identf[:, :], identf_d[:, :], w=[b_identf])
    k.v("dve", lambda: nc.vector.tensor_copy(out=identb[:, :], in_=identf[:, :]), r=[b_identf], w=[b_identb])
    k.dma(k.q_sp, cs_sb[:, :, :], ropecs[:, :, :], w=[b_cs])
    k.dma(k.q_pool, trimask[:, :, :], trimask_d[:, :, :], w=[b_tri])
    with nc.allow_non_contiguous_dma(reason="small param columns"):
        k.dma(k.q_sp, g1col[:, :], norm1_g.rearrange("(c p) -> p c", p=128), w=[b_g1])
        k.dma(k.q_sp, gqcol[:, :], q_norm_g.rearrange("(c p) -> p c", p=128), w=[b_gq])
        k.dma(k.q_sp, gkv_b[:, :], kv_norm_g.partition_broadcast(128), w=[b_gkv])
        k.dma(k.q_sp, gf_b[:, :], norm_f_g.partition_broadcast(128), w=[b_gf])
        k.dma(k.q_sp, g2col[:, :], norm2_g.rearrange("(c p) -> p c", p=128), w=[b_g2])
    for tl in (smp_ckvn, smp_vs, smp_vw):
        k.v("pool", lambda: nc.gpsimd.memset(tl[:, :], 1.0), w=[b_smp])
    k.v("pool", lambda: nc.gpsimd.memset(vs_nat[:, :, :, :], 1.0), w=b_vnat)
    k.v("pool", lambda: nc.gpsimd.memset(vw_nat[:, :, :, :], 1.0), w=b_vnat)

    def tile_rows(t):
        return (t * 128, 128) if t < 16 else (T, TS)
    FAST = bool(int(os.environ.get('KFAST', '0')))
    TILES = [16] if FAST else list(range(NT))

    stA = contextlib.ExitStack(); sbA = scoped(stA)
    w1A = sbA("w1A", [128, 8, 672], BF16); b_w1A = [Buf() for _ in range(8)]
    w_uq_b = sbA("w_uq_b", [128, 3, 768], BF16); b_wuq = Buf()
    cqnT2 = [sbA(f"cqnT{i}", [128, 3, 128], BF16) for i in range(2)]; b_cqnT2 = [Buf(), Buf()]
    stg_ = [sbA(f"wstg{i}", [128, 768], F32) for i in range(2)]; b_stg_ = [Buf(), Buf()]
    for c in range(8):
        stg = stg_[c % 2]; b_stg = b_stg_[c % 2]
        k.dma(k.q_sp, stg[:, 0:672], w_in[c * 128:(c + 1) * 128, 0:672], w=[b_stg])
        k.actf(w1A[:, c, :], stg[:, 0:672], AF.Copy, r=[b_stg, b_g1], w=[b_w1A[c]], scale=g1col[:, c:c + 1])
    for c in range(3):
        stg = stg_[c % 2]; b_stg = b_stg_[c % 2]
        k.dma(k.q_sp, stg[:, 0:768], w_uq[c * 128:(c + 1) * 128, :], w=[b_stg])
        k.actf(w_uq_b[:, c, :], stg[:, 0:768], AF.Copy, r=[b_stg, b_gq], w=[b_wuq], scale=gqcol[:, c:c + 1])
    _al = {"o": 8 * TT}

    def alias(nelem_bf16, dt, shape):
        a = big[:, _al["o"]:_al["o"] + nelem_bf16]; _al["o"] += nelem_bf16
        assert _al["o"] <= 16 * TT
        if dt == F32:
            a = a.bitcast(F32)
        if len(shape) == 4:
            a = a.rearrange("p (a b c) -> p a b c", a=shape[1], b=shape[2])
        return a
    NB1 = 4
    xt = [sbA(f"xt{i}", [128, D], F32) for i in range(2)] + [alias(2 * D, F32, [128, D]) for _ in range(2)]; b_xt = [Buf() for _ in range(NB1)]
    xn = [sbA(f"xn{i}", [128, D], BF16) for i in range(2)] + [alias(D, BF16, [128, D]) for _ in range(2)]; b_xn = [Buf() for _ in range(NB1)]
    junk = sbA("junk", [128, D], BF16); b_junk = Buf()
    st1 = [sbA(f"st1_{i}", [128, 8], F32) for i in range(NB1)]; b_st1 = [Buf() for _ in range(NB1)]
    oA = [sbA(f"oA{i}", [128, 288], F32) for i in range(2)] + [alias(576, F32, [128, 288]) for _ in range(2)]; b_oA = [Buf() for _ in range(NB1)]
    tb = [sbA(f"tb{i}", [128, 672], BF16) for i in range(2)] + [alias(672, BF16, [128, 672]) for _ in range(2)]; b_tb = [Buf() for _ in range(NB1)]
    qtok = [sbA(f"qtok{i}", [128, 800], BF16) for i in range(2)] + [alias(800, BF16, [128, 800]) for _ in range(2)]; b_qtok = [Buf() for _ in range(NB1)]
    rtmp = [sbA(f"rtmp{i}", [128, 4, 8, 16], F32) for i in range(2)] + [alias(1024, F32, [128, 4, 8, 16]) for _ in range(2)]; b_rtmp = [Buf() for _ in range(NB1)]

    for t in TILES:
        t0, n = tile_rows(t)
        s = t % NB1
        src = xp[t0:t0 + n, :] if t < 16 else xs[:, :]
        gA, gB = (0, 1) if t % 2 == 0 else (2, 3)
        psA_, psB_ = ps[gA], ps[gB]; bA_, bB_ = b_ps[gA], b_ps[gB]
        k.dma(k.q_sp, xt[s][0:n, :], src, w=[b_xt[s]])
        k.actf(junk[0:n, :], xt[s][0:n, :], AF.Square, r=[b_xt[s]], w=[b_junk, b_st1[s]], accum=st1[s][0:n, 0:1])
        k.actf(st1[s][0:n, 1:2], st1[s][0:n, 0:1], AF.Sqrt, r=[b_st1[s]], w=[b_st1[s]], scale=1.0 / D, bias=EPS)
        k.v("dve", lambda: nc.vector.reciprocal(out=st1[s][0:n, 2:3], in_=st1[s][0:n, 1:2]), r=[b_st1[s]], w=[b_st1[s]])
        k.v("dve", lambda: nc.vector.tensor_scalar(out=xn[s][0:n, :], in0=xt[s][0:n, :], scalar1=st1[s][0:n, 2:3],
                                                    scalar2=None, op0=ALU.mult), r=[b_xt[s], b_st1[s]], w=[b_xn[s]])
        psb = psbv2[0]; b_psb = b_ps[7]
        for c in range(8):
            k.tr(psb[:, c * 128:c * 128 + n], xn[s][0:n, c * 128:(c + 1) * 128], identb[0:n, 0:n],
                 r=[b_xn[s], b_identb], w=[b_psb], sig=(c == 7))
        k.v("dve", lambda: nc.vector.tensor_copy(
            out=hT[:, :, t0:t0 + n], in_=psb[:, :].rearrange("p (c n) -> p c n", c=8)[:, :, 0:n]),
            r=[b_psb], w=[b_hT[t]])
        for (pb, c0, c1) in ((gA, 0, 512), (gB, 512, 672)):
            for c in range(8):
                k.mm(ps[pb][0:n, 0:c1 - c0], hT[:, c, t0:t0 + n], w1A[:, c, c0:c1], start=(c == 0), stop=(c == 7),
                     r=[b_hT[t], b_w1A[c]], w=[b_ps[pb]])
        k.actf(junk[0:n, 0:384], psA_[0:n, 0:384], AF.Square, r=[bA_], w=[b_junk, b_st1[s]], accum=st1[s][0:n, 3:4])
        k.actf(st1[s][0:n, 4:5], st1[s][0:n, 3:4], AF.Sqrt, r=[b_st1[s]], w=[b_st1[s]], scale=1.0 / 384, bias=EPS)
        k.v("dve", lambda: nc.vector.reciprocal(out=st1[s][0:n, 4:5], in_=st1[s][0:n, 4:5]), r=[b_st1[s]], w=[b_st1[s]])
        k.v("dve", lambda: nc.vector.tensor_scalar(out=tb[s][0:n, 0:384], in0=psA_[0:n, 0:384],
                                                    scalar1=st1[s][0:n, 4:5], scalar2=None, op0=ALU.mult),
            r=[bA_, b_st1[s]], w=[b_tb[s]])
        k.actf(junk[0:n, 0:128], psA_[0:n, 384:512], AF.Square, r=[bA_], w=[b_junk, b_st1[s]], accum=st1[s][0:n, 5:6])
        k.actf(junk[0:n, 0:128], psB_[0:n, 0:128], AF.Square, r=[bB_], w=[b_junk, b_st1[s]], accum=st1[s][0:n, 6:7])
        k.v("dve", lambda: nc.vector.tensor_tensor(out=st1[s][0:n, 5:6], in0=st1[s][0:n, 5:6], in1=st1[s][0:n, 6:7],
                                                    op=ALU.add), r=[b_st1[s]], w=[b_st1[s]])
        k.actf(st1[s][0:n, 6:7], st1[s][0:n, 5:6], AF.Sqrt, r=[b_st1[s]], w=[b_st1[s]], scale=1.0 / 256, bias=EPS)
        k.v("dve", lambda: nc.vector.reciprocal(out=st1[s][0:n, 6:7], in_=st1[s][0:n, 6:7]), r=[b_st1[s]], w=[b_st1[s]])
        k.v("dve", lambda: nc.vector.scalar_tensor_tensor(out=oA[s][0:n, 0:128], in0=psA_[0:n, 384:512],
                                                           scalar=st1[s][0:n, 6:7], in1=gkv_b[0:n, 0:128],
                                                           op0=ALU.mult, op1=ALU.mult),
            r=[bA_, b_st1[s], b_gkv], w=[b_oA[s]])
        k.v("dve", lambda: nc.vector.scalar_tensor_tensor(out=oA[s][0:n, 128:256], in0=psB_[0:n, 0:128],
                                                           scalar=st1[s][0:n, 6:7], in1=gkv_b[0:n, 128:256],
                                                           op0=ALU.mult, op1=ALU.mult),
            r=[bB_, b_st1[s], b_gkv], w=[b_oA[s]])
        cos = cs_sb[0:n, t, 0:16]; sin = cs_sb[0:n, t, 16:32]
        x1 = psB_[0:n, 128:144]; x2 = psB_[0:n, 144:160]
        rt = rtmp[s]
        k.v("dve", lambda: nc.vector.tensor_tensor(out=rt[0:n, 0, 0, :], in0=x1, in1=cos, op=ALU.mult), r=[bB_, b_cs], w=[b_rtmp[s]])
        k.v("dve", lambda: nc.vector.tensor_tensor(out=rt[0:n, 1, 0, :], in0=x2, in1=sin, op=ALU.mult), r=[bB_, b_cs], w=[b_rtmp[s]])
        k.v("dve", lambda: nc.vector.tensor_tensor(out=rt[0:n, 2, 0, :], in0=x1, in1=sin, op=ALU.mult), r=[bB_, b_cs], w=[b_rtmp[s]])
        k.v("dve", lambda: nc.vector.tensor_tensor(out=rt[0:n, 3, 0, :], in0=x2, in1=cos, op=ALU.mult), r=[bB_, b_cs], w=[b_rtmp[s]])
        k.v("dve", lambda: nc.vector.tensor_tensor(out=oA[s][0:n, 256:272], in0=rt[0:n, 0, 0, :], in1=rt[0:n, 1, 0, :], op=ALU.subtract), r=[b_rtmp[s]], w=[b_oA[s]])
        k.v("dve", lambda: nc.vector.tensor_tensor(out=oA[s][0:n, 272:288], in0=rt[0:n, 2, 0, :], in1=rt[0:n, 3, 0, :], op=ALU.add), r=[b_rtmp[s]], w=[b_oA[s]])
        if t < 16:
            k.dma(k.q_pool, o_pckv[t0:t0 + n, :], oA[s][0:n, 0:256], r=[b_oA[s]])
            k.dma(k.q_pool, o_pkr[t0:t0 + n, :], oA[s][0:n, 256:288], r=[b_oA[s]])
        else:
            k.dma(k.q_pool, o_sckv[:, :], oA[s][0:n, 0:256], r=[b_oA[s]])
            k.dma(k.q_pool, o_skr[:, :], oA[s][0:n, 256:288], r=[b_oA[s]])
        k.actf(tb[s][0:n, 384:672], oA[s][0:n, 0:288], AF.Copy, r=[b_oA[s]], w=[b_tb[s]])
        psb = psbv2[1]; b_psb = b_ps[6]
        for j in range(5):
            k.tr(psb[:, j * 128:j * 128 + n], tb[s][0:n, j * 128:(j + 1) * 128], identb[0:n, 0:n],
                 r=[b_tb[s], b_identb], w=[b_psb], sig=False)
        k.tr(psb[0:32, 640:640 + n], tb[s][0:n, 640:672], identb[0:n, 0:n], r=[b_tb[s], b_identb], w=[b_psb], sig=True)
        cqnT = cqnT2[t % 2]
        k.v("dve", lambda: nc.vector.tensor_copy(out=cqnT[:, :, 0:n],
                                                  in_=psb[:, 0:384].rearrange("p (c n) -> p c n", c=3)[:, :, 0:n]),
            r=[b_psb], w=[b_cqnT2[t % 2]])
        k.v("dve", lambda: nc.vector.tensor_copy(out=ckvT[:, :, t0:t0 + n],
                                                  in_=psb[:, 384:640].rearrange("p (c n) -> p c n", c=2)[:, :, 0:n]),
            r=[b_psb], w=[b_ckvT[t]])
        k.v("dve", lambda: nc.vector.tensor_copy(out=kropeT[:, t0:t0 + n], in_=psb[0:32, 640:640 + n]), r=[b_psb], w=[b_krT[t]])
        for (pb, c0, c1) in ((4, 0, 512), (5, 512, 768)):
            for c in range(3):
                k.mm(ps[pb][0:n, 0:c1 - c0], cqnT[:, c, 0:n], w_uq_b[:, c, c0:c1], start=(c == 0), stop=(c == 2),
                     r=[b_cqnT2[t % 2], b_wuq], w=[b_ps[pb]])
        k.actf(qtok[s][0:n, 0:512], ps[4][0:n, 0:512], AF.Copy, r=[b_ps[4]], w=[b_qtok[s]])
        k.actf(qtok[s][0:n, 512:768], ps[5][0:n, 0:256], AF.Copy, r=[b_ps[5]], w=[b_qtok[s]])
        for (pb, h0, nh, base) in ((4, 0, 5, 64), (5, 5, 3, 5 * 96 + 64 - 512)):
            pv4 = ps[pb][0:n, base:base + (nh - 1) * 96 + 32]
            def hv(off):
                return bass.AP(tensor=pv4.tensor, offset=pv4.offset + off, ap=[list(pv4.ap[0]), [96, nh], [1, 16]])
            x1h = hv(0); x2h = hv(16)
            cosb = cs_sb[0:n, t, 0:16].unsqueeze(1).to_broadcast([n, nh, 16]); sinb = cs_sb[0:n, t, 16:32].unsqueeze(1).to_broadcast([n, nh, 16])
            k.v("dve", lambda: nc.vector.tensor_tensor(out=rt[0:n, 0, h0:h0 + nh, :], in0=x1h, in1=cosb, op=ALU.mult), r=[b_ps[pb], b_cs], w=[b_rtmp[s]])
            k.v("dve", lambda: nc.vector.tensor_tensor(out=rt[0:n, 1, h0:h0 + nh, :], in0=x2h, in1=sinb, op=ALU.mult), r=[b_ps[pb], b_cs], w=[b_rtmp[s]])
            k.v("dve", lambda: nc.vector.tensor_tensor(out=rt[0:n, 2, h0:h0 + nh, :], in0=x1h, in1=sinb, op=ALU.mult), r=[b_ps[pb], b_cs], w=[b_rtmp[s]])
            k.v("dve", lambda: nc.vector.tensor_tensor(out=rt[0:n, 3, h0:h0 + nh, :], in0=x2h, in1=cosb, op=ALU.mult), r=[b_ps[pb], b_cs], w=[b_rtmp[s]])
        qv = qtok[s][0:n, 0:768].rearrange("p (h e) -> p h e", h=8)
        k.v("dve", lambda: nc.vector.tensor_tensor(out=qv[:, :, 64:80], in0=rt[0:n, 0, :, :], in1=rt[0:n, 1, :, :], op=ALU.subtract), r=[b_rtmp[s]], w=[b_qtok[s]])
        k.v("dve", lambda: nc.vector.tensor_tensor(out=qv[:, :, 80:96], in0=rt[0:n, 2, :, :], in1=rt[0:n, 3, :, :], op=ALU.add), r=[b_rtmp[s]], w=[b_qtok[s]])
        psb = psbv2[0]; b_psb = b_ps[7]
        for h in range(8):
            k.tr(psb[:, h * 128:h * 128 + n], qtok[s][0:n, h * 96:h * 96 + 128], identb[0:n, 0:n],
                 r=[b_qtok[s], b_identb], w=[b_psb], sig=(h == 7))
        k.v("dve", lambda: nc.vector.tensor_copy(out=qmT[0:96, :, t0:t0 + n],
                                                  in_=psb[0:96, :].rearrange("p (c n) -> p c n", c=8)[:, :, 0:n]),
            r=[b_psb], w=[b_qmT[t]])
    k.v("dve", lambda: nc.vector.tensor_copy(out=smp_ckvn[0:TS, 0:256], in_=tb[0][0:TS, 384:640]), r=[b_tb[0]], w=[b_smp])
    k.v("dve", lambda: nc.vector.tensor_copy(out=smp_ckvT[:, :, :], in_=ckvT[:, :, T:TT]), r=[b_ckvT[16]], w=[b_smp])
    k.v("dve", lambda: nc.vector.tensor_copy(out=smp_krT[:, :], in_=kropeT[:, T:TT]), r=[b_krT[16]], w=[b_smp])
    k.v("dve", lambda: nc.vector.tensor_copy(out=smp_qm[:, :, :], in_=qmT[:, :, T:TT]), r=[b_qmT[16]], w=[b_smp])
    k.barrier()
    stA.close()

    stB = contextlib.ExitStack(); sbB = scoped(stB)
    kcT = sbB("kcT", [128, TT], BF16); vcT = sbB("vcT", [128, TT], BF16)
    ksT = sbB("ksT", [128, TT], BF16); kwT = sbB("kwT", [128, TT], BF16)
    b_kvT = [Buf() for _ in range(NT)]
    stB1 = contextlib.ExitStack(); sbB1 = scoped(stB1)
    w2A = sbB1("w2A", [128, 8, 768], BF16); b_w2A = [Buf() for _ in range(8)]
    stg2_ = [sbB1(f"wstg2_{i}", [128, 768], F32) for i in range(3)]; b_stg2_ = [Buf(), Buf(), Buf()]
    oB = [sbB1(f"oB{i}", [128, 768], F32) for i in range(2)]; b_oB = [Buf(), Buf()]
    tb2 = [sbB1(f"tb2_{i}", [128, 768], BF16) for i in range(2)]; b_tb2 = [Buf(), Buf()]
    for c in range(8):
        stg2 = stg2_[c % 3]; b_stg2 = b_stg2_[c % 3]
        k.dma(k.q_sp, stg2[:, :], w_in[c * 128:(c + 1) * 128, C_KVC:C_KVC + 768], w=[b_stg2])
        k.actf(w2A[:, c, :], stg2[:, :], AF.Copy, r=[b_stg2, b_g1], w=[b_w2A[c]], scale=g1col[:, c:c + 1])
    for t in TILES:
        t0, n = tile_rows(t)
        s = t % 2
        g2a, g2b = (2, 3) if t % 2 == 0 else (0, 1)
        psb = psbv2[t % 2]; b_psb = b_ps[7 - (t % 2)]
        for (pb, c0, c1) in ((g2a, 0, 512), (g2b, 512, 768)):
            for c in range(8):
                k.mm(ps[pb][0:n, 0:c1 - c0], hT[:, c, t0:t0 + n], w2A[:, c, c0:c1], start=(c == 0), stop=(c == 7),
                     r=[b_hT[t], b_w2A[c]], w=[b_ps[pb]])
        k.actf(oB[s][0:n, 0:512], ps[g2a][0:n, 0:512], AF.Copy, r=[b_ps[g2a]], w=[b_oB[s]])
        k.actf(oB[s][0:n, 512:768], ps[g2b][0:n, 0:256], AF.Copy, r=[b_ps[g2b]], w=[b_oB[s]])
        pre = "p_" if t < 16 else "s_"
        rows = slice(t0, t0 + n) if t < 16 else slice(0, n)
        for i, nm in enumerate(("cmp_k", "cmp_v", "slc_k", "slc_v")):
            k.dma(k.q_pool, o_kv[pre + nm][rows, :], oB[s][0:n, i * 128:(i + 1) * 128], r=[b_oB[s]])
        if 12 <= t < 16:
            k.dma(k.q_pool, o_pwk[(t - 12) * 128:(t - 11) * 128, :], oB[s][0:n, 512:640], r=[b_oB[s]])
            k.dma(k.q_pool, o_pwv[(t - 12) * 128:(t - 11) * 128, :], oB[s][0:n, 640:768], r=[b_oB[s]])
        if t == 16:
            for b in range(4):
                k.dma(k.q_pool, o_swk[b, 508:512, :], oB[s][4 * b:4 * b + 4, 512:640], r=[b_oB[s]])
                k.dma(k.q_pool, o_swv[b, 508:512, :], oB[s][4 * b:4 * b + 4, 640:768], r=[b_oB[s]])
                k.dma(k.q_sp, o_swk[b, 0:508, :], win_k_in[b, 4:512, :])
                k.dma(k.q_sp, o_swv[b, 0:508, :], win_v_in[b, 4:512, :])
        k.v("dve", lambda: nc.vector.tensor_copy(out=tb2[s][0:n, :], in_=oB[s][0:n, 0:768]), r=[b_oB[s]], w=[b_tb2[s]])
        k.v("pool", lambda: nc.gpsimd.tensor_copy(out=vs_nat[0:n, t, :, 0:64],
                                                  in_=tb2[s][0:n, 384:512].rearrange("p (g d) -> p g d", g=2)),
            r=[b_tb2[s]], w=[b_vnat[t]])
        k.v("pool", lambda: nc.gpsimd.tensor_copy(out=vw_nat[0:n, t, :, 0:64],
                                                  in_=tb2[s][0:n, 640:768].rearrange("p (g d) -> p g d", g=2)),
            r=[b_tb2[s]], w=[b_vnat[t]])
        for j, c0 in enumerate((0, 128, 256, 512)):
            k.tr(psb[:, j * 128:j * 128 + n], tb2[s][0:n, c0:c0 + 128], identb[0:n, 0:n],
                 r=[b_tb2[s], b_identb], w=[b_psb], sig=(j == 3))
        for j, dst in enumerate((kcT, vcT, ksT, kwT)):
            k.v("dve", lambda: nc.vector.tensor_copy(out=dst[:, t0:t0 + n], in_=psb[:, j * 128:j * 128 + n]), r=[b_psb], w=[b_kvT[t]])

    k.v("dve", lambda: nc.vector.tensor_copy(out=smp_vs[0:TS, 0:128], in_=tb2[0][0:TS, 384:512]), r=[b_tb2[0]], w=[b_smp])
    k.v("dve", lambda: nc.vector.tensor_copy(out=smp_vw[0:TS, 0:128], in_=tb2[0][0:TS, 640:768]), r=[b_tb2[0]], w=[b_smp])
    k.v("dve", lambda: nc.vector.tensor_copy(out=smp_ksT[:, :], in_=ksT[:, T:TT]), r=[b_kvT[16]], w=[b_smp])
    k.v("dve", lambda: nc.vector.tensor_copy(out=smp_kwT[:, :], in_=kwT[:, T:TT]), r=[b_kvT[16]], w=[b_smp])
    k.barrier()
    stB1.close()
    W1 = {kv: sbB(f"W1{kv}", [128, 32, 128], BF16) for kv in "kv"}; b_W1 = {kv: Buf() for kv in "kv"}
    W2 = {kv: sbB(f"W2{kv}", [128, 128], BF16) for kv in "kv"}; b_W2 = {kv: Buf() for kv in "kv"}
    posT = {kv: sbB(f"posT{kv}", [128, 32], BF16) for kv in "kv"}; b_posT = {kv: Buf() for kv in "kv"}
    cbias = {kv: sbB(f"cbias{kv}", [128, 1], F32) for kv in "kv"}; b_cb = {kv: Buf() for kv in "kv"}
    hidT = {kv: sbB(f"hidT{kv}", [128, 128], BF16) for kv in "kv"}; b_hid = {kv: Buf() for kv in "kv"}
    cw = [sbB(f"cwk{i}", [128, 128], F32) for i in range(3)]; b_cw = [Buf() for _ in range(3)]
    krows = sbB("krows_sb", [4, T], BF16); b_krows = Buf()
    k.dma(k.q_pool, krows[:, :], krows_d[:, :], w=[b_krows])
    for g in range(0 if FAST else 2):
        for (KA, bK, srcT) in ((KAs, b_KAs, ksT), (KAw, b_KAw, kwT)):
            k.v("pool", lambda: nc.gpsimd.memset(KA[g][64:128, :], 0.0), w=[bK[g]])
            k.v("dve", lambda: nc.vector.tensor_copy(out=KA[g][0:64, :], in_=srcT[g * 64:(g + 1) * 64, 0:T]), r=b_kvT, w=[bK[g]])
            k.v("dve", lambda: nc.vector.tensor_copy(out=KA[g][64:68, :], in_=krows[0:4, :]), r=[b_krows], w=[bK[g]])
        k.dma(k.q_pool, KAs[g][96:128, :], erows_d[:, :], w=[b_KAs[g]])
        k.v("pool", lambda: nc.gpsimd.memset(kccA[g][:, :], 0.0), w=[b_kccA[g]])
        k.dma(k.q_pool, kccA[g][64:68, :], kcrows_d[:, :], w=[b_kccA[g]])
        k.v("pool", lambda: nc.gpsimd.memset(vccA[g][:, :], 1.0), w=[b_vccA[g]])
    with nc.allow_non_contiguous_dma(reason="small weight relayout"):
        for kv in "kv":
            k.v("pool", lambda: nc.gpsimd.memset(W1[kv][:, :, :], 0.0), w=[b_W1[kv]])
            k.v("pool", lambda: nc.gpsimd.memset(W2[kv][:, :], 0.0), w=[b_W2[kv]])
            w1v = cw1[kv].rearrange("(s d) h -> d s h", d=64)
            for g in range(2):
                k.dma(k.q_pool, W1[kv][g * 64:(g + 1) * 64, :, g * 64:(g + 1) * 64], w1v, w=[b_W1[kv]])
                k.dma(k.q_pool, W2[kv][g * 64:(g + 1) * 64, g * 64:(g + 1) * 64], cw2[kv][:, :], w=[b_W2[kv]])
                k.dma(k.q_pool, posT[kv][g * 64:(g + 1) * 64, :], cpos[kv].rearrange("s d -> d s"), w=[b_posT[kv]])
    for kv, srcT in ([] if FAST else (("k", kcT), ("v", vcT))):
        for s_ in range(32):
            k.mm(ps[6][:, 0:1], W1[kv][:, s_, :], posT[kv][:, s_:s_ + 1], start=(s_ == 0), stop=(s_ == 31),
                 r=[b_W1[kv], b_posT[kv]], w=[b_ps[6]])
        k.v("dve", lambda: nc.vector.tensor_copy(out=cbias[kv][:, :], in_=ps[6][:, 0:1]), r=[b_ps[6]], w=[b_cb[kv]])
        sv = srcT[:, 0:T].rearrange("p (c s) -> p c s", s=16)
        for half in range(2):
            pb = 4 + half
            for s_ in range(16):
                k.mm(ps[pb][:, 0:128], W1[kv][:, half * 16 + s_, :], sv[:, :, s_], start=(s_ == 0), stop=(s_ == 15),
                     r=[b_W1[kv]] + b_kvT, w=[b_ps[pb]])
        k.v("dve", lambda: nc.vector.tensor_copy(out=cw[0][:, 0:127], in_=ps[5][:, 1:128]), r=[b_ps[5]], w=[b_cw[0]])
        k.v("dve", lambda: nc.vector.scalar_tensor_tensor(out=cw[0][:, 0:127], in0=ps[4][:, 0:127], scalar=cbias[kv][:, 0:1],
                                                           in1=cw[0][:, 0:127], op0=ALU.add, op1=ALU.add),
            r=[b_ps[4], b_cb[kv], b_cw[0]], w=[b_cw[0]])
        k.v("dve", lambda: nc.vector.tensor_tensor(out=cw[1][:, 0:127], in0=cw[0][:, 0:127], in1=cw[0][:, 0:127], op=ALU.mult), r=[b_cw[0]], w=[b_cw[1]])
        k.v("dve", lambda: nc.vector.tensor_scalar(out=cw[1][:, 0:127], in0=cw[1][:, 0:127], scalar1=0.044715, scalar2=1.0, op0=ALU.mult, op1=ALU.add), r=[b_cw[1]], w=[b_cw[1]])
        k.v("dve", lambda: nc.vector.tensor_tensor(out=cw[1][:, 0:127], in0=cw[1][:, 0:127], in1=cw[0][:, 0:127], op=ALU.mult), r=[b_cw[0], b_cw[1]], w=[b_cw[1]])
        k.actf(cw[2][:, 0:127], cw[1][:, 0:127], AF.Sigmoid, r=[b_cw[1]], w=[b_cw[2]], scale=1.5957691216)
        k.v("dve", lambda: nc.vector.tensor_tensor(out=hidT[kv][:, 0:127], in0=cw[2][:, 0:127], in1=cw[0][:, 0:127], op=ALU.mult), r=[b_cw[0], b_cw[2]], w=[b_hid[kv]])
        if kv == "k":
            k.mm(ps[6][:, 0:127], W2["k"][:, :], hidT["k"][:, 0:127], start=True, stop=True, r=[b_W2["k"], b_hid["k"]], w=[b_ps[6]])
            for g in range(2):
                k.v("dve", lambda: nc.vector.tensor_copy(out=kccA[g][0:64, 0:127], in_=ps[6][g * 64:(g + 1) * 64, 0:127]), r=[b_ps[6]], w=[b_kccA[g]])
        else:
            k.mm(ps[6][0:127, 0:128], hidT["v"][:, 0:127], W2["v"][:, :], start=True, stop=True, r=[b_W2["v"], b_hid["v"]], w=[b_ps[6]])
            for g in range(2):
                k.v("dve", lambda: nc.vector.tensor_copy(out=vccA[g][0:127, 0:64], in_=ps[6][0:127, g * 64:(g + 1) * 64]), r=[b_ps[6]], w=[b_vccA[g]])
    k.barrier()
    stB.close()
    def attn_chunk(st, sched, Kt, Qr, Vt, scale, finish_fn):
        nS = len(sched)
        acc_i = st["acc_i"]; st["acc_i"] = 1 - acc_i
        acc = ps[2 + acc_i]; b_acc = b_ps[2 + acc_i]
        sb_ = st["s_banks"]
        pend = []
        nPT = len(st["PT"])
        for i, (kt, c_lo, c_hi, mask, nk) in enumerate(sched):
            si = st["s_i"]; st["s_i"] = (si + 1) % len(sb_)
            psS = ps[sb_[si]]; b_S = b_ps[sb_[si]]
            lhsT, rl = Kt(kt)
            rhs, rr = Qr(c_lo, c_hi)
            k.mm(psS[0:nk, c_lo:c_hi], lhsT, rhs, start=True, stop=(mask is None), r=rl + rr, w=[b_S])
            if mask is not None:
                m_ap, m_lo, rm = mask
                k.mm(psS[0:nk, m_lo:m_lo + 128], identb[0:nk, 0:nk], m_ap, start=False, stop=True,
                     r=[b_identb] + rm, w=[b_S])
            pi = st["p_i"]; st["p_i"] = (pi + 1) % nPT
            PT = st["PT"][pi]; b_PT = st["b_PT"][pi]
            k.actf(PT[0:nk, c_lo:c_hi], psS[0:nk, c_lo:c_hi], AF.Exp, r=[b_S], w=[b_PT], scale=scale)

            def mk(kt=kt, c_lo=c_lo, c_hi=c_hi, PT=PT, b_PT=b_PT, i=i, nk=nk):
                v_ap, rv = Vt(kt)
                k.mm(acc[:, c_lo:c_hi], v_ap, PT[0:nk, c_lo:c_hi], start=(i == 0), stop=(i == nS - 1),
                     r=[b_PT] + rv, w=[b_acc])
            pend.append(mk)
            if len(pend) > 2:
                pend.pop(0)()
        for f in pend:
            f()
        finish_fn(acc, b_acc)

    stM = contextlib.ExitStack()

    def sbM(name, shape, dt):
        return stM.enter_context(nc.sbuf_tensor(name, shape, dt))
    w_uk_b = sbM("w_uk_b", [128, 2, 512], BF16); b_wuk = Buf()
    w_uv_b = sbM("w_uv_b", [128, 2, 512], BF16); b_wuv = Buf()
    KhT = [sbM(f"KhT{i}", [96, T], BF16) for i in range(2)]; b_KhT = [Buf(), Buf()]
    Vh = [sbM(f"Vh{i}", [128, 16, 128], BF16) for i in range(2)]; b_Vh = [Buf(), Buf()]
    PTs = [sbM(f"PT{i}", [128, 512], BF16) for i in range(4)]; b_PTs = [Buf() for _ in range(4)]
    rz = [sbM(f"rz{i}", [64, 512], F32) for i in range(2)]; b_rz = [Buf(), Buf()]
    ast = {"acc_i": 0, "s_i": 0, "p_i": 0, "PT": PTs, "b_PT": b_PTs, "rz_i": 0, "s_banks": [0, 1, 6]}
    for c in range(2):
        k.dma(k.q_pool, w_uk_b[:, c, :], w_uk[c * 128:(c + 1) * 128, :], w=[b_wuk])
        k.dma(k.q_pool, w_uv_b[:, c, :], w_uv[c * 128:(c + 1) * 128, :], w=[b_wuv])
    for i in range(2):
        k.v("pool", lambda: nc.gpsimd.memset(Vh[i][:, :, 64:128], 1.0), w=[b_Vh[i]])
        if not FAST:
            k.actf(KhT[i][64:96, :], kropeT[0:32, 0:T], AF.Copy, r=b_krT, w=[b_KhT[i]])

    def norm_out(dstT, h, qc):
        def fin(acc, b_acc):
            ri = ast["rz_i"]; ast["rz_i"] = 1 - ri
            k.v("dve", lambda: nc.vector.reciprocal(out=rz[ri][0:64, :], in_=acc[64:128, :]), r=[b_acc], w=[b_rz[ri]])
            p0 = (h % 2) * 64
            k.v("dve", lambda: nc.vector.tensor_tensor(out=dstT[p0:p0 + 64, h // 2, qc * 512:(qc + 1) * 512],
                                                        in0=acc[0:64, :], in1=rz[ri][0:64, :], op=ALU.mult),
                r=[b_acc, b_rz[ri]], w=[b_omla])
        return fin

    def causal_sched(qc):
        out = []
        for kt in range(4 * qc + 4):
            c_lo = max(0, kt * 128 - qc * 512)
            mask = (trimask[:, 0, :], c_lo, [b_tri]) if kt * 128 >= qc * 512 else None
            out.append((kt, c_lo, 512, mask, 128))
        return out

    NH_M = 0 if FAST else int(os.environ.get("KNH", "8"))
    for h in range(NH_M):
        s = h % 2
        for qc in range(4):
            pb = 4 + (qc % 2)
            for c in range(2):
                k.mm(ps[pb][0:64, :], w_uk_b[:, c, h * 64:(h + 1) * 64], ckvT[:, c, qc * 512:(qc + 1) * 512],
                     start=(c == 0), stop=(c == 1), r=[b_wuk] + b_ckvT, w=[b_ps[pb]])
            k.v("dve", lambda: nc.vector.tensor_copy(out=KhT[s][0:64, qc * 512:(qc + 1) * 512], in_=ps[pb][0:64, :]),
                r=[b_ps[pb]], w=[b_KhT[s]])
        for g8 in range(2):
            pb = 4 + g8
            for j in range(8):
                kt = g8 * 8 + j
                for c in range(2):
                    k.mm(ps[pb][:, j * 64:(j + 1) * 64], ckvT[:, c, kt * 128:(kt + 1) * 128], w_uv_b[:, c, h * 64:(h + 1) * 64],
                         start=(c == 0), stop=(c == 1), r=[b_wuv] + b_ckvT, w=[b_ps[pb]], sig=(c == 1 and j == 7))
            k.v("dve", lambda: nc.vector.tensor_copy(out=Vh[s][:, g8 * 8:(g8 + 1) * 8, 0:64],
                                                      in_=ps[pb][:, :].rearrange("p (j d) -> p j d", j=8)),
                r=[b_ps[pb]], w=[b_Vh[s]])
        for qc in range(4):
            attn_chunk(ast, causal_sched(qc),
                       Kt=lambda kt: (KhT[s][0:96, kt * 128:(kt + 1) * 128], [b_KhT[s]]),
                       Qr=lambda a, b_: (qmT[0:96, h, qc * 512 + a:qc * 512 + b_], b_qmT),
                       Vt=lambda kt: (Vh[s][:, kt, :], [b_Vh[s]]),
                       scale=MLA_SCALE, finish_fn=norm_out(o_mlaT, h, qc))
    if DBG:
        k.dma(k.q_pool, dbg_omla[:, :, :], o_mlaT[:, :, :], r=[b_omla])
    k.barrier()
    stM.close()
    stL2.close()

    stN = contextlib.ExitStack()

    def sbN(name, shape, dt):
        return stN.enter_context(nc.sbuf_tensor(name, shape, dt))
    QA = [sbN(f"QA{h}", [128, TT], BF16) for h in range(8)]; b_QA = [Buf() for _ in range(8)]
    w_qn = sbN("w_qn", [128, 8, 536], BF16); b_wqn = Buf()
    stgN_ = [sbN("stgN0", [128, 536], F32)] * 2; b_stgN_ = [Buf()] * 2
    gnT = sbN("gnT", [24, TT], BF16); b_gnT = Buf()
    selb = sbN("selb_sb", [24, 3, 8, 64], BF16); b_selb = Buf()
    qrows = sbN("qrows_sb", [4, T], BF16); b_qrows = Buf()
    cmpmask = sbN("cmpmask_sb", [128, T], BF16); b_cmask = Buf()
    overlapM = sbN("overlap_sb", [128, 32], BF16); b_ovl = Buf()
    topkm = sbN("topkm_sb", [128, 16, 32], F32); topka = sbN("topka_sb", [128, 16, 32], F32); b_topk = Buf()
    scoreT = [sbN(f"scoreT{g}", [32, T], F32) for g in range(2)]; b_sc = [Buf(), Buf()]
    PTn = [sbN(f"PTn{i}", [128, 512], BF16) for i in range(4)]; b_PTn = [Buf() for _ in range(4)]
    rzn = [sbN(f"rzn{i}", [64, 512], F32) for i in range(2)]; b_rzn = [Buf(), Buf()]
    tmpn = [sbN(f"tmpn{i}", [128, 512], F32) for i in range(2)]; b_tmpn = [Buf(), Buf()]
    tk = sbN("tk", [128, 6, 64], F32); b_tk = Buf()
    tmpU = sbN("tmpU", [32, 512], F32); b_tmpU = Buf()
    tkb = sbN("tkb", [128, 64], BF16); b_tkb = Buf()
    nst = {"acc_i": 0, "s_i": 0, "p_i": 0, "PT": PTn, "b_PT": b_PTn, "rz_i": 0, "s_banks": [0, 1, 4]}

    k.dma(k.q_pool, selb[:, :, :, :], selb_d[:, :, :, :], w=[b_selb])
    k.dma(k.q_pool, qrows[:, :], qrows_d[:, :], w=[b_qrows])
    k.dma(k.q_pool, cmpmask[:, :], cmpmask_d[:, :], w=[b_cmask])
    k.dma(k.q_pool, overlapM[:, :], overlap_d[:, :], w=[b_ovl])
    k.dma(k.q_sp, topkm[:, :, :], topkm_d[:, :, :], w=[b_topk])
    k.dma(k.q_sp, topka[:, :, :], topka_d[:, :, :], w=[b_topk])
    for h in range(8):
        k.v("pool", lambda: nc.gpsimd.memset(QA[h][64:128, :], 0.0), w=[b_QA[h]])
        k.actf(QA[h][64:68, 0:T], qrows[0:4, :], AF.Copy, r=[b_qrows], w=[b_QA[h]], scale=float(SLOPES[h]))
    for c in range(8):
        stgN = stgN_[c % 2]; b_stgN = b_stgN_[c % 2]
        k.dma(k.q_sp, stgN[:, 0:512], w_in[c * 128:(c + 1) * 128, C_QN:C_QN + 512], w=[b_stgN])
        k.dma(k.q_sp, stgN[:, 512:536], w_in[c * 128:(c + 1) * 128, C_GN:C_GN + 24], w=[b_stgN])
        k.actf(w_qn[:, c, :], stgN[:, :], AF.Copy, r=[b_stgN, b_g1], w=[b_wqn], scale=g1col[:, c:c + 1])
    chunks = [(0, 512), (512, 512), (1024, 512), (1536, 512), (2048, TS)]
    if FAST:
        chunks = [(2048, TS)]
    for (c0, cn) in chunks:
        for hp in range(4):
            pb = 4 + (hp % 2)
            for c in range(8):
                k.mm(ps[pb][:, 0:cn], w_qn[:, c, hp * 128:(hp + 1) * 128], hT[:, c, c0:c0 + cn], start=(c == 0), stop=(c == 7),
                     r=[b_wqn] + b_hT, w=[b_ps[pb]])
            k.v("dve", lambda: nc.vector.tensor_copy(out=QA[2 * hp][0:64, c0:c0 + cn], in_=ps[pb][0:64, 0:cn]), r=[b_ps[pb]], w=[b_QA[2 * hp]])
            k.v("dve", lambda: nc.vector.tensor_copy(out=QA[2 * hp + 1][0:64, c0:c0 + cn], in_=ps[pb][64:128, 0:cn]), r=[b_ps[pb]], w=[b_QA[2 * hp + 1]])
        if True:
            for c in range(8):
                k.mm(ps[6][0:24, 0:cn], w_qn[:, c, 512:536], hT[:, c, c0:c0 + cn], start=(c == 0), stop=(c == 7),
                     r=[b_wqn] + b_hT, w=[b_ps[6]])
            k.actf(gnT[0:24, c0:c0 + cn], ps[6][0:24, 0:cn], AF.Sigmoid, r=[b_ps[6]], w=[b_gnT])

    def nsa_finish(br, h, qc, extra=None):
        def fin(acc, b_acc):
            ri = nst["rz_i"]; nst["rz_i"] = 1 - ri
            p0 = (h % 2) * 64
            tmp = tmpn[ri][p0:p0 + 64, :]
            k.v("dve", lambda: nc.vector.tensor_scalar(out=rzn[ri][0:64, :], in0=acc[64:128, :], scalar1=1e-30, scalar2=None, op0=ALU.max), r=[b_acc], w=[b_rzn[ri]])
            k.v("dve", lambda: nc.vector.reciprocal(out=rzn[ri][0:64, :], in_=rzn[ri][0:64, :]), r=[b_rzn[ri]], w=[b_rzn[ri]])
            k.v("dve", lambda: nc.vector.tensor_tensor(out=tmp, in0=acc[0:64, :], in1=rzn[ri][0:64, :], op=ALU.mult),
                r=[b_acc, b_rzn[ri]], w=[b_tmpn[ri]])
            if extra is not None:
                extra(rzn[ri], b_rzn[ri])
            k.mm(ps[6][0:64, :], selb[0:24, br, h, :], gnT[0:24, qc * 512:(qc + 1) * 512], start=True, stop=True,
                 r=[b_selb, b_gnT], w=[b_ps[6]])
            dst = o_nsaT[p0:p0 + 64, h // 2, qc * 512:(qc + 1) * 512]
            if br == 0:
                k.v("dve", lambda: nc.vector.tensor_tensor(out=dst, in0=tmp, in1=ps[6][0:64, :], op=ALU.mult),
                    r=[b_tmpn[ri], b_ps[6]], w=[b_onsa])
            else:
                k.v("dve", lambda: nc.vector.tensor_tensor(out=tmp, in0=tmp, in1=ps[6][0:64, :], op=ALU.mult),
                    r=[b_tmpn[ri], b_ps[6]], w=[b_tmpn[ri]])
                k.v("dve", lambda: nc.vector.tensor_tensor(out=dst, in0=dst, in1=tmp, op=ALU.add),
                    r=[b_tmpn[ri], b_onsa], w=[b_onsa])
        return fin

    BR = "" if FAST else os.environ.get("KBR", "csw")
    for h in range(0 if FAST else 8):
        g = h // 4
        for qc in range(4):
            cs_ = slice(qc * 512, (qc + 1) * 512)
            si = nst["s_i"] % 2; nst["s_i"] = (nst["s_i"] + 1) % 2
            psS = ps[si]; b_S = b_ps[si]
            k.mm(psS[0:127, :], kccA[g][0:68, 0:127], QA[h][0:68, cs_], start=True, stop=False, r=[b_kccA[g], b_QA[h]], w=[b_S])
            k.mm(psS[0:127, :], identb[0:127, 0:127], cmpmask[0:127, cs_], start=False, stop=True, r=[b_identb, b_cmask], w=[b_S])
            pi = nst["p_i"]; nst["p_i"] = (pi + 1) % 4
            PT = PTn[pi]; b_PT = b_PTn[pi]
            k.actf(PT[0:127, :], psS[0:127, :], AF.Exp, r=[b_S], w=[b_PT], scale=NSA_SCALE)
            ai = nst["acc_i"]; nst["acc_i"] = 1 - ai
            acc = ps[2 + ai]; b_acc = b_ps[2 + ai]
            k.mm(acc[:, :], vccA[g][0:127, :], PT[0:127, :], start=True, stop=True, r=[b_vccA[g], b_PT], w=[b_acc])
            k.mm(ps[5][0:32, :], overlapM[0:127, 0:32], PT[0:127, :], start=True, stop=True, r=[b_ovl, b_PT], w=[b_ps[5]])

            def extra(rz_, b_rz_, h=h, g=g, cs_=cs_):
                if h % 4 == 0:
                    k.v("dve", lambda: nc.vector.tensor_tensor(out=scoreT[g][0:32, cs_], in0=ps[5][0:32, :], in1=rz_[0:32, :], op=ALU.mult),
                        r=[b_ps[5], b_rz_], w=[b_sc[g]])
                else:
                    k.v("dve", lambda: nc.vector.tensor_tensor(out=tmpU[0:32, :], in0=ps[5][0:32, :], in1=rz_[0:32, :], op=ALU.mult),
                        r=[b_ps[5], b_rz_], w=[b_tmpU])
                    k.v("dve", lambda: nc.vector.tensor_tensor(out=scoreT[g][0:32, cs_], in0=scoreT[g][0:32, cs_], in1=tmpU[0:32, :], op=ALU.add),
                        r=[b_tmpU, b_sc[g]], w=[b_sc[g]])
            nsa_finish(0, h, qc, extra)(acc, b_acc)

    psb7 = ps[7].bitcast(BF16)
    for qt in range(0 if FAST else 16):
        ts_ = slice(qt * 128, (qt + 1) * 128)
        for g in range(2):
            k.tr(ps[6][:, g * 32:(g + 1) * 32], scoreT[g][0:32, ts_], identf[0:32, 0:32], r=[b_sc[g], b_identf], w=[b_ps[6]], sig=(g == 1))
        smv = tk[:, 0, :].rearrange("p (g j) -> p g j", g=2)
        k.v("dve", lambda: nc.vector.tensor_tensor(out=smv, in0=ps[6][:, 0:64].rearrange("p (g j) -> p g j", g=2),
                                                    in1=topkm[:, qt, :].unsqueeze(1).to_broadcast([128, 2, 32]), op=ALU.mult),
            r=[b_ps[6], b_topk], w=[b_tk])
        k.v("dve", lambda: nc.vector.tensor_tensor(out=smv, in0=smv, in1=topka[:, qt, :].unsqueeze(1).to_broadcast([128, 2, 32]), op=ALU.add),
            r=[b_tk, b_topk], w=[b_tk])
        for g in range(2):
            sm = tk[:, 0, g * 32:(g + 1) * 32]
            k.v("dve", lambda: nc.vector.max(out=tk[:, 1, g * 8:g * 8 + 8], in_=sm), r=[b_tk], w=[b_tk])
            k.v("dve", lambda: nc.vector.match_replace(out=tk[:, 2, g * 32:(g + 1) * 32], in_to_replace=tk[:, 1, g * 8:g * 8 + 8], in_values=sm, imm_value=-3.0e38), r=[b_tk], w=[b_tk])
            k.v("dve", lambda: nc.vector.max(out=tk[:, 3, g * 8:g * 8 + 8], in_=tk[:, 2, g * 32:(g + 1) * 32]), r=[b_tk], w=[b_tk])
            k.v("dve", lambda: nc.vector.tensor_scalar(out=tkb[:, g * 32:(g + 1) * 32], in0=sm, scalar1=tk[:, 3, g * 8 + 7:g * 8 + 8], scalar2=-1.0,
                                                        op0=ALU.is_ge, op1=ALU.add), r=[b_tk], w=[b_tkb])
        k.tr(psb7[0:64, 0:128], tkb[:, 0:64], identb[:, :], r=[b_tkb, b_identb], w=[b_ps[7]])
        for h in range(8):
            g = h // 4
            k.v("dve", lambda: nc.vector.tensor_copy(out=QA[h][96:128, ts_], in_=psb7[g * 32:(g + 1) * 32, 0:128]), r=[b_ps[7]], w=[b_QA[h]])

    if "s" in BR:
        for h in range(8):
            g = h // 4
            for qc in range(4):
                attn_chunk(nst, causal_sched(qc),
                           Kt=lambda kt: (KAs[g][:, kt * 128:(kt + 1) * 128], [b_KAs[g]]),
                           Qr=lambda a, b_: (QA[h][:, qc * 512 + a:qc * 512 + b_], [b_QA[h]]),
                           Vt=lambda kt: (vs_nat[:, kt, g, :], b_vnat),
                           scale=NSA_SCALE, finish_fn=nsa_finish(1, h, qc))

    def win_sched(qc):
        ents = []
        for kt in range(max(0, 4 * qc - 4), 4 * qc + 4):
            dl = kt * 128 - qc * 512
            if dl >= 0:
                ents.append((kt, dl, 512, (trimask[:, 0, :], dl, [b_tri]), 128))
            else:
                c_hi = dl + 640
                ents.append((kt, 0, c_hi, (trimask[:, 1, :], c_hi - 128, [b_tri]), 128))
        ents.sort(key=lambda e: -(e[2] - e[1]))
        return ents

    if "w" in BR:
        for h in range(8):
            g = h // 4
            for qc in range(4):
                attn_chunk(nst, win_sched(qc),
                           Kt=lambda kt: (KAw[g][0:68, kt * 128:(kt + 1) * 128], [b_KAw[g]]),
                           Qr=lambda a, b_: (QA[h][0:68, qc * 512 + a:qc * 512 + b_], [b_QA[h]]),
                           Vt=lambda kt: (vw_nat[:, kt, g, :], b_vnat),
                           scale=NSA_SCALE, finish_fn=nsa_finish(2, h, qc))
    if DBG:
        k.dma(k.q_pool, dbg_onsa[:, :, :], o_nsaT[:, :, :], r=[b_onsa])
    for h in range(8):
        k.v("dve", lambda: nc.vector.tensor_copy(out=smp_qn[0:64, h, :], in_=QA[h][0:64, T:TT]), r=[b_QA[h]], w=[b_smp])
    k.v("dve", lambda: nc.vector.tensor_copy(out=smp_gn[:, :], in_=gnT[0:24, T:TT]), r=[b_gnT], w=[b_smp])
    k.barrier()
    stN.close()

    stL1.close()
    stS = contextlib.ExitStack(); sbS = scoped(stS)
    s_en = sbS("s_en_sb", [16, 4, 32], F32); s_nm = sbS("s_nm_sb", [16, 4, 32], F32); s_ebw = sbS("s_ebw_sb", [128, 4, 32], F32)
    s_lb = sbS("s_lb_sb", [3, 128], BF16); s_rbs = sbS("s_rbs_sb", [3, 128, 32], BF16); s_rbc = sbS("s_rbc_sb", [3, 8, 32], BF16)
    s_ov = sbS("s_ov_sb", [128, 8, 264], BF16); s_tkm = sbS("s_tkm_sb", [4, 264], F32); s_tka = sbS("s_tka_sb", [4, 264], F32)
    s_ssum = sbS("s_ssum_sb", [16, 4], F32); selb2 = sbS("selb2", [24, 3, 8, 64], BF16)
    b_sc_ = Buf()
    for (dst, srcd, q) in ((s_en, s_en_d, k.q_sp), (s_nm, s_nm_d, k.q_sp), (s_ebw, s_ebw_d, k.q_sp), (s_lb, s_lb_d, k.q_pool),
                           (s_rbs, s_rbs_d, k.q_pool), (s_rbc, s_rbc_d, k.q_pool), (s_ov, s_ov_d, k.q_pool),
                           (s_tkm, s_tkm_d, k.q_sp), (s_tka, s_tka_d, k.q_sp), (s_ssum, s_ssum_d, k.q_sp), (selb2, selb_d, k.q_pool)):
        k.dma(q, dst.ap() if hasattr(dst, "ap") else dst, srcd, w=[b_sc_])
    ptT = sbS("ptT", [128, 4], I32); b_pt = Buf()
    idx16 = sbS("idx16", [128, 4, 16], I32); idx8 = sbS("idx8", [128, 4, 8], I32); b_idx = Buf()
    with nc.allow_non_contiguous_dma(reason="page table transpose (512 ints)"):
        k.dma(k.q_sp, ptT[:, :], pt_d.rearrange("b p -> p b"), w=[b_pt])
    for s_ in range(16):
        k.v("dve", lambda: nc.vector.tensor_scalar(out=idx16[:, :, s_], in0=ptT[:, :], scalar1=16, scalar2=s_, op0=ALU.mult, op1=ALU.add), r=[b_pt], w=[b_idx])
    for s_ in range(8):
        k.v("dve", lambda: nc.vector.tensor_scalar(out=idx8[:, :, s_], in0=ptT[:, :], scalar1=8, scalar2=s_, op0=ALU.mult, op1=ALU.add), r=[b_pt], w=[b_idx])
    hx0_ = sbS("hx0", [128, 8, 128], F32)
    w_ukf = hx0_[:, :, :].rearrange("p (a c) n -> p a (c n)", a=2); b_wukf = Buf()
    w_ukT = sbS("w_ukT", [64, 8, 256], BF16); b_wukT = Buf()
    w_uv_s = sbS("w_uv_s", [128, 2, 512], BF16); b_wuvs = Buf()
    QsA = sbS("QsA", [128, 2, 4, 32], BF16); QsR = sbS("QsR", [32, 4, 32], BF16); Qblk = sbS("Qblk", [128, 4, 32], BF16); b_Q = Buf()
    for c in range(2):
        k.dma(k.q_sp, w_ukf[:, c, :], w_uk[c * 128:(c + 1) * 128, :], w=[b_wukf])
        k.dma(k.q_pool, w_uv_s[:, c, :], w_uv[c * 128:(c + 1) * 128, :], w=[b_wuvs])
    for h in range(8):
        for c in range(2):
            k.tr(ps[6][0:64, (h % 2) * 256 + c * 128:(h % 2) * 256 + (c + 1) * 128], w_ukf[:, c, h * 64:(h + 1) * 64], identf[:, :],
                 r=[b_wukf, b_identf], w=[b_ps[6]], sig=(c == 1))
        k.v("dve", lambda: nc.vector.tensor_copy(out=w_ukT[0:64, h, :], in_=ps[6][0:64, (h % 2) * 256:(h % 2) * 256 + 256]), r=[b_ps[6]], w=[b_wukT])
    for h in range(8):
        for c in range(2):
            k.mm(ps[6][:, 0:16], w_ukT[0:64, h, c * 128:(c + 1) * 128], smp_qm[0:64, h, :], start=True, stop=True, r=[b_wukT, b_smp], w=[b_ps[6]])
            k.v("dve", lambda: nc.vector.tensor_copy(out=QsA[:, c, :, h * 4:(h + 1) * 4], in_=ps[6][:, 0:16].rearrange("p (b t) -> p b t", b=4)), r=[b_ps[6]], w=[b_Q])
        k.v("dve", lambda: nc.vector.tensor_copy(out=QsR[0:32, :, h * 4:(h + 1) * 4], in_=smp_qm[64:96, h, :].rearrange("p (b t) -> p b t", b=4)), r=[b_smp], w=[b_Q])
    k.v("pool", lambda: nc.gpsimd.memset(Qblk[:, :, :], 0.0), w=[b_Q])
    for h in range(8):
        g = h // 4
        k.v("dve", lambda: nc.vector.tensor_copy(out=Qblk[g * 64:(g + 1) * 64, :, h * 4:(h + 1) * 4], in_=smp_qn[0:64, h, :].rearrange("p (b t) -> p b t", b=4)), r=[b_smp], w=[b_Q])
    W1s = {kv: sbS(f"W1s{kv}", [128, 32, 128], BF16) for kv in "kv"}; W2s = {kv: sbS(f"W2s{kv}", [128, 128], BF16) for kv in "kv"}
    posTs = {kv: sbS(f"posTs{kv}", [128, 32], BF16) for kv in "kv"}; cbs = {kv: sbS(f"cbs{kv}", [128, 1], F32) for kv in "kv"}; b_Ws = Buf()
    with nc.allow_non_contiguous_dma(reason="small weight relayout"):
        for kv in "kv":
            k.v("pool", lambda: nc.gpsimd.memset(W1s[kv][:, :, :], 0.0), w=[b_Ws])
            k.v("pool", lambda: nc.gpsimd.memset(W2s[kv][:, :], 0.0), w=[b_Ws])
            w1v = cw1[kv].rearrange("(s d) h -> d s h", d=64)
            for g in range(2):
                k.dma(k.q_pool, W1s[kv][g * 64:(g + 1) * 64, :, g * 64:(g + 1) * 64], w1v, w=[b_Ws])
                k.dma(k.q_pool, W2s[kv][g * 64:(g + 1) * 64, g * 64:(g + 1) * 64], cw2[kv][:, :], w=[b_Ws])
                k.dma(k.q_pool, posTs[kv][g * 64:(g + 1) * 64, :], cpos[kv].rearrange("s d -> d s"), w=[b_Ws])
    for kv in "kv":
        for s_ in range(32):
            k.mm(ps[6][:, 0:1], W1s[kv][:, s_, :], posTs[kv][:, s_:s_ + 1], start=(s_ == 0), stop=(s_ == 31), r=[b_Ws], w=[b_ps[6]])
        k.v("dve", lambda: nc.vector.tensor_copy(out=cbs[kv][:, :], in_=ps[6][:, 0:1]), r=[b_ps[6]], w=[b_Ws])
    Xc2 = [sbS(f"Xc{i}", [128, 8 * 256], BF16) for i in range(3)]; Xr2 = [sbS(f"Xr{i}", [128, 8 * 32], BF16) for i in range(3)]
    Xc = [t_[:, :].rearrange("p (r f) -> p r f", r=8) for t_ in Xc2]; Xr = [t_[:, :].rearrange("p (r f) -> p r f", r=8) for t_ in Xr2]
    ones_c = sbS("ones_c", [128, 1], BF16); b_X = [Buf(), Buf(), Buf()]
    KTc = [sbS(f"KTc{i}", [128, 2, 2, 128], BF16) for i in range(3)]; KTr = [sbS(f"KTr{i}", [32, 2, 128], BF16) for i in range(3)]; b_KT = [Buf(), Buf(), Buf()]
    PTq = [sbS(f"PTq{i}", [128, 256], BF16) for i in range(2)]; b_PTq = [Buf(), Buf()]
    PTf = [sbS(f"PTf{i}", [128, 256], F32) for i in range(2)]; b_PTf = [Buf(), Buf()]
    sm1 = sbS("sm1", [32, 8], F32); b_sm1 = Buf()
    olat = sbS("olat", [32, 256], BF16); b_olat = Buf()
    olT = sbS("olT", [128, 2, 32], BF16); b_olT = Buf()
    Xg2 = [sbS(f"Xg{i}", [128, 16 * 128], BF16) for i in range(3)]
    Xg = [t_[:, :].rearrange("p (r f) -> p r f", r=16) for t_ in Xg2]; b_Xg = [Buf(), Buf(), Buf()]
    Xk2 = [sbS(f"Xk{i}", [128, 16 * 128], BF16) for i in range(3)]
    Xk = [t_[:, :].rearrange("p (r f) -> p r f", r=16) for t_ in Xk2]; b_Xk = [Buf(), Buf(), Buf()]
    XT = [sbS(f"XT{i}", [128, 8, 128], BF16) for i in range(3)]; b_XT = [Buf(), Buf(), Buf()]
    ABf = sbS("ABf", [128, 2, 8, 128], F32); b_AB = Buf()
    hx = [hx0_, ABf[:, 1, :, :], ABf[:, 0, :, :]]; b_hx = [b_wukf, b_AB, b_AB]
    Hs = sbS("Hs", [128, 8, 128], BF16); b_Hs = Buf()
    kccS = sbS("kccS", [128, 8, 128], BF16); b_kccS = Buf()
    vccS = sbS("vccS", [128, 8, 2, 128], BF16); b_vccS = Buf()
    PTc = sbS("PTc", [128, 8, 32], BF16); b_PTc = Buf()
    og = sbS("og", [16, 64], F32); b_og = Buf()
    Un = sbS("Un", [16, 264], F32); b_Un = Buf()
    tks = sbS("tks", [4, 4, 264], F32); b_tks = Buf()
    msel = sbS("msel", [4, 2, 264], F32); b_msel = Buf()
    Mexp = sbS("Mexp", [128, 2, 32], F32); b_Mexp = Buf()
    tmpo = sbS("tmpo", [64, 16], F32); b_tmpo = Buf()
    tmpS = sbS("tmpS", [128, 16], F32); b_tmpS = Buf()
    nk_f = sbS("nk_f", [16, 32], F32); nk_b = sbS("nk_b", [16, 32], BF16); b_nk = Buf()
    Kw = sbS("Kw", [128, 4, 128], BF16); Vw = sbS("Vw", [128, 4, 129], BF16); KwTs = sbS("KwTs", [128, 4, 128], BF16); b_Kw = Buf()
    k.v("pool", lambda: nc.gpsimd.memset(ones_c[:, :], 1.0), w=[b_sc_])
    k.v("pool", lambda: nc.gpsimd.memset(Vw[:, :, 128:129], 1.0), w=[b_Kw])
    k.v("pool", lambda: nc.gpsimd.memset(vccS[:, :, :, :], 1.0), w=[b_vccS])
    ckv_v = c_ckv.rearrange("(q r) f -> q (r f)", r=8); kr_v = c_kr.rearrange("(q r) f -> q (r f)", r=8)
    ck_v = {"k": c_ck.rearrange("(q r) f -> q (r f)", r=16), "v": c_cv.rearrange("(q r) f -> q (r f)", r=16)}
    sk_v = c_sk.rearrange("(q r) f -> q (r f)", r=16); sv_v = c_sv.rearrange("(q r) f -> q (r f)", r=16)
    cnt = {"x": 0, "kt": 0, "pt": 0, "xg": 0, "xt": 0, "s": 0, "tb": 0}
    psbv = {i_: ps[i_].bitcast(BF16) for i_ in (4, 5, 6, 7)}

    def trbank(choices):
        i_ = choices[cnt["tb"] % len(choices)]; cnt["tb"] += 1
        return psbv[i_], b_ps[i_]

    def s_gate(br, g, b):
        k.tr(ps[7][0:64, 0:16], og[0:16, :], identf[0:16, 0:16], r=[b_og, b_identf], w=[b_ps[7]])
        k.v("dve", lambda: nc.vector.tensor_copy(out=tmpo[:, :], in_=ps[7][0:64, 0:16]), r=[b_ps[7]], w=[b_tmpo])
        for hl in range(4):
            h = 4 * g + hl
            k.mm(ps[6][0:64, hl * 4:(hl + 1) * 4], selb2[0:24, br, h, :], smp_gn[0:24, 4 * b:4 * b + 4], start=True, stop=True,
                 r=[b_sc_, b_smp], w=[b_ps[6]], sig=(hl == 3))
        for hl in range(4):
            h = 4 * g + hl
            p0 = (h % 2) * 64
            cs_ = slice(hl * 4, hl * 4 + 4)
            dst = o_nsaT[p0:p0 + 64, h // 2, T + 4 * b:T + 4 * b + 4]
            if br == 0:
                k.v("dve", lambda: nc.vector.tensor_tensor(out=dst, in0=tmpo[0:64, cs_], in1=ps[6][0:64, cs_], op=ALU.mult), r=[b_tmpo, b_ps[6]], w=[b_onsa])
            else:
                k.v("dve", lambda: nc.vector.tensor_tensor(out=tmpS[p0:p0 + 64, cs_], in0=tmpo[0:64, cs_], in1=ps[6][0:64, cs_], op=ALU.mult), r=[b_tmpo, b_ps[6]], w=[b_tmpS])
                k.v("dve", lambda: nc.vector.tensor_tensor(out=dst, in0=dst, in1=tmpS[p0:p0 + 64, cs_], op=ALU.add), r=[b_tmpS, b_onsa], w=[b_onsa])

    def s_norm(acc, zc, vc0, g_unused=None):
        k.v("dve", lambda: nc.vector.tensor_scalar(out=sm1[0:16, 0:1], in0=acc[0:16, zc:zc + 1], scalar1=1e-30, scalar2=None, op0=ALU.max), r=[], w=[b_sm1])
        k.v("dve", lambda: nc.vector.reciprocal(out=sm1[0:16, 0:1], in_=sm1[0:16, 0:1]), r=[b_sm1], w=[b_sm1])

    SB_ = os.environ.get("KSB", "mcsw")
    for b in range(int(os.environ.get('KNB', '4'))):
        if "m" in SB_:
            acc = ps[2]; b_acc = b_ps[2]
            NJ = int(os.environ.get('KNJ', '16'))
            tl = []
            fst = {"v": True}

            def mG(j):
                xi = j % 3
                def f():
                    k.gather(Xc2[xi][:, :], ckv_v, idx16[:, b, j:j + 1], r=[b_idx], w=[b_X[xi]])
                    k.gather(Xr2[xi][:, :], kr_v, idx16[:, b, j:j + 1], r=[b_idx], w=[b_X[xi]])
                return f

            def mT(j, rp):
                xi = j % 3; ki = (4 * j + rp) % 3
                def f():
                    tpb, b_tpb = trbank((7, 4, 5))
                    for rr_ in range(2):
                        r_ = rp * 2 + rr_
                        for c in range(2):
                            k.tr(tpb[:, (rr_ * 3 + c) * 128:(rr_ * 3 + c + 1) * 128], Xc[xi][:, r_, c * 128:(c + 1) * 128], identb[:, :], r=[b_X[xi], b_identb], w=[b_tpb], sig=False)
                        k.tr(tpb[0:32, (rr_ * 3 + 2) * 128:(rr_ * 3 + 3) * 128], Xr[xi][:, r_, :], identb[:, :], r=[b_X[xi], b_identb], w=[b_tpb], sig=(rr_ == 1))
                    pv = tpb[:, 0:768].rearrange("p (a c n) -> p a c n", a=2, c=3)
                    k.v("dve", lambda: nc.vector.tensor_copy(out=KTc[ki][:, :, :, :], in_=pv[:, :, 0:2, :]), r=[b_tpb], w=[b_KT[ki]])
                    k.v("dve", lambda: nc.vector.tensor_copy(out=KTr[ki][0:32, :, :], in_=pv[0:32, :, 2, :]), r=[b_tpb], w=[b_KT[ki]])
                return f

            def mS(j, rp):
                ki = (4 * j + rp) % 3; psS = ps[j % 2]; b_S = b_ps[j % 2]
                def f():
                    for rr_ in range(2):
                        r_ = rp * 2 + rr_
                        oS = psS[:, r_ * 32:(r_ + 1) * 32]
                        k.mm(oS, KTc[ki][:, rr_, 0, :], QsA[:, 0, b, :], start=True, stop=False, r=[b_KT[ki], b_Q], w=[b_S])
                        k.mm(oS, KTc[ki][:, rr_, 1, :], QsA[:, 1, b, :], start=False, stop=False, r=[b_KT[ki], b_Q], w=[b_S])
                        k.mm(oS, KTr[ki][0:32, rr_, :], QsR[0:32, b, :], start=False, stop=True, r=[b_KT[ki], b_Q], w=[b_S], sig=(rp == 3 and rr_ == 1))
                return f

            def mE(j):
                def f():
                    k.actf(PTq[j % 2][:, :], ps[j % 2][:, 0:256], AF.Exp, r=[b_ps[j % 2]], w=[b_PTq[j % 2]], scale=MLA_SCALE)
                return f

            def mP(j):
                xi = j % 3; pi = j % 2
                def f():
                    for r_ in range(8):
                        k.mm(acc[0:32, 0:256], PTq[pi][:, r_ * 32:(r_ + 1) * 32], Xc[xi][:, r_, :], start=fst["v"], stop=False, r=[b_PTq[pi], b_X[xi]], w=[b_acc], sig=False)
                        k.mm(ps[3][0:32, 0:1], PTq[pi][:, r_ * 32:(r_ + 1) * 32], ones_c[:, 0:1], start=fst["v"], stop=False, r=[b_PTq[pi], b_sc_], w=[b_ps[3]], sig=False)
                        fst["v"] = False
                return f
            for j in range(NJ):
                tl.append((8 * (j - 1) - 0.5 if j >= 2 else (-2 + j), mG(j)))
                for rp in range(4):
                    u = 4 * j + rp
                    tl.append((2 * u, mT(j, rp)))
                    tl.append((2 * u + 3, mS(j, rp)))
                tl.append((2 * (4 * j + 3) + 3.5, mE(j)))
                tl.append((2 * (4 * j + 3) + 7.6, mP(j)))
            for _, f_ in sorted(tl, key=lambda e: e[0]):
                f_()
            si = cnt["s"] % 2; cnt["s"] += 1
            psS = ps[si]; b_S = b_ps[si]
            k.mm(psS[0:16, 0:32], smp_ckvT[:, 0, :], QsA[:, 0, b, :], start=True, stop=False, r=[b_smp, b_Q], w=[b_S])
            k.mm(psS[0:16, 0:32], smp_ckvT[:, 1, :], QsA[:, 1, b, :], start=False, stop=False, r=[b_smp, b_Q], w=[b_S])
            k.mm(psS[0:16, 0:32], smp_krT[0:32, :], QsR[0:32, b, :], start=False, stop=True, r=[b_smp, b_Q], w=[b_S])
            k.actf(nk_f[:, :], psS[0:16, 0:32], AF.Exp, r=[b_S], w=[b_nk], scale=MLA_SCALE)
            k.v("dve", lambda: nc.vector.tensor_tensor(out=nk_b[:, :], in0=nk_f[:, :], in1=s_nm[:, b, :], op=ALU.mult), r=[b_nk, b_sc_], w=[b_nk])
            k.mm(acc[0:32, 0:256], nk_b[0:16, :], smp_ckvn[0:16, 0:256], start=False, stop=True, r=[b_nk, b_smp], w=[b_acc])
            k.mm(ps[3][0:32, 0:1], nk_b[0:16, :], ones_c[0:16, 0:1], start=False, stop=True, r=[b_nk, b_sc_], w=[b_ps[3]])
            k.v("dve", lambda: nc.vector.reciprocal(out=sm1[0:32, 1:2], in_=ps[3][0:32, 0:1]), r=[b_ps[3]], w=[b_sm1])
            k.v("dve", lambda: nc.vector.tensor_scalar(out=olat[:, :], in0=acc[0:32, 0:256], scalar1=sm1[0:32, 1:2], scalar2=None, op0=ALU.mult), r=[b_acc, b_sm1], w=[b_olat])
            for c in range(2):
                k.tr(psb[:, c * 128:c * 128 + 32], olat[0:32, c * 128:(c + 1) * 128], identb[0:32, 0:32], r=[b_olat, b_identb], w=[b_ps[7]], sig=(c == 1))
            k.v("dve", lambda: nc.vector.tensor_copy(out=olT[:, :, :], in_=psb[:, 0:256].rearrange("p (c n) -> p c n", c=2)[:, :, 0:32]), r=[b_ps[7]], w=[b_olT])
            for h in range(8):
                for c in range(2):
                    k.mm(ps[6][0:64, h * 4:(h + 1) * 4], w_uv_s[:, c, h * 64:(h + 1) * 64], olT[:, c, h * 4:(h + 1) * 4], start=(c == 0), stop=(c == 1),
                         r=[b_wuvs, b_olT], w=[b_ps[6]], sig=(c == 1 and h == 7))
            for h in range(8):
                p0 = (h % 2) * 64
                k.v("dve", lambda: nc.vector.tensor_copy(out=o_mlaT[p0:p0 + 64, h // 2, T + 4 * b:T + 4 * b + 4], in_=ps[6][0:64, h * 4:(h + 1) * 4]), r=[b_ps[6]], w=[b_omla])

        if "c" in SB_:
            for kv in "kv":
                tl = []

                def cG(c):
                    def f():
                        k.gather(Xg2[c % 3][:, :], ck_v[kv], idx8[:, b, c:c + 1], r=[b_idx], w=[b_Xg[c % 3]])
                    return f

                def cT(c, hf):
                    u = 2 * c + hf
                    def f():
                        tpb, b_tpb = trbank((7, 6))
                        for s8 in range(8):
                            k.tr(tpb[:, s8 * 128:(s8 + 1) * 128], Xg[c % 3][:, hf * 8 + s8, :], identb[:, :], r=[b_Xg[c % 3], b_identb], w=[b_tpb], sig=(s8 == 7))
                        k.v("dve", lambda: nc.vector.tensor_copy(out=XT[u % 3][:, :, :], in_=tpb[:, :].rearrange("p (s n) -> p s n", s=8)), r=[b_tpb], w=[b_XT[u % 3]])
                    return f

                def cM(c, hf):
                    u = 2 * c + hf; ba = 4 if c % 2 == 0 else 2
                    def f():
                        for s8 in range(8):
                            s_ = hf * 8 + s8
                            k.mm(ps[ba][:, 0:128], W1s[kv][:, s_, :], XT[u % 3][:, s8, :], start=(s_ == 0), stop=(s_ == 15), r=[b_Ws, b_XT[u % 3]], w=[b_ps[ba]])
                            k.mm(ps[ba + 1][:, 0:128], W1s[kv][:, 16 + s_, :], XT[u % 3][:, s8, :], start=(s_ == 0), stop=(s_ == 15), r=[b_Ws, b_XT[u % 3]], w=[b_ps[ba + 1]])
                    return f

                def cV(c):
                    ba = 4 if c % 2 == 0 else 2
                    def f():
                        k.actf(ABf[:, 0, c, :], ps[ba][:, 0:128], AF.Copy, r=[b_ps[ba]], w=[b_AB])
                        k.actf(ABf[:, 1, c, :], ps[ba + 1][:, 0:128], AF.Copy, r=[b_ps[ba + 1]], w=[b_AB])
                    return f
                for c in range(8):
                    tl.append((4 * c - 5.5 if c >= 2 else (-2 + c), cG(c)))
                    for hf in range(2):
                        u = 2 * c + hf
                        tl.append((2 * u, cT(c, hf)))
                        tl.append((2 * u + 3, cM(c, hf)))
                    tl.append((2 * (2 * c + 1) + 3.5, cV(c)))
                for _, f_ in sorted(tl, key=lambda e: e[0]):
                    f_()
                x_ = hx[0]
                k.v("pool", lambda: nc.gpsimd.memset(x_[:, 7, 127:128], 0.0), w=[b_hx[0]])
                k.v("dve", lambda: nc.vector.tensor_tensor(out=x_[:, 0:7, :], in0=ABf[:, 0, 0:7, :], in1=ABf[:, 1, 1:8, :], op=ALU.add), r=[b_AB], w=[b_hx[0]])
                k.v("dve", lambda: nc.vector.tensor_tensor(out=x_[:, 7, 0:127], in0=ABf[:, 0, 7, 0:127], in1=ABf[:, 1, 0, 1:128], op=ALU.add), r=[b_AB], w=[b_hx[0]])
                k.v("dve", lambda: nc.vector.tensor_scalar(out=x_[:, :, :], in0=x_[:, :, :], scalar1=cbs[kv][:, 0:1], scalar2=None, op0=ALU.add), r=[b_hx[0], b_Ws], w=[b_hx[0]])
                k.v("dve", lambda: nc.vector.tensor_tensor(out=hx[1][:, :, :], in0=x_[:, :, :], in1=x_[:, :, :], op=ALU.mult), r=[b_hx[0]], w=[b_hx[1]])
                k.v("dve", lambda: nc.vector.tensor_scalar(out=hx[1][:, :, :], in0=hx[1][:, :, :], scalar1=0.044715, scalar2=1.0, op0=ALU.mult, op1=ALU.add), r=[b_hx[1]], w=[b_hx[1]])
                k.v("dve", lambda: nc.vector.tensor_tensor(out=hx[1][:, :, :], in0=hx[1][:, :, :], in1=x_[:, :, :], op=ALU.mult), r=[b_hx[0], b_hx[1]], w=[b_hx[1]])
                k.actf(hx[2][:, :, :], hx[1][:, :, :], AF.Sigmoid, r=[b_hx[1]], w=[b_hx[2]], scale=1.5957691216)
                k.v("dve", lambda: nc.vector.tensor_tensor(out=Hs[:, :, :], in0=hx[2][:, :, :], in1=x_[:, :, :], op=ALU.mult), r=[b_hx[0], b_hx[2]], w=[b_Hs])
                if kv == "k":
                    for hf in range(2):
                        k.mm(ps[6][:, :], W2s["k"][:, :], Hs[:, hf * 4:(hf + 1) * 4, :], start=True, stop=True, r=[b_Ws, b_Hs], w=[b_ps[6]])
                        k.v("dve", lambda: nc.vector.tensor_copy(out=kccS[:, hf * 4:(hf + 1) * 4, :], in_=ps[6][:, :].rearrange("p (c n) -> p c n", c=4)), r=[b_ps[6]], w=[b_kccS])
                else:
                    for hf in range(2):
                        for c4 in range(4):
                            k.mm(ps[6][:, c4 * 128:(c4 + 1) * 128], Hs[:, hf * 4 + c4, :], W2s["v"][:, :], start=True, stop=True, r=[b_Ws, b_Hs], w=[b_ps[6]], sig=(c4 == 3))
                        k.v("dve", lambda: nc.vector.tensor_copy(out=vccS[:, hf * 4:(hf + 1) * 4, :, 0:64], in_=ps[6][:, :].rearrange("p (c g d) -> p c g d", c=4, g=2)), r=[b_ps[6]], w=[b_vccS])
            si = cnt["s"] % 2; cnt["s"] += 1
            psS = ps[si]; b_S = b_ps[si]
            for c in range(8):
                k.mm(psS[:, c * 32:(c + 1) * 32], kccS[:, c, :], Qblk[:, b, :], start=True, stop=False, r=[b_kccS, b_Q], w=[b_S])
                k.mm(psS[:, c * 32:(c + 1) * 32], s_lb[0:3, :], s_rbc[0:3, c, :], start=False, stop=True, r=[b_sc_], w=[b_S], sig=(c == 7))
            k.actf(PTc[:, :, :], psS[:, 0:256].rearrange("p (c n) -> p c n", c=8), AF.Exp, r=[b_S], w=[b_PTc], scale=NSA_SCALE)
            for g in range(2):
                acc = ps[2 + g]; b_acc = b_ps[2 + g]
                for c in range(8):
                    k.mm(acc[0:16, 0:128], PTc[:, c, g * 16:(g + 1) * 16], vccS[:, c, g, :], start=(c == 0), stop=(c == 7), r=[b_PTc, b_vccS], w=[b_acc])
                U = ps[4 + g]; b_U = b_ps[4 + g]
                for c in range(8):
                    k.mm(U[0:16, 0:264], PTc[:, c, g * 16:(g + 1) * 16], s_ov[:, c, :], start=(c == 0), stop=(c == 7), r=[b_PTc, b_sc_], w=[b_U])
                k.v("dve", lambda: nc.vector.tensor_scalar(out=sm1[0:16, 0:1], in0=acc[0:16, 64:65], scalar1=1e-30, scalar2=None, op0=ALU.max), r=[b_acc], w=[b_sm1])
                k.v("dve", lambda: nc.vector.reciprocal(out=sm1[0:16, 0:1], in_=sm1[0:16, 0:1]), r=[b_sm1], w=[b_sm1])
                k.v("dve", lambda: nc.vector.tensor_scalar(out=og[:, :], in0=acc[0:16, 0:64], scalar1=sm1[0:16, 0:1], scalar2=None, op0=ALU.mult), r=[b_acc, b_sm1], w=[b_og])
                k.v("dve", lambda: nc.vector.tensor_scalar(out=Un[:, :], in0=U[0:16, 0:264], scalar1=sm1[0:16, 0:1], scalar2=None, op0=ALU.mult), r=[b_U, b_sm1], w=[b_Un])
                k.mm(ps[6][0:4, 0:264], s_ssum[0:16, 0:4], Un[0:16, :], start=True, stop=True, r=[b_sc_, b_Un], w=[b_ps[6]])
                k.v("dve", lambda: nc.vector.tensor_tensor(out=tks[0:4, 0, :], in0=ps[6][0:4, 0:264], in1=s_tkm[0:4, :], op=ALU.mult), r=[b_ps[6], b_sc_], w=[b_tks])
                k.v("dve", lambda: nc.vector.tensor_tensor(out=tks[0:4, 0, :], in0=tks[0:4, 0, :], in1=s_tka[0:4, :], op=ALU.add), r=[b_tks, b_sc_], w=[b_tks])
                k.v("dve", lambda: nc.vector.max(out=tks[0:4, 1, 0:8], in_=tks[0:4, 0, :]), r=[b_tks], w=[b_tks])
                k.v("dve", lambda: nc.vector.match_replace(out=tks[0:4, 2, :], in_to_replace=tks[0:4, 1, 0:8], in_values=tks[0:4, 0, :], imm_value=-3.0e38), r=[b_tks], w=[b_tks])
                k.v("dve", lambda: nc.vector.max(out=tks[0:4, 3, 0:8], in_=tks[0:4, 2, :]), r=[b_tks], w=[b_tks])
                k.v("dve", lambda: nc.vector.tensor_scalar(out=msel[0:4, g, :], in0=tks[0:4, 0, :], scalar1=tks[0:4, 3, 7:8], scalar2=None, op0=ALU.is_ge), r=[b_tks], w=[b_msel])
                s_gate(0, g, b)
            for g in range(2):
                for par in range(2):
                    q_ = g * 2 + par
                    k.tr(ps[7][:, q_ * 4:(q_ + 1) * 4], msel[0:4, g, par:par + 256:2], identf[0:4, 0:4], r=[b_msel, b_identf], w=[b_ps[7]], sig=(q_ == 3))
            for g in range(2):
                for par in range(2):
                    q_ = g * 2 + par
                    k.v("dve", lambda: nc.vector.tensor_copy(out=Mexp[:, par, g * 16:(g + 1) * 16].rearrange("p (a t) -> p a t", a=4),
                                                              in_=ps[7][:, q_ * 4:(q_ + 1) * 4].unsqueeze(1).to_broadcast([128, 4, 4])), r=[b_ps[7]], w=[b_Mexp])

        if "s" in SB_:
            tl = []
            fst = {"v": True}

            def sG(c):
                def f():
                    k.gather(Xk2[c % 3][:, :], sk_v, idx8[:, b, c:c + 1], r=[b_idx], w=[b_Xk[c % 3]])
                    k.gather(Xg2[c % 3][:, :], sv_v, idx8[:, b, c:c + 1], r=[b_idx], w=[b_Xg[c % 3]])
                return f

            def sT(c, hf):
                u = 2 * c + hf
                def f():
                    tpb, b_tpb = trbank((7, 6))
                    for s8 in range(8):
                        k.tr(tpb[:, s8 * 128:(s8 + 1) * 128], Xk[c % 3][:, hf * 8 + s8, :], identb[:, :], r=[b_Xk[c % 3], b_identb], w=[b_tpb], sig=(s8 == 7))
                    k.v("dve", lambda: nc.vector.tensor_copy(out=XT[u % 3][:, :, :], in_=tpb[:, :].rearrange("p (s n) -> p s n", s=8)), r=[b_tpb], w=[b_XT[u % 3]])
                return f

            def sS(c, hf):
                u = 2 * c + hf; psS = ps[u % 2]; b_S = b_ps[u % 2]
                def f():
                    for s8 in range(8):
                        rr = c * 16 + hf * 8 + s8
                        k.mm(psS[:, s8 * 32:(s8 + 1) * 32], XT[u % 3][:, s8, :], Qblk[:, b, :], start=True, stop=False, r=[b_XT[u % 3], b_Q], w=[b_S])
                        k.mm(psS[:, s8 * 32:(s8 + 1) * 32], s_lb[0:2, :], s_rbs[0:2, rr, :], start=False, stop=True, r=[b_sc_], w=[b_S], sig=(s8 == 7))
                return f

            def sE(c, hf):
                u = 2 * c + hf; pi = u % 2; par = 1 if c >= 4 else 0
                def f():
                    k.actf(PTf[pi][:, :], ps[u % 2][:, 0:256], AF.Exp, r=[b_ps[u % 2]], w=[b_PTf[pi]], scale=NSA_SCALE)
                    k.v("dve", lambda: nc.vector.tensor_tensor(out=PTq[pi][:, :].rearrange("p (s n) -> p s n", s=8), in0=PTf[pi][:, :].rearrange("p (s n) -> p s n", s=8),
                                                                in1=Mexp[:, par, :].unsqueeze(1).to_broadcast([128, 8, 32]), op=ALU.mult), r=[b_PTf[pi], b_Mexp], w=[b_PTq[pi]])
                return f

            def sP(c, hf):
                u = 2 * c + hf; pi = u % 2
                def f():
                    for s8 in range(8):
                        for g in range(2):
                            k.mm(ps[2 + g][0:16, 0:128], PTq[pi][:, s8 * 32 + g * 16:s8 * 32 + (g + 1) * 16], Xg[c % 3][:, hf * 8 + s8, :], start=fst["v"], stop=False,
                                 r=[b_PTq[pi], b_Xg[c % 3]], w=[b_ps[2 + g]], sig=False)
                            k.mm(ps[4 + g][0:16, 0:1], PTq[pi][:, s8 * 32 + g * 16:s8 * 32 + (g + 1) * 16], ones_c[:, 0:1], start=fst["v"], stop=False,
                                 r=[b_PTq[pi], b_sc_], w=[b_ps[4 + g]], sig=False)
                        fst["v"] = False
                return f
            for c in range(8):
                tl.append((4 * c - 2.2 if c >= 1 else -1, sG(c)))
                for hf in range(2):
                    u = 2 * c + hf
                    tl.append((2 * u, sT(c, hf)))
                    tl.append((2 * u + 3, sS(c, hf)))
                    tl.append((2 * u + 3.5, sE(c, hf)))
                    tl.append((2 * u + 5.6, sP(c, hf)))
            for _, f_ in sorted(tl, key=lambda e: e[0]):
                f_()
            si = cnt["s"] % 2; cnt["s"] += 1
            psS = ps[si]; b_S = b_ps[si]
            k.mm(psS[0:16, 0:32], smp_ksT[:, :], Qblk[:, b, :], start=True, stop=True, r=[b_smp, b_Q], w=[b_S])
            k.actf(nk_f[:, :], psS[0:16, 0:32], AF.Exp, r=[b_S], w=[b_nk], scale=NSA_SCALE)
            k.v("dve", lambda: nc.vector.tensor_tensor(out=nk_b[:, :], in0=nk_f[:, :], in1=s_en[:, b, :], op=ALU.mult), r=[b_nk, b_sc_], w=[b_nk])
            for g in range(2):
                acc = ps[2 + g]; b_acc = b_ps[2 + g]
                k.mm(acc[0:16, 0:128], nk_b[0:16, g * 16:(g + 1) * 16], smp_vs[0:16, 0:128], start=False, stop=True, r=[b_nk, b_smp], w=[b_acc])
                k.mm(ps[4 + g][0:16, 0:1], nk_b[0:16, g * 16:(g + 1) * 16], ones_c[0:16, 0:1], start=False, stop=True, r=[b_nk, b_sc_], w=[b_ps[4 + g]])
                k.v("dve", lambda: nc.vector.tensor_scalar(out=sm1[0:16, 0:1], in0=ps[4 + g][0:16, 0:1], scalar1=1e-30, scalar2=None, op0=ALU.max), r=[b_ps[4 + g]], w=[b_sm1])
                k.v("dve", lambda: nc.vector.reciprocal(out=sm1[0:16, 0:1], in_=sm1[0:16, 0:1]), r=[b_sm1], w=[b_sm1])
                k.v("dve", lambda: nc.vector.tensor_scalar(out=og[:, :], in0=acc[0:16, g * 64:(g + 1) * 64], scalar1=sm1[0:16, 0:1], scalar2=None, op0=ALU.mult), r=[b_acc, b_sm1], w=[b_og])
                s_gate(1, g, b)

        if "w" in SB_:
            k.dma(k.q_pool, Kw[:, :, :], win_k_in[b].rearrange("(kt p) f -> p kt f", p=128), w=[b_Kw])
            k.dma(k.q_pool, Vw[:, :, 0:128], win_v_in[b].rearrange("(kt p) f -> p kt f", p=128), w=[b_Kw])
            for kt in range(4):
                k.tr(psb[:, kt * 128:(kt + 1) * 128], Kw[:, kt, :], identb[:, :], r=[b_Kw, b_identb], w=[b_ps[7]], sig=(kt == 3))
            k.v("dve", lambda: nc.vector.tensor_copy(out=KwTs[:, :, :], in_=psb[:, 0:512].rearrange("p (s n) -> p s n", s=4)), r=[b_ps[7]], w=[b_Kw])
            si = cnt["s"] % 2; cnt["s"] += 1
            psS = ps[si]; b_S = b_ps[si]
            for kt in range(4):
                k.mm(psS[:, kt * 32:(kt + 1) * 32], KwTs[:, kt, :], Qblk[:, b, :], start=True, stop=True, r=[b_Kw, b_Q], w=[b_S], sig=(kt == 3))
            pi = cnt["pt"] % 2; cnt["pt"] += 1
            k.actf(PTf[pi][:, 0:128], psS[:, 0:128], AF.Exp, r=[b_S], w=[b_PTf[pi]], scale=NSA_SCALE)
            k.v("dve", lambda: nc.vector.tensor_tensor(out=PTq[pi][:, 0:128], in0=PTf[pi][:, 0:128], in1=s_ebw[:, :, :].rearrange("p a n -> p (a n)"), op=ALU.mult),
                r=[b_PTf[pi], b_sc_], w=[b_PTq[pi]])
            si = cnt["s"] % 2; cnt["s"] += 1
            psS2 = ps[si]; b_S2 = b_ps[si]
            k.mm(psS2[0:16, 0:32], smp_kwT[:, :], Qblk[:, b, :], start=True, stop=True, r=[b_smp, b_Q], w=[b_S2])
            k.actf(nk_f[:, :], psS2[0:16, 0:32], AF.Exp, r=[b_S2], w=[b_nk], scale=NSA_SCALE)
            k.v("dve", lambda: nc.vector.tensor_tensor(out=nk_b[:, :], in0=nk_f[:, :], in1=s_en[:, b, :], op=ALU.mult), r=[b_nk, b_sc_], w=[b_nk])
            for g in range(2):
                acc = ps[2 + g]; b_acc = b_ps[2 + g]
                for kt in range(4):
                    k.mm(acc[0:16, 0:129], PTq[pi][:, kt * 32 + g * 16:kt * 32 + (g + 1) * 16], Vw[:, kt, :], start=(kt == 0), stop=False, r=[b_PTq[pi], b_Kw], w=[b_acc], sig=False)
                k.mm(acc[0:16, 0:129], nk_b[0:16, g * 16:(g + 1) * 16], smp_vw[0:16, :], start=False, stop=True, r=[b_nk, b_smp], w=[b_acc])
                k.v("dve", lambda: nc.vector.tensor_scalar(out=sm1[0:16, 0:1], in0=acc[0:16, 128:129], scalar1=1e-30, scalar2=None, op0=ALU.max), r=[b_acc], w=[b_sm1])
                k.v("dve", lambda: nc.vector.reciprocal(out=sm1[0:16, 0:1], in_=sm1[0:16, 0:1]), r=[b_sm1], w=[b_sm1])
                k.v("dve", lambda: nc.vector.tensor_scalar(out=og[:, :], in0=acc[0:16, g * 64:(g + 1) * 64], scalar1=sm1[0:16, 0:1], scalar2=None, op0=ALU.mult), r=[b_acc, b_sm1], w=[b_og])
                s_gate(2, g, b)
    k.barrier()
    stS.close()
    if os.environ.get('KSTOP') == 'S':
        k.finish()
        return nc
    chunks5 = [(0, 512), (512, 512), (1024, 512), (1536, 512), (2048, TS)]
    stC = contextlib.ExitStack(); sbC = scoped(stC)
    h2T = sbC("h2T", [128, 8, TT], BF16); b_h2T = [Buf() for _ in range(NT)]
    stCm = contextlib.ExitStack(); sbCm = scoped(stCm)
    mTall = sbCm("mTall", [128, 8, TT], BF16); b_mT = [Buf() for _ in range(5)]
    stC1 = contextlib.ExitStack(); sbC1 = scoped(stC1)
    wpm_b = sbC1("wpm_b", [128, 4, D], BF16); b_wpm = Buf()
    wpn_b = sbC1("wpn_b", [128, 4, D], BF16); b_wpn = Buf()
    wG = [sbC1(f"wG{i}", [128, 8, 256], BF16) for i in range(2)]; b_wG = [Buf(), Buf()]
    stgG = [sbC1(f"stgG{i}", [128, 8, 256], F32) for i in range(2)]; b_stgG = [Buf(), Buf()]
    w_in_v = w_in.rearrange("(c p) f -> p c f", p=128)
    gsa = [sbC1(f"gsa{i}", [128, 512], F32) for i in range(2)]; b_gsa = [Buf(), Buf()]
    gsb = [sbC1(f"gsb{i}", [128, 512], F32) for i in range(2)]; b_gsb = [Buf(), Buf()]
    for c in range(4):
        k.dma(k.q_pool, wpm_b[:, c, :], w_pm[c * 128:(c + 1) * 128, :], w=[b_wpm])
        k.dma(k.q_pool, wpn_b[:, c, :], w_pn[c * 128:(c + 1) * 128, :], w=[b_wpn])
    it = 0
    for m in range(8):
        wi = m % 2
        k.dma(k.q_sp, stgG[wi][:, :, 0:128], w_in_v[:, :, C_GA + m * 128:C_GA + (m + 1) * 128], w=[b_stgG[wi]])
        k.dma(k.q_sp, stgG[wi][:, :, 128:256], w_in_v[:, :, C_GB + m * 128:C_GB + (m + 1) * 128], w=[b_stgG[wi]])
        k.v("pool", lambda: nc.gpsimd.tensor_tensor(out=wG[wi][:, :, :], in0=stgG[wi][:, :, :], in1=g1col[:, :].unsqueeze(2).to_broadcast([128, 8, 256]), op=ALU.mult),
            r=[b_stgG[wi], b_g1], w=[b_wG[wi]])
        for ci, (c0, cn) in enumerate(chunks5):
            hb = b_hT[4 * ci:4 * ci + 4] if ci < 4 else [b_hT[16]]
            i2 = it % 2; it += 1
            o_ = 4 * i2
            for c in range(8):
                k.mm(ps[o_ + 0][:, 0:cn], wG[wi][:, c, 0:128], hT[:, c, c0:c0 + cn], start=(c == 0), stop=(c == 7), r=[b_wG[wi]] + hb, w=[b_ps[o_ + 0]])
            for c in range(8):
                k.mm(ps[o_ + 1][:, 0:cn], wG[wi][:, c, 128:256], hT[:, c, c0:c0 + cn], start=(c == 0), stop=(c == 7), r=[b_wG[wi]] + hb, w=[b_ps[o_ + 1]])
            for c in range(4):
                k.mm(ps[o_ + 2][:, 0:cn], wpm_b[:, c, m * 128:(m + 1) * 128], o_mlaT[:, c, c0:c0 + cn], start=(c == 0), stop=(c == 3), r=[b_wpm, b_omla], w=[b_ps[o_ + 2]])
            for c in range(4):
                k.mm(ps[o_ + 3][:, 0:cn], wpn_b[:, c, m * 128:(m + 1) * 128], o_nsaT[:, c, c0:c0 + cn], start=(c == 0), stop=(c == 3), r=[b_wpn, b_onsa], w=[b_ps[o_ + 3]])
            k.actf(gsa[i2][:, 0:cn], ps[o_ + 0][:, 0:cn], AF.Sigmoid, r=[b_ps[o_ + 0]], w=[b_gsa[i2]])
            k.actf(gsb[i2][:, 0:cn], ps[o_ + 1][:, 0:cn], AF.Sigmoid, r=[b_ps[o_ + 1]], w=[b_gsb[i2]])
            k.v("dve", lambda: nc.vector.tensor_tensor(out=gsa[i2][:, 0:cn], in0=gsa[i2][:, 0:cn], in1=ps[o_ + 2][:, 0:cn], op=ALU.mult), r=[b_gsa[i2], b_ps[o_ + 2]], w=[b_gsa[i2]])
            k.v("dve", lambda: nc.vector.tensor_tensor(out=gsb[i2][:, 0:cn], in0=gsb[i2][:, 0:cn], in1=ps[o_ + 3][:, 0:cn], op=ALU.mult), r=[b_gsb[i2], b_ps[o_ + 3]], w=[b_gsb[i2]])
            k.v("dve", lambda: nc.vector.tensor_tensor(out=mTall[:, m, c0:c0 + cn], in0=gsa[i2][:, 0:cn], in1=gsb[i2][:, 0:cn], op=ALU.add), r=[b_gsa[i2], b_gsb[i2]], w=[b_mT[ci]])
    k.barrier()
    stC1.close()
    stC2 = contextlib.ExitStack(); sbC2 = scoped(stC2)
    wout_b = sbC2("wout_b", [128, 8, D], BF16); b_wout = Buf()
    xt2 = [sbC2(f"xt2_{i}", [128, D], F32) for i in range(2)]; b_xt2 = [Buf(), Buf()]
    xn2 = [sbC2(f"xn2_{i}", [128, D], BF16) for i in range(2)]; b_xn2 = [Buf(), Buf()]
    junk2 = sbC2("junk2", [128, D], BF16); b_junk2 = Buf()
    st2 = [sbC2(f"st2_{i}", [128, 4], F32) for i in range(2)]; b_st2 = [Buf(), Buf()]
    for c in range(8):
        k.dma(k.q_pool, wout_b[:, c, :], w_out[c * 128:(c + 1) * 128, :], w=[b_wout])

    def x1_tile(t):
        return x1all[:, t, :] if t < 16 else x1s[:, :]
    for t in range(NT):
        t0, n = tile_rows(t)
        s = t % 2
        ci = min(t // 4, 4)
        src = xp[t0:t0 + n, :] if t < 16 else xs[:, :]
        k.dma(k.q_sp, xt2[s][0:n, :], src, w=[b_xt2[s]])
        xd = x1_tile(t)
        for half in range(2):
            pb = 2 * (t % 2) + half
            for m in range(8):
                k.mm(ps[pb][0:n, :], mTall[:, m, t0:t0 + n], wout_b[:, m, half * 512:(half + 1) * 512], start=(m == 0), stop=(m == 7),
                     r=[b_mT[ci], b_wout], w=[b_ps[pb]])
            k.v("dve", lambda: nc.vector.tensor_tensor(out=xd[0:n, half * 512:(half + 1) * 512], in0=xt2[s][0:n, half * 512:(half + 1) * 512],
                                                        in1=ps[pb][0:n, :], op=ALU.add), r=[b_xt2[s], b_ps[pb]], w=[b_x1[t]])
        k.actf(junk2[0:n, :], xd[0:n, :], AF.Square, r=[b_x1[t]], w=[b_junk2, b_st2[s]], accum=st2[s][0:n, 0:1])
        k.actf(st2[s][0:n, 1:2], st2[s][0:n, 0:1], AF.Sqrt, r=[b_st2[s]], w=[b_st2[s]], scale=1.0 / D, bias=EPS)
        k.v("dve", lambda: nc.vector.reciprocal(out=st2[s][0:n, 2:3], in_=st2[s][0:n, 1:2]), r=[b_st2[s]], w=[b_st2[s]])
        k.v("dve", lambda: nc.vector.tensor_scalar(out=xn2[s][0:n, :], in0=xd[0:n, :], scalar1=st2[s][0:n, 2:3], scalar2=None, op0=ALU.mult),
            r=[b_x1[t], b_st2[s]], w=[b_xn2[s]])
        psbc = psbv2[t % 2]; b_psbc = b_ps[7 - (t % 2)]
        for c in range(8):
            k.tr(psbc[:, c * 128:c * 128 + n], xn2[s][0:n, c * 128:(c + 1) * 128], identb[0:n, 0:n], r=[b_xn2[s], b_identb], w=[b_psbc], sig=(c == 7))
        k.v("dve", lambda: nc.vector.tensor_copy(out=h2T[:, :, t0:t0 + n], in_=psbc[:, :].rearrange("p (c n) -> p c n", c=8)[:, :, 0:n]),
            r=[b_psbc], w=[b_h2T[t]])
    k.barrier()
    stC2.close()
    stCm.close()
    stD = contextlib.ExitStack(); sbD = scoped(stD)
    GW = 2
    NG = NFC // GW
    stgW = [sbD(f"stgW{i}", [128, 8, 128 * GW], F32) for i in range(2)]; b_stgW = [Buf(), Buf()]
    wg_b = [sbD(f"wg_b{i}", [128, 8, 128 * GW], BF16) for i in range(2)]; b_wg = [Buf(), Buf()]
    wu_b = [sbD(f"wu_b{i}", [128, 8, 128 * GW], BF16) for i in range(2)]; b_wu = [Buf(), Buf()]
    wd_b = [sbD(f"wd_b{i}", [128, GW, D], BF16) for i in range(2)]; b_wd = [Buf(), Buf()]
    gbuf = sbD("gbuf", [128, 2 + T], F32); b_gbuf = Buf()
    gbs = sbD("gbs", [128, 4, 6], F32); b_gbs = Buf()
    cv = [sbD(f"cv{i}", [128, 512], F32) for i in range(2)]; b_cv = [Buf(), Buf()]
    sg = [sbD(f"sg{i}", [128, 512], F32) for i in range(2)]; b_sg = [Buf(), Buf()]
    mTg = [sbD(f"mTg{i}", [128, GW, TT], BF16) for i in range(2)]; b_mTg = [Buf(), Buf()]
    histT = sbD("histT", [128, NFC, 8], F32); b_hist = Buf()
    cwT = sbD("cwT", [128, NFC, 4], F32); b_cwT = Buf()
    yo = [sbD(f"yo{i}", [128, D], F32) for i in range(2)]; b_yo = [Buf(), Buf()]
    st3 = [sbD(f"st3_{i}", [128, 4], F32) for i in range(2)]; b_st3 = [Buf(), Buf()]
    junk3 = sbD("junk3", [128, D], BF16); b_junk3 = Buf()
    with nc.allow_non_contiguous_dma(reason="tiny conv params, feature-major"):
        for fc in range(NFC):
            fsl = slice(fc * 128, (fc + 1) * 128)
            k.dma(k.q_sp, histT[:, fc, :], ffn_state[:, fsl].rearrange("e p -> p e"), w=[b_hist])
            k.dma(k.q_sp, cwT[:, fc, 0:3], conv_w[:, fsl].rearrange("j p -> p j"), w=[b_cwT])
            k.dma(k.q_sp, cwT[:, fc, 3:4], conv_b[fsl].rearrange("(p o) -> p o", o=1), w=[b_cwT])
    k.v("pool", lambda: nc.gpsimd.memset(gbuf[:, 0:2], 0.0), w=[b_gbuf])
    wgv = w_gate.rearrange("(c p) f -> p c f", p=128)
    wuv = w_up.rearrange("(c p) f -> p c f", p=128)
    wdv = w_down.rearrange("(c p) d -> p c d", p=128)
    cvi = 0
    with nc.allow_non_contiguous_dma(reason="conv state outputs are tiny feature-major slices"):
        for gi in range(NG):
            wi = gi % 2
            fs = slice(gi * 128 * GW, (gi + 1) * 128 * GW)
            for (wsrc, wdst, bw) in ((wgv, wg_b, b_wg), (wuv, wu_b, b_wu)):
                si_ = cvi % 2; cvi += 1
                k.dma(k.q_sp, stgW[si_][:, :, :], wsrc[:, :, fs], w=[b_stgW[si_]])
                for c in range(8):
                    k.actf(wdst[wi][:, c, :], stgW[si_][:, c, :], AF.Copy, r=[b_stgW[si_], b_g2], w=[bw[wi]], scale=g2col[:, c:c + 1])
            k.dma(k.q_pool, wd_b[wi][:, :, :], wdv[:, gi * GW:(gi + 1) * GW, :], w=[b_wd[wi]])
            for fl in range(GW):
                fc = gi * GW + fl
                w0 = cwT[:, fc, 0:1]; w1_ = cwT[:, fc, 1:2]; w2_ = cwT[:, fc, 2:3]; bc = cwT[:, fc, 3:4]
                for ci, (c0, cn) in enumerate(chunks5):
                    hb = b_h2T[4 * ci:4 * ci + 4] if ci < 4 else [b_h2T[16]]
                    i2 = (fc * 5 + ci) % 2
                    for c in range(8):
                        k.mm(ps[0 + i2][:, 0:cn], wg_b[wi][:, c, fl * 128:(fl + 1) * 128], h2T[:, c, c0:c0 + cn], start=(c == 0), stop=(c == 7), r=[b_wg[wi]] + hb, w=[b_ps[0 + i2]])
                    for c in range(8):
                        k.mm(ps[2 + i2][:, 0:cn], wu_b[wi][:, c, fl * 128:(fl + 1) * 128], h2T[:, c, c0:c0 + cn], start=(c == 0), stop=(c == 7), r=[b_wu[wi]] + hb, w=[b_ps[2 + i2]])
                    if ci < 4:
                        k.actf(gbuf[:, 2 + c0:2 + c0 + cn], ps[0 + i2][:, 0:cn], AF.Copy, r=[b_ps[0 + i2]], w=[b_gbuf])
                        gm2 = gbuf[:, c0:c0 + cn]; gm1 = gbuf[:, c0 + 1:c0 + 1 + cn]; g0 = gbuf[:, c0 + 2:c0 + 2 + cn]
                        cvt = cv[i2][:, 0:cn]; sgt = sg[i2][:, 0:cn]; mdst = mTg[wi][:, fl, c0:c0 + cn]; ups = ps[2 + i2][:, 0:cn]
                        rb = [b_gbuf]
                    else:
                        k.v("dve", lambda: nc.vector.tensor_copy(out=gbs[:, :, 0:2], in_=histT[:, fc, :].rearrange("p (b j) -> p b j", b=4)), r=[b_hist], w=[b_gbs])
                        k.actf(gbs[:, :, 2:6], ps[0 + i2][:, 0:TS].rearrange("p (b j) -> p b j", b=4), AF.Copy, r=[b_ps[0 + i2]], w=[b_gbs])
                        gm2 = gbs[:, :, 0:4]; gm1 = gbs[:, :, 1:5]; g0 = gbs[:, :, 2:6]
                        cvt = cv[i2][:, 0:TS].rearrange("p (b j) -> p b j", b=4); sgt = sg[i2][:, 0:TS].rearrange("p (b j) -> p b j", b=4)
                        mdst = mTg[wi][:, fl, c0:c0 + cn].rearrange("p (b j) -> p b j", b=4); ups = ps[2 + i2][:, 0:TS].rearrange("p (b j) -> p b j", b=4)
                        rb = [b_gbs]
                    k.v("dve", lambda: nc.vector.tensor_scalar(out=cvt, in0=gm2, scalar1=w0, scalar2=bc, op0=ALU.mult, op1=ALU.add), r=rb + [b_cwT], w=[b_cv[i2]])
                    k.v("dve", lambda: nc.vector.scalar_tensor_tensor(out=cvt, in0=gm1, scalar=w1_, in1=cvt, op0=ALU.mult, op1=ALU.add), r=rb + [b_cwT, b_cv[i2]], w=[b_cv[i2]])
                    k.v("dve", lambda: nc.vector.scalar_tensor_tensor(out=cvt, in0=g0, scalar=w2_, in1=cvt, op0=ALU.mult, op1=ALU.add), r=rb + [b_cwT, b_cv[i2]], w=[b_cv[i2]])
                    k.actf(sgt, cvt, AF.Silu, r=[b_cv[i2]], w=[b_sg[i2]])
                    k.v("dve", lambda: nc.vector.tensor_tensor(out=mdst, in0=sgt, in1=ups, op=ALU.mult), r=[b_sg[i2], b_ps[2 + i2]], w=[b_mTg[wi]])
                    if ci == 3:
                        k.dma(k.q_pool, o_pconv[:, fc * 128:(fc + 1) * 128].rearrange("j f -> f j"), gbuf[:, T:T + 2], r=[b_gbuf])
                    if ci == 4:
                        for b in range(4):
                            k.dma(k.q_pool, o_sconv[b, :, fc * 128:(fc + 1) * 128].rearrange("j f -> f j"), gbs[:, b, 4:6], r=[b_gbs])
            for t in range(NT):
                t0, n = tile_rows(t)
                xd = x1_tile(t)
                for half in range(2):
                    pb = 4 + ((t * 2 + half) % 2)
                    for fl in range(GW):
                        k.mm(ps[pb][0:n, :], mTg[wi][:, fl, t0:t0 + n], wd_b[wi][:, fl, half * 512:(half + 1) * 512], start=(fl == 0), stop=(fl == GW - 1),
                             r=[b_mTg[wi], b_wd[wi]], w=[b_ps[pb]])
                    k.v("dve", lambda: nc.vector.tensor_tensor(out=xd[0:n, half * 512:(half + 1) * 512], in0=xd[0:n, half * 512:(half + 1) * 512],
                                                                in1=ps[pb][0:n, :], op=ALU.add), r=[b_x1[t], b_ps[pb]], w=[b_x1[t]])
    for t in range(NT):
        t0, n = tile_rows(t)
        s = t % 2
        xd = x1_tile(t)
        k.actf(junk3[0:n, :], xd[0:n, :], AF.Square, r=[b_x1[t]], w=[b_junk3, b_st3[s]], accum=st3[s][0:n, 0:1])
        k.actf(st3[s][0:n, 1:2], st3[s][0:n, 0:1], AF.Sqrt, r=[b_st3[s]], w=[b_st3[s]], scale=1.0 / D, bias=EPS)
        k.v("dve", lambda: nc.vector.reciprocal(out=st3[s][0:n, 2:3], in_=st3[s][0:n, 1:2]), r=[b_st3[s]], w=[b_st3[s]])
        k.v("dve", lambda: nc.vector.scalar_tensor_tensor(out=yo[s][0:n, :], in0=xd[0:n, :], scalar=st3[s][0:n, 2:3], in1=gf_b[0:n, :],
                                                           op0=ALU.mult, op1=ALU.mult), r=[b_x1[t], b_st3[s], b_gf], w=[b_yo[s]])
        if t < 16:
            k.dma(k.q_pool, o_yp[t0:t0 + n, :], yo[s][0:n, :], r=[b_yo[s]])
        else:
            k.dma(k.q_pool, o_ys[:, :], yo[s][0:n, :], r=[b_yo[s]])
    k.barrier()
    stD.close()
    stC.close()
    k.finish()
    return nc


def _inputs_for_core(c, inp):
    m = {
        "xp": np.ascontiguousarray(inp["x_prompt"][c]),
        "xs": np.ascontiguousarray(inp["x_sample"][4 * c:4 * c + 4].reshape(TS, D)),
        "w_in": np.ascontiguousarray(inp["w_in"][0]),
        "norm1_g": np.ascontiguousarray(inp["norm1_g"][0]),
        "q_norm_g": np.ascontiguousarray(inp["q_norm_g"][0]),
        "kv_norm_g": np.ascontiguousarray(inp["kv_norm_g"][0]),
        "w_uq": np.ascontiguousarray(inp["w_uq"][0].reshape(384, 768)),
        "ropecs": rope_tables(),
        "w_uk": np.ascontiguousarray(inp["w_uk"][0].reshape(256, 512)),
        "w_uv": np.ascontiguousarray(inp["w_uv"][0].reshape(256, 512)),
        "trimask": tri_masks(),
        "norm_f_g": np.ascontiguousarray(inp["norm_f_g"]), "norm2_g": np.ascontiguousarray(inp["norm2_g"][0]),
        "w_proj_mla": np.ascontiguousarray(inp["w_proj_mla"][0]), "w_proj_nsa": np.ascontiguousarray(inp["w_proj_nsa"][0]),
        "w_out": np.ascontiguousarray(inp["w_out"][0]), "w_gate": np.ascontiguousarray(inp["w_gate"][0]),
        "w_up": np.ascontiguousarray(inp["w_up"][0]), "w_down": np.ascontiguousarray(inp["w_down"][0]),
        "conv_w": np.ascontiguousarray(inp["conv_w"][0]), "conv_b": np.ascontiguousarray(inp["conv_b"][0]),
        "state_ffn_conv": np.ascontiguousarray(inp["state_ffn_conv"][0, 4 * c:4 * c + 4].reshape(8, DFF)),
        **nsa_consts(),
        **sample_consts(),
        "page_table": np.ascontiguousarray(inp["page_table"][4 * c:4 * c + 4]).astype(np.int32),
        "cache_mla_ckv": inp["cache_mla_ckv"].reshape(-1, 256), "cache_mla_krope": inp["cache_mla_krope"].reshape(-1, 32),
        "cache_nsa_cmp_k": inp["cache_nsa_cmp_k"].reshape(-1, 128), "cache_nsa_cmp_v": inp["cache_nsa_cmp_v"].reshape(-1, 128),
        "cache_nsa_slc_k": inp["cache_nsa_slc_k"].reshape(-1, 128), "cache_nsa_slc_v": inp["cache_nsa_slc_v"].reshape(-1, 128),
        "cmp_w1_k": np.ascontiguousarray(inp["cmp_w1_k"][0]), "cmp_w2_k": np.ascontiguousarray(inp["cmp_w2_k"][0]),
        "cmp_w1_v": np.ascontiguousarray(inp["cmp_w1_v"][0]), "cmp_w2_v": np.ascontiguousarray(inp["cmp_w2_v"][0]),
        "cmp_pos_k": np.ascontiguousarray(inp["cmp_pos_k"][0]), "cmp_pos_v": np.ascontiguousarray(inp["cmp_pos_v"][0]),
        "identf": np.eye(128, dtype=np.float32),
        "state_win_k": np.ascontiguousarray(inp["state_win_k"][0, 4 * c:4 * c + 4].reshape(4, 512, 128)),
        "state_win_v": np.ascontiguousarray(inp["state_win_v"][0, 4 * c:4 * c + 4].reshape(4, 512, 128)),
    }
    return m


def kernel(**inp):
    nc = build()
    in_maps = [_inputs_for_core(c, inp) for c in range(8)]
    res = run_bass_kernel_spmd(nc, in_maps, core_ids=list(range(8)))
    R = res.results

    def cat(nm, shape_per_core, default=None):
        outs = []
        for c in range(8):
            if nm in R[c]:
                outs.append(np.asarray(R[c][nm], dtype=np.float32).reshape(shape_per_core))
            else:
                outs.append(np.zeros(shape_per_core, np.float32))
        return outs

    y_prompt = np.stack(cat("yp", (T, D)), 0)
    y_sample = np.concatenate(cat("ys", (4, 4, D)), 0)
    outs = [y_prompt, y_sample]
    for nm, w in (("ckv", (256,)), ("krope", (32,)), ("cmp_k", (2, 64)), ("cmp_v", (2, 64)),
                  ("slc_k", (2, 64)), ("slc_v", (2, 64))):
        outs.append(np.stack(cat("p_" + nm, (T,) + w), 0)[None])
        outs.append(np.concatenate(cat("s_" + nm, (4, 4) + w), 0)[None])
    for nm in ("win_k", "win_v"):
        outs.append(np.stack(cat("p_" + nm, (512, 2, 64)), 0)[None])
        outs.append(np.concatenate(cat("s_" + nm, (4, 512, 2, 64)), 0)[None])
    outs.append(np.stack(cat("p_conv", (2, DFF)), 0)[None])
    outs.append(np.concatenate(cat("s_conv", (4, 2, DFF)), 0)[None])
    return tuple(outs)
```

```python
import os
import contextlib
import numpy as np
import concourse.bass as bass
import concourse.mybir as mybir
from concourse.bass_utils import run_bass_kernel_spmd

F32 = mybir.dt.float32
BF16 = mybir.dt.bfloat16
I32 = mybir.dt.int32
AF = mybir.ActivationFunctionType
ALU = mybir.AluOpType
AX = mybir.AxisListType

D = 1024
T = 2048
TS = 16
TT = T + TS
NT = 17
EPS = 1e-6
MLA_SCALE = 96 ** -0.5
NSA_SCALE = 0.125
DFF = 2816
NFC = 22
PAST = 16384
C_CQ, C_CKV, C_KR, C_QN, C_KVC, C_KVS, C_KVW, C_GN, C_GA, C_GB = 0, 384, 640, 672, 1184, 1440, 1696, 1952, 1976, 3000


class Buf:
    __slots__ = ("name", "w", "r", "excl")

    def __init__(self, name="", excl=False):
        self.name = name
        self.excl = excl
        self.w = None
        self.r = {}


class Eng:
    def __init__(self, k, name, eng, is_pe=False):
        self.k = k
        self.name = name
        self.e = eng
        self.is_pe = is_pe
        self.sem = k.nc.alloc_semaphore(f"s_{name}_0")
        self.nsem = 1
        self.cnt = 0
        self.seen = {}

    def wait(self, tok):
        sem, val = tok
        if self.is_pe and sem is self.sem:
            return
        key = id(sem)
        if self.seen.get(key, 0) >= val:
            return
        self.seen[key] = val
        self.e.wait_ge(sem, val)

    def next_tok(self):
        if self.cnt >= 30000:
            self.sem = self.k.nc.alloc_semaphore(f"s_{self.name}_{self.nsem}")
            self.nsem += 1
            self.cnt = 0
        return (self.sem, self.cnt + 1)


class DmaQ:
    def __init__(self, k, name, eng_obj, nsem):
        self.k = k
        self.name = name
        self.E = eng_obj
        self.sems = [k.nc.alloc_semaphore(f"d_{name}_{i}") for i in range(nsem)]
        self.cnts = [0] * nsem
        self.i = 0


class K:
    def __init__(self, nc):
        self.nc = nc
        self.pe = Eng(self, "pe", nc.tensor, is_pe=True)
        self.act = Eng(self, "act", nc.scalar)
        self.dve = Eng(self, "dve", nc.vector)
        self.pool = Eng(self, "pool", nc.gpsimd)
        self.sp = Eng(self, "sp", nc.sync)
        self.q_sp = DmaQ(self, "sp", self.sp, 40)
        self.q_pool = DmaQ(self, "pool", self.pool, 24)
        self.all_dma_toks = []

    def _deps(self, E, r, w):
        for b in r:
            if b.w is not None:
                E.wait(b.w)
        for b in w:
            if b.w is not None:
                E.wait(b.w)
            for t in b.r.values():
                E.wait(t)

    def _mark(self, tok, r, w):
        for b in r:
            key = id(tok[0])
            old = b.r.get(key)
            if old is None or old[1] < tok[1]:
                b.r[key] = tok
        for b in w:
            b.w = tok
            b.r = {}

    def op(self, E, fn, r=(), w=(), inc=True):
        ex = [b for b in r if b.excl]
        if ex:
            r = [b for b in r if not b.excl]
            w = list(w) + ex
        self._deps(E, r, w)
        tok = E.next_tok()
        ins = fn()
        if inc:
            ins.then_inc(tok[0], 1)
            E.cnt += 1
        self._mark(tok, r, w)
        return ins

    def dma(self, q, out, in_, r=(), w=(), **kw):
        E = q.E
        self._deps(E, r, w)
        i = q.i
        q.i = (q.i + 1) % len(q.sems)
        if q.cnts[i] > 0:
            E.wait((q.sems[i], q.cnts[i]))
        if q.cnts[i] >= 30000:
            q.sems[i] = self.nc.alloc_semaphore(f"d_{q.name}_{i}_r{q.cnts[i]}")
            q.cnts[i] = 0
        q.cnts[i] += 16
        tok = (q.sems[i], q.cnts[i])
        E.e.dma_start(out=out, in_=in_, **kw).then_inc(tok[0], 16)
        self._mark(tok, r, w)
        return tok

    def gather(self, out, in_, idx_ap, r=(), w=()):
        q = self.q_pool
        E = q.E
        self._deps(E, r, w)
        i = q.i
        q.i = (q.i + 1) % len(q.sems)
        if q.cnts[i] > 0:
            E.wait((q.sems[i], q.cnts[i]))
        q.cnts[i] += 16
        tok = (q.sems[i], q.cnts[i])
        E.e.indirect_dma_start(out=out, out_offset=None, in_=in_,
                               in_offset=bass.IndirectOffsetOnAxis(ap=idx_ap, axis=0)).then_inc(tok[0], 16)
        self._mark(tok, r, w)
        return tok

    def barrier(self):
        toks = []
        for E in (self.pe, self.act, self.dve, self.pool, self.sp):
            if E.cnt > 0:
                toks.append((E.sem, E.cnt))
        for q in (self.q_sp, self.q_pool):
            for s, c in zip(q.sems, q.cnts):
                if c > 0:
                    toks.append((s, c))
        for E in (self.pe, self.act, self.dve, self.pool, self.sp):
            for t in toks:
                if E.is_pe and t[0] is E.sem:
                    continue
                E.wait(t)

    def finish(self):
        for q in (self.q_sp, self.q_pool):
            for s, c in zip(q.sems, q.cnts):
                if c > 0:
                    self.sp.wait((s, c))
        for E in (self.pe, self.act, self.dve, self.pool):
            if E.cnt > 0:
                self.sp.wait((E.sem, E.cnt))

    def mm(self, out, lhsT, rhs, start, stop, r=(), w=(), sig=None):
        if sig is None:
            sig = stop
        return self.op(self.pe, lambda: self.nc.tensor.matmul(out, lhsT, rhs, start=start, stop=stop),
                       r=r, w=w, inc=sig)

    def tr(self, out, in_, ident, r=(), w=(), sig=True):
        return self.op(self.pe, lambda: self.nc.tensor.transpose(out, in_, ident), r=r, w=w, inc=sig)

    def actf(self, out, in_, func, r=(), w=(), scale=1.0, bias=0.0, accum=None):
        kw = {}
        if accum is not None:
            kw["accum_out"] = accum
        return self.op(self.act, lambda: self.nc.scalar.activation(out=out, in_=in_, func=func, bias=bias,
                                                                   scale=scale, **kw), r=r, w=w)

    def v(self, which, fn, r=(), w=()):
        E = {"dve": self.dve, "pool": self.pool}[which]
        return self.op(E, fn, r=r, w=w)


def rope_tables():
    inv = (10000.0 ** (-np.arange(0, 32, 2, dtype=np.float32) / 32)).astype(np.float32)
    pos = np.concatenate([np.arange(T, dtype=np.float32),
                          np.tile(PAST + np.arange(4, dtype=np.float32), 4)])
    pos = np.concatenate([pos, np.zeros(NT * 128 - TT, np.float32)])
    ang = (pos[:, None] * inv[None, :]).astype(np.float32)
    cs = np.concatenate([np.cos(ang), np.sin(ang)], axis=1).astype(np.float32)
    return cs.reshape(NT, 128, 32).transpose(1, 0, 2).copy()


def tri_masks():
    kk = np.arange(128)[:, None]; qq = np.arange(128)[None, :]
    m = np.zeros((128, 2, 128), np.float32)
    m[:, 0, :] = np.where(kk > qq, -30000.0, 0.0)
    m[:, 1, :] = np.where(qq >= kk, -30000.0, 0.0)
    return m


def nsa_consts():
    c = {}
    q = np.arange(T)
    qrows = np.stack([np.full(T, 8.0), np.full(T, 8.0), -8.0 * (q // 128 * 128), -8.0 * (q % 128)]).astype(np.float32)
    krows = np.stack([(q // 128 * 128), (q % 128), np.ones(T), np.ones(T)]).astype(np.float32)
    e = 16 * np.arange(128) + 31
    kcrows = np.stack([(e // 128 * 128), (e % 128), np.ones(128), np.ones(128)]).astype(np.float32)
    c["qrows"] = qrows; c["krows"] = krows; c["kcrows"] = kcrows
    erows = np.zeros((32, T), np.float32)
    erows[q // 64, q] = 32768.0
    c["erows"] = erows
    cm = np.where(e[:, None] > q[None, :], -30000.0, 0.0).astype(np.float32)
    c["cmpmask"] = cm
    i = np.arange(128)[:, None]; j = np.arange(32)[None, :]
    ov = ((i * 16 < (j + 1) * 64) & (i * 16 + 32 > j * 64)).astype(np.float32)
    ov[127] = 0
    c["overlap"] = ov
    selb = np.zeros((24, 3, 8, 64), np.float32)
    for h in range(8):
        for br in range(3):
            selb[h * 3 + br, br, h, :] = 1.0
    c["selb"] = selb
    mt = np.zeros((128, 16, 32), np.float32); at = np.zeros((128, 16, 32), np.float32)
    for qt in range(16):
        for p in range(128):
            cur = (qt * 128 + p) // 64
            jj = np.arange(32)
            m = np.ones(32, np.float32); a = np.zeros(32, np.float32)
            m[jj > cur] = 0; a[jj > cur] = -1e30
            for f, val in ((0, 1e9), (cur, 2e9), (cur - 1, 4e9)):
                if f >= 0:
                    m[f] = 0; a[f] = val
            mt[p, qt] = m; at[p, qt] = a
    c["topk_m"] = mt; c["topk_a"] = at
    return c


SLOPES = [2.0 ** (-8.0 * (h + 1) / 8) for h in range(8)]


def sample_consts():
    c = {}
    sl = np.array(SLOPES, np.float64)
    hh = np.repeat(np.arange(8), 4); tt = np.tile(np.arange(4), 8)
    kb = np.arange(16) // 4; kt = np.arange(16) % 4
    en = np.zeros((16, 4, 32), np.float64); nm = np.zeros((16, 4, 32), np.float64)
    for b in range(4):
        ok = (kb[:, None] == b) & (kt[:, None] <= tt[None, :])
        en[:, b, :] = np.where(ok, np.exp(-sl[hh][None, :] * (tt[None, :] - kt[:, None])), 0.0)
        nm[:, b, :] = ok
    c["s_en"] = en.astype(np.float32); c["s_nm"] = nm.astype(np.float32)
    i = (np.arange(4)[None, :, None] * 128 + np.arange(128)[:, None, None])
    dist = tt[None, None, :] + 512 - i
    c["s_ebw"] = np.where(dist < 512, np.exp(-sl[hh][None, None, :] * dist), 0.0).astype(np.float32)
    p = np.arange(128)
    c["s_lb"] = np.stack([16384.0 - 128.0 * p, np.ones(128), (p == 127).astype(np.float64)]).astype(np.float32)
    rr = np.arange(128)
    rbs = np.zeros((3, 128, 32), np.float64)
    rbs[0] = -8.0 * sl[hh][None, :]
    rbs[1] = -8.0 * sl[hh][None, :] * (tt[None, :] - rr[:, None])
    c["s_rbs"] = rbs.astype(np.float32)
    cc = np.arange(8)
    rbc = np.zeros((3, 8, 32), np.float64)
    rbc[0] = -8.0 * sl[hh][None, :]
    rbc[1] = -8.0 * sl[hh][None, :] * (tt[None, :] - 16 * cc[:, None] - 31)
    rbc[2, 7, :] = -30000.0
    c["s_rbc"] = rbc.astype(np.float32)
    ii = (8 * p[:, None] + cc[None, :])[:, :, None]; jj = np.arange(264)[None, None, :]
    ov = ((ii * 16 < (jj + 1) * 64) & (ii * 16 + 32 > jj * 64) & (jj < 257) & (ii < 1023)).astype(np.float32)
    c["s_ov"] = ov
    m = np.ones((4, 264), np.float32); a = np.zeros((4, 264), np.float32)
    for f, val in ((0, 1e9), (256, 2e9), (255, 4e9)):
        m[:, f] = 0; a[:, f] = val
    m[:, 257:] = 0; a[:, 257:] = -1e30
    c["s_tkm"] = m; c["s_tka"] = a
    ss = np.zeros((16, 4), np.float32)
    for hl in range(4):
        for t in range(4):
            ss[hl * 4 + t, t] = 1.0
    c["s_ssum"] = ss
    return c


def build(phases=("A",)):
    nc = bass.Bass("TRN2", target_bir_lowering=False)
    k = K(nc)

    def din(name, shape, dt=F32):
        return nc.dram_tensor(name, list(shape), dt, kind="ExternalInput").ap()

    def dout(name, shape, dt=F32):
        return nc.dram_tensor(name, list(shape), dt, kind="ExternalOutput").ap()

    xp = din("xp", [T, D])
    xs = din("xs", [TS, D])
    w_in = din("w_in", [D, 4024])
    norm1_g = din("norm1_g", [D])
    q_norm_g = din("q_norm_g", [384])
    kv_norm_g = din("kv_norm_g", [256])
    w_uq = din("w_uq", [384, 768])
    ropecs = din("ropecs", [128, NT, 32])
    identf_d = din("identf", [128, 128])
    w_uk = din("w_uk", [256, 512])
    w_uv = din("w_uv", [256, 512])
    trimask_d = din("trimask", [128, 2, 128])
    qrows_d = din("qrows", [4, T]); krows_d = din("krows", [4, T]); kcrows_d = din("kcrows", [4, 128])
    erows_d = din("erows", [32, T]); cmpmask_d = din("cmpmask", [128, T]); overlap_d = din("overlap", [128, 32])
    selb_d = din("selb", [24, 3, 8, 64]); topkm_d = din("topk_m", [128, 16, 32]); topka_d = din("topk_a", [128, 16, 32])
    cw1 = {"k": din("cmp_w1_k", [2048, 64]), "v": din("cmp_w1_v", [2048, 64])}
    cw2 = {"k": din("cmp_w2_k", [64, 64]), "v": din("cmp_w2_v", [64, 64])}
    cpos = {"k": din("cmp_pos_k", [32, 64]), "v": din("cmp_pos_v", [32, 64])}
    DBG = bool(int(os.environ.get("KDBG", "0")))
    if DBG:
        dbg_omla = dout("dbg_omla", [128, 4, TT])
        dbg_onsa = dout("dbg_onsa", [128, 4, TT])
    norm_f_g = din("norm_f_g", [D]); norm2_g = din("norm2_g", [D])
    w_pm = din("w_proj_mla", [512, D]); w_pn = din("w_proj_nsa", [512, D]); w_out = din("w_out", [D, D])
    w_gate = din("w_gate", [D, DFF]); w_up = din("w_up", [D, DFF]); w_down = din("w_down", [DFF, D])
    conv_w = din("conv_w", [3, DFF]); conv_b = din("conv_b", [DFF]); ffn_state = din("state_ffn_conv", [8, DFF])
    o_yp = dout("yp", [T, D]); o_ys = dout("ys", [TS, D])
    o_pconv = dout("p_conv", [2, DFF]); o_sconv = dout("s_conv", [4, 2, DFF])
    NPG = 5120
    pt_d = din("page_table", [4, 128], I32)
    c_ckv = din("cache_mla_ckv", [NPG * 128, 256]); c_kr = din("cache_mla_krope", [NPG * 128, 32])
    c_ck = din("cache_nsa_cmp_k", [NPG * 128, 128]); c_cv = din("cache_nsa_cmp_v", [NPG * 128, 128])
    c_sk = din("cache_nsa_slc_k", [NPG * 128, 128]); c_sv = din("cache_nsa_slc_v", [NPG * 128, 128])
    s_en_d = din("s_en", [16, 4, 32]); s_nm_d = din("s_nm", [16, 4, 32]); s_ebw_d = din("s_ebw", [128, 4, 32])
    s_lb_d = din("s_lb", [3, 128]); s_rbs_d = din("s_rbs", [3, 128, 32]); s_rbc_d = din("s_rbc", [3, 8, 32])
    s_ov_d = din("s_ov", [128, 8, 264]); s_tkm_d = din("s_tkm", [4, 264]); s_tka_d = din("s_tka", [4, 264]); s_ssum_d = din("s_ssum", [16, 4])
    win_k_in = din("state_win_k", [4, 512, 128])
    win_v_in = din("state_win_v", [4, 512, 128])

    o_pckv = dout("p_ckv", [T, 256]); o_sckv = dout("s_ckv", [TS, 256])
    o_pkr = dout("p_krope", [T, 32]); o_skr = dout("s_krope", [TS, 32])
    o_kv = {}
    for nm in ("cmp_k", "cmp_v", "slc_k", "slc_v"):
        o_kv["p_" + nm] = dout("p_" + nm, [T, 128])
        o_kv["s_" + nm] = dout("s_" + nm, [TS, 128])
    o_pwk = dout("p_win_k", [512, 128]); o_pwv = dout("p_win_v", [512, 128])
    o_swk = dout("s_win_k", [4, 512, 128]); o_swv = dout("s_win_v", [4, 512, 128])

    sb = nc.alloc_sbuf_tensor

    def scoped(stack):
        def f(name, shape, dt):
            return stack.enter_context(nc.sbuf_tensor(name, shape, dt))
        return f
    identf = sb("identf_sb", [128, 128], F32); b_identf = Buf()
    identb = sb("identb_sb", [128, 128], BF16); b_identb = Buf()
    cs_sb = sb("cs_sb", [128, NT, 32], F32); b_cs = Buf()
    big = sb("big", [128, 16 * TT], BF16)
    hT = big[:, 0:8 * TT].rearrange("p (c t) -> p c t", c=8); b_hT = [Buf() for _ in range(NT)]
    g1col = sb("g1col", [128, 8], F32); b_g1 = Buf()
    gqcol = sb("gqcol", [128, 3], F32); b_gq = Buf()
    gkv_b = sb("gkv_b", [128, 256], F32); b_gkv = Buf()
    o_mlaT = big[:, 8 * TT:12 * TT].rearrange("p (c t) -> p c t", c=4); b_omla = Buf()
    o_nsaT = big[:, 12 * TT:16 * TT].rearrange("p (c t) -> p c t", c=4); b_onsa = Buf()
    x1all = big[:, 0:32768].bitcast(F32).rearrange("p (t d) -> p t d", t=16)
    x1s = sb("x1s", [128, D], F32)
    smp_qm = sb("smp_qm", [96, 8, TS], BF16); smp_ckvT = sb("smp_ckvT", [128, 2, TS], BF16); smp_krT = sb("smp_krT", [32, TS], BF16)
    smp_ckvn = sb("smp_ckvn", [TS, 257], BF16); smp_qn = sb("smp_qn", [64, 8, TS], BF16)
    smp_ksT = sb("smp_ksT", [128, TS], BF16); smp_kwT = sb("smp_kwT", [128, TS], BF16)
    smp_vs = sb("smp_vs", [TS, 129], BF16); smp_vw = sb("smp_vw", [TS, 129], BF16); smp_gn = sb("smp_gn", [24, TS], BF16)
    b_smp = Buf()
    b_x1 = [Buf() for _ in range(NT)]
    gf_b = sb("gf_b", [128, D], F32); b_gf = Buf()
    g2col = sb("g2col", [128, 8], F32); b_g2 = Buf()
    trimask = sb("trimask_sb", [128, 2, 128], BF16); b_tri = Buf()
    stL1 = contextlib.ExitStack(); sbL1 = scoped(stL1)
    vs_nat = sbL1("vs_nat", [128, NT, 2, 128], BF16)
    vw_nat = sbL1("vw_nat", [128, NT, 2, 128], BF16)
    b_vnat = [Buf() for _ in range(NT)]
    KAs = [sbL1(f"KAs{g}", [128, T], BF16) for g in range(2)]; b_KAs = [Buf(), Buf()]
    KAw = [sbL1(f"KAw{g}", [128, T], BF16) for g in range(2)]; b_KAw = [Buf(), Buf()]
    kccA = [sbL1(f"kccA{g}", [128, 128], BF16) for g in range(2)]; b_kccA = [Buf(), Buf()]
    vccA = [sbL1(f"vccA{g}", [128, 128], BF16) for g in range(2)]; b_vccA = [Buf(), Buf()]
    stL2 = contextlib.ExitStack(); sbL2 = scoped(stL2)
    qmT = sbL2("qmT", [96, 8, TT], BF16); b_qmT = [Buf() for _ in range(NT)]
    ckvT = sbL2("ckvT", [128, 2, TT], BF16); b_ckvT = [Buf() for _ in range(NT)]
    kropeT = sbL2("kropeT", [32, TT], BF16); b_krT = [Buf() for _ in range(NT)]

    ps = [nc.alloc_psum_tensor(f"ps{i}", [128, 512], F32) for i in range(8)]
    b_ps = [Buf(f"ps{i}", excl=True) for i in range(8)]
    psb = ps[7].bitcast(BF16)
    psbv2 = [ps[7].bitcast(BF16), ps[6].bitcast(BF16)]

    k.dma(k.q_sp, identf[:, :], identf_d[:, :], w=[b_identf])
    k.v("dve", lambda: nc.vector.tensor_copy(out=identb[:, :], in_=identf[:, :]), r=[b_identf], w=[b_identb])
    k.dma(k.q_sp, cs_sb[:, :, :], ropecs[:, :, :], w=[b_cs])
    k.dma(k.q_pool, trimask[:, :, :], trimask_d[:, :, :], w=[b_tri])
    with nc.allow_non_contiguous_dma(reason="small param columns"):
        k.dma(k.q_sp, g1col[:, :], norm1_g.rearrange("(c p) -> p c", p=128), w=[b_g1])
        k.dma(k.q_sp, gqcol[:, :], q_norm_g.rearrange("(c p) -> p c", p=128), w=[b_gq])
        k.dma(k.q_sp, gkv_b[:, :], kv_norm_g.partition_broadcast(128), w=[b_gkv])
        k.dma(k.q_sp, gf_b[:, :], norm_f_g.partition_broadcast(128), w=[b_gf])
        k.dma(k.q_sp, g2col[:, :], norm2_g.rearrange("(c p) -> p c", p=128), w=[b_g2])
    for tl in (smp_ckvn, smp_vs, smp_vw):
        k.v("pool", lambda: nc.gpsimd.memset(tl[:, :], 1.0), w=[b_smp])
    k.v("pool", lambda: nc.gpsimd.memset(vs_nat[:, :, :, :], 1.0), w=b_vnat)
    k.v("pool", lambda: nc.gpsimd.memset(vw_nat[:, :, :, :], 1.0), w=b_vnat)

    def tile_rows(t):
        return (t * 128, 128) if t < 16 else (T, TS)
    FAST = bool(int(os.environ.get('KFAST', '0')))
    TILES = [16] if FAST else list(range(NT))

    stA = contextlib.ExitStack(); sbA = scoped(stA)
    w1A = sbA("w1A", [128, 8, 672], BF16); b_w1A = [Buf() for _ in range(8)]
    w_uq_b = sbA("w_uq_b", [128, 3, 768], BF16); b_wuq = Buf()
    cqnT2 = [sbA(f"cqnT{i}", [128, 3, 128], BF16) for i in range(2)]; b_cqnT2 = [Buf(), Buf()]
    stg_ = [sbA(f"wstg{i}", [128, 768], F32) for i in range(2)]; b_stg_ = [Buf(), Buf()]
    for c in range(8):
        stg = stg_[c % 2]; b_stg = b_stg_[c % 2]
        k.dma(k.q_sp, stg[:, 0:672], w_in[c * 128:(c + 1) * 128, 0:672], w=[b_stg])
        k.actf(w1A[:, c, :], stg[:, 0:672], AF.Copy, r=[b_stg, b_g1], w=[b_w1A[c]], scale=g1col[:, c:c + 1])
    for c in range(3):
        stg = stg_[c % 2]; b_stg = b_stg_[c % 2]
        k.dma(k.q_sp, stg[:, 0:768], w_uq[c * 128:(c + 1) * 128, :], w=[b_stg])
        k.actf(w_uq_b[:, c, :], stg[:, 0:768], AF.Copy, r=[b_stg, b_gq], w=[b_wuq], scale=gqcol[:, c:c + 1])
    _al = {"o": 8 * TT}

    def alias(nelem_bf16, dt, shape):
        a = big[:, _al["o"]:_al["o"] + nelem_bf16]; _al["o"] += nelem_bf16
        assert _al["o"] <= 16 * TT
        if dt == F32:
            a = a.bitcast(F32)
        if len(shape) == 4:
            a = a.rearrange("p (a b c) -> p a b c", a=shape[1], b=shape[2])
        return a
    NB1 = 4
    xt = [sbA(f"xt{i}", [128, D], F32) for i in range(2)] + [alias(2 * D, F32, [128, D]) for _ in range(2)]; b_xt = [Buf() for _ in range(NB1)]
    xn = [sbA(f"xn{i}", [128, D], BF16) for i in range(2)] + [alias(D, BF16, [128, D]) for _ in range(2)]; b_xn = [Buf() for _ in range(NB1)]
    junk = sbA("junk", [128, D], BF16); b_junk = Buf()
    st1 = [sbA(f"st1_{i}", [128, 8], F32) for i in range(NB1)]; b_st1 = [Buf() for _ in range(NB1)]
    oA = [sbA(f"oA{i}", [128, 288], F32) for i in range(2)] + [alias(576, F32, [128, 288]) for _ in range(2)]; b_oA = [Buf() for _ in range(NB1)]
    tb = [sbA(f"tb{i}", [128, 672], BF16) for i in range(2)] + [alias(672, BF16, [128, 672]) for _ in range(2)]; b_tb = [Buf() for _ in range(NB1)]
    qtok = [sbA(f"qtok{i}", [128, 800], BF16) for i in range(2)] + [alias(800, BF16, [128, 800]) for _ in range(2)]; b_qtok = [Buf() for _ in range(NB1)]
    rtmp = [sbA(f"rtmp{i}", [128, 4, 8, 16], F32) for i in range(2)] + [alias(1024, F32, [128, 4, 8, 16]) for _ in range(2)]; b_rtmp = [Buf() for _ in range(NB1)]

    def p1_stages(t):
        t0, n = tile_rows(t)
        s = t % NB1
        src = xp[t0:t0 + n, :] if t < 16 else xs[:, :]
        gA, gB = (0, 1) if t % 2 == 0 else (2, 3)
        psA_, psB_ = ps[gA], ps[gB]; bA_, bB_ = b_ps[gA], b_ps[gB]
        cos = cs_sb[0:n, t, 0:16]; sin = cs_sb[0:n, t, 16:32]
        rt = rtmp[s]
        cqnT = cqnT2[t % 2]
        fs = []
        def f_S1():
            k.dma(k.q_sp, xt[s][0:n, :], src, w=[b_xt[s]])
            k.actf(junk[0:n, :], xt[s][0:n, :], AF.Square, r=[b_xt[s]], w=[b_junk, b_st1[s]], accum=st1[s][0:n, 0:1])
            k.actf(st1[s][0:n, 1:2], st1[s][0:n, 0:1], AF.Sqrt, r=[b_st1[s]], w=[b_st1[s]], scale=1.0 / D, bias=EPS)
            k.v("dve", lambda: nc.vector.reciprocal(out=st1[s][0:n, 2:3], in_=st1[s][0:n, 1:2]), r=[b_st1[s]], w=[b_st1[s]])
            k.v("dve", lambda: nc.vector.tensor_scalar(out=xn[s][0:n, :], in0=xt[s][0:n, :], scalar1=st1[s][0:n, 2:3],
                                                        scalar2=None, op0=ALU.mult), r=[b_xt[s], b_st1[s]], w=[b_xn[s]])
        fs.append(f_S1)
        def f_S2():
            psb = psbv2[0]; b_psb = b_ps[7]
            for c in range(8):
                k.tr(psb[:, c * 128:c * 128 + n], xn[s][0:n, c * 128:(c + 1) * 128], identb[0:n, 0:n],
                     r=[b_xn[s], b_identb], w=[b_psb], sig=(c == 7))
            k.v("dve", lambda: nc.vector.tensor_copy(
                out=hT[:, :, t0:t0 + n], in_=psb[:, :].rearrange("p (c n) -> p c n", c=8)[:, :, 0:n]),
                r=[b_psb], w=[b_hT[t]])
        fs.append(f_S2)
        def f_S3():
            for (pb, c0, c1) in ((gA, 0, 512), (gB, 512, 672)):
                for c in range(8):
                    k.mm(ps[pb][0:n, 0:c1 - c0], hT[:, c, t0:t0 + n], w1A[:, c, c0:c1], start=(c == 0), stop=(c == 7),
                         r=[b_hT[t], b_w1A[c]], w=[b_ps[pb]])
        fs.append(f_S3)
        def f_S4():
            k.actf(junk[0:n, 0:384], psA_[0:n, 0:384], AF.Square, r=[bA_], w=[b_junk, b_st1[s]], accum=st1[s][0:n, 3:4])
            k.actf(st1[s][0:n, 4:5], st1[s][0:n, 3:4], AF.Sqrt, r=[b_st1[s]], w=[b_st1[s]], scale=1.0 / 384, bias=EPS)
            k.v("dve", lambda: nc.vector.reciprocal(out=st1[s][0:n, 4:5], in_=st1[s][0:n, 4:5]), r=[b_st1[s]], w=[b_st1[s]])
            k.v("dve", lambda: nc.vector.tensor_scalar(out=tb[s][0:n, 0:384], in0=psA_[0:n, 0:384],
                                                        scalar1=st1[s][0:n, 4:5], scalar2=None, op0=ALU.mult),
                r=[bA_, b_st1[s]], w=[b_tb[s]])
            k.actf(junk[0:n, 0:128], psA_[0:n, 384:512], AF.Square, r=[bA_], w=[b_junk, b_st1[s]], accum=st1[s][0:n, 5:6])
            k.actf(junk[0:n, 0:128], psB_[0:n, 0:128], AF.Square, r=[bB_], w=[b_junk, b_st1[s]], accum=st1[s][0:n, 6:7])
            k.v("dve", lambda: nc.vector.tensor_tensor(out=st1[s][0:n, 5:6], in0=st1[s][0:n, 5:6], in1=st1[s][0:n, 6:7],
                                                        op=ALU.add), r=[b_st1[s]], w=[b_st1[s]])
            k.actf(st1[s][0:n, 6:7], st1[s][0:n, 5:6], AF.Sqrt, r=[b_st1[s]], w=[b_st1[s]], scale=1.0 / 256, bias=EPS)
            k.v("dve", lambda: nc.vector.reciprocal(out=st1[s][0:n, 6:7], in_=st1[s][0:n, 6:7]), r=[b_st1[s]], w=[b_st1[s]])
            k.v("dve", lambda: nc.vector.scalar_tensor_tensor(out=oA[s][0:n, 0:128], in0=psA_[0:n, 384:512],
                                                               scalar=st1[s][0:n, 6:7], in1=gkv_b[0:n, 0:128],
                                                               op0=ALU.mult, op1=ALU.mult),
                r=[bA_, b_st1[s], b_gkv], w=[b_oA[s]])
            k.v("dve", lambda: nc.vector.scalar_tensor_tensor(out=oA[s][0:n, 128:256], in0=psB_[0:n, 0:128],
                                                               scalar=st1[s][0:n, 6:7], in1=gkv_b[0:n, 128:256],
                                                               op0=ALU.mult, op1=ALU.mult),
                r=[bB_, b_st1[s], b_gkv], w=[b_oA[s]])
            cos = cs_sb[0:n, t, 0:16]; sin = cs_sb[0:n, t, 16:32]
            x1 = psB_[0:n, 128:144]; x2 = psB_[0:n, 144:160]
            rt = rtmp[s]
            k.v("dve", lambda: nc.vector.tensor_tensor(out=rt[0:n, 0, 0, :], in0=x1, in1=cos, op=ALU.mult), r=[bB_, b_cs], w=[b_rtmp[s]])
            k.v("dve", lambda: nc.vector.tensor_tensor(out=rt[0:n, 1, 0, :], in0=x2, in1=sin, op=ALU.mult), r=[bB_, b_cs], w=[b_rtmp[s]])
            k.v("dve", lambda: nc.vector.tensor_tensor(out=rt[0:n, 2, 0, :], in0=x1, in1=sin, op=ALU.mult), r=[bB_, b_cs], w=[b_rtmp[s]])
            k.v("dve", lambda: nc.vector.tensor_tensor(out=rt[0:n, 3, 0, :], in0=x2, in1=cos, op=ALU.mult), r=[bB_, b_cs], w=[b_rtmp[s]])
            k.v("dve", lambda: nc.vector.tensor_tensor(out=oA[s][0:n, 256:272], in0=rt[0:n, 0, 0, :], in1=rt[0:n, 1, 0, :], op=ALU.subtract), r=[b_rtmp[s]], w=[b_oA[s]])
            k.v("dve", lambda: nc.vector.tensor_tensor(out=oA[s][0:n, 272:288], in0=rt[0:n, 2, 0, :], in1=rt[0:n, 3, 0, :], op=ALU.add), r=[b_rtmp[s]], w=[b_oA[s]])
            if t < 16:
                k.dma(k.q_pool, o_pckv[t0:t0 + n, :], oA[s][0:n, 0:256], r=[b_oA[s]])
                k.dma(k.q_pool, o_pkr[t0:t0 + n, :], oA[s][0:n, 256:288], r=[b_oA[s]])
            else:
                k.dma(k.q_pool, o_sckv[:, :], oA[s][0:n, 0:256], r=[b_oA[s]])
                k.dma(k.q_pool, o_skr[:, :], oA[s][0:n, 256:288], r=[b_oA[s]])
            k.actf(tb[s][0:n, 384:672], oA[s][0:n, 0:288], AF.Copy, r=[b_oA[s]], w=[b_tb[s]])
        fs.append(f_S4)
        def f_S5():
            psb = psbv2[1]; b_psb = b_ps[6]
            for j in range(5):
                k.tr(psb[:, j * 128:j * 128 + n], tb[s][0:n, j * 128:(j + 1) * 128], identb[0:n, 0:n],
                     r=[b_tb[s], b_identb], w=[b_psb], sig=False)
            k.tr(psb[0:32, 640:640 + n], tb[s][0:n, 640:672], identb[0:n, 0:n], r=[b_tb[s], b_identb], w=[b_psb], sig=True)
            cqnT = cqnT2[t % 2]
            k.v("dve", lambda: nc.vector.tensor_copy(out=cqnT[:, :, 0:n],
                                                      in_=psb[:, 0:384].rearrange("p (c n) -> p c n", c=3)[:, :, 0:n]),
                r=[b_psb], w=[b_cqnT2[t % 2]])
            k.v("dve", lambda: nc.vector.tensor_copy(out=ckvT[:, :, t0:t0 + n],
                                                      in_=psb[:, 384:640].rearrange("p (c n) -> p c n", c=2)[:, :, 0:n]),
                r=[b_psb], w=[b_ckvT[t]])
            k.v("dve", lambda: nc.vector.tensor_copy(out=kropeT[:, t0:t0 + n], in_=psb[0:32, 640:640 + n]), r=[b_psb], w=[b_krT[t]])
        fs.append(f_S5)
        def f_S6():
            for (pb, c0, c1) in ((4, 0, 512), (5, 512, 768)):
                for c in range(3):
                    k.mm(ps[pb][0:n, 0:c1 - c0], cqnT[:, c, 0:n], w_uq_b[:, c, c0:c1], start=(c == 0), stop=(c == 2),
                         r=[b_cqnT2[t % 2], b_wuq], w=[b_ps[pb]])
        fs.append(f_S6)
        def f_S7():
            k.actf(qtok[s][0:n, 0:512], ps[4][0:n, 0:512], AF.Copy, r=[b_ps[4]], w=[b_qtok[s]])
            k.actf(qtok[s][0:n, 512:768], ps[5][0:n, 0:256], AF.Copy, r=[b_ps[5]], w=[b_qtok[s]])
            for (pb, h0, nh, base) in ((4, 0, 5, 64), (5, 5, 3, 5 * 96 + 64 - 512)):
                pv4 = ps[pb][0:n, base:base + (nh - 1) * 96 + 32]
                def hv(off):
                    return bass.AP(tensor=pv4.tensor, offset=pv4.offset + off, ap=[list(pv4.ap[0]), [96, nh], [1, 16]])
                x1h = hv(0); x2h = hv(16)
                cosb = cs_sb[0:n, t, 0:16].unsqueeze(1).to_broadcast([n, nh, 16]); sinb = cs_sb[0:n, t, 16:32].unsqueeze(1).to_broadcast([n, nh, 16])
                k.v("dve", lambda: nc.vector.tensor_tensor(out=rt[0:n, 0, h0:h0 + nh, :], in0=x1h, in1=cosb, op=ALU.mult), r=[b_ps[pb], b_cs], w=[b_rtmp[s]])
                k.v("dve", lambda: nc.vector.tensor_tensor(out=rt[0:n, 1, h0:h0 + nh, :], in0=x2h, in1=sinb, op=ALU.mult), r=[b_ps[pb], b_cs], w=[b_rtmp[s]])
                k.v("dve", lambda: nc.vector.tensor_tensor(out=rt[0:n, 2, h0:h0 + nh, :], in0=x1h, in1=sinb, op=ALU.mult), r=[b_ps[pb], b_cs], w=[b_rtmp[s]])
                k.v("dve", lambda: nc.vector.tensor_tensor(out=rt[0:n, 3, h0:h0 + nh, :], in0=x2h, in1=cosb, op=ALU.mult), r=[b_ps[pb], b_cs], w=[b_rtmp[s]])
            qv = qtok[s][0:n, 0:768].rearrange("p (h e) -> p h e", h=8)
            k.v("dve", lambda: nc.vector.tensor_tensor(out=qv[:, :, 64:80], in0=rt[0:n, 0, :, :], in1=rt[0:n, 1, :, :], op=ALU.subtract), r=[b_rtmp[s]], w=[b_qtok[s]])
            k.v("dve", lambda: nc.vector.tensor_tensor(out=qv[:, :, 80:96], in0=rt[0:n, 2, :, :], in1=rt[0:n, 3, :, :], op=ALU.add), r=[b_rtmp[s]], w=[b_qtok[s]])
        fs.append(f_S7)
        def f_S8():
            psb = psbv2[0]; b_psb = b_ps[7]
            for h in range(8):
                k.tr(psb[:, h * 128:h * 128 + n], qtok[s][0:n, h * 96:h * 96 + 128], identb[0:n, 0:n],
                     r=[b_qtok[s], b_identb], w=[b_psb], sig=(h == 7))
            k.v("dve", lambda: nc.vector.tensor_copy(out=qmT[0:96, :, t0:t0 + n],
                                                      in_=psb[0:96, :].rearrange("p (c n) -> p c n", c=8)[:, :, 0:n]),
                r=[b_psb], w=[b_qmT[t]])
        fs.append(f_S8)
        return fs
    OFFS = [0.0, 1.0, 1.1, 1.2, 2.3, 2.4, 2.5, 3.6]
    tl1 = []
    for ti_, t in enumerate(TILES):
        for kk_, f_ in enumerate(p1_stages(t)):
            tl1.append((ti_ + OFFS[kk_], f_))
    for _, f_ in sorted(tl1, key=lambda e: e[0]):
        f_()
    k.v("dve", lambda: nc.vector.tensor_copy(out=smp_ckvn[0:TS, 0:256], in_=tb[0][0:TS, 384:640]), r=[b_tb[0]], w=[b_smp])
    k.v("dve", lambda: nc.vector.tensor_copy(out=smp_ckvT[:, :, :], in_=ckvT[:, :, T:TT]), r=[b_ckvT[16]], w=[b_smp])
    k.v("dve", lambda: nc.vector.tensor_copy(out=smp_krT[:, :], in_=kropeT[:, T:TT]), r=[b_krT[16]], w=[b_smp])
    k.v("dve", lambda: nc.vector.tensor_copy(out=smp_qm[:, :, :], in_=qmT[:, :, T:TT]), r=[b_qmT[16]], w=[b_smp])
    k.barrier()
    stA.close()

    stB = contextlib.ExitStack(); sbB = scoped(stB)
    kcT = sbB("kcT", [128, TT], BF16); vcT = sbB("vcT", [128, TT], BF16)
    ksT = sbB("ksT", [128, TT], BF16); kwT = sbB("kwT", [128, TT], BF16)
    b_kvT = [Buf() for _ in range(NT)]
    stB1 = contextlib.ExitStack(); sbB1 = scoped(stB1)
    w2A = sbB1("w2A", [128, 8, 768], BF16); b_w2A = [Buf() for _ in range(8)]
    stg2_ = [sbB1(f"wstg2_{i}", [128, 768], F32) for i in range(3)]; b_stg2_ = [Buf(), Buf(), Buf()]
    oB = [sbB1(f"oB{i}", [128, 768], F32) for i in range(2)]; b_oB = [Buf(), Buf()]
    tb2 = [sbB1(f"tb2_{i}", [128, 768], BF16) for i in range(2)]; b_tb2 = [Buf(), Buf()]
    for c in range(8):
        stg2 = stg2_[c % 3]; b_stg2 = b_stg2_[c % 3]
        k.dma(k.q_sp, stg2[:, :], w_in[c * 128:(c + 1) * 128, C_KVC:C_KVC + 768], w=[b_stg2])
        k.actf(w2A[:, c, :], stg2[:, :], AF.Copy, r=[b_stg2, b_g1], w=[b_w2A[c]], scale=g1col[:, c:c + 1])
    for t in TILES:
        t0, n = tile_rows(t)
        s = t % 2
        g2a, g2b = (2, 3) if t % 2 == 0 else (0, 1)
        psb = psbv2[t % 2]; b_psb = b_ps[7 - (t % 2)]
        for (pb, c0, c1) in ((g2a, 0, 512), (g2b, 512, 768)):
            for c in range(8):
                k.mm(ps[pb][0:n, 0:c1 - c0], hT[:, c, t0:t0 + n], w2A[:, c, c0:c1], start=(c == 0), stop=(c == 7),
                     r=[b_hT[t], b_w2A[c]], w=[b_ps[pb]])
        k.actf(oB[s][0:n, 0:512], ps[g2a][0:n, 0:512], AF.Copy, r=[b_ps[g2a]], w=[b_oB[s]])
        k.actf(oB[s][0:n, 512:768], ps[g2b][0:n, 0:256], AF.Copy, r=[b_ps[g2b]], w=[b_oB[s]])
        pre = "p_" if t < 16 else "s_"
        rows = slice(t0, t0 + n) if t < 16 else slice(0, n)
        for i, nm in enumerate(("cmp_k", "cmp_v", "slc_k", "slc_v")):
            k.dma(k.q_pool, o_kv[pre + nm][rows, :], oB[s][0:n, i * 128:(i + 1) * 128], r=[b_oB[s]])
        if 12 <= t < 16:
            k.dma(k.q_pool, o_pwk[(t - 12) * 128:(t - 11) * 128, :], oB[s][0:n, 512:640], r=[b_oB[s]])
            k.dma(k.q_pool, o_pwv[(t - 12) * 128:(t - 11) * 128, :], oB[s][0:n, 640:768], r=[b_oB[s]])
        if t == 16:
            for b in range(4):
                k.dma(k.q_pool, o_swk[b, 508:512, :], oB[s][4 * b:4 * b + 4, 512:640], r=[b_oB[s]])
                k.dma(k.q_pool, o_swv[b, 508:512, :], oB[s][4 * b:4 * b + 4, 640:768], r=[b_oB[s]])
                k.dma(k.q_sp, o_swk[b, 0:508, :], win_k_in[b, 4:512, :])
                k.dma(k.q_sp, o_swv[b, 0:508, :], win_v_in[b, 4:512, :])
        k.v("dve", lambda: nc.vector.tensor_copy(out=tb2[s][0:n, :], in_=oB[s][0:n, 0:768]), r=[b_oB[s]], w=[b_tb2[s]])
        k.v("pool", lambda: nc.gpsimd.tensor_copy(out=vs_nat[0:n, t, :, 0:64],
                                                  in_=tb2[s][0:n, 384:512].rearrange("p (g d) -> p g d", g=2)),
            r=[b_tb2[s]], w=[b_vnat[t]])
        k.v("pool", lambda: nc.gpsimd.tensor_copy(out=vw_nat[0:n, t, :, 0:64],
                                                  in_=tb2[s][0:n, 640:768].rearrange("p (g d) -> p g d", g=2)),
            r=[b_tb2[s]], w=[b_vnat[t]])
        for j, c0 in enumerate((0, 128, 256, 512)):
            k.tr(psb[:, j * 128:j * 128 + n], tb2[s][0:n, c0:c0 + 128], identb[0:n, 0:n],
                 r=[b_tb2[s], b_identb], w=[b_psb], sig=(j == 3))
        for j, dst in enumerate((kcT, vcT, ksT, kwT)):
            k.v("dve", lambda: nc.vector.tensor_copy(out=dst[:, t0:t0 + n], in_=psb[:, j * 128:j * 128 + n]), r=[b_psb], w=[b_kvT[t]])

    k.v("dve", lambda: nc.vector.tensor_copy(out=smp_vs[0:TS, 0:128], in_=tb2[0][0:TS, 384:512]), r=[b_tb2[0]], w=[b_smp])
    k.v("dve", lambda: nc.vector.tensor_copy(out=smp_vw[0:TS, 0:128], in_=tb2[0][0:TS, 640:768]), r=[b_tb2[0]], w=[b_smp])
    k.v("dve", lambda: nc.vector.tensor_copy(out=smp_ksT[:, :], in_=ksT[:, T:TT]), r=[b_kvT[16]], w=[b_smp])
    k.v("dve", lambda: nc.vector.tensor_copy(out=smp_kwT[:, :], in_=kwT[:, T:TT]), r=[b_kvT[16]], w=[b_smp])
    k.barrier()
    stB1.close()
    W1 = {kv: sbB(f"W1{kv}", [128, 32, 128], BF16) for kv in "kv"}; b_W1 = {kv: Buf() for kv in "kv"}
    W2 = {kv: sbB(f"W2{kv}", [128, 128], BF16) for kv in "kv"}; b_W2 = {kv: Buf() for kv in "kv"}
    posT = {kv: sbB(f"posT{kv}", [128, 32], BF16) for kv in "kv"}; b_posT = {kv: Buf() for kv in "kv"}
    cbias = {kv: sbB(f"cbias{kv}", [128, 1], F32) for kv in "kv"}; b_cb = {kv: Buf() for kv in "kv"}
    hidT = {kv: sbB(f"hidT{kv}", [128, 128], BF16) for kv in "kv"}; b_hid = {kv: Buf() for kv in "kv"}
    cw = [sbB(f"cwk{i}", [128, 128], F32) for i in range(3)]; b_cw = [Buf() for _ in range(3)]
    krows = sbB("krows_sb", [4, T], BF16); b_krows = Buf()
    k.dma(k.q_pool, krows[:, :], krows_d[:, :], w=[b_krows])
    for g in range(0 if FAST else 2):
        for (KA, bK, srcT) in ((KAs, b_KAs, ksT), (KAw, b_KAw, kwT)):
            k.v("pool", lambda: nc.gpsimd.memset(KA[g][64:128, :], 0.0), w=[bK[g]])
            k.v("dve", lambda: nc.vector.tensor_copy(out=KA[g][0:64, :], in_=srcT[g * 64:(g + 1) * 64, 0:T]), r=b_kvT, w=[bK[g]])
            k.v("dve", lambda: nc.vector.tensor_copy(out=KA[g][64:68, :], in_=krows[0:4, :]), r=[b_krows], w=[bK[g]])
        k.dma(k.q_pool, KAs[g][96:128, :], erows_d[:, :], w=[b_KAs[g]])
        k.v("pool", lambda: nc.gpsimd.memset(kccA[g][:, :], 0.0), w=[b_kccA[g]])
        k.dma(k.q_pool, kccA[g][64:68, :], kcrows_d[:, :], w=[b_kccA[g]])
        k.v("pool", lambda: nc.gpsimd.memset(vccA[g][:, :], 1.0), w=[b_vccA[g]])
    with nc.allow_non_contiguous_dma(reason="small weight relayout"):
        for kv in "kv":
            k.v("pool", lambda: nc.gpsimd.memset(W1[kv][:, :, :], 0.0), w=[b_W1[kv]])
            k.v("pool", lambda: nc.gpsimd.memset(W2[kv][:, :], 0.0), w=[b_W2[kv]])
            w1v = cw1[kv].rearrange("(s d) h -> d s h", d=64)
            for g in range(2):
                k.dma(k.q_pool, W1[kv][g * 64:(g + 1) * 64, :, g * 64:(g + 1) * 64], w1v, w=[b_W1[kv]])
                k.dma(k.q_pool, W2[kv][g * 64:(g + 1) * 64, g * 64:(g + 1) * 64], cw2[kv][:, :], w=[b_W2[kv]])
                k.dma(k.q_pool, posT[kv][g * 64:(g + 1) * 64, :], cpos[kv].rearrange("s d -> d s"), w=[b_posT[kv]])
    for kv, srcT in ([] if FAST else (("k", kcT), ("v", vcT))):
        for s_ in range(32):
            k.mm(ps[6][:, 0:1], W1[kv][:, s_, :], posT[kv][:, s_:s_ + 1], start=(s_ == 0), stop=(s_ == 31),
                 r=[b_W1[kv], b_posT[kv]], w=[b_ps[6]])
        k.v("dve", lambda: nc.vector.tensor_copy(out=cbias[kv][:, :], in_=ps[6][:, 0:1]), r=[b_ps[6]], w=[b_cb[kv]])
        sv = srcT[:, 0:T].rearrange("p (c s) -> p c s", s=16)
        for half in range(2):
            pb = 4 + half
            for s_ in range(16):
                k.mm(ps[pb][:, 0:128], W1[kv][:, half * 16 + s_, :], sv[:, :, s_], start=(s_ == 0), stop=(s_ == 15),
                     r=[b_W1[kv]] + b_kvT, w=[b_ps[pb]])
        k.v("dve", lambda: nc.vector.tensor_copy(out=cw[0][:, 0:127], in_=ps[5][:, 1:128]), r=[b_ps[5]], w=[b_cw[0]])
        k.v("dve", lambda: nc.vector.scalar_tensor_tensor(out=cw[0][:, 0:127], in0=ps[4][:, 0:127], scalar=cbias[kv][:, 0:1],
                                                           in1=cw[0][:, 0:127], op0=ALU.add, op1=ALU.add),
            r=[b_ps[4], b_cb[kv], b_cw[0]], w=[b_cw[0]])
        k.v("dve", lambda: nc.vector.tensor_tensor(out=cw[1][:, 0:127], in0=cw[0][:, 0:127], in1=cw[0][:, 0:127], op=ALU.mult), r=[b_cw[0]], w=[b_cw[1]])
        k.v("dve", lambda: nc.vector.tensor_scalar(out=cw[1][:, 0:127], in0=cw[1][:, 0:127], scalar1=0.044715, scalar2=1.0, op0=ALU.mult, op1=ALU.add), r=[b_cw[1]], w=[b_cw[1]])
        k.v("dve", lambda: nc.vector.tensor_tensor(out=cw[1][:, 0:127], in0=cw[1][:, 0:127], in1=cw[0][:, 0:127], op=ALU.mult), r=[b_cw[0], b_cw[1]], w=[b_cw[1]])
        k.actf(cw[2][:, 0:127], cw[1][:, 0:127], AF.Sigmoid, r=[b_cw[1]], w=[b_cw[2]], scale=1.5957691216)
        k.v("dve", lambda: nc.vector.tensor_tensor(out=hidT[kv][:, 0:127], in0=cw[2][:, 0:127], in1=cw[0][:, 0:127], op=ALU.mult), r=[b_cw[0], b_cw[2]], w=[b_hid[kv]])
        if kv == "k":
            k.mm(ps[6][:, 0:127], W2["k"][:, :], hidT["k"][:, 0:127], start=True, stop=True, r=[b_W2["k"], b_hid["k"]], w=[b_ps[6]])
            for g in range(2):
                k.v("dve", lambda: nc.vector.tensor_copy(out=kccA[g][0:64, 0:127], in_=ps[6][g * 64:(g + 1) * 64, 0:127]), r=[b_ps[6]], w=[b_kccA[g]])
        else:
            k.mm(ps[6][0:127, 0:128], hidT["v"][:, 0:127], W2["v"][:, :], start=True, stop=True, r=[b_W2["v"], b_hid["v"]], w=[b_ps[6]])
            for g in range(2):
                k.v("dve", lambda: nc.vector.tensor_copy(out=vccA[g][0:127, 0:64], in_=ps[6][0:127, g * 64:(g + 1) * 64]), r=[b_ps[6]], w=[b_vccA[g]])
    k.barrier()
    stB.close()
    def attn_chunk(st, sched, Kt, Qr, Vt, scale, finish_fn):
        nS = len(sched)
        acc_i = st["acc_i"]; st["acc_i"] = 1 - acc_i
        acc = ps[2 + acc_i]; b_acc = b_ps[2 + acc_i]
        sb_ = st["s_banks"]
        pend = []
        nPT = len(st["PT"])
        for i, (kt, c_lo, c_hi, mask, nk) in enumerate(sched):
            si = st["s_i"]; st["s_i"] = (si + 1) % len(sb_)
            psS = ps[sb_[si]]; b_S = b_ps[sb_[si]]
            lhsT, rl = Kt(kt)
            rhs, rr = Qr(c_lo, c_hi)
            k.mm(psS[0:nk, c_lo:c_hi], lhsT, rhs, start=True, stop=(mask is None), r=rl + rr, w=[b_S])
            if mask is not None:
                m_ap, m_lo, rm = mask
                k.mm(psS[0:nk, m_lo:m_lo + 128], identb[0:nk, 0:nk], m_ap, start=False, stop=True,
                     r=[b_identb] + rm, w=[b_S])
            pi = st["p_i"]; st["p_i"] = (pi + 1) % nPT
            PT = st["PT"][pi]; b_PT = st["b_PT"][pi]
            k.actf(PT[0:nk, c_lo:c_hi], psS[0:nk, c_lo:c_hi], AF.Exp, r=[b_S], w=[b_PT], scale=scale)

            def mk(kt=kt, c_lo=c_lo, c_hi=c_hi, PT=PT, b_PT=b_PT, i=i, nk=nk):
                v_ap, rv = Vt(kt)
                k.mm(acc[:, c_lo:c_hi], v_ap, PT[0:nk, c_lo:c_hi], start=(i == 0), stop=(i == nS - 1),
                     r=[b_PT] + rv, w=[b_acc])
            pend.append(mk)
            if len(pend) > 2:
                pend.pop(0)()
        for f in pend:
            f()
        finish_fn(acc, b_acc)

    stM = contextlib.ExitStack()

    def sbM(name, shape, dt):
        return stM.enter_context(nc.sbuf_tensor(name, shape, dt))
    w_uk_b = sbM("w_uk_b", [128, 2, 512], BF16); b_wuk = Buf()
    w_uv_b = sbM("w_uv_b", [128, 2, 512], BF16); b_wuv = Buf()
    KhT = [sbM(f"KhT{i}", [96, T], BF16) for i in range(2)]; b_KhT = [Buf(), Buf()]
    Vh = [sbM(f"Vh{i}", [128, 16, 128], BF16) for i in range(2)]; b_Vh = [Buf(), Buf()]
    PTs = [sbM(f"PT{i}", [128, 512], BF16) for i in range(4)]; b_PTs = [Buf() for _ in range(4)]
    rz = [sbM(f"rz{i}", [64, 512], F32) for i in range(2)]; b_rz = [Buf(), Buf()]
    ast = {"acc_i": 0, "s_i": 0, "p_i": 0, "PT": PTs, "b_PT": b_PTs, "rz_i": 0, "s_banks": [0, 1, 6]}
    for c in range(2):
        k.dma(k.q_pool, w_uk_b[:, c, :], w_uk[c * 128:(c + 1) * 128, :], w=[b_wuk])
        k.dma(k.q_pool, w_uv_b[:, c, :], w_uv[c * 128:(c + 1) * 128, :], w=[b_wuv])
    for i in range(2):
        k.v("pool", lambda: nc.gpsimd.memset(Vh[i][:, :, 64:128], 1.0), w=[b_Vh[i]])
        if not FAST:
            k.actf(KhT[i][64:96, :], kropeT[0:32, 0:T], AF.Copy, r=b_krT, w=[b_KhT[i]])

    def norm_out(dstT, h, qc):
        def fin(acc, b_acc):
            ri = ast["rz_i"]; ast["rz_i"] = 1 - ri
            k.v("dve", lambda: nc.vector.reciprocal(out=rz[ri][0:64, :], in_=acc[64:128, :]), r=[b_acc], w=[b_rz[ri]])
            p0 = (h % 2) * 64
            k.v("dve", lambda: nc.vector.tensor_tensor(out=dstT[p0:p0 + 64, h // 2, qc * 512:(qc + 1) * 512],
                                                        in0=acc[0:64, :], in1=rz[ri][0:64, :], op=ALU.mult),
                r=[b_acc, b_rz[ri]], w=[b_omla])
        return fin

    def causal_sched(qc):
        out = []
        for kt in range(4 * qc + 4):
            c_lo = max(0, kt * 128 - qc * 512)
            mask = (trimask[:, 0, :], c_lo, [b_tri]) if kt * 128 >= qc * 512 else None
            out.append((kt, c_lo, 512, mask, 128))
        return out

    NH_M = 0 if FAST else int(os.environ.get("KNH", "8"))
    def mla_jit(h):
        s = h % 2
        for qc in range(4):
            pb = 4 + (qc % 2)
            for c in range(2):
                k.mm(ps[pb][0:64, :], w_uk_b[:, c, h * 64:(h + 1) * 64], ckvT[:, c, qc * 512:(qc + 1) * 512],
                     start=(c == 0), stop=(c == 1), r=[b_wuk] + b_ckvT, w=[b_ps[pb]])
            k.v("dve", lambda: nc.vector.tensor_copy(out=KhT[s][0:64, qc * 512:(qc + 1) * 512], in_=ps[pb][0:64, :]),
                r=[b_ps[pb]], w=[b_KhT[s]])
        for g8 in range(2):
            pb = 4 + g8
            for j in range(8):
                kt = g8 * 8 + j
                for c in range(2):
                    k.mm(ps[pb][:, j * 64:(j + 1) * 64], ckvT[:, c, kt * 128:(kt + 1) * 128], w_uv_b[:, c, h * 64:(h + 1) * 64],
                         start=(c == 0), stop=(c == 1), r=[b_wuv] + b_ckvT, w=[b_ps[pb]], sig=(c == 1 and j == 7))
            k.v("dve", lambda: nc.vector.tensor_copy(out=Vh[s][:, g8 * 8:(g8 + 1) * 8, 0:64],
                                                      in_=ps[pb][:, :].rearrange("p (j d) -> p j d", j=8)),
                r=[b_ps[pb]], w=[b_Vh[s]])

    if NH_M > 0:
        mla_jit(0)
    for h in range(NH_M):
        s = h % 2
        for qc in range(4):
            if qc == 2 and h + 1 < NH_M:
                mla_jit(h + 1)
            attn_chunk(ast, causal_sched(qc),
                       Kt=lambda kt: (KhT[s][0:96, kt * 128:(kt + 1) * 128], [b_KhT[s]]),
                       Qr=lambda a, b_: (qmT[0:96, h, qc * 512 + a:qc * 512 + b_], b_qmT),
                       Vt=lambda kt: (Vh[s][:, kt, :], [b_Vh[s]]),
                       scale=MLA_SCALE, finish_fn=norm_out(o_mlaT, h, qc))
    if DBG:
        k.dma(k.q_pool, dbg_omla[:, :, :], o_mlaT[:, :, :], r=[b_omla])
    k.barrier()
    stM.close()
    stL2.close()

    stN = contextlib.ExitStack()

    def sbN(name, shape, dt):
        return stN.enter_context(nc.sbuf_tensor(name, shape, dt))
    QA = [sbN(f"QA{h}", [128, TT], BF16) for h in range(8)]; b_QA = [Buf() for _ in range(8)]
    w_qn = sbN("w_qn", [128, 8, 536], BF16); b_wqn = Buf()
    stgN_ = [sbN("stgN0", [128, 536], F32)] * 2; b_stgN_ = [Buf()] * 2
    gnT = sbN("gnT", [24, TT], BF16); b_gnT = Buf()
    selb = sbN("selb_sb", [24, 3, 8, 64], BF16); b_selb = Buf()
    qrows = sbN("qrows_sb", [4, T], BF16); b_qrows = Buf()
    cmpmask = sbN("cmpmask_sb", [128, T], BF16); b_cmask = Buf()
    overlapM = sbN("overlap_sb", [128, 32], BF16); b_ovl = Buf()
    topkm = sbN("topkm_sb", [128, 16, 32], F32); topka = sbN("topka_sb", [128, 16, 32], F32); b_topk = Buf()
    scoreT = [sbN(f"scoreT{g}", [32, T], F32) for g in range(2)]; b_sc = [Buf(), Buf()]
    PTn = [sbN(f"PTn{i}", [128, 512], BF16) for i in range(4)]; b_PTn = [Buf() for _ in range(4)]
    rzn = [sbN(f"rzn{i}", [64, 512], F32) for i in range(2)]; b_rzn = [Buf(), Buf()]
    tmpn = [sbN(f"tmpn{i}", [128, 512], F32) for i in range(2)]; b_tmpn = [Buf(), Buf()]
    tk = sbN("tk", [128, 6, 64], F32); b_tk = Buf()
    tmpU = sbN("tmpU", [32, 512], F32); b_tmpU = Buf()
    tkb = sbN("tkb", [128, 64], BF16); b_tkb = Buf()
    nst = {"acc_i": 0, "s_i": 0, "p_i": 0, "PT": PTn, "b_PT": b_PTn, "rz_i": 0, "s_banks": [0, 1, 4]}

    k.dma(k.q_pool, selb[:, :, :, :], selb_d[:, :, :, :], w=[b_selb])
    k.dma(k.q_pool, qrows[:, :], qrows_d[:, :], w=[b_qrows])
    k.dma(k.q_pool, cmpmask[:, :], cmpmask_d[:, :], w=[b_cmask])
    k.dma(k.q_pool, overlapM[:, :], overlap_d[:, :], w=[b_ovl])
    k.dma(k.q_sp, topkm[:, :, :], topkm_d[:, :, :], w=[b_topk])
    k.dma(k.q_sp, topka[:, :, :], topka_d[:, :, :], w=[b_topk])
    for h in range(8):
        k.v("pool", lambda: nc.gpsimd.memset(QA[h][64:128, :], 0.0), w=[b_QA[h]])
        k.actf(QA[h][64:68, 0:T], qrows[0:4, :], AF.Copy, r=[b_qrows], w=[b_QA[h]], scale=float(SLOPES[h]))
    for c in range(8):
        stgN = stgN_[c % 2]; b_stgN = b_stgN_[c % 2]
        k.dma(k.q_sp, stgN[:, 0:512], w_in[c * 128:(c + 1) * 128, C_QN:C_QN + 512], w=[b_stgN])
        k.dma(k.q_sp, stgN[:, 512:536], w_in[c * 128:(c + 1) * 128, C_GN:C_GN + 24], w=[b_stgN])
        k.actf(w_qn[:, c, :], stgN[:, :], AF.Copy, r=[b_stgN, b_g1], w=[b_wqn], scale=g1col[:, c:c + 1])
    chunks = [(0, 512), (512, 512), (1024, 512), (1536, 512), (2048, TS)]
    if FAST:
        chunks = [(2048, TS)]
    for (c0, cn) in chunks:
        for hp in range(4):
            pb = 4 + (hp % 2)
            for c in range(8):
                k.mm(ps[pb][:, 0:cn], w_qn[:, c, hp * 128:(hp + 1) * 128], hT[:, c, c0:c0 + cn], start=(c == 0), stop=(c == 7),
                     r=[b_wqn] + b_hT, w=[b_ps[pb]])
            k.v("dve", lambda: nc.vector.tensor_copy(out=QA[2 * hp][0:64, c0:c0 + cn], in_=ps[pb][0:64, 0:cn]), r=[b_ps[pb]], w=[b_QA[2 * hp]])
            k.v("dve", lambda: nc.vector.tensor_copy(out=QA[2 * hp + 1][0:64, c0:c0 + cn], in_=ps[pb][64:128, 0:cn]), r=[b_ps[pb]], w=[b_QA[2 * hp + 1]])
        if True:
            for c in range(8):
                k.mm(ps[6][0:24, 0:cn], w_qn[:, c, 512:536], hT[:, c, c0:c0 + cn], start=(c == 0), stop=(c == 7),
                     r=[b_wqn] + b_hT, w=[b_ps[6]])
            k.actf(gnT[0:24, c0:c0 + cn], ps[6][0:24, 0:cn], AF.Sigmoid, r=[b_ps[6]], w=[b_gnT])

    def nsa_finish(br, h, qc, extra=None):
        def fin(acc, b_acc):
            ri = nst["rz_i"]; nst["rz_i"] = 1 - ri
            p0 = (h % 2) * 64
            tmp = tmpn[ri][p0:p0 + 64, :]
            k.v("dve", lambda: nc.vector.tensor_scalar(out=rzn[ri][0:64, :], in0=acc[64:128, :], scalar1=1e-30, scalar2=None, op0=ALU.max), r=[b_acc], w=[b_rzn[ri]])
            k.v("dve", lambda: nc.vector.reciprocal(out=rzn[ri][0:64, :], in_=rzn[ri][0:64, :]), r=[b_rzn[ri]], w=[b_rzn[ri]])
            k.v("dve", lambda: nc.vector.tensor_tensor(out=tmp, in0=acc[0:64, :], in1=rzn[ri][0:64, :], op=ALU.mult),
                r=[b_acc, b_rzn[ri]], w=[b_tmpn[ri]])
            if extra is not None:
                extra(rzn[ri], b_rzn[ri])
            k.mm(ps[6][0:64, :], selb[0:24, br, h, :], gnT[0:24, qc * 512:(qc + 1) * 512], start=True, stop=True,
                 r=[b_selb, b_gnT], w=[b_ps[6]])
            dst = o_nsaT[p0:p0 + 64, h // 2, qc * 512:(qc + 1) * 512]
            if br == 0:
                k.v("dve", lambda: nc.vector.tensor_tensor(out=dst, in0=tmp, in1=ps[6][0:64, :], op=ALU.mult),
                    r=[b_tmpn[ri], b_ps[6]], w=[b_onsa])
            else:
                k.v("dve", lambda: nc.vector.tensor_tensor(out=tmp, in0=tmp, in1=ps[6][0:64, :], op=ALU.mult),
                    r=[b_tmpn[ri], b_ps[6]], w=[b_tmpn[ri]])
                k.v("pool", lambda: nc.gpsimd.tensor_tensor(out=dst, in0=dst, in1=tmp, op=ALU.add),
                    r=[b_tmpn[ri], b_onsa], w=[b_onsa])
        return fin

    BR = "" if FAST else os.environ.get("KBR", "csw")
    for h in range(0 if FAST else 8):
        g = h // 4
        for qc in range(4):
            cs_ = slice(qc * 512, (qc + 1) * 512)
            si = nst["s_i"] % 2; nst["s_i"] = (nst["s_i"] + 1) % 2
            psS = ps[si]; b_S = b_ps[si]
            k.mm(psS[0:127, :], kccA[g][0:68, 0:127], QA[h][0:68, cs_], start=True, stop=False, r=[b_kccA[g], b_QA[h]], w=[b_S])
            k.mm(psS[0:127, :], identb[0:127, 0:127], cmpmask[0:127, cs_], start=False, stop=True, r=[b_identb, b_cmask], w=[b_S])
            pi = nst["p_i"]; nst["p_i"] = (pi + 1) % 4
            PT = PTn[pi]; b_PT = b_PTn[pi]
            k.actf(PT[0:127, :], psS[0:127, :], AF.Exp, r=[b_S], w=[b_PT], scale=NSA_SCALE)
            ai = nst["acc_i"]; nst["acc_i"] = 1 - ai
            acc = ps[2 + ai]; b_acc = b_ps[2 + ai]
            k.mm(acc[:, :], vccA[g][0:127, :], PT[0:127, :], start=True, stop=True, r=[b_vccA[g], b_PT], w=[b_acc])
            k.mm(ps[5][0:32, :], overlapM[0:127, 0:32], PT[0:127, :], start=True, stop=True, r=[b_ovl, b_PT], w=[b_ps[5]])

            def extra(rz_, b_rz_, h=h, g=g, cs_=cs_):
                if h % 4 == 0:
                    k.v("dve", lambda: nc.vector.tensor_tensor(out=scoreT[g][0:32, cs_], in0=ps[5][0:32, :], in1=rz_[0:32, :], op=ALU.mult),
                        r=[b_ps[5], b_rz_], w=[b_sc[g]])
                else:
                    k.v("dve", lambda: nc.vector.tensor_tensor(out=tmpU[0:32, :], in0=ps[5][0:32, :], in1=rz_[0:32, :], op=ALU.mult),
                        r=[b_ps[5], b_rz_], w=[b_tmpU])
                    k.v("dve", lambda: nc.vector.tensor_tensor(out=scoreT[g][0:32, cs_], in0=scoreT[g][0:32, cs_], in1=tmpU[0:32, :], op=ALU.add),
                        r=[b_tmpU, b_sc[g]], w=[b_sc[g]])
            nsa_finish(0, h, qc, extra)(acc, b_acc)

    psb7 = ps[7].bitcast(BF16)
    for qt in range(0 if FAST else 16):
        ts_ = slice(qt * 128, (qt + 1) * 128)
        for g in range(2):
            k.tr(ps[6][:, g * 32:(g + 1) * 32], scoreT[g][0:32, ts_], identf[0:32, 0:32], r=[b_sc[g], b_identf], w=[b_ps[6]], sig=(g == 1))
        smv = tk[:, 0, :].rearrange("p (g j) -> p g j", g=2)
        k.v("dve", lambda: nc.vector.tensor_tensor(out=smv, in0=ps[6][:, 0:64].rearrange("p (g j) -> p g j", g=2),
                                                    in1=topkm[:, qt, :].unsqueeze(1).to_broadcast([128, 2, 32]), op=ALU.mult),
            r=[b_ps[6], b_topk], w=[b_tk])
        k.v("dve", lambda: nc.vector.tensor_tensor(out=smv, in0=smv, in1=topka[:, qt, :].unsqueeze(1).to_broadcast([128, 2, 32]), op=ALU.add),
            r=[b_tk, b_topk], w=[b_tk])
        for g in range(2):
            sm = tk[:, 0, g * 32:(g + 1) * 32]
            k.v("dve", lambda: nc.vector.max(out=tk[:, 1, g * 8:g * 8 + 8], in_=sm), r=[b_tk], w=[b_tk])
            k.v("dve", lambda: nc.vector.match_replace(out=tk[:, 2, g * 32:(g + 1) * 32], in_to_replace=tk[:, 1, g * 8:g * 8 + 8], in_values=sm, imm_value=-3.0e38), r=[b_tk], w=[b_tk])
            k.v("dve", lambda: nc.vector.max(out=tk[:, 3, g * 8:g * 8 + 8], in_=tk[:, 2, g * 32:(g + 1) * 32]), r=[b_tk], w=[b_tk])
            k.v("dve", lambda: nc.vector.tensor_scalar(out=tkb[:, g * 32:(g + 1) * 32], in0=sm, scalar1=tk[:, 3, g * 8 + 7:g * 8 + 8], scalar2=-1.0,
                                                        op0=ALU.is_ge, op1=ALU.add), r=[b_tk], w=[b_tkb])
        k.tr(psb7[0:64, 0:128], tkb[:, 0:64], identb[:, :], r=[b_tkb, b_identb], w=[b_ps[7]])
        for h in range(8):
            g = h // 4
            k.v("dve", lambda: nc.vector.tensor_copy(out=QA[h][96:128, ts_], in_=psb7[g * 32:(g + 1) * 32, 0:128]), r=[b_ps[7]], w=[b_QA[h]])

    if "s" in BR:
        for h in range(8):
            g = h // 4
            for qc in range(4):
                attn_chunk(nst, causal_sched(qc),
                           Kt=lambda kt: (KAs[g][:, kt * 128:(kt + 1) * 128], [b_KAs[g]]),
                           Qr=lambda a, b_: (QA[h][:, qc * 512 + a:qc * 512 + b_], [b_QA[h]]),
                           Vt=lambda kt: (vs_nat[:, kt, g, :], b_vnat),
                           scale=NSA_SCALE, finish_fn=nsa_finish(1, h, qc))

    def win_sched(qc):
        ents = []
        for kt in range(max(0, 4 * qc - 4), 4 * qc + 4):
            dl = kt * 128 - qc * 512
            if dl >= 0:
                ents.append((kt, dl, 512, (trimask[:, 0, :], dl, [b_tri]), 128))
            else:
                c_hi = dl + 640
                ents.append((kt, 0, c_hi, (trimask[:, 1, :], c_hi - 128, [b_tri]), 128))
        ents.sort(key=lambda e: -(e[2] - e[1]))
        return ents

    if "w" in BR:
        for h in range(8):
            g = h // 4
            for qc in range(4):
                attn_chunk(nst, win_sched(qc),
                           Kt=lambda kt: (KAw[g][0:68, kt * 128:(kt + 1) * 128], [b_KAw[g]]),
                           Qr=lambda a, b_: (QA[h][0:68, qc * 512 + a:qc * 512 + b_], [b_QA[h]]),
                           Vt=lambda kt: (vw_nat[:, kt, g, :], b_vnat),
                           scale=NSA_SCALE, finish_fn=nsa_finish(2, h, qc))
    if DBG:
        k.dma(k.q_pool, dbg_onsa[:, :, :], o_nsaT[:, :, :], r=[b_onsa])
    for h in range(8):
        k.v("dve", lambda: nc.vector.tensor_copy(out=smp_qn[0:64, h, :], in_=QA[h][0:64, T:TT]), r=[b_QA[h]], w=[b_smp])
    k.v("dve", lambda: nc.vector.tensor_copy(out=smp_gn[:, :], in_=gnT[0:24, T:TT]), r=[b_gnT], w=[b_smp])
    k.barrier()
    stN.close()

    stL1.close()
    stS = contextlib.ExitStack(); sbS = scoped(stS)
    s_en = sbS("s_en_sb", [16, 4, 32], F32); s_nm = sbS("s_nm_sb", [16, 4, 32], F32); s_ebw = sbS("s_ebw_sb", [128, 4, 32], F32)
    s_lb = sbS("s_lb_sb", [3, 128], BF16); s_rbs = sbS("s_rbs_sb", [3, 128, 32], BF16); s_rbc = sbS("s_rbc_sb", [3, 8, 32], BF16)
    s_ov = sbS("s_ov_sb", [128, 8, 264], BF16); s_tkm = sbS("s_tkm_sb", [4, 264], F32); s_tka = sbS("s_tka_sb", [4, 264], F32)
    s_ssum = sbS("s_ssum_sb", [16, 4], F32); selb2 = sbS("selb2", [24, 3, 8, 64], BF16)
    b_sc_ = Buf()
    for (dst, srcd, q) in ((s_en, s_en_d, k.q_sp), (s_nm, s_nm_d, k.q_sp), (s_ebw, s_ebw_d, k.q_sp), (s_lb, s_lb_d, k.q_pool),
                           (s_rbs, s_rbs_d, k.q_pool), (s_rbc, s_rbc_d, k.q_pool), (s_ov, s_ov_d, k.q_pool),
                           (s_tkm, s_tkm_d, k.q_sp), (s_tka, s_tka_d, k.q_sp), (s_ssum, s_ssum_d, k.q_sp), (selb2, selb_d, k.q_pool)):
        k.dma(q, dst.ap() if hasattr(dst, "ap") else dst, srcd, w=[b_sc_])
    ptT = sbS("ptT", [128, 4], I32); b_pt = Buf()
    idx16 = sbS("idx16", [128, 4, 16], I32); idx8 = sbS("idx8", [128, 4, 8], I32); b_idx = Buf()
    with nc.allow_non_contiguous_dma(reason="page table transpose (512 ints)"):
        k.dma(k.q_sp, ptT[:, :], pt_d.rearrange("b p -> p b"), w=[b_pt])
    for s_ in range(16):
        k.v("dve", lambda: nc.vector.tensor_scalar(out=idx16[:, :, s_], in0=ptT[:, :], scalar1=16, scalar2=s_, op0=ALU.mult, op1=ALU.add), r=[b_pt], w=[b_idx])
    for s_ in range(8):
        k.v("dve", lambda: nc.vector.tensor_scalar(out=idx8[:, :, s_], in0=ptT[:, :], scalar1=8, scalar2=s_, op0=ALU.mult, op1=ALU.add), r=[b_pt], w=[b_idx])
    hx0_ = sbS("hx0", [128, 8, 128], F32)
    w_ukf = hx0_[:, :, :].rearrange("p (a c) n -> p a (c n)", a=2); b_wukf = Buf()
    w_ukT = sbS("w_ukT", [64, 8, 256], BF16); b_wukT = Buf()
    w_uv_s = sbS("w_uv_s", [128, 2, 512], BF16); b_wuvs = Buf()
    QsA = sbS("QsA", [128, 2, 4, 32], BF16); QsR = sbS("QsR", [32, 4, 32], BF16); Qblk = sbS("Qblk", [128, 4, 32], BF16); b_Q = Buf()
    for c in range(2):
        k.dma(k.q_sp, w_ukf[:, c, :], w_uk[c * 128:(c + 1) * 128, :], w=[b_wukf])
        k.dma(k.q_pool, w_uv_s[:, c, :], w_uv[c * 128:(c + 1) * 128, :], w=[b_wuvs])
    for h in range(8):
        for c in range(2):
            k.tr(ps[6][0:64, (h % 2) * 256 + c * 128:(h % 2) * 256 + (c + 1) * 128], w_ukf[:, c, h * 64:(h + 1) * 64], identf[:, :],
                 r=[b_wukf, b_identf], w=[b_ps[6]], sig=(c == 1))
        k.v("dve", lambda: nc.vector.tensor_copy(out=w_ukT[0:64, h, :], in_=ps[6][0:64, (h % 2) * 256:(h % 2) * 256 + 256]), r=[b_ps[6]], w=[b_wukT])
    for h in range(8):
        for c in range(2):
            k.mm(ps[6][:, 0:16], w_ukT[0:64, h, c * 128:(c + 1) * 128], smp_qm[0:64, h, :], start=True, stop=True, r=[b_wukT, b_smp], w=[b_ps[6]])
            k.v("dve", lambda: nc.vector.tensor_copy(out=QsA[:, c, :, h * 4:(h + 1) * 4], in_=ps[6][:, 0:16].rearrange("p (b t) -> p b t", b=4)), r=[b_ps[6]], w=[b_Q])
        k.v("dve", lambda: nc.vector.tensor_copy(out=QsR[0:32, :, h * 4:(h + 1) * 4], in_=smp_qm[64:96, h, :].rearrange("p (b t) -> p b t", b=4)), r=[b_smp], w=[b_Q])
    k.v("pool", lambda: nc.gpsimd.memset(Qblk[:, :, :], 0.0), w=[b_Q])
    for h in range(8):
        g = h // 4
        k.v("dve", lambda: nc.vector.tensor_copy(out=Qblk[g * 64:(g + 1) * 64, :, h * 4:(h + 1) * 4], in_=smp_qn[0:64, h, :].rearrange("p (b t) -> p b t", b=4)), r=[b_smp], w=[b_Q])
    W1s = {kv: sbS(f"W1s{kv}", [128, 32, 128], BF16) for kv in "kv"}; W2s = {kv: sbS(f"W2s{kv}", [128, 128], BF16) for kv in "kv"}
    posTs = {kv: sbS(f"posTs{kv}", [128, 32], BF16) for kv in "kv"}; cbs = {kv: sbS(f"cbs{kv}", [128, 1], F32) for kv in "kv"}; b_Ws = Buf()
    with nc.allow_non_contiguous_dma(reason="small weight relayout"):
        for kv in "kv":
            k.v("pool", lambda: nc.gpsimd.memset(W1s[kv][:, :, :], 0.0), w=[b_Ws])
            k.v("pool", lambda: nc.gpsimd.memset(W2s[kv][:, :], 0.0), w=[b_Ws])
            w1v = cw1[kv].rearrange("(s d) h -> d s h", d=64)
            for g in range(2):
                k.dma(k.q_pool, W1s[kv][g * 64:(g + 1) * 64, :, g * 64:(g + 1) * 64], w1v, w=[b_Ws])
                k.dma(k.q_pool, W2s[kv][g * 64:(g + 1) * 64, g * 64:(g + 1) * 64], cw2[kv][:, :], w=[b_Ws])
                k.dma(k.q_pool, posTs[kv][g * 64:(g + 1) * 64, :], cpos[kv].rearrange("s d -> d s"), w=[b_Ws])
    for kv in "kv":
        for s_ in range(32):
            k.mm(ps[6][:, 0:1], W1s[kv][:, s_, :], posTs[kv][:, s_:s_ + 1], start=(s_ == 0), stop=(s_ == 31), r=[b_Ws], w=[b_ps[6]])
        k.v("dve", lambda: nc.vector.tensor_copy(out=cbs[kv][:, :], in_=ps[6][:, 0:1]), r=[b_ps[6]], w=[b_Ws])
    Xc2 = [sbS(f"Xc{i}", [128, 8 * 256], BF16) for i in range(3)]; Xr2 = [sbS(f"Xr{i}", [128, 8 * 32], BF16) for i in range(3)]
    Xc = [t_[:, :].rearrange("p (r f) -> p r f", r=8) for t_ in Xc2]; Xr = [t_[:, :].rearrange("p (r f) -> p r f", r=8) for t_ in Xr2]
    ones_c = sbS("ones_c", [128, 1], BF16); b_X = [Buf(), Buf(), Buf()]
    KTc = [sbS(f"KTc{i}", [128, 2, 2, 128], BF16) for i in range(3)]; KTr = [sbS(f"KTr{i}", [32, 2, 128], BF16) for i in range(3)]; b_KT = [Buf(), Buf(), Buf()]
    PTq = [sbS(f"PTq{i}", [128, 256], BF16) for i in range(2)]; b_PTq = [Buf(), Buf()]
    PTf = [sbS(f"PTf{i}", [128, 256], F32) for i in range(2)]; b_PTf = [Buf(), Buf()]
    sm1 = sbS("sm1", [32, 8], F32); b_sm1 = Buf()
    olat = sbS("olat", [32, 256], BF16); b_olat = Buf()
    olT = sbS("olT", [128, 2, 32], BF16); b_olT = Buf()
    Xg2 = [sbS(f"Xg{i}", [128, 16 * 128], BF16) for i in range(3)]
    Xg = [t_[:, :].rearrange("p (r f) -> p r f", r=16) for t_ in Xg2]; b_Xg = [Buf(), Buf(), Buf()]
    Xk2 = [sbS(f"Xk{i}", [128, 16 * 128], BF16) for i in range(3)]
    Xk = [t_[:, :].rearrange("p (r f) -> p r f", r=16) for t_ in Xk2]; b_Xk = [Buf(), Buf(), Buf()]
    XT = [sbS(f"XT{i}", [128, 8, 128], BF16) for i in range(3)]; b_XT = [Buf(), Buf(), Buf()]
    ABf = sbS("ABf", [128, 2, 8, 128], F32); b_AB = Buf()
    hx = [hx0_, ABf[:, 1, :, :], ABf[:, 0, :, :]]; b_hx = [b_wukf, b_AB, b_AB]
    Hs = sbS("Hs", [128, 8, 128], BF16); b_Hs = Buf()
    kccS = sbS("kccS", [128, 8, 128], BF16); b_kccS = Buf()
    vccS = sbS("vccS", [128, 8, 2, 128], BF16); b_vccS = Buf()
    PTc = sbS("PTc", [128, 8, 32], BF16); b_PTc = Buf()
    og = sbS("og", [16, 64], F32); b_og = Buf()
    Un = sbS("Un", [16, 264], F32); b_Un = Buf()
    tks = sbS("tks", [4, 4, 264], F32); b_tks = Buf()
    msel = sbS("msel", [4, 2, 264], F32); b_msel = Buf()
    Mexp = sbS("Mexp", [128, 2, 32], F32); b_Mexp = Buf()
    tmpo = sbS("tmpo", [64, 16], F32); b_tmpo = Buf()
    tmpS = sbS("tmpS", [128, 16], F32); b_tmpS = Buf()
    nk_f = sbS("nk_f", [16, 32], F32); nk_b = sbS("nk_b", [16, 32], BF16); b_nk = Buf()
    Kw = sbS("Kw", [128, 4, 128], BF16); Vw = sbS("Vw", [128, 4, 129], BF16); KwTs = sbS("KwTs", [128, 4, 128], BF16); b_Kw = Buf()
    k.v("pool", lambda: nc.gpsimd.memset(ones_c[:, :], 1.0), w=[b_sc_])
    k.v("pool", lambda: nc.gpsimd.memset(Vw[:, :, 128:129], 1.0), w=[b_Kw])
    k.v("pool", lambda: nc.gpsimd.memset(vccS[:, :, :, :], 1.0), w=[b_vccS])
    ckv_v = c_ckv.rearrange("(q r) f -> q (r f)", r=8); kr_v = c_kr.rearrange("(q r) f -> q (r f)", r=8)
    ck_v = {"k": c_ck.rearrange("(q r) f -> q (r f)", r=16), "v": c_cv.rearrange("(q r) f -> q (r f)", r=16)}
    sk_v = c_sk.rearrange("(q r) f -> q (r f)", r=16); sv_v = c_sv.rearrange("(q r) f -> q (r f)", r=16)
    cnt = {"x": 0, "kt": 0, "pt": 0, "xg": 0, "xt": 0, "s": 0, "tb": 0}
    psbv = {i_: ps[i_].bitcast(BF16) for i_ in (4, 5, 6, 7)}

    def trbank(choices):
        i_ = choices[cnt["tb"] % len(choices)]; cnt["tb"] += 1
        return psbv[i_], b_ps[i_]

    def s_gate(br, g, b):
        k.tr(ps[7][0:64, 0:16], og[0:16, :], identf[0:16, 0:16], r=[b_og, b_identf], w=[b_ps[7]])
        k.v("dve", lambda: nc.vector.tensor_copy(out=tmpo[:, :], in_=ps[7][0:64, 0:16]), r=[b_ps[7]], w=[b_tmpo])
        for hl in range(4):
            h = 4 * g + hl
            k.mm(ps[6][0:64, hl * 4:(hl + 1) * 4], selb2[0:24, br, h, :], smp_gn[0:24, 4 * b:4 * b + 4], start=True, stop=True,
                 r=[b_sc_, b_smp], w=[b_ps[6]], sig=(hl == 3))
        for hl in range(4):
            h = 4 * g + hl
            p0 = (h % 2) * 64
            cs_ = slice(hl * 4, hl * 4 + 4)
            dst = o_nsaT[p0:p0 + 64, h // 2, T + 4 * b:T + 4 * b + 4]
            if br == 0:
                k.v("dve", lambda: nc.vector.tensor_tensor(out=dst, in0=tmpo[0:64, cs_], in1=ps[6][0:64, cs_], op=ALU.mult), r=[b_tmpo, b_ps[6]], w=[b_onsa])
            else:
                k.v("dve", lambda: nc.vector.tensor_tensor(out=tmpS[p0:p0 + 64, cs_], in0=tmpo[0:64, cs_], in1=ps[6][0:64, cs_], op=ALU.mult), r=[b_tmpo, b_ps[6]], w=[b_tmpS])
                k.v("dve", lambda: nc.vector.tensor_tensor(out=dst, in0=dst, in1=tmpS[p0:p0 + 64, cs_], op=ALU.add), r=[b_tmpS, b_onsa], w=[b_onsa])

    def s_norm(acc, zc, vc0, g_unused=None):
        k.v("dve", lambda: nc.vector.tensor_scalar(out=sm1[0:16, 0:1], in0=acc[0:16, zc:zc + 1], scalar1=1e-30, scalar2=None, op0=ALU.max), r=[], w=[b_sm1])
        k.v("dve", lambda: nc.vector.reciprocal(out=sm1[0:16, 0:1], in_=sm1[0:16, 0:1]), r=[b_sm1], w=[b_sm1])

    SB_ = os.environ.get("KSB", "mcsw")
    for b in range(int(os.environ.get('KNB', '4'))):
        if "m" in SB_:
            acc = ps[2]; b_acc = b_ps[2]
            NJ = int(os.environ.get('KNJ', '16'))
            tl = []
            fst = {"v": True}

            def mG(j):
                xi = j % 3
                def f():
                    k.gather(Xc2[xi][:, :], ckv_v, idx16[:, b, j:j + 1], r=[b_idx], w=[b_X[xi]])
                    k.gather(Xr2[xi][:, :], kr_v, idx16[:, b, j:j + 1], r=[b_idx], w=[b_X[xi]])
                return f

            def mT(j, rp):
                xi = j % 3; ki = (4 * j + rp) % 3
                def f():
                    tpb, b_tpb = trbank((7, 4, 5))
                    for rr_ in range(2):
                        r_ = rp * 2 + rr_
                        for c in range(2):
                            k.tr(tpb[:, (rr_ * 3 + c) * 128:(rr_ * 3 + c + 1) * 128], Xc[xi][:, r_, c * 128:(c + 1) * 128], identb[:, :], r=[b_X[xi], b_identb], w=[b_tpb], sig=False)
                        k.tr(tpb[0:32, (rr_ * 3 + 2) * 128:(rr_ * 3 + 3) * 128], Xr[xi][:, r_, :], identb[:, :], r=[b_X[xi], b_identb], w=[b_tpb], sig=(rr_ == 1))
                    pv = tpb[:, 0:768].rearrange("p (a c n) -> p a c n", a=2, c=3)
                    k.v("dve", lambda: nc.vector.tensor_copy(out=KTc[ki][:, :, :, :], in_=pv[:, :, 0:2, :]), r=[b_tpb], w=[b_KT[ki]])
                    k.v("dve", lambda: nc.vector.tensor_copy(out=KTr[ki][0:32, :, :], in_=pv[0:32, :, 2, :]), r=[b_tpb], w=[b_KT[ki]])
                return f

            def mS(j, rp):
                ki = (4 * j + rp) % 3; psS = ps[j % 2]; b_S = b_ps[j % 2]
                def f():
                    for rr_ in range(2):
                        r_ = rp * 2 + rr_
                        oS = psS[:, r_ * 32:(r_ + 1) * 32]
                        k.mm(oS, KTc[ki][:, rr_, 0, :], QsA[:, 0, b, :], start=True, stop=False, r=[b_KT[ki], b_Q], w=[b_S])
                        k.mm(oS, KTc[ki][:, rr_, 1, :], QsA[:, 1, b, :], start=False, stop=False, r=[b_KT[ki], b_Q], w=[b_S])
                        k.mm(oS, KTr[ki][0:32, rr_, :], QsR[0:32, b, :], start=False, stop=True, r=[b_KT[ki], b_Q], w=[b_S], sig=(rp == 3 and rr_ == 1))
                return f

            def mE(j):
                def f():
                    k.actf(PTq[j % 2][:, :], ps[j % 2][:, 0:256], AF.Exp, r=[b_ps[j % 2]], w=[b_PTq[j % 2]], scale=MLA_SCALE)
                return f

            def mP(j):
                xi = j % 3; pi = j % 2
                def f():
                    for r_ in range(8):
                        k.mm(acc[0:32, 0:256], PTq[pi][:, r_ * 32:(r_ + 1) * 32], Xc[xi][:, r_, :], start=fst["v"], stop=False, r=[b_PTq[pi], b_X[xi]], w=[b_acc], sig=False)
                        k.mm(ps[3][0:32, 0:1], PTq[pi][:, r_ * 32:(r_ + 1) * 32], ones_c[:, 0:1], start=fst["v"], stop=False, r=[b_PTq[pi], b_sc_], w=[b_ps[3]], sig=False)
                        fst["v"] = False
                return f
            for j in range(NJ):
                tl.append((8 * (j - 1) - 0.5 if j >= 2 else (-2 + j), mG(j)))
                for rp in range(4):
                    u = 4 * j + rp
                    tl.append((2 * u, mT(j, rp)))
                    tl.append((2 * u + 3, mS(j, rp)))
                tl.append((2 * (4 * j + 3) + 3.5, mE(j)))
                tl.append((2 * (4 * j + 3) + 7.6, mP(j)))
            for _, f_ in sorted(tl, key=lambda e: e[0]):
                f_()
            si = cnt["s"] % 2; cnt["s"] += 1
            psS = ps[si]; b_S = b_ps[si]
            k.mm(psS[0:16, 0:32], smp_ckvT[:, 0, :], QsA[:, 0, b, :], start=True, stop=False, r=[b_smp, b_Q], w=[b_S])
            k.mm(psS[0:16, 0:32], smp_ckvT[:, 1, :], QsA[:, 1, b, :], start=False, stop=False, r=[b_smp, b_Q], w=[b_S])
            k.mm(psS[0:16, 0:32], smp_krT[0:32, :], QsR[0:32, b, :], start=False, stop=True, r=[b_smp, b_Q], w=[b_S])
            k.actf(nk_f[:, :], psS[0:16, 0:32], AF.Exp, r=[b_S], w=[b_nk], scale=MLA_SCALE)
            k.v("dve", lambda: nc.vector.tensor_tensor(out=nk_b[:, :], in0=nk_f[:, :], in1=s_nm[:, b, :], op=ALU.mult), r=[b_nk, b_sc_], w=[b_nk])
            k.mm(acc[0:32, 0:256], nk_b[0:16, :], smp_ckvn[0:16, 0:256], start=False, stop=True, r=[b_nk, b_smp], w=[b_acc])
            k.mm(ps[3][0:32, 0:1], nk_b[0:16, :], ones_c[0:16, 0:1], start=False, stop=True, r=[b_nk, b_sc_], w=[b_ps[3]])
            k.v("dve", lambda: nc.vector.reciprocal(out=sm1[0:32, 1:2], in_=ps[3][0:32, 0:1]), r=[b_ps[3]], w=[b_sm1])
            k.v("dve", lambda: nc.vector.tensor_scalar(out=olat[:, :], in0=acc[0:32, 0:256], scalar1=sm1[0:32, 1:2], scalar2=None, op0=ALU.mult), r=[b_acc, b_sm1], w=[b_olat])
            for c in range(2):
                k.tr(psb[:, c * 128:c * 128 + 32], olat[0:32, c * 128:(c + 1) * 128], identb[0:32, 0:32], r=[b_olat, b_identb], w=[b_ps[7]], sig=(c == 1))
            k.v("dve", lambda: nc.vector.tensor_copy(out=olT[:, :, :], in_=psb[:, 0:256].rearrange("p (c n) -> p c n", c=2)[:, :, 0:32]), r=[b_ps[7]], w=[b_olT])
            for h in range(8):
                for c in range(2):
                    k.mm(ps[6][0:64, h * 4:(h + 1) * 4], w_uv_s[:, c, h * 64:(h + 1) * 64], olT[:, c, h * 4:(h + 1) * 4], start=(c == 0), stop=(c == 1),
                         r=[b_wuvs, b_olT], w=[b_ps[6]], sig=(c == 1 and h == 7))
            for h in range(8):
                p0 = (h % 2) * 64
                k.v("dve", lambda: nc.vector.tensor_copy(out=o_mlaT[p0:p0 + 64, h // 2, T + 4 * b:T + 4 * b + 4], in_=ps[6][0:64, h * 4:(h + 1) * 4]), r=[b_ps[6]], w=[b_omla])

        if "c" in SB_:
            for kv in "kv":
                tl = []

                def cG(c):
                    def f():
                        k.gather(Xg2[c % 3][:, :], ck_v[kv], idx8[:, b, c:c + 1], r=[b_idx], w=[b_Xg[c % 3]])
                    return f

                def cT(c, hf):
                    u = 2 * c + hf
                    def f():
                        tpb, b_tpb = trbank((7, 6))
                        for s8 in range(8):
                            k.tr(tpb[:, s8 * 128:(s8 + 1) * 128], Xg[c % 3][:, hf * 8 + s8, :], identb[:, :], r=[b_Xg[c % 3], b_identb], w=[b_tpb], sig=(s8 == 7))
                        k.v("dve", lambda: nc.vector.tensor_copy(out=XT[u % 3][:, :, :], in_=tpb[:, :].rearrange("p (s n) -> p s n", s=8)), r=[b_tpb], w=[b_XT[u % 3]])
                    return f

                def cM(c, hf):
                    u = 2 * c + hf; ba = 4 if c % 2 == 0 else 2
                    def f():
                        for s8 in range(8):
                            s_ = hf * 8 + s8
                            k.mm(ps[ba][:, 0:128], W1s[kv][:, s_, :], XT[u % 3][:, s8, :], start=(s_ == 0), stop=(s_ == 15), r=[b_Ws, b_XT[u % 3]], w=[b_ps[ba]])
                            k.mm(ps[ba + 1][:, 0:128], W1s[kv][:, 16 + s_, :], XT[u % 3][:, s8, :], start=(s_ == 0), stop=(s_ == 15), r=[b_Ws, b_XT[u % 3]], w=[b_ps[ba + 1]])
                    return f

                def cV(c):
                    ba = 4 if c % 2 == 0 else 2
                    def f():
                        k.actf(ABf[:, 0, c, :], ps[ba][:, 0:128], AF.Copy, r=[b_ps[ba]], w=[b_AB])
                        k.actf(ABf[:, 1, c, :], ps[ba + 1][:, 0:128], AF.Copy, r=[b_ps[ba + 1]], w=[b_AB])
                    return f
                for c in range(8):
                    tl.append((4 * c - 5.5 if c >= 2 else (-2 + c), cG(c)))
                    for hf in range(2):
                        u = 2 * c + hf
                        tl.append((2 * u, cT(c, hf)))
                        tl.append((2 * u + 3, cM(c, hf)))
                    tl.append((2 * (2 * c + 1) + 3.5, cV(c)))
                for _, f_ in sorted(tl, key=lambda e: e[0]):
                    f_()
                x_ = hx[0]
                k.v("pool", lambda: nc.gpsimd.memset(x_[:, 7, 127:128], 0.0), w=[b_hx[0]])
                k.v("dve", lambda: nc.vector.tensor_tensor(out=x_[:, 0:7, :], in0=ABf[:, 0, 0:7, :], in1=ABf[:, 1, 1:8, :], op=ALU.add), r=[b_AB], w=[b_hx[0]])
                k.v("dve", lambda: nc.vector.tensor_tensor(out=x_[:, 7, 0:127], in0=ABf[:, 0, 7, 0:127], in1=ABf[:, 1, 0, 1:128], op=ALU.add), r=[b_AB], w=[b_hx[0]])
                k.v("dve", lambda: nc.vector.tensor_scalar(out=x_[:, :, :], in0=x_[:, :, :], scalar1=cbs[kv][:, 0:1], scalar2=None, op0=ALU.add), r=[b_hx[0], b_Ws], w=[b_hx[0]])
                k.v("dve", lambda: nc.vector.tensor_tensor(out=hx[1][:, :, :], in0=x_[:, :, :], in1=x_[:, :, :], op=ALU.mult), r=[b_hx[0]], w=[b_hx[1]])
                k.v("dve", lambda: nc.vector.tensor_scalar(out=hx[1][:, :, :], in0=hx[1][:, :, :], scalar1=0.044715, scalar2=1.0, op0=ALU.mult, op1=ALU.add), r=[b_hx[1]], w=[b_hx[1]])
                k.v("dve", lambda: nc.vector.tensor_tensor(out=hx[1][:, :, :], in0=hx[1][:, :, :], in1=x_[:, :, :], op=ALU.mult), r=[b_hx[0], b_hx[1]], w=[b_hx[1]])
                k.actf(hx[2][:, :, :], hx[1][:, :, :], AF.Sigmoid, r=[b_hx[1]], w=[b_hx[2]], scale=1.5957691216)
                k.v("dve", lambda: nc.vector.tensor_tensor(out=Hs[:, :, :], in0=hx[2][:, :, :], in1=x_[:, :, :], op=ALU.mult), r=[b_hx[0], b_hx[2]], w=[b_Hs])
                if kv == "k":
                    for hf in range(2):
                        k.mm(ps[6][:, :], W2s["k"][:, :], Hs[:, hf * 4:(hf + 1) * 4, :], start=True, stop=True, r=[b_Ws, b_Hs], w=[b_ps[6]])
                        k.v("dve", lambda: nc.vector.tensor_copy(out=kccS[:, hf * 4:(hf + 1) * 4, :], in_=ps[6][:, :].rearrange("p (c n) -> p c n", c=4)), r=[b_ps[6]], w=[b_kccS])
                else:
                    for hf in range(2):
                        for c4 in range(4):
                            k.mm(ps[6][:, c4 * 128:(c4 + 1) * 128], Hs[:, hf * 4 + c4, :], W2s["v"][:, :], start=True, stop=True, r=[b_Ws, b_Hs], w=[b_ps[6]], sig=(c4 == 3))
                        k.v("dve", lambda: nc.vector.tensor_copy(out=vccS[:, hf * 4:(hf + 1) * 4, :, 0:64], in_=ps[6][:, :].rearrange("p (c g d) -> p c g d", c=4, g=2)), r=[b_ps[6]], w=[b_vccS])
            si = cnt["s"] % 2; cnt["s"] += 1
            psS = ps[si]; b_S = b_ps[si]
            for c in range(8):
                k.mm(psS[:, c * 32:(c + 1) * 32], kccS[:, c, :], Qblk[:, b, :], start=True, stop=False, r=[b_kccS, b_Q], w=[b_S])
                k.mm(psS[:, c * 32:(c + 1) * 32], s_lb[0:3, :], s_rbc[0:3, c, :], start=False, stop=True, r=[b_sc_], w=[b_S], sig=(c == 7))
            k.actf(PTc[:, :, :], psS[:, 0:256].rearrange("p (c n) -> p c n", c=8), AF.Exp, r=[b_S], w=[b_PTc], scale=NSA_SCALE)
            for g in range(2):
                acc = ps[2 + g]; b_acc = b_ps[2 + g]
                for c in range(8):
                    k.mm(acc[0:16, 0:128], PTc[:, c, g * 16:(g + 1) * 16], vccS[:, c, g, :], start=(c == 0), stop=(c == 7), r=[b_PTc, b_vccS], w=[b_acc])
                U = ps[4 + g]; b_U = b_ps[4 + g]
                for c in range(8):
                    k.mm(U[0:16, 0:264], PTc[:, c, g * 16:(g + 1) * 16], s_ov[:, c, :], start=(c == 0), stop=(c == 7), r=[b_PTc, b_sc_], w=[b_U])
                k.v("dve", lambda: nc.vector.tensor_scalar(out=sm1[0:16, 0:1], in0=acc[0:16, 64:65], scalar1=1e-30, scalar2=None, op0=ALU.max), r=[b_acc], w=[b_sm1])
                k.v("dve", lambda: nc.vector.reciprocal(out=sm1[0:16, 0:1], in_=sm1[0:16, 0:1]), r=[b_sm1], w=[b_sm1])
                k.v("dve", lambda: nc.vector.tensor_scalar(out=og[:, :], in0=acc[0:16, 0:64], scalar1=sm1[0:16, 0:1], scalar2=None, op0=ALU.mult), r=[b_acc, b_sm1], w=[b_og])
                k.v("dve", lambda: nc.vector.tensor_scalar(out=Un[:, :], in0=U[0:16, 0:264], scalar1=sm1[0:16, 0:1], scalar2=None, op0=ALU.mult), r=[b_U, b_sm1], w=[b_Un])
                k.mm(ps[6][0:4, 0:264], s_ssum[0:16, 0:4], Un[0:16, :], start=True, stop=True, r=[b_sc_, b_Un], w=[b_ps[6]])
                k.v("dve", lambda: nc.vector.tensor_tensor(out=tks[0:4, 0, :], in0=ps[6][0:4, 0:264], in1=s_tkm[0:4, :], op=ALU.mult), r=[b_ps[6], b_sc_], w=[b_tks])
                k.v("dve", lambda: nc.vector.tensor_tensor(out=tks[0:4, 0, :], in0=tks[0:4, 0, :], in1=s_tka[0:4, :], op=ALU.add), r=[b_tks, b_sc_], w=[b_tks])
                k.v("dve", lambda: nc.vector.max(out=tks[0:4, 1, 0:8], in_=tks[0:4, 0, :]), r=[b_tks], w=[b_tks])
                k.v("dve", lambda: nc.vector.match_replace(out=tks[0:4, 2, :], in_to_replace=tks[0:4, 1, 0:8], in_values=tks[0:4, 0, :], imm_value=-3.0e38), r=[b_tks], w=[b_tks])
                k.v("dve", lambda: nc.vector.max(out=tks[0:4, 3, 0:8], in_=tks[0:4, 2, :]), r=[b_tks], w=[b_tks])
                k.v("dve", lambda: nc.vector.tensor_scalar(out=msel[0:4, g, :], in0=tks[0:4, 0, :], scalar1=tks[0:4, 3, 7:8], scalar2=None, op0=ALU.is_ge), r=[b_tks], w=[b_msel])
                s_gate(0, g, b)
            for g in range(2):
                for par in range(2):
                    q_ = g * 2 + par
                    k.tr(ps[7][:, q_ * 4:(q_ + 1) * 4], msel[0:4, g, par:par + 256:2], identf[0:4, 0:4], r=[b_msel, b_identf], w=[b_ps[7]], sig=(q_ == 3))
            for g in range(2):
                for par in range(2):
                    q_ = g * 2 + par
                    k.v("dve", lambda: nc.vector.tensor_copy(out=Mexp[:, par, g * 16:(g + 1) * 16].rearrange("p (a t) -> p a t", a=4),
                                                              in_=ps[7][:, q_ * 4:(q_ + 1) * 4].unsqueeze(1).to_broadcast([128, 4, 4])), r=[b_ps[7]], w=[b_Mexp])

        if "s" in SB_:
            tl = []
            fst = {"v": True}

            def sG(c):
                def f():
                    k.gather(Xk2[c % 3][:, :], sk_v, idx8[:, b, c:c + 1], r=[b_idx], w=[b_Xk[c % 3]])
                    k.gather(Xg2[c % 3][:, :], sv_v, idx8[:, b, c:c + 1], r=[b_idx], w=[b_Xg[c % 3]])
                return f

            def sT(c, hf):
                u = 2 * c + hf
                def f():
                    tpb, b_tpb = trbank((7, 6))
                    for s8 in range(8):
                        k.tr(tpb[:, s8 * 128:(s8 + 1) * 128], Xk[c % 3][:, hf * 8 + s8, :], identb[:, :], r=[b_Xk[c % 3], b_identb], w=[b_tpb], sig=(s8 == 7))
                    k.v("dve", lambda: nc.vector.tensor_copy(out=XT[u % 3][:, :, :], in_=tpb[:, :].rearrange("p (s n) -> p s n", s=8)), r=[b_tpb], w=[b_XT[u % 3]])
                return f

            def sS(c, hf):
                u = 2 * c + hf; psS = ps[u % 2]; b_S = b_ps[u % 2]
                def f():
                    for s8 in range(8):
                        rr = c * 16 + hf * 8 + s8
                        k.mm(psS[:, s8 * 32:(s8 + 1) * 32], XT[u % 3][:, s8, :], Qblk[:, b, :], start=True, stop=False, r=[b_XT[u % 3], b_Q], w=[b_S])
                        k.mm(psS[:, s8 * 32:(s8 + 1) * 32], s_lb[0:2, :], s_rbs[0:2, rr, :], start=False, stop=True, r=[b_sc_], w=[b_S], sig=(s8 == 7))
                return f

            def sE(c, hf):
                u = 2 * c + hf; pi = u % 2; par = 1 if c >= 4 else 0
                def f():
                    k.actf(PTf[pi][:, :], ps[u % 2][:, 0:256], AF.Exp, r=[b_ps[u % 2]], w=[b_PTf[pi]], scale=NSA_SCALE)
                    k.v("dve", lambda: nc.vector.tensor_tensor(out=PTq[pi][:, :].rearrange("p (s n) -> p s n", s=8), in0=PTf[pi][:, :].rearrange("p (s n) -> p s n", s=8),
                                                                in1=Mexp[:, par, :].unsqueeze(1).to_broadcast([128, 8, 32]), op=ALU.mult), r=[b_PTf[pi], b_Mexp], w=[b_PTq[pi]])
                return f

            def sP(c, hf):
                u = 2 * c + hf; pi = u % 2
                def f():
                    for s8 in range(8):
                        for g in range(2):
                            k.mm(ps[2 + g][0:16, 0:128], PTq[pi][:, s8 * 32 + g * 16:s8 * 32 + (g + 1) * 16], Xg[c % 3][:, hf * 8 + s8, :], start=fst["v"], stop=False,
                                 r=[b_PTq[pi], b_Xg[c % 3]], w=[b_ps[2 + g]], sig=False)
                            k.mm(ps[4 + g][0:16, 0:1], PTq[pi][:, s8 * 32 + g * 16:s8 * 32 + (g + 1) * 16], ones_c[:, 0:1], start=fst["v"], stop=False,
                                 r=[b_PTq[pi], b_sc_], w=[b_ps[4 + g]], sig=False)
                        fst["v"] = False
                return f
            for c in range(8):
                tl.append((4 * c - 2.2 if c >= 1 else -1, sG(c)))
                for hf in range(2):
                    u = 2 * c + hf
                    tl.append((2 * u, sT(c, hf)))
                    tl.append((2 * u + 3, sS(c, hf)))
                    tl.append((2 * u + 3.5, sE(c, hf)))
                    tl.append((2 * u + 5.6, sP(c, hf)))
            for _, f_ in sorted(tl, key=lambda e: e[0]):
                f_()
            si = cnt["s"] % 2; cnt["s"] += 1
            psS = ps[si]; b_S = b_ps[si]
            k.mm(psS[0:16, 0:32], smp_ksT[:, :], Qblk[:, b, :], start=True, stop=True, r=[b_smp, b_Q], w=[b_S])
            k.actf(nk_f[:, :], psS[0:16, 0:32], AF.Exp, r=[b_S], w=[b_nk], scale=NSA_SCALE)
            k.v("dve", lambda: nc.vector.tensor_tensor(out=nk_b[:, :], in0=nk_f[:, :], in1=s_en[:, b, :], op=ALU.mult), r=[b_nk, b_sc_], w=[b_nk])
            for g in range(2):
                acc = ps[2 + g]; b_acc = b_ps[2 + g]
                k.mm(acc[0:16, 0:128], nk_b[0:16, g * 16:(g + 1) * 16], smp_vs[0:16, 0:128], start=False, stop=True, r=[b_nk, b_smp], w=[b_acc])
                k.mm(ps[4 + g][0:16, 0:1], nk_b[0:16, g * 16:(g + 1) * 16], ones_c[0:16, 0:1], start=False, stop=True, r=[b_nk, b_sc_], w=[b_ps[4 + g]])
                k.v("dve", lambda: nc.vector.tensor_scalar(out=sm1[0:16, 0:1], in0=ps[4 + g][0:16, 0:1], scalar1=1e-30, scalar2=None, op0=ALU.max), r=[b_ps[4 + g]], w=[b_sm1])
                k.v("dve", lambda: nc.vector.reciprocal(out=sm1[0:16, 0:1], in_=sm1[0:16, 0:1]), r=[b_sm1], w=[b_sm1])
                k.v("dve", lambda: nc.vector.tensor_scalar(out=og[:, :], in0=acc[0:16, g * 64:(g + 1) * 64], scalar1=sm1[0:16, 0:1], scalar2=None, op0=ALU.mult), r=[b_acc, b_sm1], w=[b_og])
                s_gate(1, g, b)

        if "w" in SB_:
            k.dma(k.q_pool, Kw[:, :, :], win_k_in[b].rearrange("(kt p) f -> p kt f", p=128), w=[b_Kw])
            k.dma(k.q_pool, Vw[:, :, 0:128], win_v_in[b].rearrange("(kt p) f -> p kt f", p=128), w=[b_Kw])
            for kt in range(4):
                k.tr(psb[:, kt * 128:(kt + 1) * 128], Kw[:, kt, :], identb[:, :], r=[b_Kw, b_identb], w=[b_ps[7]], sig=(kt == 3))
            k.v("dve", lambda: nc.vector.tensor_copy(out=KwTs[:, :, :], in_=psb[:, 0:512].rearrange("p (s n) -> p s n", s=4)), r=[b_ps[7]], w=[b_Kw])
            si = cnt["s"] % 2; cnt["s"] += 1
            psS = ps[si]; b_S = b_ps[si]
            for kt in range(4):
                k.mm(psS[:, kt * 32:(kt + 1) * 32], KwTs[:, kt, :], Qblk[:, b, :], start=True, stop=True, r=[b_Kw, b_Q], w=[b_S], sig=(kt == 3))
            pi = cnt["pt"] % 2; cnt["pt"] += 1
            k.actf(PTf[pi][:, 0:128], psS[:, 0:128], AF.Exp, r=[b_S], w=[b_PTf[pi]], scale=NSA_SCALE)
            k.v("dve", lambda: nc.vector.tensor_tensor(out=PTq[pi][:, 0:128], in0=PTf[pi][:, 0:128], in1=s_ebw[:, :, :].rearrange("p a n -> p (a n)"), op=ALU.mult),
                r=[b_PTf[pi], b_sc_], w=[b_PTq[pi]])
            si = cnt["s"] % 2; cnt["s"] += 1
            psS2 = ps[si]; b_S2 = b_ps[si]
            k.mm(psS2[0:16, 0:32], smp_kwT[:, :], Qblk[:, b, :], start=True, stop=True, r=[b_smp, b_Q], w=[b_S2])
            k.actf(nk_f[:, :], psS2[0:16, 0:32], AF.Exp, r=[b_S2], w=[b_nk], scale=NSA_SCALE)
            k.v("dve", lambda: nc.vector.tensor_tensor(out=nk_b[:, :], in0=nk_f[:, :], in1=s_en[:, b, :], op=ALU.mult), r=[b_nk, b_sc_], w=[b_nk])
            for g in range(2):
                acc = ps[2 + g]; b_acc = b_ps[2 + g]
                for kt in range(4):
                    k.mm(acc[0:16, 0:129], PTq[pi][:, kt * 32 + g * 16:kt * 32 + (g + 1) * 16], Vw[:, kt, :], start=(kt == 0), stop=False, r=[b_PTq[pi], b_Kw], w=[b_acc], sig=False)
                k.mm(acc[0:16, 0:129], nk_b[0:16, g * 16:(g + 1) * 16], smp_vw[0:16, :], start=False, stop=True, r=[b_nk, b_smp], w=[b_acc])
                k.v("dve", lambda: nc.vector.tensor_scalar(out=sm1[0:16, 0:1], in0=acc[0:16, 128:129], scalar1=1e-30, scalar2=None, op0=ALU.max), r=[b_acc], w=[b_sm1])
                k.v("dve", lambda: nc.vector.reciprocal(out=sm1[0:16, 0:1], in_=sm1[0:16, 0:1]), r=[b_sm1], w=[b_sm1])
                k.v("dve", lambda: nc.vector.tensor_scalar(out=og[:, :], in0=acc[0:16, g * 64:(g + 1) * 64], scalar1=sm1[0:16, 0:1], scalar2=None, op0=ALU.mult), r=[b_acc, b_sm1], w=[b_og])
                s_gate(2, g, b)
    k.barrier()
    stS.close()
    if os.environ.get('KSTOP') == 'S':
        k.finish()
        return nc
    chunks5 = [(0, 512), (512, 512), (1024, 512), (1536, 512), (2048, TS)]
    stC = contextlib.ExitStack(); sbC = scoped(stC)
    h2T = sbC("h2T", [128, 8, TT], BF16); b_h2T = [Buf() for _ in range(NT)]
    stCm = contextlib.ExitStack(); sbCm = scoped(stCm)
    mTall = sbCm("mTall", [128, 8, TT], BF16); b_mT = [Buf() for _ in range(5)]
    stC1 = contextlib.ExitStack(); sbC1 = scoped(stC1)
    wpm_b = sbC1("wpm_b", [128, 4, D], BF16); b_wpm = Buf()
    wpn_b = sbC1("wpn_b", [128, 4, D], BF16); b_wpn = Buf()
    wG = [sbC1(f"wG{i}", [128, 8, 256], BF16) for i in range(2)]; b_wG = [Buf(), Buf()]
    stgG = [sbC1(f"stgG{i}", [128, 8, 256], F32) for i in range(2)]; b_stgG = [Buf(), Buf()]
    w_in_v = w_in.rearrange("(c p) f -> p c f", p=128)
    gsa = [sbC1(f"gsa{i}", [128, 512], F32) for i in range(2)]; b_gsa = [Buf(), Buf()]
    gsb = [sbC1(f"gsb{i}", [128, 512], F32) for i in range(2)]; b_gsb = [Buf(), Buf()]
    for c in range(4):
        k.dma(k.q_pool, wpm_b[:, c, :], w_pm[c * 128:(c + 1) * 128, :], w=[b_wpm])
        k.dma(k.q_pool, wpn_b[:, c, :], w_pn[c * 128:(c + 1) * 128, :], w=[b_wpn])
    it = 0
    for m in range(8):
        wi = m % 2
        k.dma(k.q_sp, stgG[wi][:, :, 0:128], w_in_v[:, :, C_GA + m * 128:C_GA + (m + 1) * 128], w=[b_stgG[wi]])
        k.dma(k.q_sp, stgG[wi][:, :, 128:256], w_in_v[:, :, C_GB + m * 128:C_GB + (m + 1) * 128], w=[b_stgG[wi]])
        k.v("pool", lambda: nc.gpsimd.tensor_tensor(out=wG[wi][:, :, :], in0=stgG[wi][:, :, :], in1=g1col[:, :].unsqueeze(2).to_broadcast([128, 8, 256]), op=ALU.mult),
            r=[b_stgG[wi], b_g1], w=[b_wG[wi]])
        for ci, (c0, cn) in enumerate(chunks5):
            hb = b_hT[4 * ci:4 * ci + 4] if ci < 4 else [b_hT[16]]
            i2 = it % 2; it += 1
            o_ = 4 * i2
            for c in range(8):
                k.mm(ps[o_ + 0][:, 0:cn], wG[wi][:, c, 0:128], hT[:, c, c0:c0 + cn], start=(c == 0), stop=(c == 7), r=[b_wG[wi]] + hb, w=[b_ps[o_ + 0]])
            for c in range(8):
                k.mm(ps[o_ + 1][:, 0:cn], wG[wi][:, c, 128:256], hT[:, c, c0:c0 + cn], start=(c == 0), stop=(c == 7), r=[b_wG[wi]] + hb, w=[b_ps[o_ + 1]])
            for c in range(4):
                k.mm(ps[o_ + 2][:, 0:cn], wpm_b[:, c, m * 128:(m + 1) * 128], o_mlaT[:, c, c0:c0 + cn], start=(c == 0), stop=(c == 3), r=[b_wpm, b_omla], w=[b_ps[o_ + 2]])
            for c in range(4):
                k.mm(ps[o_ + 3][:, 0:cn], wpn_b[:, c, m * 128:(m + 1) * 128], o_nsaT[:, c, c0:c0 + cn], start=(c == 0), stop=(c == 3), r=[b_wpn, b_onsa], w=[b_ps[o_ + 3]])
            k.actf(gsa[i2][:, 0:cn], ps[o_ + 0][:, 0:cn], AF.Sigmoid, r=[b_ps[o_ + 0]], w=[b_gsa[i2]])
            k.actf(gsb[i2][:, 0:cn], ps[o_ + 1][:, 0:cn], AF.Sigmoid, r=[b_ps[o_ + 1]], w=[b_gsb[i2]])
            k.v("dve", lambda: nc.vector.tensor_tensor(out=gsa[i2][:, 0:cn], in0=gsa[i2][:, 0:cn], in1=ps[o_ + 2][:, 0:cn], op=ALU.mult), r=[b_gsa[i2], b_ps[o_ + 2]], w=[b_gsa[i2]])
            k.v("dve", lambda: nc.vector.tensor_tensor(out=gsb[i2][:, 0:cn], in0=gsb[i2][:, 0:cn], in1=ps[o_ + 3][:, 0:cn], op=ALU.mult), r=[b_gsb[i2], b_ps[o_ + 3]], w=[b_gsb[i2]])
            k.v("dve", lambda: nc.vector.tensor_tensor(out=mTall[:, m, c0:c0 + cn], in0=gsa[i2][:, 0:cn], in1=gsb[i2][:, 0:cn], op=ALU.add), r=[b_gsa[i2], b_gsb[i2]], w=[b_mT[ci]])
    k.barrier()
    stC1.close()
    stC2 = contextlib.ExitStack(); sbC2 = scoped(stC2)
    wout_b = sbC2("wout_b", [128, 8, D], BF16); b_wout = Buf()
    xt2 = [sbC2(f"xt2_{i}", [128, D], F32) for i in range(2)]; b_xt2 = [Buf(), Buf()]
    xn2 = [sbC2(f"xn2_{i}", [128, D], BF16) for i in range(2)]; b_xn2 = [Buf(), Buf()]
    junk2 = sbC2("junk2", [128, D], BF16); b_junk2 = Buf()
    st2 = [sbC2(f"st2_{i}", [128, 4], F32) for i in range(2)]; b_st2 = [Buf(), Buf()]
    for c in range(8):
        k.dma(k.q_pool, wout_b[:, c, :], w_out[c * 128:(c + 1) * 128, :], w=[b_wout])

    def x1_tile(t):
        return x1all[:, t, :] if t < 16 else x1s[:, :]
    for t in range(NT):
        t0, n = tile_rows(t)
        s = t % 2
        ci = min(t // 4, 4)
        src = xp[t0:t0 + n, :] if t < 16 else xs[:, :]
        k.dma(k.q_sp, xt2[s][0:n, :], src, w=[b_xt2[s]])
        xd = x1_tile(t)
        for half in range(2):
            pb = 2 * (t % 2) + half
            for m in range(8):
                k.mm(ps[pb][0:n, :], mTall[:, m, t0:t0 + n], wout_b[:, m, half * 512:(half + 1) * 512], start=(m == 0), stop=(m == 7),
                     r=[b_mT[ci], b_wout], w=[b_ps[pb]])
            k.v("dve", lambda: nc.vector.tensor_tensor(out=xd[0:n, half * 512:(half + 1) * 512], in0=xt2[s][0:n, half * 512:(half + 1) * 512],
                                                        in1=ps[pb][0:n, :], op=ALU.add), r=[b_xt2[s], b_ps[pb]], w=[b_x1[t]])
        k.actf(junk2[0:n, :], xd[0:n, :], AF.Square, r=[b_x1[t]], w=[b_junk2, b_st2[s]], accum=st2[s][0:n, 0:1])
        k.actf(st2[s][0:n, 1:2], st2[s][0:n, 0:1], AF.Sqrt, r=[b_st2[s]], w=[b_st2[s]], scale=1.0 / D, bias=EPS)
        k.v("dve", lambda: nc.vector.reciprocal(out=st2[s][0:n, 2:3], in_=st2[s][0:n, 1:2]), r=[b_st2[s]], w=[b_st2[s]])
        k.v("dve", lambda: nc.vector.tensor_scalar(out=xn2[s][0:n, :], in0=xd[0:n, :], scalar1=st2[s][0:n, 2:3], scalar2=None, op0=ALU.mult),
            r=[b_x1[t], b_st2[s]], w=[b_xn2[s]])
        psbc = psbv2[t % 2]; b_psbc = b_ps[7 - (t % 2)]
        for c in range(8):
            k.tr(psbc[:, c * 128:c * 128 + n], xn2[s][0:n, c * 128:(c + 1) * 128], identb[0:n, 0:n], r=[b_xn2[s], b_identb], w=[b_psbc], sig=(c == 7))
        k.v("dve", lambda: nc.vector.tensor_copy(out=h2T[:, :, t0:t0 + n], in_=psbc[:, :].rearrange("p (c n) -> p c n", c=8)[:, :, 0:n]),
            r=[b_psbc], w=[b_h2T[t]])
    k.barrier()
    stC2.close()
    stCm.close()
    stD = contextlib.ExitStack(); sbD = scoped(stD)
    GW = 2
    NG = NFC // GW
    stgW = [sbD(f"stgW{i}", [128, 8, 128 * GW], F32) for i in range(2)]; b_stgW = [Buf(), Buf()]
    wg_b = [sbD(f"wg_b{i}", [128, 8, 128 * GW], BF16) for i in range(2)]; b_wg = [Buf(), Buf()]
    wu_b = [sbD(f"wu_b{i}", [128, 8, 128 * GW], BF16) for i in range(2)]; b_wu = [Buf(), Buf()]
    wd_b = [sbD(f"wd_b{i}", [128, GW, D], BF16) for i in range(2)]; b_wd = [Buf(), Buf()]
    gbuf = sbD("gbuf", [128, 2 + T], F32); b_gbuf = Buf()
    gbs = sbD("gbs", [128, 4, 6], F32); b_gbs = Buf()
    cv = [sbD(f"cv{i}", [128, 512], F32) for i in range(2)]; b_cv = [Buf(), Buf()]
    sg = [sbD(f"sg{i}", [128, 512], F32) for i in range(2)]; b_sg = [Buf(), Buf()]
    mTg = [sbD(f"mTg{i}", [128, GW, TT], BF16) for i in range(2)]; b_mTg = [Buf(), Buf()]
    histT = sbD("histT", [128, NFC, 8], F32); b_hist = Buf()
    cwT = sbD("cwT", [128, NFC, 4], F32); b_cwT = Buf()
    yo = [sbD(f"yo{i}", [128, D], F32) for i in range(2)]; b_yo = [Buf(), Buf()]
    st3 = [sbD(f"st3_{i}", [128, 4], F32) for i in range(2)]; b_st3 = [Buf(), Buf()]
    junk3 = sbD("junk3", [128, D], BF16); b_junk3 = Buf()
    with nc.allow_non_contiguous_dma(reason="tiny conv params, feature-major"):
        for fc in range(NFC):
            fsl = slice(fc * 128, (fc + 1) * 128)
            k.dma(k.q_sp, histT[:, fc, :], ffn_state[:, fsl].rearrange("e p -> p e"), w=[b_hist])
            k.dma(k.q_sp, cwT[:, fc, 0:3], conv_w[:, fsl].rearrange("j p -> p j"), w=[b_cwT])
            k.dma(k.q_sp, cwT[:, fc, 3:4], conv_b[fsl].rearrange("(p o) -> p o", o=1), w=[b_cwT])
    k.v("pool", lambda: nc.gpsimd.memset(gbuf[:, 0:2], 0.0), w=[b_gbuf])
    wgv = w_gate.rearrange("(c p) f -> p c f", p=128)
    wuv = w_up.rearrange("(c p) f -> p c f", p=128)
    wdv = w_down.rearrange("(c p) d -> p c d", p=128)
    cvi = 0
    with nc.allow_non_contiguous_dma(reason="conv state outputs are tiny feature-major slices"):
        for gi in range(NG):
            wi = gi % 2
            fs = slice(gi * 128 * GW, (gi + 1) * 128 * GW)
            for (wsrc, wdst, bw) in ((wgv, wg_b, b_wg), (wuv, wu_b, b_wu)):
                si_ = cvi % 2; cvi += 1
                k.dma(k.q_sp, stgW[si_][:, :, :], wsrc[:, :, fs], w=[b_stgW[si_]])
                for c in range(8):
                    k.actf(wdst[wi][:, c, :], stgW[si_][:, c, :], AF.Copy, r=[b_stgW[si_], b_g2], w=[bw[wi]], scale=g2col[:, c:c + 1])
            k.dma(k.q_pool, wd_b[wi][:, :, :], wdv[:, gi * GW:(gi + 1) * GW, :], w=[b_wd[wi]])
            for fl in range(GW):
                fc = gi * GW + fl
                w0 = cwT[:, fc, 0:1]; w1_ = cwT[:, fc, 1:2]; w2_ = cwT[:, fc, 2:3]; bc = cwT[:, fc, 3:4]
                for ci, (c0, cn) in enumerate(chunks5):
                    hb = b_h2T[4 * ci:4 * ci + 4] if ci < 4 else [b_h2T[16]]
                    i2 = (fc * 5 + ci) % 2
                    for c in range(8):
                        k.mm(ps[0 + i2][:, 0:cn], wg_b[wi][:, c, fl * 128:(fl + 1) * 128], h2T[:, c, c0:c0 + cn], start=(c == 0), stop=(c == 7), r=[b_wg[wi]] + hb, w=[b_ps[0 + i2]])
                    for c in range(8):
                        k.mm(ps[2 + i2][:, 0:cn], wu_b[wi][:, c, fl * 128:(fl + 1) * 128], h2T[:, c, c0:c0 + cn], start=(c == 0), stop=(c == 7), r=[b_wu[wi]] + hb, w=[b_ps[2 + i2]])
                    if ci < 4:
                        k.actf(gbuf[:, 2 + c0:2 + c0 + cn], ps[0 + i2][:, 0:cn], AF.Copy, r=[b_ps[0 + i2]], w=[b_gbuf])
                        gm2 = gbuf[:, c0:c0 + cn]; gm1 = gbuf[:, c0 + 1:c0 + 1 + cn]; g0 = gbuf[:, c0 + 2:c0 + 2 + cn]
                        cvt = cv[i2][:, 0:cn]; sgt = sg[i2][:, 0:cn]; mdst = mTg[wi][:, fl, c0:c0 + cn]; ups = ps[2 + i2][:, 0:cn]
                        rb = [b_gbuf]
                    else:
                        k.v("dve", lambda: nc.vector.tensor_copy(out=gbs[:, :, 0:2], in_=histT[:, fc, :].rearrange("p (b j) -> p b j", b=4)), r=[b_hist], w=[b_gbs])
                        k.actf(gbs[:, :, 2:6], ps[0 + i2][:, 0:TS].rearrange("p (b j) -> p b j", b=4), AF.Copy, r=[b_ps[0 + i2]], w=[b_gbs])
                        gm2 = gbs[:, :, 0:4]; gm1 = gbs[:, :, 1:5]; g0 = gbs[:, :, 2:6]
                        cvt = cv[i2][:, 0:TS].rearrange("p (b j) -> p b j", b=4); sgt = sg[i2][:, 0:TS].rearrange("p (b j) -> p b j", b=4)
                        mdst = mTg[wi][:, fl, c0:c0 + cn].rearrange("p (b j) -> p b j", b=4); ups = ps[2 + i2][:, 0:TS].rearrange("p (b j) -> p b j", b=4)
                        rb = [b_gbs]
                    k.v("pool", lambda: nc.gpsimd.tensor_scalar(out=cvt, in0=gm2, scalar1=w0, scalar2=bc, op0=ALU.mult, op1=ALU.add), r=rb + [b_cwT], w=[b_cv[i2]])
                    k.v("dve", lambda: nc.vector.scalar_tensor_tensor(out=cvt, in0=gm1, scalar=w1_, in1=cvt, op0=ALU.mult, op1=ALU.add), r=rb + [b_cwT, b_cv[i2]], w=[b_cv[i2]])
                    k.v("dve", lambda: nc.vector.scalar_tensor_tensor(out=cvt, in0=g0, scalar=w2_, in1=cvt, op0=ALU.mult, op1=ALU.add), r=rb + [b_cwT, b_cv[i2]], w=[b_cv[i2]])
                    k.actf(sgt, cvt, AF.Silu, r=[b_cv[i2]], w=[b_sg[i2]])
                    k.v("dve", lambda: nc.vector.tensor_tensor(out=mdst, in0=sgt, in1=ups, op=ALU.mult), r=[b_sg[i2], b_ps[2 + i2]], w=[b_mTg[wi]])
                    if ci == 3:
                        k.dma(k.q_pool, o_pconv[:, fc * 128:(fc + 1) * 128].rearrange("j f -> f j"), gbuf[:, T:T + 2], r=[b_gbuf])
                    if ci == 4:
                        for b in range(4):
                            k.dma(k.q_pool, o_sconv[b, :, fc * 128:(fc + 1) * 128].rearrange("j f -> f j"), gbs[:, b, 4:6], r=[b_gbs])
            for t in range(NT):
                t0, n = tile_rows(t)
                xd = x1_tile(t)
                for half in range(2):
                    pb = 4 + ((t * 2 + half) % 2)
                    for fl in range(GW):
                        k.mm(ps[pb][0:n, :], mTg[wi][:, fl, t0:t0 + n], wd_b[wi][:, fl, half * 512:(half + 1) * 512], start=(fl == 0), stop=(fl == GW - 1),
                             r=[b_mTg[wi], b_wd[wi]], w=[b_ps[pb]])
                    k.v("dve", lambda: nc.vector.tensor_tensor(out=xd[0:n, half * 512:(half + 1) * 512], in0=xd[0:n, half * 512:(half + 1) * 512],
                                                                in1=ps[pb][0:n, :], op=ALU.add), r=[b_x1[t], b_ps[pb]], w=[b_x1[t]])
    for t in range(NT):
        t0, n = tile_rows(t)
        s = t % 2
        xd = x1_tile(t)
        k.actf(junk3[0:n, :], xd[0:n, :], AF.Square, r=[b_x1[t]], w=[b_junk3, b_st3[s]], accum=st3[s][0:n, 0:1])
        k.actf(st3[s][0:n, 1:2], st3[s][0:n, 0:1], AF.Sqrt, r=[b_st3[s]], w=[b_st3[s]], scale=1.0 / D, bias=EPS)
        k.v("dve", lambda: nc.vector.reciprocal(out=st3[s][0:n, 2:3], in_=st3[s][0:n, 1:2]), r=[b_st3[s]], w=[b_st3[s]])
        k.v("dve", lambda: nc.vector.scalar_tensor_tensor(out=yo[s][0:n, :], in0=xd[0:n, :], scalar=st3[s][0:n, 2:3], in1=gf_b[0:n, :],
                                                           op0=ALU.mult, op1=ALU.mult), r=[b_x1[t], b_st3[s], b_gf], w=[b_yo[s]])
        if t < 16:
            k.dma(k.q_pool, o_yp[t0:t0 + n, :], yo[s][0:n, :], r=[b_yo[s]])
        else:
            k.dma(k.q_pool, o_ys[:, :], yo[s][0:n, :], r=[b_yo[s]])
    k.barrier()
    stD.close()
    stC.close()
    k.finish()
    return nc


def _inputs_for_core(c, inp):
    m = {
        "xp": np.ascontiguousarray(inp["x_prompt"][c]),
        "xs": np.ascontiguousarray(inp["x_sample"][4 * c:4 * c + 4].reshape(TS, D)),
        "w_in": np.ascontiguousarray(inp["w_in"][0]),
        "norm1_g": np.ascontiguousarray(inp["norm1_g"][0]),
        "q_norm_g": np.ascontiguousarray(inp["q_norm_g"][0]),
        "kv_norm_g": np.ascontiguousarray(inp["kv_norm_g"][0]),
        "w_uq": np.ascontiguousarray(inp["w_uq"][0].reshape(384, 768)),
        "ropecs": rope_tables(),
        "w_uk": np.ascontiguousarray(inp["w_uk"][0].reshape(256, 512)),
        "w_uv": np.ascontiguousarray(inp["w_uv"][0].reshape(256, 512)),
        "trimask": tri_masks(),
        "norm_f_g": np.ascontiguousarray(inp["norm_f_g"]), "norm2_g": np.ascontiguousarray(inp["norm2_g"][0]),
        "w_proj_mla": np.ascontiguousarray(inp["w_proj_mla"][0]), "w_proj_nsa": np.ascontiguousarray(inp["w_proj_nsa"][0]),
        "w_out": np.ascontiguousarray(inp["w_out"][0]), "w_gate": np.ascontiguousarray(inp["w_gate"][0]),
        "w_up": np.ascontiguousarray(inp["w_up"][0]), "w_down": np.ascontiguousarray(inp["w_down"][0]),
        "conv_w": np.ascontiguousarray(inp["conv_w"][0]), "conv_b": np.ascontiguousarray(inp["conv_b"][0]),
        "state_ffn_conv": np.ascontiguousarray(inp["state_ffn_conv"][0, 4 * c:4 * c + 4].reshape(8, DFF)),
        **nsa_consts(),
        **sample_consts(),
        "page_table": np.ascontiguousarray(inp["page_table"][4 * c:4 * c + 4]).astype(np.int32),
        "cache_mla_ckv": inp["cache_mla_ckv"].reshape(-1, 256), "cache_mla_krope": inp["cache_mla_krope"].reshape(-1, 32),
        "cache_nsa_cmp_k": inp["cache_nsa_cmp_k"].reshape(-1, 128), "cache_nsa_cmp_v": inp["cache_nsa_cmp_v"].reshape(-1, 128),
        "cache_nsa_slc_k": inp["cache_nsa_slc_k"].reshape(-1, 128), "cache_nsa_slc_v": inp["cache_nsa_slc_v"].reshape(-1, 128),
        "cmp_w1_k": np.ascontiguousarray(inp["cmp_w1_k"][0]), "cmp_w2_k": np.ascontiguousarray(inp["cmp_w2_k"][0]),
        "cmp_w1_v": np.ascontiguousarray(inp["cmp_w1_v"][0]), "cmp_w2_v": np.ascontiguousarray(inp["cmp_w2_v"][0]),
        "cmp_pos_k": np.ascontiguousarray(inp["cmp_pos_k"][0]), "cmp_pos_v": np.ascontiguousarray(inp["cmp_pos_v"][0]),
        "identf": np.eye(128, dtype=np.float32),
        "state_win_k": np.ascontiguousarray(inp["state_win_k"][0, 4 * c:4 * c + 4].reshape(4, 512, 128)),
        "state_win_v": np.ascontiguousarray(inp["state_win_v"][0, 4 * c:4 * c + 4].reshape(4, 512, 128)),
    }
    return m


def kernel(**inp):
    nc = build()
    in_maps = [_inputs_for_core(c, inp) for c in range(8)]
    res = run_bass_kernel_spmd(nc, in_maps, core_ids=list(range(8)))
    R = res.results

    def cat(nm, shape_per_core, default=None):
        outs = []
        for c in range(8):
            if nm in R[c]:
                outs.append(np.asarray(R[c][nm], dtype=np.float32).reshape(shape_per_core))
            else:
                outs.append(np.zeros(shape_per_core, np.float32))
        return outs

    y_prompt = np.stack(cat("yp", (T, D)), 0)
    y_sample = np.concatenate(cat("ys", (4, 4, D)), 0)
    outs = [y_prompt, y_sample]
    for nm, w in (("ckv", (256,)), ("krope", (32,)), ("cmp_k", (2, 64)), ("cmp_v", (2, 64)),
                  ("slc_k", (2, 64)), ("slc_v", (2, 64))):
        outs.append(np.stack(cat("p_" + nm, (T,) + w), 0)[None])
        outs.append(np.concatenate(cat("s_" + nm, (4, 4) + w), 0)[None])
    for nm in ("win_k", "win_v"):
        outs.append(np.stack(cat("p_" + nm, (512, 2, 64)), 0)[None])
        outs.append(np.concatenate(cat("s_" + nm, (4, 512, 2, 64)), 0)[None])
    outs.append(np.stack(cat("p_conv", (2, DFF)), 0)[None])
    outs.append(np.concatenate(cat("s_conv", (4, 2, DFF)), 0)[None])
    return tuple(outs)
```

```python
import os
import contextlib
import numpy as np
import concourse.bass as bass
import concourse.mybir as mybir
from concourse.bass_utils import run_bass_kernel_spmd

F32 = mybir.dt.float32
BF16 = mybir.dt.bfloat16
I32 = mybir.dt.int32
AF = mybir.ActivationFunctionType
ALU = mybir.AluOpType
AX = mybir.AxisListType

D = 1024
T = 2048
TS = 16
TT = T + TS
NT = 17
EPS = 1e-6
MLA_SCALE = 96 ** -0.5
NSA_SCALE = 0.125
DFF = 2816
NFC = 22
PAST = 16384
C_CQ, C_CKV, C_KR, C_QN, C_KVC, C_KVS, C_KVW, C_GN, C_GA, C_GB = 0, 384, 640, 672, 1184, 1440, 1696, 1952, 1976, 3000


class Buf:
    __slots__ = ("name", "w", "r", "excl")

    def __init__(self, name="", excl=False):
        self.name = name
        self.excl = excl
        self.w = None
        self.r = {}


class Eng:
    def __init__(self, k, name, eng, is_pe=False):
        self.k = k
        self.name = name
        self.e = eng
        self.is_pe = is_pe
        self.sem = k.nc.alloc_semaphore(f"s_{name}_0")
        self.nsem = 1
        self.cnt = 0
        self.seen = {}

    def wait(self, tok):
        sem, val = tok
        if self.is_pe and sem is self.sem:
            return
        key = id(sem)
        if self.seen.get(key, 0) >= val:
            return
        self.seen[key] = val
        self.e.wait_ge(sem, val)

    def next_tok(self):
        if self.cnt >= 30000:
            self.sem = self.k.nc.alloc_semaphore(f"s_{self.name}_{self.nsem}")
            self.nsem += 1
            self.cnt = 0
        return (self.sem, self.cnt + 1)


class DmaQ:
    def __init__(self, k, name, eng_obj, nsem):
        self.k = k
        self.name = name
        self.E = eng_obj
        self.sems = [k.nc.alloc_semaphore(f"d_{name}_{i}") for i in range(nsem)]
        self.cnts = [0] * nsem
        self.i = 0


class K:
    def __init__(self, nc):
        self.nc = nc
        self.pe = Eng(self, "pe", nc.tensor, is_pe=True)
        self.act = Eng(self, "act", nc.scalar)
        self.dve = Eng(self, "dve", nc.vector)
        self.pool = Eng(self, "pool", nc.gpsimd)
        self.sp = Eng(self, "sp", nc.sync)
        self.q_sp = DmaQ(self, "sp", self.sp, 40)
        self.q_pool = DmaQ(self, "pool", self.pool, 24)
        self.all_dma_toks = []

    def _deps(self, E, r, w):
        for b in r:
            if b.w is not None:
                E.wait(b.w)
        for b in w:
            if b.w is not None:
                E.wait(b.w)
            for t in b.r.values():
                E.wait(t)

    def _mark(self, tok, r, w):
        for b in r:
            key = id(tok[0])
            old = b.r.get(key)
            if old is None or old[1] < tok[1]:
                b.r[key] = tok
        for b in w:
            b.w = tok
            b.r = {}

    def op(self, E, fn, r=(), w=(), inc=True):
        ex = [b for b in r if b.excl]
        if ex:
            r = [b for b in r if not b.excl]
            w = list(w) + ex
        self._deps(E, r, w)
        tok = E.next_tok()
        ins = fn()
        if inc:
            ins.then_inc(tok[0], 1)
            E.cnt += 1
        self._mark(tok, r, w)
        return ins

    def dma(self, q, out, in_, r=(), w=(), **kw):
        E = q.E
        self._deps(E, r, w)
        i = q.i
        q.i = (q.i + 1) % len(q.sems)
        if q.cnts[i] > 0:
            E.wait((q.sems[i], q.cnts[i]))
        if q.cnts[i] >= 30000:
            q.sems[i] = self.nc.alloc_semaphore(f"d_{q.name}_{i}_r{q.cnts[i]}")
            q.cnts[i] = 0
        q.cnts[i] += 16
        tok = (q.sems[i], q.cnts[i])
        E.e.dma_start(out=out, in_=in_, **kw).then_inc(tok[0], 16)
        self._mark(tok, r, w)
        return tok

    def gather(self, out, in_, idx_ap, r=(), w=()):
        q = self.q_pool
        E = q.E
        self._deps(E, r, w)
        i = q.i
        q.i = (q.i + 1) % len(q.sems)
        if q.cnts[i] > 0:
            E.wait((q.sems[i], q.cnts[i]))
        q.cnts[i] += 16
        tok = (q.sems[i], q.cnts[i])
        E.e.indirect_dma_start(out=out, out_offset=None, in_=in_,
                               in_offset=bass.IndirectOffsetOnAxis(ap=idx_ap, axis=0)).then_inc(tok[0], 16)
        self._mark(tok, r, w)
        return tok

    def barrier(self):
        toks = []
        for E in (self.pe, self.act, self.dve, self.pool, self.sp):
            if E.cnt > 0:
                toks.append((E.sem, E.cnt))
        for q in (self.q_sp, self.q_pool):
            for s, c in zip(q.sems, q.cnts):
                if c > 0:
                    toks.append((s, c))
        for E in (self.pe, self.act, self.dve, self.pool, self.sp):
            for t in toks:
                if E.is_pe and t[0] is E.sem:
                    continue
                E.wait(t)

    def finish(self):
        for q in (self.q_sp, self.q_pool):
            for s, c in zip(q.sems, q.cnts):
                if c > 0:
                    self.sp.wait((s, c))
        for E in (self.pe, self.act, self.dve, self.pool):
            if E.cnt > 0:
                self.sp.wait((E.sem, E.cnt))

    def mm(self, out, lhsT, rhs, start, stop, r=(), w=(), sig=None):
        if sig is None:
            sig = stop
        return self.op(self.pe, lambda: self.nc.tensor.matmul(out, lhsT, rhs, start=start, stop=stop),
                       r=r, w=w, inc=sig)

    def tr(self, out, in_, ident, r=(), w=(), sig=True):
        return self.op(self.pe, lambda: self.nc.tensor.transpose(out, in_, ident), r=r, w=w, inc=sig)

    def actf(self, out, in_, func, r=(), w=(), scale=1.0, bias=0.0, accum=None):
        kw = {}
        if accum is not None:
            kw["accum_out"] = accum
        return self.op(self.act, lambda: self.nc.scalar.activation(out=out, in_=in_, func=func, bias=bias,
                                                                   scale=scale, **kw), r=r, w=w)

    def v(self, which, fn, r=(), w=()):
        E = {"dve": self.dve, "pool": self.pool}[which]
        return self.op(E, fn, r=r, w=w)


def rope_tables():
    inv = (10000.0 ** (-np.arange(0, 32, 2, dtype=np.float32) / 32)).astype(np.float32)
    pos = np.concatenate([np.arange(T, dtype=np.float32),
                          np.tile(PAST + np.arange(4, dtype=np.float32), 4)])
    pos = np.concatenate([pos, np.zeros(NT * 128 - TT, np.float32)])
    ang = (pos[:, None] * inv[None, :]).astype(np.float32)
    cs = np.concatenate([np.cos(ang), np.sin(ang)], axis=1).astype(np.float32)
    return cs.reshape(NT, 128, 32).transpose(1, 0, 2).copy()


def tri_masks():
    kk = np.arange(128)[:, None]; qq = np.arange(128)[None, :]
    m = np.zeros((128, 2, 128), np.float32)
    m[:, 0, :] = np.where(kk > qq, -30000.0, 0.0)
    m[:, 1, :] = np.where(qq >= kk, -30000.0, 0.0)
    return m


def nsa_consts():
    c = {}
    q = np.arange(T)
    qrows = np.stack([np.full(T, 8.0), np.full(T, 8.0), -8.0 * (q // 128 * 128), -8.0 * (q % 128)]).astype(np.float32)
    krows = np.stack([(q // 128 * 128), (q % 128), np.ones(T), np.ones(T)]).astype(np.float32)
    e = 16 * np.arange(128) + 31
    kcrows = np.stack([(e // 128 * 128), (e % 128), np.ones(128), np.ones(128)]).astype(np.float32)
    c["qrows"] = qrows; c["krows"] = krows; c["kcrows"] = kcrows
    erows = np.zeros((32, T), np.float32)
    erows[q // 64, q] = 32768.0
    c["erows"] = erows
    cm = np.where(e[:, None] > q[None, :], -30000.0, 0.0).astype(np.float32)
    c["cmpmask"] = cm
    i = np.arange(128)[:, None]; j = np.arange(32)[None, :]
    ov = ((i * 16 < (j + 1) * 64) & (i * 16 + 32 > j * 64)).astype(np.float32)
    ov[127] = 0
    c["overlap"] = ov
    selb = np.zeros((24, 3, 8, 64), np.float32)
    for h in range(8):
        for br in range(3):
            selb[h * 3 + br, br, h, :] = 1.0
    c["selb"] = selb
    mt = np.zeros((128, 16, 32), np.float32); at = np.zeros((128, 16, 32), np.float32)
    for qt in range(16):
        for p in range(128):
            cur = (qt * 128 + p) // 64
            jj = np.arange(32)
            m = np.ones(32, np.float32); a = np.zeros(32, np.float32)
            m[jj > cur] = 0; a[jj > cur] = -1e30
            for f, val in ((0, 1e9), (cur, 2e9), (cur - 1, 4e9)):
                if f >= 0:
                    m[f] = 0; a[f] = val
            mt[p, qt] = m; at[p, qt] = a
    c["topk_m"] = mt; c["topk_a"] = at
    return c


SLOPES = [2.0 ** (-8.0 * (h + 1) / 8) for h in range(8)]


def sample_consts():
    c = {}
    sl = np.array(SLOPES, np.float64)
    hh = np.repeat(np.arange(8), 4); tt = np.tile(np.arange(4), 8)
    kb = np.arange(16) // 4; kt = np.arange(16) % 4
    en = np.zeros((16, 4, 32), np.float64); nm = np.zeros((16, 4, 32), np.float64)
    for b in range(4):
        ok = (kb[:, None] == b) & (kt[:, None] <= tt[None, :])
        en[:, b, :] = np.where(ok, np.exp(-sl[hh][None, :] * (tt[None, :] - kt[:, None])), 0.0)
        nm[:, b, :] = ok
    c["s_en"] = en.astype(np.float32); c["s_nm"] = nm.astype(np.float32)
    i = (np.arange(4)[None, :, None] * 128 + np.arange(128)[:, None, None])
    dist = tt[None, None, :] + 512 - i
    c["s_ebw"] = np.where(dist < 512, np.exp(-sl[hh][None, None, :] * dist), 0.0).astype(np.float32)
    p = np.arange(128)
    c["s_lb"] = np.stack([16384.0 - 128.0 * p, np.ones(128), (p == 127).astype(np.float64)]).astype(np.float32)
    rr = np.arange(128)
    rbs = np.zeros((3, 128, 32), np.float64)
    rbs[0] = -8.0 * sl[hh][None, :]
    rbs[1] = -8.0 * sl[hh][None, :] * (tt[None, :] - rr[:, None])
    c["s_rbs"] = rbs.astype(np.float32)
    cc = np.arange(8)
    rbc = np.zeros((3, 8, 32), np.float64)
    rbc[0] = -8.0 * sl[hh][None, :]
    rbc[1] = -8.0 * sl[hh][None, :] * (tt[None, :] - 16 * cc[:, None] - 31)
    rbc[2, 7, :] = -30000.0
    c["s_rbc"] = rbc.astype(np.float32)
    ii = (8 * p[:, None] + cc[None, :])[:, :, None]; jj = np.arange(264)[None, None, :]
    ov = ((ii * 16 < (jj + 1) * 64) & (ii * 16 + 32 > jj * 64) & (jj < 257) & (ii < 1023)).astype(np.float32)
    c["s_ov"] = ov
    m = np.ones((4, 264), np.float32); a = np.zeros((4, 264), np.float32)
    for f, val in ((0, 1e9), (256, 2e9), (255, 4e9)):
        m[:, f] = 0; a[:, f] = val
    m[:, 257:] = 0; a[:, 257:] = -1e30
    c["s_tkm"] = m; c["s_tka"] = a
    ss = np.zeros((16, 4), np.float32)
    for hl in range(4):
        for t in range(4):
            ss[hl * 4 + t, t] = 1.0
    c["s_ssum"] = ss
    return c


def build(phases=("A",)):
    nc = bass.Bass("TRN2", target_bir_lowering=False)
    k = K(nc)

    def din(name, shape, dt=F32):
        return nc.dram_tensor(name, list(shape), dt, kind="ExternalInput").ap()

    def dout(name, shape, dt=F32):
        return nc.dram_tensor(name, list(shape), dt, kind="ExternalOutput").ap()

    xp = din("xp", [T, D])
    xs = din("xs", [TS, D])
    w_in = din("w_in", [D, 4024])
    norm1_g = din("norm1_g", [D])
    q_norm_g = din("q_norm_g", [384])
    kv_norm_g = din("kv_norm_g", [256])
    w_uq = din("w_uq", [384, 768])
    ropecs = din("ropecs", [128, NT, 32])
    identf_d = din("identf", [128, 128])
    w_uk = din("w_uk", [256, 512])
    w_uv = din("w_uv", [256, 512])
    trimask_d = din("trimask", [128, 2, 128])
    qrows_d = din("qrows", [4, T]); krows_d = din("krows", [4, T]); kcrows_d = din("kcrows", [4, 128])
    erows_d = din("erows", [32, T]); cmpmask_d = din("cmpmask", [128, T]); overlap_d = din("overlap", [128, 32])
    selb_d = din("selb", [24, 3, 8, 64]); topkm_d = din("topk_m", [128, 16, 32]); topka_d = din("topk_a", [128, 16, 32])
    cw1 = {"k": din("cmp_w1_k", [2048, 64]), "v": din("cmp_w1_v", [2048, 64])}
    cw2 = {"k": din("cmp_w2_k", [64, 64]), "v": din("cmp_w2_v", [64, 64])}
    cpos = {"k": din("cmp_pos_k", [32, 64]), "v": din("cmp_pos_v", [32, 64])}
    DBG = bool(int(os.environ.get("KDBG", "0")))
    if DBG:
        dbg_omla = dout("dbg_omla", [128, 4, TT])
        dbg_onsa = dout("dbg_onsa", [128, 4, TT])
    norm_f_g = din("norm_f_g", [D]); norm2_g = din("norm2_g", [D])
    w_pm = din("w_proj_mla", [512, D]); w_pn = din("w_proj_nsa", [512, D]); w_out = din("w_out", [D, D])
    w_gate = din("w_gate", [D, DFF]); w_up = din("w_up", [D, DFF]); w_down = din("w_down", [DFF, D])
    conv_w = din("conv_w", [3, DFF]); conv_b = din("conv_b", [DFF]); ffn_state = din("state_ffn_conv", [8, DFF])
    o_yp = dout("yp", [T, D]); o_ys = dout("ys", [TS, D])
    o_pconv = dout("p_conv", [2, DFF]); o_sconv = dout("s_conv", [4, 2, DFF])
    NPG = 5120
    pt_d = din("page_table", [4, 128], I32)
    c_ckv = din("cache_mla_ckv", [NPG * 128, 256]); c_kr = din("cache_mla_krope", [NPG * 128, 32])
    c_ck = din("cache_nsa_cmp_k", [NPG * 128, 128]); c_cv = din("cache_nsa_cmp_v", [NPG * 128, 128])
    c_sk = din("cache_nsa_slc_k", [NPG * 128, 128]); c_sv = din("cache_nsa_slc_v", [NPG * 128, 128])
    s_en_d = din("s_en", [16, 4, 32]); s_nm_d = din("s_nm", [16, 4, 32]); s_ebw_d = din("s_ebw", [128, 4, 32])
    s_lb_d = din("s_lb", [3, 128]); s_rbs_d = din("s_rbs", [3, 128, 32]); s_rbc_d = din("s_rbc", [3, 8, 32])
    s_ov_d = din("s_ov", [128, 8, 264]); s_tkm_d = din("s_tkm", [4, 264]); s_tka_d = din("s_tka", [4, 264]); s_ssum_d = din("s_ssum", [16, 4])
    win_k_in = din("state_win_k", [4, 512, 128])
    win_v_in = din("state_win_v", [4, 512, 128])

    o_pckv = dout("p_ckv", [T, 256]); o_sckv = dout("s_ckv", [TS, 256])
    o_pkr = dout("p_krope", [T, 32]); o_skr = dout("s_krope", [TS, 32])
    o_kv = {}
    for nm in ("cmp_k", "cmp_v", "slc_k", "slc_v"):
        o_kv["p_" + nm] = dout("p_" + nm, [T, 128])
        o_kv["s_" + nm] = dout("s_" + nm, [TS, 128])
    o_pwk = dout("p_win_k", [512, 128]); o_pwv = dout("p_win_v", [512, 128])
    o_swk = dout("s_win_k", [4, 512, 128]); o_swv = dout("s_win_v", [4, 512, 128])

    sb = nc.alloc_sbuf_tensor

    def scoped(stack):
        def f(name, shape, dt):
            return stack.enter_context(nc.sbuf_tensor(name, shape, dt))
        return f
    identf = sb("identf_sb", [128, 128], F32); b_identf = Buf()
    identb = sb("identb_sb", [128, 128], BF16); b_identb = Buf()
    cs_sb = sb("cs_sb", [128, NT, 32], F32); b_cs = Buf()
    big = sb("big", [128, 16 * TT], BF16)
    hT = big[:, 0:8 * TT].rearrange("p (c t) -> p c t", c=8); b_hT = [Buf() for _ in range(NT)]
    g1col = sb("g1col", [128, 8], F32); b_g1 = Buf()
    gqcol = sb("gqcol", [128, 3], F32); b_gq = Buf()
    gkv_b = sb("gkv_b", [128, 256], F32); b_gkv = Buf()
    o_mlaT = big[:, 8 * TT:12 * TT].rearrange("p (c t) -> p c t", c=4); b_omla = Buf()
    o_nsaT = big[:, 12 * TT:16 * TT].rearrange("p (c t) -> p c t", c=4); b_onsa = Buf()
    x1all = big[:, 0:32768].bitcast(F32).rearrange("p (t d) -> p t d", t=16)
    x1s = sb("x1s", [128, D], F32)
    smp_qm = sb("smp_qm", [96, 8, TS], BF16); smp_ckvT = sb("smp_ckvT", [128, 2, TS], BF16); smp_krT = sb("smp_krT", [32, TS], BF16)
    smp_ckvn = sb("smp_ckvn", [TS, 257], BF16); smp_qn = sb("smp_qn", [64, 8, TS], BF16)
    smp_ksT = sb("smp_ksT", [128, TS], BF16); smp_kwT = sb("smp_kwT", [128, TS], BF16)
    smp_vs = sb("smp_vs", [TS, 129], BF16); smp_vw = sb("smp_vw", [TS, 129], BF16); smp_gn = sb("smp_gn", [24, TS], BF16)
    b_smp = Buf()
    b_x1 = [Buf() for _ in range(NT)]
    gf_b = sb("gf_b", [128, D], F32); b_gf = Buf()
    g2col = sb("g2col", [128, 8], F32); b_g2 = Buf()
    trimask = sb("trimask_sb", [128, 2, 128], BF16); b_tri = Buf()
    stL1 = contextlib.ExitStack(); sbL1 = scoped(stL1)
    vs_nat = sbL1("vs_nat", [128, NT, 2, 128], BF16)
    vw_nat = sbL1("vw_nat", [128, NT, 2, 128], BF16)
    b_vnat = [Buf() for _ in range(NT)]
    KAs = [sbL1(f"KAs{g}", [128, T], BF16) for g in range(2)]; b_KAs = [Buf(), Buf()]
    KAw = [sbL1(f"KAw{g}", [128, T], BF16) for g in range(2)]; b_KAw = [Buf(), Buf()]
    kccA = [sbL1(f"kccA{g}", [128, 128], BF16) for g in range(2)]; b_kccA = [Buf(), Buf()]
    vccA = [sbL1(f"vccA{g}", [128, 128], BF16) for g in range(2)]; b_vccA = [Buf(), Buf()]
    stL2 = contextlib.ExitStack(); sbL2 = scoped(stL2)
    qmT = sbL2("qmT", [96, 8, TT], BF16); b_qmT = [Buf() for _ in range(NT)]
    ckvT = sbL2("ckvT", [128, 2, TT], BF16); b_ckvT = [Buf() for _ in range(NT)]
    kropeT = sbL2("kropeT", [32, TT], BF16); b_krT = [Buf() for _ in range(NT)]

    ps = [nc.alloc_psum_tensor(f"ps{i}", [128, 512], F32) for i in range(8)]
    b_ps = [Buf(f"ps{i}", excl=True) for i in range(8)]
    psb = ps[7].bitcast(BF16)
    psbv2 = [ps[7].bitcast(BF16), ps[6].bitcast(BF16)]

    k.dma(k.q_sp, identf[:, :], identf_d[:, :], w=[b_identf])
    k.v("dve", lambda: nc.vector.tensor_copy(out=identb[:, :], in_=identf[:, :]), r=[b_identf], w=[b_identb])
    k.dma(k.q_sp, cs_sb[:, :, :], ropecs[:, :, :], w=[b_cs])
    k.dma(k.q_pool, trimask[:, :, :], trimask_d[:, :, :], w=[b_tri])
    with nc.allow_non_contiguous_dma(reason="small param columns"):
        k.dma(k.q_sp, g1col[:, :], norm1_g.rearrange("(c p) -> p c", p=128), w=[b_g1])
        k.dma(k.q_sp, gqcol[:, :], q_norm_g.rearrange("(c p) -> p c", p=128), w=[b_gq])
        k.dma(k.q_sp, gkv_b[:, :], kv_norm_g.partition_broadcast(128), w=[b_gkv])
        k.dma(k.q_sp, gf_b[:, :], norm_f_g.partition_broadcast(128), w=[b_gf])
        k.dma(k.q_sp, g2col[:, :], norm2_g.rearrange("(c p) -> p c", p=128), w=[b_g2])
    for tl in (smp_ckvn, smp_vs, smp_vw):
        k.v("pool", lambda: nc.gpsimd.memset(tl[:, :], 1.0), w=[b_smp])
    k.v("pool", lambda: nc.gpsimd.memset(vs_nat[:, :, :, :], 1.0), w=b_vnat)
    k.v("pool", lambda: nc.gpsimd.memset(vw_nat[:, :, :, :], 1.0), w=b_vnat)

    def tile_rows(t):
        return (t * 128, 128) if t < 16 else (T, TS)
    FAST = bool(int(os.environ.get('KFAST', '0')))
    TILES = [16] if FAST else list(range(NT))

    stA = contextlib.ExitStack(); sbA = scoped(stA)
    w1A = sbA("w1A", [128, 8, 672], BF16); b_w1A = [Buf() for _ in range(8)]
    w_uq_b = sbA("w_uq_b", [128, 3, 768], BF16); b_wuq = Buf()
    cqnT2 = [sbA(f"cqnT{i}", [128, 3, 128], BF16) for i in range(2)]; b_cqnT2 = [Buf(), Buf()]
    stg_ = [sbA(f"wstg{i}", [128, 768], F32) for i in range(2)]; b_stg_ = [Buf(), Buf()]
    for c in range(8):
        stg = stg_[c % 2]; b_stg = b_stg_[c % 2]
        k.dma(k.q_sp, stg[:, 0:672], w_in[c * 128:(c + 1) * 128, 0:672], w=[b_stg])
        k.actf(w1A[:, c, :], stg[:, 0:672], AF.Copy, r=[b_stg, b_g1], w=[b_w1A[c]], scale=g1col[:, c:c + 1])
    for c in range(3):
        stg = stg_[c % 2]; b_stg = b_stg_[c % 2]
        k.dma(k.q_sp, stg[:, 0:768], w_uq[c * 128:(c + 1) * 128, :], w=[b_stg])
        k.actf(w_uq_b[:, c, :], stg[:, 0:768], AF.Copy, r=[b_stg, b_gq], w=[b_wuq], scale=gqcol[:, c:c + 1])
    _al = {"o": 8 * TT}

    def alias(nelem_bf16, dt, shape):
        a = big[:, _al["o"]:_al["o"] + nelem_bf16]; _al["o"] += nelem_bf16
        assert _al["o"] <= 16 * TT
        if dt == F32:
            a = a.bitcast(F32)
        if len(shape) == 4:
            a = a.rearrange("p (a b c) -> p a b c", a=shape[1], b=shape[2])
        return a
    NB1 = 4
    xt = [sbA(f"xt{i}", [128, D], F32) for i in range(2)] + [alias(2 * D, F32, [128, D]) for _ in range(2)]; b_xt = [Buf() for _ in range(NB1)]
    xn = [sbA(f"xn{i}", [128, D], BF16) for i in range(2)] + [alias(D, BF16, [128, D]) for _ in range(2)]; b_xn = [Buf() for _ in range(NB1)]
    junk = sbA("junk", [128, D], BF16); b_junk = Buf()
    st1 = [sbA(f"st1_{i}", [128, 8], F32) for i in range(NB1)]; b_st1 = [Buf() for _ in range(NB1)]
    oA = [sbA(f"oA{i}", [128, 288], F32) for i in range(2)] + [alias(576, F32, [128, 288]) for _ in range(2)]; b_oA = [Buf() for _ in range(NB1)]
    tb = [sbA(f"tb{i}", [128, 672], BF16) for i in range(2)] + [alias(672, BF16, [128, 672]) for _ in range(2)]; b_tb = [Buf() for _ in range(NB1)]
    qtok = [sbA(f"qtok{i}", [128, 800], BF16) for i in range(2)] + [alias(800, BF16, [128, 800]) for _ in range(2)]; b_qtok = [Buf() for _ in range(NB1)]
    rtmp = [sbA(f"rtmp{i}", [128, 4, 8, 16], F32) for i in range(2)] + [alias(1024, F32, [128, 4, 8, 16]) for _ in range(2)]; b_rtmp = [Buf() for _ in range(NB1)]

    def p1_stages(t):
        t0, n = tile_rows(t)
        s = t % NB1
        src = xp[t0:t0 + n, :] if t < 16 else xs[:, :]
        gA, gB = (0, 1) if t % 2 == 0 else (2, 3)
        psA_, psB_ = ps[gA], ps[gB]; bA_, bB_ = b_ps[gA], b_ps[gB]
        cos = cs_sb[0:n, t, 0:16]; sin = cs_sb[0:n, t, 16:32]
        rt = rtmp[s]
        cqnT = cqnT2[t % 2]
        fs = []
        def f_S1():
            k.dma(k.q_sp, xt[s][0:n, :], src, w=[b_xt[s]])
            k.actf(junk[0:n, :], xt[s][0:n, :], AF.Square, r=[b_xt[s]], w=[b_junk, b_st1[s]], accum=st1[s][0:n, 0:1])
            k.actf(st1[s][0:n, 1:2], st1[s][0:n, 0:1], AF.Sqrt, r=[b_st1[s]], w=[b_st1[s]], scale=1.0 / D, bias=EPS)
            k.v("dve", lambda: nc.vector.reciprocal(out=st1[s][0:n, 2:3], in_=st1[s][0:n, 1:2]), r=[b_st1[s]], w=[b_st1[s]])
            k.v("dve", lambda: nc.vector.tensor_scalar(out=xn[s][0:n, :], in0=xt[s][0:n, :], scalar1=st1[s][0:n, 2:3],
                                                        scalar2=None, op0=ALU.mult), r=[b_xt[s], b_st1[s]], w=[b_xn[s]])
        fs.append(f_S1)
        def f_S2():
            psb = psbv2[0]; b_psb = b_ps[7]
            for c in range(8):
                k.tr(psb[:, c * 128:c * 128 + n], xn[s][0:n, c * 128:(c + 1) * 128], identb[0:n, 0:n],
                     r=[b_xn[s], b_identb], w=[b_psb], sig=(c == 7))
            k.v("dve", lambda: nc.vector.tensor_copy(
                out=hT[:, :, t0:t0 + n], in_=psb[:, :].rearrange("p (c n) -> p c n", c=8)[:, :, 0:n]),
                r=[b_psb], w=[b_hT[t]])
        fs.append(f_S2)
        def f_S3():
            for (pb, c0, c1) in ((gA, 0, 512), (gB, 512, 672)):
                for c in range(8):
                    k.mm(ps[pb][0:n, 0:c1 - c0], hT[:, c, t0:t0 + n], w1A[:, c, c0:c1], start=(c == 0), stop=(c == 7),
                         r=[b_hT[t], b_w1A[c]], w=[b_ps[pb]])
        fs.append(f_S3)
        def f_S4():
            k.actf(junk[0:n, 0:384], psA_[0:n, 0:384], AF.Square, r=[bA_], w=[b_junk, b_st1[s]], accum=st1[s][0:n, 3:4])
            k.actf(st1[s][0:n, 4:5], st1[s][0:n, 3:4], AF.Sqrt, r=[b_st1[s]], w=[b_st1[s]], scale=1.0 / 384, bias=EPS)
            k.v("dve", lambda: nc.vector.reciprocal(out=st1[s][0:n, 4:5], in_=st1[s][0:n, 4:5]), r=[b_st1[s]], w=[b_st1[s]])
            k.v("dve", lambda: nc.vector.tensor_scalar(out=tb[s][0:n, 0:384], in0=psA_[0:n, 0:384],
                                                        scalar1=st1[s][0:n, 4:5], scalar2=None, op0=ALU.mult),
                r=[bA_, b_st1[s]], w=[b_tb[s]])
            k.actf(junk[0:n, 0:128], psA_[0:n, 384:512], AF.Square, r=[bA_], w=[b_junk, b_st1[s]], accum=st1[s][0:n, 5:6])
            k.actf(junk[0:n, 0:128], psB_[0:n, 0:128], AF.Square, r=[bB_], w=[b_junk, b_st1[s]], accum=st1[s][0:n, 6:7])
            k.v("dve", lambda: nc.vector.tensor_tensor(out=st1[s][0:n, 5:6], in0=st1[s][0:n, 5:6], in1=st1[s][0:n, 6:7],
                                                        op=ALU.add), r=[b_st1[s]], w=[b_st1[s]])
            k.actf(st1[s][0:n, 6:7], st1[s][0:n, 5:6], AF.Sqrt, r=[b_st1[s]], w=[b_st1[s]], scale=1.0 / 256, bias=EPS)
            k.v("dve", lambda: nc.vector.reciprocal(out=st1[s][0:n, 6:7], in_=st1[s][0:n, 6:7]), r=[b_st1[s]], w=[b_st1[s]])
            k.v("dve", lambda: nc.vector.scalar_tensor_tensor(out=oA[s][0:n, 0:128], in0=psA_[0:n, 384:512],
                                                               scalar=st1[s][0:n, 6:7], in1=gkv_b[0:n, 0:128],
                                                               op0=ALU.mult, op1=ALU.mult),
                r=[bA_, b_st1[s], b_gkv], w=[b_oA[s]])
            k.v("dve", lambda: nc.vector.scalar_tensor_tensor(out=oA[s][0:n, 128:256], in0=psB_[0:n, 0:128],
                                                               scalar=st1[s][0:n, 6:7], in1=gkv_b[0:n, 128:256],
                                                               op0=ALU.mult, op1=ALU.mult),
                r=[bB_, b_st1[s], b_gkv], w=[b_oA[s]])
            cos = cs_sb[0:n, t, 0:16]; sin = cs_sb[0:n, t, 16:32]
            x1 = psB_[0:n, 128:144]; x2 = psB_[0:n, 144:160]
            rt = rtmp[s]
            k.v("dve", lambda: nc.vector.tensor_tensor(out=rt[0:n, 0, 0, :], in0=x1, in1=cos, op=ALU.mult), r=[bB_, b_cs], w=[b_rtmp[s]])
            k.v("dve", lambda: nc.vector.tensor_tensor(out=rt[0:n, 1, 0, :], in0=x2, in1=sin, op=ALU.mult), r=[bB_, b_cs], w=[b_rtmp[s]])
            k.v("dve", lambda: nc.vector.tensor_tensor(out=rt[0:n, 2, 0, :], in0=x1, in1=sin, op=ALU.mult), r=[bB_, b_cs], w=[b_rtmp[s]])
            k.v("dve", lambda: nc.vector.tensor_tensor(out=rt[0:n, 3, 0, :], in0=x2, in1=cos, op=ALU.mult), r=[bB_, b_cs], w=[b_rtmp[s]])
            k.v("dve", lambda: nc.vector.tensor_tensor(out=oA[s][0:n, 256:272], in0=rt[0:n, 0, 0, :], in1=rt[0:n, 1, 0, :], op=ALU.subtract), r=[b_rtmp[s]], w=[b_oA[s]])
            k.v("dve", lambda: nc.vector.tensor_tensor(out=oA[s][0:n, 272:288], in0=rt[0:n, 2, 0, :], in1=rt[0:n, 3, 0, :], op=ALU.add), r=[b_rtmp[s]], w=[b_oA[s]])
            if t < 16:
                k.dma(k.q_pool, o_pckv[t0:t0 + n, :], oA[s][0:n, 0:256], r=[b_oA[s]])
                k.dma(k.q_pool, o_pkr[t0:t0 + n, :], oA[s][0:n, 256:288], r=[b_oA[s]])
            else:
                k.dma(k.q_pool, o_sckv[:, :], oA[s][0:n, 0:256], r=[b_oA[s]])
                k.dma(k.q_pool, o_skr[:, :], oA[s][0:n, 256:288], r=[b_oA[s]])
            k.actf(tb[s][0:n, 384:672], oA[s][0:n, 0:288], AF.Copy, r=[b_oA[s]], w=[b_tb[s]])
        fs.append(f_S4)
        def f_S5():
            psb = psbv2[1]; b_psb = b_ps[6]
            for j in range(5):
                k.tr(psb[:, j * 128:j * 128 + n], tb[s][0:n, j * 128:(j + 1) * 128], identb[0:n, 0:n],
                     r=[b_tb[s], b_identb], w=[b_psb], sig=False)
            k.tr(psb[0:32, 640:640 + n], tb[s][0:n, 640:672], identb[0:n, 0:n], r=[b_tb[s], b_identb], w=[b_psb], sig=True)
            cqnT = cqnT2[t % 2]
            k.v("dve", lambda: nc.vector.tensor_copy(out=cqnT[:, :, 0:n],
                                                      in_=psb[:, 0:384].rearrange("p (c n) -> p c n", c=3)[:, :, 0:n]),
                r=[b_psb], w=[b_cqnT2[t % 2]])
            k.v("dve", lambda: nc.vector.tensor_copy(out=ckvT[:, :, t0:t0 + n],
                                                      in_=psb[:, 384:640].rearrange("p (c n) -> p c n", c=2)[:, :, 0:n]),
                r=[b_psb], w=[b_ckvT[t]])
            k.v("dve", lambda: nc.vector.tensor_copy(out=kropeT[:, t0:t0 + n], in_=psb[0:32, 640:640 + n]), r=[b_psb], w=[b_krT[t]])
        fs.append(f_S5)
        def f_S6():
            for (pb, c0, c1) in ((4, 0, 512), (5, 512, 768)):
                for c in range(3):
                    k.mm(ps[pb][0:n, 0:c1 - c0], cqnT[:, c, 0:n], w_uq_b[:, c, c0:c1], start=(c == 0), stop=(c == 2),
                         r=[b_cqnT2[t % 2], b_wuq], w=[b_ps[pb]])
        fs.append(f_S6)
        def f_S7():
            k.actf(qtok[s][0:n, 0:512], ps[4][0:n, 0:512], AF.Copy, r=[b_ps[4]], w=[b_qtok[s]])
            k.actf(qtok[s][0:n, 512:768], ps[5][0:n, 0:256], AF.Copy, r=[b_ps[5]], w=[b_qtok[s]])
            for (pb, h0, nh, base) in ((4, 0, 5, 64), (5, 5, 3, 5 * 96 + 64 - 512)):
                pv4 = ps[pb][0:n, base:base + (nh - 1) * 96 + 32]
                def hv(off):
                    return bass.AP(tensor=pv4.tensor, offset=pv4.offset + off, ap=[list(pv4.ap[0]), [96, nh], [1, 16]])
                x1h = hv(0); x2h = hv(16)
                cosb = cs_sb[0:n, t, 0:16].unsqueeze(1).to_broadcast([n, nh, 16]); sinb = cs_sb[0:n, t, 16:32].unsqueeze(1).to_broadcast([n, nh, 16])
                k.v("dve", lambda: nc.vector.tensor_tensor(out=rt[0:n, 0, h0:h0 + nh, :], in0=x1h, in1=cosb, op=ALU.mult), r=[b_ps[pb], b_cs], w=[b_rtmp[s]])
                k.v("dve", lambda: nc.vector.tensor_tensor(out=rt[0:n, 1, h0:h0 + nh, :], in0=x2h, in1=sinb, op=ALU.mult), r=[b_ps[pb], b_cs], w=[b_rtmp[s]])
                k.v("dve", lambda: nc.vector.tensor_tensor(out=rt[0:n, 2, h0:h0 + nh, :], in0=x1h, in1=sinb, op=ALU.mult), r=[b_ps[pb], b_cs], w=[b_rtmp[s]])
                k.v("dve", lambda: nc.vector.tensor_tensor(out=rt[0:n, 3, h0:h0 + nh, :], in0=x2h, in1=cosb, op=ALU.mult), r=[b_ps[pb], b_cs], w=[b_rtmp[s]])
            qv = qtok[s][0:n, 0:768].rearrange("p (h e) -> p h e", h=8)
            k.v("dve", lambda: nc.vector.tensor_tensor(out=qv[:, :, 64:80], in0=rt[0:n, 0, :, :], in1=rt[0:n, 1, :, :], op=ALU.subtract), r=[b_rtmp[s]], w=[b_qtok[s]])
            k.v("dve", lambda: nc.vector.tensor_tensor(out=qv[:, :, 80:96], in0=rt[0:n, 2, :, :], in1=rt[0:n, 3, :, :], op=ALU.add), r=[b_rtmp[s]], w=[b_qtok[s]])
        fs.append(f_S7)
        def f_S8():
            psb = psbv2[0]; b_psb = b_ps[7]
            for h in range(8):
                k.tr(psb[:, h * 128:h * 128 + n], qtok[s][0:n, h * 96:h * 96 + 128], identb[0:n, 0:n],
                     r=[b_qtok[s], b_identb], w=[b_psb], sig=(h == 7))
            k.v("dve", lambda: nc.vector.tensor_copy(out=qmT[0:96, :, t0:t0 + n],
                                                      in_=psb[0:96, :].rearrange("p (c n) -> p c n", c=8)[:, :, 0:n]),
                r=[b_psb], w=[b_qmT[t]])
        fs.append(f_S8)
        return fs
    OFFS = [0.0, 1.0, 1.1, 1.2, 2.3, 2.4, 2.5, 3.6]
    tl1 = []
    for ti_, t in enumerate(TILES):
        for kk_, f_ in enumerate(p1_stages(t)):
            tl1.append((ti_ + OFFS[kk_], f_))
    for _, f_ in sorted(tl1, key=lambda e: e[0]):
        f_()
    k.v("dve", lambda: nc.vector.tensor_copy(out=smp_ckvn[0:TS, 0:256], in_=tb[0][0:TS, 384:640]), r=[b_tb[0]], w=[b_smp])
    k.v("dve", lambda: nc.vector.tensor_copy(out=smp_ckvT[:, :, :], in_=ckvT[:, :, T:TT]), r=[b_ckvT[16]], w=[b_smp])
    k.v("dve", lambda: nc.vector.tensor_copy(out=smp_krT[:, :], in_=kropeT[:, T:TT]), r=[b_krT[16]], w=[b_smp])
    k.v("dve", lambda: nc.vector.tensor_copy(out=smp_qm[:, :, :], in_=qmT[:, :, T:TT]), r=[b_qmT[16]], w=[b_smp])
    k.barrier()
    stA.close()

    stB = contextlib.ExitStack(); sbB = scoped(stB)
    kcT = sbB("kcT", [128, TT], BF16); vcT = sbB("vcT", [128, TT], BF16)
    ksT = sbB("ksT", [128, TT], BF16); kwT = sbB("kwT", [128, TT], BF16)
    b_kvT = [Buf() for _ in range(NT)]
    stB1 = contextlib.ExitStack(); sbB1 = scoped(stB1)
    w2A = sbB1("w2A", [128, 8, 768], BF16); b_w2A = [Buf() for _ in range(8)]
    stg2_ = [sbB1(f"wstg2_{i}", [128, 768], F32) for i in range(3)]; b_stg2_ = [Buf(), Buf(), Buf()]
    oB = [sbB1(f"oB{i}", [128, 768], F32) for i in range(2)]; b_oB = [Buf(), Buf()]
    tb2 = [sbB1(f"tb2_{i}", [128, 768], BF16) for i in range(2)]; b_tb2 = [Buf(), Buf()]
    for c in range(8):
        stg2 = stg2_[c % 3]; b_stg2 = b_stg2_[c % 3]
        k.dma(k.q_sp, stg2[:, :], w_in[c * 128:(c + 1) * 128, C_KVC:C_KVC + 768], w=[b_stg2])
        k.actf(w2A[:, c, :], stg2[:, :], AF.Copy, r=[b_stg2, b_g1], w=[b_w2A[c]], scale=g1col[:, c:c + 1])
    for t in TILES:
        t0, n = tile_rows(t)
        s = t % 2
        g2a, g2b = (2, 3) if t % 2 == 0 else (0, 1)
        psb = psbv2[t % 2]; b_psb = b_ps[7 - (t % 2)]
        for (pb, c0, c1) in ((g2a, 0, 512), (g2b, 512, 768)):
            for c in range(8):
                k.mm(ps[pb][0:n, 0:c1 - c0], hT[:, c, t0:t0 + n], w2A[:, c, c0:c1], start=(c == 0), stop=(c == 7),
                     r=[b_hT[t], b_w2A[c]], w=[b_ps[pb]])
        k.actf(oB[s][0:n, 0:512], ps[g2a][0:n, 0:512], AF.Copy, r=[b_ps[g2a]], w=[b_oB[s]])
        k.actf(oB[s][0:n, 512:768], ps[g2b][0:n, 0:256], AF.Copy, r=[b_ps[g2b]], w=[b_oB[s]])
        pre = "p_" if t < 16 else "s_"
        rows = slice(t0, t0 + n) if t < 16 else slice(0, n)
        for i, nm in enumerate(("cmp_k", "cmp_v", "slc_k", "slc_v")):
            k.dma(k.q_pool, o_kv[pre + nm][rows, :], oB[s][0:n, i * 128:(i + 1) * 128], r=[b_oB[s]])
        if 12 <= t < 16:
            k.dma(k.q_pool, o_pwk[(t - 12) * 128:(t - 11) * 128, :], oB[s][0:n, 512:640], r=[b_oB[s]])
            k.dma(k.q_pool, o_pwv[(t - 12) * 128:(t - 11) * 128, :], oB[s][0:n, 640:768], r=[b_oB[s]])
        if t == 16:
            for b in range(4):
                k.dma(k.q_pool, o_swk[b, 508:512, :], oB[s][4 * b:4 * b + 4, 512:640], r=[b_oB[s]])
                k.dma(k.q_pool, o_swv[b, 508:512, :], oB[s][4 * b:4 * b + 4, 640:768], r=[b_oB[s]])
                k.dma(k.q_sp, o_swk[b, 0:508, :], win_k_in[b, 4:512, :])
                k.dma(k.q_sp, o_swv[b, 0:508, :], win_v_in[b, 4:512, :])
        k.v("dve", lambda: nc.vector.tensor_copy(out=tb2[s][0:n, :], in_=oB[s][0:n, 0:768]), r=[b_oB[s]], w=[b_tb2[s]])
        k.v("pool", lambda: nc.gpsimd.tensor_copy(out=vs_nat[0:n, t, :, 0:64],
                                                  in_=tb2[s][0:n, 384:512].rearrange("p (g d) -> p g d", g=2)),
            r=[b_tb2[s]], w=[b_vnat[t]])
        k.v("pool", lambda: nc.gpsimd.tensor_copy(out=vw_nat[0:n, t, :, 0:64],
                                                  in_=tb2[s][0:n, 640:768].rearrange("p (g d) -> p g d", g=2)),
            r=[b_tb2[s]], w=[b_vnat[t]])
        for j, c0 in enumerate((0, 128, 256, 512)):
            k.tr(psb[:, j * 128:j * 128 + n], tb2[s][0:n, c0:c0 + 128], identb[0:n, 0:n],
                 r=[b_tb2[s], b_identb], w=[b_psb], sig=(j == 3))
        for j, dst in enumerate((kcT, vcT, ksT, kwT)):
            k.v("dve", lambda: nc.vector.tensor_copy(out=dst[:, t0:t0 + n], in_=psb[:, j * 128:j * 128 + n]), r=[b_psb], w=[b_kvT[t]])

    k.v("dve", lambda: nc.vector.tensor_copy(out=smp_vs[0:TS, 0:128], in_=tb2[0][0:TS, 384:512]), r=[b_tb2[0]], w=[b_smp])
    k.v("dve", lambda: nc.vector.tensor_copy(out=smp_vw[0:TS, 0:128], in_=tb2[0][0:TS, 640:768]), r=[b_tb2[0]], w=[b_smp])
    k.v("dve", lambda: nc.vector.tensor_copy(out=smp_ksT[:, :], in_=ksT[:, T:TT]), r=[b_kvT[16]], w=[b_smp])
    k.v("dve", lambda: nc.vector.tensor_copy(out=smp_kwT[:, :], in_=kwT[:, T:TT]), r=[b_kvT[16]], w=[b_smp])
    k.barrier()
    stB1.close()
    W1 = {kv: sbB(f"W1{kv}", [128, 32, 128], BF16) for kv in "kv"}; b_W1 = {kv: Buf() for kv in "kv"}
    W2 = {kv: sbB(f"W2{kv}", [128, 128], BF16) for kv in "kv"}; b_W2 = {kv: Buf() for kv in "kv"}
    posT = {kv: sbB(f"posT{kv}", [128, 32], BF16) for kv in "kv"}; b_posT = {kv: Buf() for kv in "kv"}
    cbias = {kv: sbB(f"cbias{kv}", [128, 1], F32) for kv in "kv"}; b_cb = {kv: Buf() for kv in "kv"}
    hidT = {kv: sbB(f"hidT{kv}", [128, 128], BF16) for kv in "kv"}; b_hid = {kv: Buf() for kv in "kv"}
    cw = [sbB(f"cwk{i}", [128, 128], F32) for i in range(3)]; b_cw = [Buf() for _ in range(3)]
    krows = sbB("krows_sb", [4, T], BF16); b_krows = Buf()
    k.dma(k.q_pool, krows[:, :], krows_d[:, :], w=[b_krows])
    for g in range(0 if FAST else 2):
        for (KA, bK, srcT) in ((KAs, b_KAs, ksT), (KAw, b_KAw, kwT)):
            k.v("pool", lambda: nc.gpsimd.memset(KA[g][64:128, :], 0.0), w=[bK[g]])
            k.v("dve", lambda: nc.vector.tensor_copy(out=KA[g][0:64, :], in_=srcT[g * 64:(g + 1) * 64, 0:T]), r=b_kvT, w=[bK[g]])
            k.v("dve", lambda: nc.vector.tensor_copy(out=KA[g][64:68, :], in_=krows[0:4, :]), r=[b_krows], w=[bK[g]])
        k.dma(k.q_pool, KAs[g][96:128, :], erows_d[:, :], w=[b_KAs[g]])
        k.v("pool", lambda: nc.gpsimd.memset(kccA[g][:, :], 0.0), w=[b_kccA[g]])
        k.dma(k.q_pool, kccA[g][64:68, :], kcrows_d[:, :], w=[b_kccA[g]])
        k.v("pool", lambda: nc.gpsimd.memset(vccA[g][:, :], 1.0), w=[b_vccA[g]])
    with nc.allow_non_contiguous_dma(reason="small weight relayout"):
        for kv in "kv":
            k.v("pool", lambda: nc.gpsimd.memset(W1[kv][:, :, :], 0.0), w=[b_W1[kv]])
            k.v("pool", lambda: nc.gpsimd.memset(W2[kv][:, :], 0.0), w=[b_W2[kv]])
            w1v = cw1[kv].rearrange("(s d) h -> d s h", d=64)
            for g in range(2):
                k.dma(k.q_pool, W1[kv][g * 64:(g + 1) * 64, :, g * 64:(g + 1) * 64], w1v, w=[b_W1[kv]])
                k.dma(k.q_pool, W2[kv][g * 64:(g + 1) * 64, g * 64:(g + 1) * 64], cw2[kv][:, :], w=[b_W2[kv]])
                k.dma(k.q_pool, posT[kv][g * 64:(g + 1) * 64, :], cpos[kv].rearrange("s d -> d s"), w=[b_posT[kv]])
    for kv, srcT in ([] if FAST else (("k", kcT), ("v", vcT))):
        for s_ in range(32):
            k.mm(ps[6][:, 0:1], W1[kv][:, s_, :], posT[kv][:, s_:s_ + 1], start=(s_ == 0), stop=(s_ == 31),
                 r=[b_W1[kv], b_posT[kv]], w=[b_ps[6]])
        k.v("dve", lambda: nc.vector.tensor_copy(out=cbias[kv][:, :], in_=ps[6][:, 0:1]), r=[b_ps[6]], w=[b_cb[kv]])
        sv = srcT[:, 0:T].rearrange("p (c s) -> p c s", s=16)
        for half in range(2):
            pb = 4 + half
            for s_ in range(16):
                k.mm(ps[pb][:, 0:128], W1[kv][:, half * 16 + s_, :], sv[:, :, s_], start=(s_ == 0), stop=(s_ == 15),
                     r=[b_W1[kv]] + b_kvT, w=[b_ps[pb]])
        k.v("dve", lambda: nc.vector.tensor_copy(out=cw[0][:, 0:127], in_=ps[5][:, 1:128]), r=[b_ps[5]], w=[b_cw[0]])
        k.v("dve", lambda: nc.vector.scalar_tensor_tensor(out=cw[0][:, 0:127], in0=ps[4][:, 0:127], scalar=cbias[kv][:, 0:1],
                                                           in1=cw[0][:, 0:127], op0=ALU.add, op1=ALU.add),
            r=[b_ps[4], b_cb[kv], b_cw[0]], w=[b_cw[0]])
        k.v("dve", lambda: nc.vector.tensor_tensor(out=cw[1][:, 0:127], in0=cw[0][:, 0:127], in1=cw[0][:, 0:127], op=ALU.mult), r=[b_cw[0]], w=[b_cw[1]])
        k.v("dve", lambda: nc.vector.tensor_scalar(out=cw[1][:, 0:127], in0=cw[1][:, 0:127], scalar1=0.044715, scalar2=1.0, op0=ALU.mult, op1=ALU.add), r=[b_cw[1]], w=[b_cw[1]])
        k.v("dve", lambda: nc.vector.tensor_tensor(out=cw[1][:, 0:127], in0=cw[1][:, 0:127], in1=cw[0][:, 0:127], op=ALU.mult), r=[b_cw[0], b_cw[1]], w=[b_cw[1]])
        k.actf(cw[2][:, 0:127], cw[1][:, 0:127], AF.Sigmoid, r=[b_cw[1]], w=[b_cw[2]], scale=1.5957691216)
        k.v("dve", lambda: nc.vector.tensor_tensor(out=hidT[kv][:, 0:127], in0=cw[2][:, 0:127], in1=cw[0][:, 0:127], op=ALU.mult), r=[b_cw[0], b_cw[2]], w=[b_hid[kv]])
        if kv == "k":
            k.mm(ps[6][:, 0:127], W2["k"][:, :], hidT["k"][:, 0:127], start=True, stop=True, r=[b_W2["k"], b_hid["k"]], w=[b_ps[6]])
            for g in range(2):
                k.v("dve", lambda: nc.vector.tensor_copy(out=kccA[g][0:64, 0:127], in_=ps[6][g * 64:(g + 1) * 64, 0:127]), r=[b_ps[6]], w=[b_kccA[g]])
        else:
            k.mm(ps[6][0:127, 0:128], hidT["v"][:, 0:127], W2["v"][:, :], start=True, stop=True, r=[b_W2["v"], b_hid["v"]], w=[b_ps[6]])
            for g in range(2):
                k.v("dve", lambda: nc.vector.tensor_copy(out=vccA[g][0:127, 0:64], in_=ps[6][0:127, g * 64:(g + 1) * 64]), r=[b_ps[6]], w=[b_vccA[g]])
    k.barrier()
    stB.close()
    def attn_chunk(st, sched, Kt, Qr, Vt, scale, finish_fn):
        nS = len(sched)
        acc_i = st["acc_i"]; st["acc_i"] = 1 - acc_i
        acc = ps[2 + acc_i]; b_acc = b_ps[2 + acc_i]
        sb_ = st["s_banks"]
        pend = []
        nPT = len(st["PT"])
        for i, (kt, c_lo, c_hi, mask, nk) in enumerate(sched):
            si = st["s_i"]; st["s_i"] = (si + 1) % len(sb_)
            psS = ps[sb_[si]]; b_S = b_ps[sb_[si]]
            lhsT, rl = Kt(kt)
            rhs, rr = Qr(c_lo, c_hi)
            k.mm(psS[0:nk, c_lo:c_hi], lhsT, rhs, start=True, stop=(mask is None), r=rl + rr, w=[b_S])
            if mask is not None:
                m_ap, m_lo, rm = mask
                k.mm(psS[0:nk, m_lo:m_lo + 128], identb[0:nk, 0:nk], m_ap, start=False, stop=True,
                     r=[b_identb] + rm, w=[b_S])
            pi = st["p_i"]; st["p_i"] = (pi + 1) % nPT
            PT = st["PT"][pi]; b_PT = st["b_PT"][pi]
            k.actf(PT[0:nk, c_lo:c_hi], psS[0:nk, c_lo:c_hi], AF.Exp, r=[b_S], w=[b_PT], scale=scale)

            def mk(kt=kt, c_lo=c_lo, c_hi=c_hi, PT=PT, b_PT=b_PT, i=i, nk=nk):
                v_ap, rv = Vt(kt)
                k.mm(acc[:, c_lo:c_hi], v_ap, PT[0:nk, c_lo:c_hi], start=(i == 0), stop=(i == nS - 1),
                     r=[b_PT] + rv, w=[b_acc])
            pend.append(mk)
            if len(pend) > 2:
                pend.pop(0)()
        for f in pend:
            f()
        finish_fn(acc, b_acc)

    stM = contextlib.ExitStack()

    def sbM(name, shape, dt):
        return stM.enter_context(nc.sbuf_tensor(name, shape, dt))
    w_uk_b = sbM("w_uk_b", [128, 2, 512], BF16); b_wuk = Buf()
    w_uv_b = sbM("w_uv_b", [128, 2, 512], BF16); b_wuv = Buf()
    KhT = [sbM(f"KhT{i}", [96, T], BF16) for i in range(2)]; b_KhT = [Buf(), Buf()]
    Vh = [sbM(f"Vh{i}", [128, 16, 128], BF16) for i in range(2)]; b_Vh = [Buf(), Buf()]
    PTs = [sbM(f"PT{i}", [128, 512], BF16) for i in range(4)]; b_PTs = [Buf() for _ in range(4)]
    rz = [sbM(f"rz{i}", [64, 512], F32) for i in range(2)]; b_rz = [Buf(), Buf()]
    ast = {"acc_i": 0, "s_i": 0, "p_i": 0, "PT": PTs, "b_PT": b_PTs, "rz_i": 0, "s_banks": [0, 1, 6]}
    for c in range(2):
        k.dma(k.q_pool, w_uk_b[:, c, :], w_uk[c * 128:(c + 1) * 128, :], w=[b_wuk])
        k.dma(k.q_pool, w_uv_b[:, c, :], w_uv[c * 128:(c + 1) * 128, :], w=[b_wuv])
    for i in range(2):
        k.v("pool", lambda: nc.gpsimd.memset(Vh[i][:, :, 64:128], 1.0), w=[b_Vh[i]])
        if not FAST:
            k.actf(KhT[i][64:96, :], kropeT[0:32, 0:T], AF.Copy, r=b_krT, w=[b_KhT[i]])

    def norm_out(dstT, h, qc):
        def fin(acc, b_acc):
            ri = ast["rz_i"]; ast["rz_i"] = 1 - ri
            k.v("dve", lambda: nc.vector.reciprocal(out=rz[ri][0:64, :], in_=acc[64:128, :]), r=[b_acc], w=[b_rz[ri]])
            p0 = (h % 2) * 64
            k.v("dve", lambda: nc.vector.tensor_tensor(out=dstT[p0:p0 + 64, h // 2, qc * 512:(qc + 1) * 512],
                                                        in0=acc[0:64, :], in1=rz[ri][0:64, :], op=ALU.mult),
                r=[b_acc, b_rz[ri]], w=[b_omla])
        return fin

    def causal_sched(qc):
        out = []
        for kt in range(4 * qc + 4):
            c_lo = max(0, kt * 128 - qc * 512)
            mask = (trimask[:, 0, :], c_lo, [b_tri]) if kt * 128 >= qc * 512 else None
            out.append((kt, c_lo, 512, mask, 128))
        return out

    NH_M = 0 if FAST else int(os.environ.get("KNH", "8"))
    def mla_jit(h):
        s = h % 2
        for qc in range(4):
            pb = 4 + (qc % 2)
            for c in range(2):
                k.mm(ps[pb][0:64, :], w_uk_b[:, c, h * 64:(h + 1) * 64], ckvT[:, c, qc * 512:(qc + 1) * 512],
                     start=(c == 0), stop=(c == 1), r=[b_wuk] + b_ckvT, w=[b_ps[pb]])
            k.v("dve", lambda: nc.vector.tensor_copy(out=KhT[s][0:64, qc * 512:(qc + 1) * 512], in_=ps[pb][0:64, :]),
                r=[b_ps[pb]], w=[b_KhT[s]])
        for g8 in range(2):
            pb = 4 + g8
            for j in range(8):
                kt = g8 * 8 + j
                for c in range(2):
                    k.mm(ps[pb][:, j * 64:(j + 1) * 64], ckvT[:, c, kt * 128:(kt + 1) * 128], w_uv_b[:, c, h * 64:(h + 1) * 64],
                         start=(c == 0), stop=(c == 1), r=[b_wuv] + b_ckvT, w=[b_ps[pb]], sig=(c == 1 and j == 7))
            k.v("dve", lambda: nc.vector.tensor_copy(out=Vh[s][:, g8 * 8:(g8 + 1) * 8, 0:64],
                                                      in_=ps[pb][:, :].rearrange("p (j d) -> p j d", j=8)),
                r=[b_ps[pb]], w=[b_Vh[s]])

    if NH_M > 0:
        mla_jit(0)
    for h in range(NH_M):
        s = h % 2
        for qc in range(4):
            if qc == 2 and h + 1 < NH_M:
                mla_jit(h + 1)
            attn_chunk(ast, causal_sched(qc),
                       Kt=lambda kt: (KhT[s][0:96, kt * 128:(kt + 1) * 128], [b_KhT[s]]),
                       Qr=lambda a, b_: (qmT[0:96, h, qc * 512 + a:qc * 512 + b_], b_qmT),
                       Vt=lambda kt: (Vh[s][:, kt, :], [b_Vh[s]]),
                       scale=MLA_SCALE, finish_fn=norm_out(o_mlaT, h, qc))
    if DBG:
        k.dma(k.q_pool, dbg_omla[:, :, :], o_mlaT[:, :, :], r=[b_omla])
    k.barrier()
    stM.close()
    stL2.close()

    stN = contextlib.ExitStack()

    def sbN(name, shape, dt):
        return stN.enter_context(nc.sbuf_tensor(name, shape, dt))
    QA = [sbN(f"QA{h}", [128, TT], BF16) for h in range(8)]; b_QA = [Buf() for _ in range(8)]
    w_qn = sbN("w_qn", [128, 8, 536], BF16); b_wqn = Buf()
    stgN_ = [sbN("stgN0", [128, 536], F32)] * 2; b_stgN_ = [Buf()] * 2
    gnT = sbN("gnT", [24, TT], BF16); b_gnT = Buf()
    selb = sbN("selb_sb", [24, 3, 8, 64], BF16); b_selb = Buf()
    qrows = sbN("qrows_sb", [4, T], BF16); b_qrows = Buf()
    cmpmask = sbN("cmpmask_sb", [128, T], BF16); b_cmask = Buf()
    overlapM = sbN("overlap_sb", [128, 32], BF16); b_ovl = Buf()
    topkm = sbN("topkm_sb", [128, 16, 32], F32); topka = sbN("topka_sb", [128, 16, 32], F32); b_topk = Buf()
    scoreT = [sbN(f"scoreT{g}", [32, T], F32) for g in range(2)]; b_sc = [Buf(), Buf()]
    PTn = [sbN(f"PTn{i}", [128, 512], BF16) for i in range(4)]; b_PTn = [Buf() for _ in range(4)]
    rzn = [sbN(f"rzn{i}", [64, 512], F32) for i in range(2)]; b_rzn = [Buf(), Buf()]
    tmpn = [sbN(f"tmpn{i}", [128, 512], F32) for i in range(2)]; b_tmpn = [Buf(), Buf()]
    tk = sbN("tk", [128, 6, 64], F32); b_tk = Buf()
    tmpU = sbN("tmpU", [32, 512], F32); b_tmpU = Buf()
    tkb = sbN("tkb", [128, 64], BF16); b_tkb = Buf()
    nst = {"acc_i": 0, "s_i": 0, "p_i": 0, "PT": PTn, "b_PT": b_PTn, "rz_i": 0, "s_banks": [0, 1, 4]}

    k.dma(k.q_pool, selb[:, :, :, :], selb_d[:, :, :, :], w=[b_selb])
    k.dma(k.q_pool, qrows[:, :], qrows_d[:, :], w=[b_qrows])
    k.dma(k.q_pool, cmpmask[:, :], cmpmask_d[:, :], w=[b_cmask])
    k.dma(k.q_pool, overlapM[:, :], overlap_d[:, :], w=[b_ovl])
    k.dma(k.q_sp, topkm[:, :, :], topkm_d[:, :, :], w=[b_topk])
    k.dma(k.q_sp, topka[:, :, :], topka_d[:, :, :], w=[b_topk])
    for h in range(8):
        k.v("pool", lambda: nc.gpsimd.memset(QA[h][64:128, :], 0.0), w=[b_QA[h]])
        k.actf(QA[h][64:68, 0:T], qrows[0:4, :], AF.Copy, r=[b_qrows], w=[b_QA[h]], scale=float(SLOPES[h]))
    for c in range(8):
        stgN = stgN_[c % 2]; b_stgN = b_stgN_[c % 2]
        k.dma(k.q_sp, stgN[:, 0:512], w_in[c * 128:(c + 1) * 128, C_QN:C_QN + 512], w=[b_stgN])
        k.dma(k.q_sp, stgN[:, 512:536], w_in[c * 128:(c + 1) * 128, C_GN:C_GN + 24], w=[b_stgN])
        k.actf(w_qn[:, c, :], stgN[:, :], AF.Copy, r=[b_stgN, b_g1], w=[b_wqn], scale=g1col[:, c:c + 1])
    chunks = [(0, 512), (512, 512), (1024, 512), (1536, 512), (2048, TS)]
    if FAST:
        chunks = [(2048, TS)]
    for (c0, cn) in chunks:
        for hp in range(4):
            pb = 4 + (hp % 2)
            for c in range(8):
                k.mm(ps[pb][:, 0:cn], w_qn[:, c, hp * 128:(hp + 1) * 128], hT[:, c, c0:c0 + cn], start=(c == 0), stop=(c == 7),
                     r=[b_wqn] + b_hT, w=[b_ps[pb]])
            k.v("dve", lambda: nc.vector.tensor_copy(out=QA[2 * hp][0:64, c0:c0 + cn], in_=ps[pb][0:64, 0:cn]), r=[b_ps[pb]], w=[b_QA[2 * hp]])
            k.v("dve", lambda: nc.vector.tensor_copy(out=QA[2 * hp + 1][0:64, c0:c0 + cn], in_=ps[pb][64:128, 0:cn]), r=[b_ps[pb]], w=[b_QA[2 * hp + 1]])
        if True:
            for c in range(8):
                k.mm(ps[6][0:24, 0:cn], w_qn[:, c, 512:536], hT[:, c, c0:c0 + cn], start=(c == 0), stop=(c == 7),
                     r=[b_wqn] + b_hT, w=[b_ps[6]])
            k.actf(gnT[0:24, c0:c0 + cn], ps[6][0:24, 0:cn], AF.Sigmoid, r=[b_ps[6]], w=[b_gnT])

    def nsa_finish(br, h, qc, extra=None):
        def fin(acc, b_acc):
            ri = nst["rz_i"]; nst["rz_i"] = 1 - ri
            p0 = (h % 2) * 64
            tmp = tmpn[ri][p0:p0 + 64, :]
            k.v("dve", lambda: nc.vector.tensor_scalar(out=rzn[ri][0:64, :], in0=acc[64:128, :], scalar1=1e-30, scalar2=None, op0=ALU.max), r=[b_acc], w=[b_rzn[ri]])
            k.v("dve", lambda: nc.vector.reciprocal(out=rzn[ri][0:64, :], in_=rzn[ri][0:64, :]), r=[b_rzn[ri]], w=[b_rzn[ri]])
            k.v("dve", lambda: nc.vector.tensor_tensor(out=tmp, in0=acc[0:64, :], in1=rzn[ri][0:64, :], op=ALU.mult),
                r=[b_acc, b_rzn[ri]], w=[b_tmpn[ri]])
            if extra is not None:
                extra(rzn[ri], b_rzn[ri])
            k.mm(ps[6][0:64, :], selb[0:24, br, h, :], gnT[0:24, qc * 512:(qc + 1) * 512], start=True, stop=True,
                 r=[b_selb, b_gnT], w=[b_ps[6]])
            dst = o_nsaT[p0:p0 + 64, h // 2, qc * 512:(qc + 1) * 512]
            if br == 0:
                k.v("dve", lambda: nc.vector.tensor_tensor(out=dst, in0=tmp, in1=ps[6][0:64, :], op=ALU.mult),
                    r=[b_tmpn[ri], b_ps[6]], w=[b_onsa])
            else:
                k.v("dve", lambda: nc.vector.tensor_tensor(out=tmp, in0=tmp, in1=ps[6][0:64, :], op=ALU.mult),
                    r=[b_tmpn[ri], b_ps[6]], w=[b_tmpn[ri]])
                k.v("pool", lambda: nc.gpsimd.tensor_tensor(out=dst, in0=dst, in1=tmp, op=ALU.add),
                    r=[b_tmpn[ri], b_onsa], w=[b_onsa])
        return fin

    BR = "" if FAST else os.environ.get("KBR", "csw")
    for h in range(0 if FAST else 8):
        g = h // 4
        for qc in range(4):
            cs_ = slice(qc * 512, (qc + 1) * 512)
            si = nst["s_i"] % 2; nst["s_i"] = (nst["s_i"] + 1) % 2
            psS = ps[si]; b_S = b_ps[si]
            k.mm(psS[0:127, :], kccA[g][0:68, 0:127], QA[h][0:68, cs_], start=True, stop=False, r=[b_kccA[g], b_QA[h]], w=[b_S])
            k.mm(psS[0:127, :], identb[0:127, 0:127], cmpmask[0:127, cs_], start=False, stop=True, r=[b_identb, b_cmask], w=[b_S])
            pi = nst["p_i"]; nst["p_i"] = (pi + 1) % 4
            PT = PTn[pi]; b_PT = b_PTn[pi]
            k.actf(PT[0:127, :], psS[0:127, :], AF.Exp, r=[b_S], w=[b_PT], scale=NSA_SCALE)
            ai = nst["acc_i"]; nst["acc_i"] = 1 - ai
            acc = ps[2 + ai]; b_acc = b_ps[2 + ai]
            k.mm(acc[:, :], vccA[g][0:127, :], PT[0:127, :], start=True, stop=True, r=[b_vccA[g], b_PT], w=[b_acc])
            k.mm(ps[5][0:32, :], overlapM[0:127, 0:32], PT[0:127, :], start=True, stop=True, r=[b_ovl, b_PT], w=[b_ps[5]])

            def extra(rz_, b_rz_, h=h, g=g, cs_=cs_):
                if h % 4 == 0:
                    k.v("dve", lambda: nc.vector.tensor_tensor(out=scoreT[g][0:32, cs_], in0=ps[5][0:32, :], in1=rz_[0:32, :], op=ALU.mult),
                        r=[b_ps[5], b_rz_], w=[b_sc[g]])
                else:
                    k.v("dve", lambda: nc.vector.tensor_tensor(out=tmpU[0:32, :], in0=ps[5][0:32, :], in1=rz_[0:32, :], op=ALU.mult),
                        r=[b_ps[5], b_rz_], w=[b_tmpU])
                    k.v("dve", lambda: nc.vector.tensor_tensor(out=scoreT[g][0:32, cs_], in0=scoreT[g][0:32, cs_], in1=tmpU[0:32, :], op=ALU.add),
                        r=[b_tmpU, b_sc[g]], w=[b_sc[g]])
            nsa_finish(0, h, qc, extra)(acc, b_acc)

    psb7 = ps[7].bitcast(BF16)
    for qt in range(0 if FAST else 16):
        ts_ = slice(qt * 128, (qt + 1) * 128)
        for g in range(2):
            k.tr(ps[6][:, g * 32:(g + 1) * 32], scoreT[g][0:32, ts_], identf[0:32, 0:32], r=[b_sc[g], b_identf], w=[b_ps[6]], sig=(g == 1))
        smv = tk[:, 0, :].rearrange("p (g j) -> p g j", g=2)
        k.v("dve", lambda: nc.vector.tensor_tensor(out=smv, in0=ps[6][:, 0:64].rearrange("p (g j) -> p g j", g=2),
                                                    in1=topkm[:, qt, :].unsqueeze(1).to_broadcast([128, 2, 32]), op=ALU.mult),
            r=[b_ps[6], b_topk], w=[b_tk])
        k.v("dve", lambda: nc.vector.tensor_tensor(out=smv, in0=smv, in1=topka[:, qt, :].unsqueeze(1).to_broadcast([128, 2, 32]), op=ALU.add),
            r=[b_tk, b_topk], w=[b_tk])
        for g in range(2):
            sm = tk[:, 0, g * 32:(g + 1) * 32]
            k.v("dve", lambda: nc.vector.max(out=tk[:, 1, g * 8:g * 8 + 8], in_=sm), r=[b_tk], w=[b_tk])
            k.v("dve", lambda: nc.vector.match_replace(out=tk[:, 2, g * 32:(g + 1) * 32], in_to_replace=tk[:, 1, g * 8:g * 8 + 8], in_values=sm, imm_value=-3.0e38), r=[b_tk], w=[b_tk])
            k.v("dve", lambda: nc.vector.max(out=tk[:, 3, g * 8:g * 8 + 8], in_=tk[:, 2, g * 32:(g + 1) * 32]), r=[b_tk], w=[b_tk])
            k.v("dve", lambda: nc.vector.tensor_scalar(out=tkb[:, g * 32:(g + 1) * 32], in0=sm, scalar1=tk[:, 3, g * 8 + 7:g * 8 + 8], scalar2=-1.0,
                                                        op0=ALU.is_ge, op1=ALU.add), r=[b_tk], w=[b_tkb])
        k.tr(psb7[0:64, 0:128], tkb[:, 0:64], identb[:, :], r=[b_tkb, b_identb], w=[b_ps[7]])
        for h in range(8):
            g = h // 4
            k.v("dve", lambda: nc.vector.tensor_copy(out=QA[h][96:128, ts_], in_=psb7[g * 32:(g + 1) * 32, 0:128]), r=[b_ps[7]], w=[b_QA[h]])

    if "s" in BR:
        for h in range(8):
            g = h // 4
            for qc in range(4):
                attn_chunk(nst, causal_sched(qc),
                           Kt=lambda kt: (KAs[g][:, kt * 128:(kt + 1) * 128], [b_KAs[g]]),
                           Qr=lambda a, b_: (QA[h][:, qc * 512 + a:qc * 512 + b_], [b_QA[h]]),
                           Vt=lambda kt: (vs_nat[:, kt, g, :], b_vnat),
                           scale=NSA_SCALE, finish_fn=nsa_finish(1, h, qc))

    def win_sched(qc):
        ents = []
        for kt in range(max(0, 4 * qc - 4), 4 * qc + 4):
            dl = kt * 128 - qc * 512
            if dl >= 0:
                ents.append((kt, dl, 512, (trimask[:, 0, :], dl, [b_tri]), 128))
            else:
                c_hi = dl + 640
                ents.append((kt, 0, c_hi, (trimask[:, 1, :], c_hi - 128, [b_tri]), 128))
        ents.sort(key=lambda e: -(e[2] - e[1]))
        return ents

    if "w" in BR:
        for h in range(8):
            g = h // 4
            for qc in range(4):
                attn_chunk(nst, win_sched(qc),
                           Kt=lambda kt: (KAw[g][0:68, kt * 128:(kt + 1) * 128], [b_KAw[g]]),
                           Qr=lambda a, b_: (QA[h][0:68, qc * 512 + a:qc * 512 + b_], [b_QA[h]]),
                           Vt=lambda kt: (vw_nat[:, kt, g, :], b_vnat),
                           scale=NSA_SCALE, finish_fn=nsa_finish(2, h, qc))
    if DBG:
        k.dma(k.q_pool, dbg_onsa[:, :, :], o_nsaT[:, :, :], r=[b_onsa])
    for h in range(8):
        k.v("dve", lambda: nc.vector.tensor_copy(out=smp_qn[0:64, h, :], in_=QA[h][0:64, T:TT]), r=[b_QA[h]], w=[b_smp])
    k.v("dve", lambda: nc.vector.tensor_copy(out=smp_gn[:, :], in_=gnT[0:24, T:TT]), r=[b_gnT], w=[b_smp])
    k.barrier()
    stN.close()

    stL1.close()
    stS = contextlib.ExitStack(); sbS = scoped(stS)
    s_en = sbS("s_en_sb", [16, 4, 32], F32); s_nm = sbS("s_nm_sb", [16, 4, 32], F32); s_ebw = sbS("s_ebw_sb", [128, 4, 32], F32)
    s_lb = sbS("s_lb_sb", [3, 128], BF16); s_rbs = sbS("s_rbs_sb", [3, 128, 32], BF16); s_rbc = sbS("s_rbc_sb", [3, 8, 32], BF16)
    s_ov = sbS("s_ov_sb", [128, 8, 264], BF16); s_tkm = sbS("s_tkm_sb", [4, 264], F32); s_tka = sbS("s_tka_sb", [4, 264], F32)
    s_ssum = sbS("s_ssum_sb", [16, 4], F32); selb2 = sbS("selb2", [24, 3, 8, 64], BF16)
    b_sc_ = Buf()
    for (dst, srcd, q) in ((s_en, s_en_d, k.q_sp), (s_nm, s_nm_d, k.q_sp), (s_ebw, s_ebw_d, k.q_sp), (s_lb, s_lb_d, k.q_pool),
                           (s_rbs, s_rbs_d, k.q_pool), (s_rbc, s_rbc_d, k.q_pool), (s_ov, s_ov_d, k.q_pool),
                           (s_tkm, s_tkm_d, k.q_sp), (s_tka, s_tka_d, k.q_sp), (s_ssum, s_ssum_d, k.q_sp), (selb2, selb_d, k.q_pool)):
        k.dma(q, dst.ap() if hasattr(dst, "ap") else dst, srcd, w=[b_sc_])
    ptT = sbS("ptT", [128, 4], I32); b_pt = Buf()
    idx16 = sbS("idx16", [128, 4, 16], I32); idx8 = sbS("idx8", [128, 4, 8], I32); b_idx = Buf()
    with nc.allow_non_contiguous_dma(reason="page table transpose (512 ints)"):
        k.dma(k.q_sp, ptT[:, :], pt_d.rearrange("b p -> p b"), w=[b_pt])
    for s_ in range(16):
        k.v("dve", lambda: nc.vector.tensor_scalar(out=idx16[:, :, s_], in0=ptT[:, :], scalar1=16, scalar2=s_, op0=ALU.mult, op1=ALU.add), r=[b_pt], w=[b_idx])
    for s_ in range(8):
        k.v("dve", lambda: nc.vector.tensor_scalar(out=idx8[:, :, s_], in0=ptT[:, :], scalar1=8, scalar2=s_, op0=ALU.mult, op1=ALU.add), r=[b_pt], w=[b_idx])
    hx0_ = sbS("hx0", [128, 8, 128], F32)
    w_ukf = hx0_[:, :, :].rearrange("p (a c) n -> p a (c n)", a=2); b_wukf = Buf()
    w_ukT = sbS("w_ukT", [64, 8, 256], BF16); b_wukT = Buf()
    w_uv_s = sbS("w_uv_s", [128, 2, 512], BF16); b_wuvs = Buf()
    QsA = sbS("QsA", [128, 2, 4, 32], BF16); QsR = sbS("QsR", [32, 4, 32], BF16); Qblk = sbS("Qblk", [128, 4, 32], BF16); b_Q = Buf()
    for c in range(2):
        k.dma(k.q_sp, w_ukf[:, c, :], w_uk[c * 128:(c + 1) * 128, :], w=[b_wukf])
        k.dma(k.q_pool, w_uv_s[:, c, :], w_uv[c * 128:(c + 1) * 128, :], w=[b_wuvs])
    for h in range(8):
        for c in range(2):
            k.tr(ps[6][0:64, (h % 2) * 256 + c * 128:(h % 2) * 256 + (c + 1) * 128], w_ukf[:, c, h * 64:(h + 1) * 64], identf[:, :],
                 r=[b_wukf, b_identf], w=[b_ps[6]], sig=(c == 1))
        k.v("dve", lambda: nc.vector.tensor_copy(out=w_ukT[0:64, h, :], in_=ps[6][0:64, (h % 2) * 256:(h % 2) * 256 + 256]), r=[b_ps[6]], w=[b_wukT])
    for h in range(8):
        for c in range(2):
            k.mm(ps[6][:, 0:16], w_ukT[0:64, h, c * 128:(c + 1) * 128], smp_qm[0:64, h, :], start=True, stop=True, r=[b_wukT, b_smp], w=[b_ps[6]])
            k.v("dve", lambda: nc.vector.tensor_copy(out=QsA[:, c, :, h * 4:(h + 1) * 4], in_=ps[6][:, 0:16].rearrange("p (b t) -> p b t", b=4)), r=[b_ps[6]], w=[b_Q])
        k.v("dve", lambda: nc.vector.tensor_copy(out=QsR[0:32, :, h * 4:(h + 1) * 4], in_=smp_qm[64:96, h, :].rearrange("p (b t) -> p b t", b=4)), r=[b_smp], w=[b_Q])
    k.v("pool", lambda: nc.gpsimd.memset(Qblk[:, :, :], 0.0), w=[b_Q])
    for h in range(8):
        g = h // 4
        k.v("dve", lambda: nc.vector.tensor_copy(out=Qblk[g * 64:(g + 1) * 64, :, h * 4:(h + 1) * 4], in_=smp_qn[0:64, h, :].rearrange("p (b t) -> p b t", b=4)), r=[b_smp], w=[b_Q])
    W1s = {kv: sbS(f"W1s{kv}", [128, 32, 128], BF16) for kv in "kv"}; W2s = {kv: sbS(f"W2s{kv}", [128, 128], BF16) for kv in "kv"}
    posTs = {kv: sbS(f"posTs{kv}", [128, 32], BF16) for kv in "kv"}; cbs = {kv: sbS(f"cbs{kv}", [128, 1], F32) for kv in "kv"}; b_Ws = Buf()
    with nc.allow_non_contiguous_dma(reason="small weight relayout"):
        for kv in "kv":
            k.v("pool", lambda: nc.gpsimd.memset(W1s[kv][:, :, :], 0.0), w=[b_Ws])
            k.v("pool", lambda: nc.gpsimd.memset(W2s[kv][:, :], 0.0), w=[b_Ws])
            w1v = cw1[kv].rearrange("(s d) h -> d s h", d=64)
            for g in range(2):
                k.dma(k.q_pool, W1s[kv][g * 64:(g + 1) * 64, :, g * 64:(g + 1) * 64], w1v, w=[b_Ws])
                k.dma(k.q_pool, W2s[kv][g * 64:(g + 1) * 64, g * 64:(g + 1) * 64], cw2[kv][:, :], w=[b_Ws])
                k.dma(k.q_pool, posTs[kv][g * 64:(g + 1) * 64, :], cpos[kv].rearrange("s d -> d s"), w=[b_Ws])
    for kv in "kv":
        for s_ in range(32):
            k.mm(ps[6][:, 0:1], W1s[kv][:, s_, :], posTs[kv][:, s_:s_ + 1], start=(s_ == 0), stop=(s_ == 31), r=[b_Ws], w=[b_ps[6]])
        k.v("dve", lambda: nc.vector.tensor_copy(out=cbs[kv][:, :], in_=ps[6][:, 0:1]), r=[b_ps[6]], w=[b_Ws])
    Xc2 = [sbS(f"Xc{i}", [128, 8 * 256], BF16) for i in range(3)]; Xr2 = [sbS(f"Xr{i}", [128, 8 * 32], BF16) for i in range(3)]
    Xc = [t_[:, :].rearrange("p (r f) -> p r f", r=8) for t_ in Xc2]; Xr = [t_[:, :].rearrange("p (r f) -> p r f", r=8) for t_ in Xr2]
    ones_c = sbS("ones_c", [128, 1], BF16)
    ones_f = sbS("ones_f", [128, 1], F32); Zacc = sbS("Zacc", [128, 32], F32); tmpz = sbS("tmpz", [128, 32], F32); b_Z = Buf(); b_X = [Buf(), Buf(), Buf()]
    KTc = [sbS(f"KTc{i}", [128, 2, 2, 128], BF16) for i in range(3)]; KTr = [sbS(f"KTr{i}", [32, 2, 128], BF16) for i in range(3)]; b_KT = [Buf(), Buf(), Buf()]
    PTq = [sbS(f"PTq{i}", [128, 256], BF16) for i in range(2)]; b_PTq = [Buf(), Buf()]
    PTf = [sbS(f"PTf{i}", [128, 256], F32) for i in range(2)]; b_PTf = [Buf(), Buf()]
    sm1 = sbS("sm1", [32, 8], F32); b_sm1 = Buf()
    olat = sbS("olat", [32, 256], BF16); b_olat = Buf()
    olT = sbS("olT", [128, 2, 32], BF16); b_olT = Buf()
    Xg2 = [sbS(f"Xg{i}", [128, 16 * 128], BF16) for i in range(3)]
    Xg = [t_[:, :].rearrange("p (r f) -> p r f", r=16) for t_ in Xg2]; b_Xg = [Buf(), Buf(), Buf()]
    Xk2 = [sbS(f"Xk{i}", [128, 16 * 128], BF16) for i in range(3)]
    Xk = [t_[:, :].rearrange("p (r f) -> p r f", r=16) for t_ in Xk2]; b_Xk = [Buf(), Buf(), Buf()]
    XT = [sbS(f"XT{i}", [128, 8, 128], BF16) for i in range(3)]; b_XT = [Buf(), Buf(), Buf()]
    ABf = sbS("ABf", [128, 2, 8, 128], F32); b_AB = Buf()
    hx = [hx0_, ABf[:, 1, :, :], ABf[:, 0, :, :]]; b_hx = [b_wukf, b_AB, b_AB]
    Hs = sbS("Hs", [128, 8, 128], BF16); b_Hs = Buf()
    kccS = sbS("kccS", [128, 8, 128], BF16); b_kccS = Buf()
    vccS = sbS("vccS", [128, 8, 2, 128], BF16); b_vccS = Buf()
    PTc = sbS("PTc", [128, 8, 32], BF16); b_PTc = Buf()
    og = sbS("og", [16, 64], F32); b_og = Buf()
    Un = sbS("Un", [16, 264], F32); b_Un = Buf()
    tks = sbS("tks", [4, 4, 264], F32); b_tks = Buf()
    msel = sbS("msel", [4, 2, 264], F32); b_msel = Buf()
    Mexp = sbS("Mexp", [128, 2, 32], F32); b_Mexp = Buf()
    tmpo = sbS("tmpo", [64, 16], F32); b_tmpo = Buf()
    tmpS = sbS("tmpS", [128, 16], F32); b_tmpS = Buf()
    nk_f = sbS("nk_f", [16, 32], F32); nk_b = sbS("nk_b", [16, 32], BF16); b_nk = Buf()
    Kw = sbS("Kw", [128, 4, 128], BF16); Vw = sbS("Vw", [128, 4, 129], BF16); KwTs = sbS("KwTs", [128, 4, 128], BF16); b_Kw = Buf()
    k.v("pool", lambda: nc.gpsimd.memset(ones_c[:, :], 1.0), w=[b_sc_])
    k.v("pool", lambda: nc.gpsimd.memset(ones_f[:, :], 1.0), w=[b_sc_])
    k.v("pool", lambda: nc.gpsimd.memset(Vw[:, :, 128:129], 1.0), w=[b_Kw])
    k.v("pool", lambda: nc.gpsimd.memset(vccS[:, :, :, :], 1.0), w=[b_vccS])
    ckv_v = c_ckv.rearrange("(q r) f -> q (r f)", r=8); kr_v = c_kr.rearrange("(q r) f -> q (r f)", r=8)
    ck_v = {"k": c_ck.rearrange("(q r) f -> q (r f)", r=16), "v": c_cv.rearrange("(q r) f -> q (r f)", r=16)}
    sk_v = c_sk.rearrange("(q r) f -> q (r f)", r=16); sv_v = c_sv.rearrange("(q r) f -> q (r f)", r=16)
    cnt = {"x": 0, "kt": 0, "pt": 0, "xg": 0, "xt": 0, "s": 0, "tb": 0}
    psbv = {i_: ps[i_].bitcast(BF16) for i_ in (4, 5, 6, 7)}

    def trbank(choices):
        i_ = choices[cnt["tb"] % len(choices)]; cnt["tb"] += 1
        return psbv[i_], b_ps[i_]

    def s_gate(br, g, b):
        k.tr(ps[7][0:64, 0:16], og[0:16, :], identf[0:16, 0:16], r=[b_og, b_identf], w=[b_ps[7]])
        k.v("dve", lambda: nc.vector.tensor_copy(out=tmpo[:, :], in_=ps[7][0:64, 0:16]), r=[b_ps[7]], w=[b_tmpo])
        for hl in range(4):
            h = 4 * g + hl
            k.mm(ps[6][0:64, hl * 4:(hl + 1) * 4], selb2[0:24, br, h, :], smp_gn[0:24, 4 * b:4 * b + 4], start=True, stop=True,
                 r=[b_sc_, b_smp], w=[b_ps[6]], sig=(hl == 3))
        for hl in range(4):
            h = 4 * g + hl
            p0 = (h % 2) * 64
            cs_ = slice(hl * 4, hl * 4 + 4)
            dst = o_nsaT[p0:p0 + 64, h // 2, T + 4 * b:T + 4 * b + 4]
            if br == 0:
                k.v("dve", lambda: nc.vector.tensor_tensor(out=dst, in0=tmpo[0:64, cs_], in1=ps[6][0:64, cs_], op=ALU.mult), r=[b_tmpo, b_ps[6]], w=[b_onsa])
            else:
                k.v("dve", lambda: nc.vector.tensor_tensor(out=tmpS[p0:p0 + 64, cs_], in0=tmpo[0:64, cs_], in1=ps[6][0:64, cs_], op=ALU.mult), r=[b_tmpo, b_ps[6]], w=[b_tmpS])
                k.v("dve", lambda: nc.vector.tensor_tensor(out=dst, in0=dst, in1=tmpS[p0:p0 + 64, cs_], op=ALU.add), r=[b_tmpS, b_onsa], w=[b_onsa])

    def s_norm(acc, zc, vc0, g_unused=None):
        k.v("dve", lambda: nc.vector.tensor_scalar(out=sm1[0:16, 0:1], in0=acc[0:16, zc:zc + 1], scalar1=1e-30, scalar2=None, op0=ALU.max), r=[], w=[b_sm1])
        k.v("dve", lambda: nc.vector.reciprocal(out=sm1[0:16, 0:1], in_=sm1[0:16, 0:1]), r=[b_sm1], w=[b_sm1])

    SB_ = os.environ.get("KSB", "mcsw")
    for b in range(int(os.environ.get('KNB', '4'))):
        if "m" in SB_:
            acc = ps[2]; b_acc = b_ps[2]
            k.v("pool", lambda: nc.gpsimd.memset(Zacc[:, :], 0.0), w=[b_Z])
            NJ = int(os.environ.get('KNJ', '16'))
            tl = []
            fst = {"v": True}

            def mG(j):
                xi = j % 3
                def f():
                    k.gather(Xc2[xi][:, :], ckv_v, idx16[:, b, j:j + 1], r=[b_idx], w=[b_X[xi]])
                    k.gather(Xr2[xi][:, :], kr_v, idx16[:, b, j:j + 1], r=[b_idx], w=[b_X[xi]])
                return f

            def mT(j, rp):
                xi = j % 3; ki = (4 * j + rp) % 3
                def f():
                    tpb, b_tpb = trbank((7, 4, 5))
                    for rr_ in range(2):
                        r_ = rp * 2 + rr_
                        for c in range(2):
                            k.tr(tpb[:, (rr_ * 3 + c) * 128:(rr_ * 3 + c + 1) * 128], Xc[xi][:, r_, c * 128:(c + 1) * 128], identb[:, :], r=[b_X[xi], b_identb], w=[b_tpb], sig=False)
                        k.tr(tpb[0:32, (rr_ * 3 + 2) * 128:(rr_ * 3 + 3) * 128], Xr[xi][:, r_, :], identb[:, :], r=[b_X[xi], b_identb], w=[b_tpb], sig=(rr_ == 1))
                    pv = tpb[:, 0:768].rearrange("p (a c n) -> p a c n", a=2, c=3)
                    k.v("dve", lambda: nc.vector.tensor_copy(out=KTc[ki][:, :, :, :], in_=pv[:, :, 0:2, :]), r=[b_tpb], w=[b_KT[ki]])
                    k.v("dve", lambda: nc.vector.tensor_copy(out=KTr[ki][0:32, :, :], in_=pv[0:32, :, 2, :]), r=[b_tpb], w=[b_KT[ki]])
                return f

            def mS(j, rp):
                ki = (4 * j + rp) % 3; psS = ps[j % 2]; b_S = b_ps[j % 2]
                def f():
                    for rr_ in range(2):
                        r_ = rp * 2 + rr_
                        oS = psS[:, r_ * 32:(r_ + 1) * 32]
                        k.mm(oS, KTc[ki][:, rr_, 0, :], QsA[:, 0, b, :], start=True, stop=False, r=[b_KT[ki], b_Q], w=[b_S])
                        k.mm(oS, KTc[ki][:, rr_, 1, :], QsA[:, 1, b, :], start=False, stop=False, r=[b_KT[ki], b_Q], w=[b_S])
                        k.mm(oS, KTr[ki][0:32, rr_, :], QsR[0:32, b, :], start=False, stop=True, r=[b_KT[ki], b_Q], w=[b_S], sig=(rp == 3 and rr_ == 1))
                return f

            def mE(j):
                def f():
                    k.actf(PTq[j % 2][:, :], ps[j % 2][:, 0:256], AF.Exp, r=[b_ps[j % 2]], w=[b_PTq[j % 2]], scale=MLA_SCALE)
                    k.v("dve", lambda: nc.vector.tensor_reduce(out=tmpz[:, :], in_=PTq[j % 2][:, :].rearrange("p (s n) -> p n s", s=8), axis=AX.X, op=ALU.add), r=[b_PTq[j % 2]], w=[b_Z])
                    k.v("dve", lambda: nc.vector.tensor_tensor(out=Zacc[:, :], in0=Zacc[:, :], in1=tmpz[:, :], op=ALU.add), r=[b_Z], w=[b_Z])
                return f

            def mP(j):
                xi = j % 3; pi = j % 2
                def f():
                    for r_ in range(8):
                        k.mm(acc[0:32, 0:256], PTq[pi][:, r_ * 32:(r_ + 1) * 32], Xc[xi][:, r_, :], start=fst["v"], stop=False, r=[b_PTq[pi], b_X[xi]], w=[b_acc], sig=False)
                        fst["v"] = False
                return f
            for j in range(NJ):
                tl.append((8 * (j - 1) - 0.5 if j >= 2 else (-2 + j), mG(j)))
                for rp in range(4):
                    u = 4 * j + rp
                    tl.append((2 * u, mT(j, rp)))
                    tl.append((2 * u + 3, mS(j, rp)))
                tl.append((2 * (4 * j + 3) + 3.5, mE(j)))
                tl.append((2 * (4 * j + 3) + 7.6, mP(j)))
            for _, f_ in sorted(tl, key=lambda e: e[0]):
                f_()
            si = cnt["s"] % 2; cnt["s"] += 1
            psS = ps[si]; b_S = b_ps[si]
            k.mm(psS[0:16, 0:32], smp_ckvT[:, 0, :], QsA[:, 0, b, :], start=True, stop=False, r=[b_smp, b_Q], w=[b_S])
            k.mm(psS[0:16, 0:32], smp_ckvT[:, 1, :], QsA[:, 1, b, :], start=False, stop=False, r=[b_smp, b_Q], w=[b_S])
            k.mm(psS[0:16, 0:32], smp_krT[0:32, :], QsR[0:32, b, :], start=False, stop=True, r=[b_smp, b_Q], w=[b_S])
            k.actf(nk_f[:, :], psS[0:16, 0:32], AF.Exp, r=[b_S], w=[b_nk], scale=MLA_SCALE)
            k.v("dve", lambda: nc.vector.tensor_tensor(out=nk_b[:, :], in0=nk_f[:, :], in1=s_nm[:, b, :], op=ALU.mult), r=[b_nk, b_sc_], w=[b_nk])
            k.mm(acc[0:32, 0:256], nk_b[0:16, :], smp_ckvn[0:16, 0:256], start=False, stop=True, r=[b_nk, b_smp], w=[b_acc])
            k.mm(ps[3][0:32, 0:1], Zacc[:, 0:32], ones_f[:, 0:1], start=True, stop=False, r=[b_Z, b_sc_], w=[b_ps[3]], sig=False)
            k.mm(ps[3][0:32, 0:1], nk_b[0:16, :], ones_c[0:16, 0:1], start=False, stop=True, r=[b_nk, b_sc_], w=[b_ps[3]])
            k.v("dve", lambda: nc.vector.reciprocal(out=sm1[0:32, 1:2], in_=ps[3][0:32, 0:1]), r=[b_ps[3]], w=[b_sm1])
            k.v("dve", lambda: nc.vector.tensor_scalar(out=olat[:, :], in0=acc[0:32, 0:256], scalar1=sm1[0:32, 1:2], scalar2=None, op0=ALU.mult), r=[b_acc, b_sm1], w=[b_olat])
            for c in range(2):
                k.tr(psb[:, c * 128:c * 128 + 32], olat[0:32, c * 128:(c + 1) * 128], identb[0:32, 0:32], r=[b_olat, b_identb], w=[b_ps[7]], sig=(c == 1))
            k.v("dve", lambda: nc.vector.tensor_copy(out=olT[:, :, :], in_=psb[:, 0:256].rearrange("p (c n) -> p c n", c=2)[:, :, 0:32]), r=[b_ps[7]], w=[b_olT])
            for h in range(8):
                for c in range(2):
                    k.mm(ps[6][0:64, h * 4:(h + 1) * 4], w_uv_s[:, c, h * 64:(h + 1) * 64], olT[:, c, h * 4:(h + 1) * 4], start=(c == 0), stop=(c == 1),
                         r=[b_wuvs, b_olT], w=[b_ps[6]], sig=(c == 1 and h == 7))
            for h in range(8):
                p0 = (h % 2) * 64
                k.v("dve", lambda: nc.vector.tensor_copy(out=o_mlaT[p0:p0 + 64, h // 2, T + 4 * b:T + 4 * b + 4], in_=ps[6][0:64, h * 4:(h + 1) * 4]), r=[b_ps[6]], w=[b_omla])

        if "c" in SB_:
            for kv in "kv":
                tl = []

                def cG(c):
                    def f():
                        k.gather(Xg2[c % 3][:, :], ck_v[kv], idx8[:, b, c:c + 1], r=[b_idx], w=[b_Xg[c % 3]])
                    return f

                def cT(c, hf):
                    u = 2 * c + hf
                    def f():
                        tpb, b_tpb = trbank((7, 6))
                        for s8 in range(8):
                            k.tr(tpb[:, s8 * 128:(s8 + 1) * 128], Xg[c % 3][:, hf * 8 + s8, :], identb[:, :], r=[b_Xg[c % 3], b_identb], w=[b_tpb], sig=(s8 == 7))
                        k.v("dve", lambda: nc.vector.tensor_copy(out=XT[u % 3][:, :, :], in_=tpb[:, :].rearrange("p (s n) -> p s n", s=8)), r=[b_tpb], w=[b_XT[u % 3]])
                    return f

                def cM(c, hf):
                    u = 2 * c + hf; ba = 4 if c % 2 == 0 else 2
                    def f():
                        for s8 in range(8):
                            s_ = hf * 8 + s8
                            k.mm(ps[ba][:, 0:128], W1s[kv][:, s_, :], XT[u % 3][:, s8, :], start=(s_ == 0), stop=(s_ == 15), r=[b_Ws, b_XT[u % 3]], w=[b_ps[ba]])
                            k.mm(ps[ba + 1][:, 0:128], W1s[kv][:, 16 + s_, :], XT[u % 3][:, s8, :], start=(s_ == 0), stop=(s_ == 15), r=[b_Ws, b_XT[u % 3]], w=[b_ps[ba + 1]])
                    return f

                def cV(c):
                    ba = 4 if c % 2 == 0 else 2
                    def f():
                        k.actf(ABf[:, 0, c, :], ps[ba][:, 0:128], AF.Copy, r=[b_ps[ba]], w=[b_AB])
                        k.actf(ABf[:, 1, c, :], ps[ba + 1][:, 0:128], AF.Copy, r=[b_ps[ba + 1]], w=[b_AB])
                    return f
                for c in range(8):
                    tl.append((4 * c - 5.5 if c >= 2 else (-2 + c), cG(c)))
                    for hf in range(2):
                        u = 2 * c + hf
                        tl.append((2 * u, cT(c, hf)))
                        tl.append((2 * u + 3, cM(c, hf)))
                    tl.append((2 * (2 * c + 1) + 3.5, cV(c)))
                for _, f_ in sorted(tl, key=lambda e: e[0]):
                    f_()
                x_ = hx[0]
                k.v("pool", lambda: nc.gpsimd.memset(x_[:, 7, 127:128], 0.0), w=[b_hx[0]])
                k.v("dve", lambda: nc.vector.tensor_tensor(out=x_[:, 0:7, :], in0=ABf[:, 0, 0:7, :], in1=ABf[:, 1, 1:8, :], op=ALU.add), r=[b_AB], w=[b_hx[0]])
                k.v("dve", lambda: nc.vector.tensor_tensor(out=x_[:, 7, 0:127], in0=ABf[:, 0, 7, 0:127], in1=ABf[:, 1, 0, 1:128], op=ALU.add), r=[b_AB], w=[b_hx[0]])
                k.v("dve", lambda: nc.vector.tensor_scalar(out=x_[:, :, :], in0=x_[:, :, :], scalar1=cbs[kv][:, 0:1], scalar2=None, op0=ALU.add), r=[b_hx[0], b_Ws], w=[b_hx[0]])
                k.v("dve", lambda: nc.vector.tensor_tensor(out=hx[1][:, :, :], in0=x_[:, :, :], in1=x_[:, :, :], op=ALU.mult), r=[b_hx[0]], w=[b_hx[1]])
                k.v("dve", lambda: nc.vector.tensor_scalar(out=hx[1][:, :, :], in0=hx[1][:, :, :], scalar1=0.044715, scalar2=1.0, op0=ALU.mult, op1=ALU.add), r=[b_hx[1]], w=[b_hx[1]])
                k.v("dve", lambda: nc.vector.tensor_tensor(out=hx[1][:, :, :], in0=hx[1][:, :, :], in1=x_[:, :, :], op=ALU.mult), r=[b_hx[0], b_hx[1]], w=[b_hx[1]])
                k.actf(hx[2][:, :, :], hx[1][:, :, :], AF.Sigmoid, r=[b_hx[1]], w=[b_hx[2]], scale=1.5957691216)
                k.v("dve", lambda: nc.vector.tensor_tensor(out=Hs[:, :, :], in0=hx[2][:, :, :], in1=x_[:, :, :], op=ALU.mult), r=[b_hx[0], b_hx[2]], w=[b_Hs])
                if kv == "k":
                    for hf in range(2):
                        k.mm(ps[6][:, :], W2s["k"][:, :], Hs[:, hf * 4:(hf + 1) * 4, :], start=True, stop=True, r=[b_Ws, b_Hs], w=[b_ps[6]])
                        k.v("dve", lambda: nc.vector.tensor_copy(out=kccS[:, hf * 4:(hf + 1) * 4, :], in_=ps[6][:, :].rearrange("p (c n) -> p c n", c=4)), r=[b_ps[6]], w=[b_kccS])
                else:
                    for hf in range(2):
                        for c4 in range(4):
                            k.mm(ps[6][:, c4 * 128:(c4 + 1) * 128], Hs[:, hf * 4 + c4, :], W2s["v"][:, :], start=True, stop=True, r=[b_Ws, b_Hs], w=[b_ps[6]], sig=(c4 == 3))
                        k.v("dve", lambda: nc.vector.tensor_copy(out=vccS[:, hf * 4:(hf + 1) * 4, :, 0:64], in_=ps[6][:, :].rearrange("p (c g d) -> p c g d", c=4, g=2)), r=[b_ps[6]], w=[b_vccS])
            si = cnt["s"] % 2; cnt["s"] += 1
            psS = ps[si]; b_S = b_ps[si]
            for c in range(8):
                k.mm(psS[:, c * 32:(c + 1) * 32], kccS[:, c, :], Qblk[:, b, :], start=True, stop=False, r=[b_kccS, b_Q], w=[b_S])
                k.mm(psS[:, c * 32:(c + 1) * 32], s_lb[0:3, :], s_rbc[0:3, c, :], start=False, stop=True, r=[b_sc_], w=[b_S], sig=(c == 7))
            k.actf(PTc[:, :, :], psS[:, 0:256].rearrange("p (c n) -> p c n", c=8), AF.Exp, r=[b_S], w=[b_PTc], scale=NSA_SCALE)
            for g in range(2):
                acc = ps[2 + g]; b_acc = b_ps[2 + g]
                for c in range(8):
                    k.mm(acc[0:16, 0:128], PTc[:, c, g * 16:(g + 1) * 16], vccS[:, c, g, :], start=(c == 0), stop=(c == 7), r=[b_PTc, b_vccS], w=[b_acc])
                U = ps[4 + g]; b_U = b_ps[4 + g]
                for c in range(8):
                    k.mm(U[0:16, 0:264], PTc[:, c, g * 16:(g + 1) * 16], s_ov[:, c, :], start=(c == 0), stop=(c == 7), r=[b_PTc, b_sc_], w=[b_U])
                k.v("dve", lambda: nc.vector.tensor_scalar(out=sm1[0:16, 0:1], in0=acc[0:16, 64:65], scalar1=1e-30, scalar2=None, op0=ALU.max), r=[b_acc], w=[b_sm1])
                k.v("dve", lambda: nc.vector.reciprocal(out=sm1[0:16, 0:1], in_=sm1[0:16, 0:1]), r=[b_sm1], w=[b_sm1])
                k.v("dve", lambda: nc.vector.tensor_scalar(out=og[:, :], in0=acc[0:16, 0:64], scalar1=sm1[0:16, 0:1], scalar2=None, op0=ALU.mult), r=[b_acc, b_sm1], w=[b_og])
                k.v("dve", lambda: nc.vector.tensor_scalar(out=Un[:, :], in0=U[0:16, 0:264], scalar1=sm1[0:16, 0:1], scalar2=None, op0=ALU.mult), r=[b_U, b_sm1], w=[b_Un])
                k.mm(ps[6][0:4, 0:264], s_ssum[0:16, 0:4], Un[0:16, :], start=True, stop=True, r=[b_sc_, b_Un], w=[b_ps[6]])
                k.v("dve", lambda: nc.vector.tensor_tensor(out=tks[0:4, 0, :], in0=ps[6][0:4, 0:264], in1=s_tkm[0:4, :], op=ALU.mult), r=[b_ps[6], b_sc_], w=[b_tks])
                k.v("dve", lambda: nc.vector.tensor_tensor(out=tks[0:4, 0, :], in0=tks[0:4, 0, :], in1=s_tka[0:4, :], op=ALU.add), r=[b_tks, b_sc_], w=[b_tks])
                k.v("dve", lambda: nc.vector.max(out=tks[0:4, 1, 0:8], in_=tks[0:4, 0, :]), r=[b_tks], w=[b_tks])
                k.v("dve", lambda: nc.vector.match_replace(out=tks[0:4, 2, :], in_to_replace=tks[0:4, 1, 0:8], in_values=tks[0:4, 0, :], imm_value=-3.0e38), r=[b_tks], w=[b_tks])
                k.v("dve", lambda: nc.vector.max(out=tks[0:4, 3, 0:8], in_=tks[0:4, 2, :]), r=[b_tks], w=[b_tks])
                k.v("dve", lambda: nc.vector.tensor_scalar(out=msel[0:4, g, :], in0=tks[0:4, 0, :], scalar1=tks[0:4, 3, 7:8], scalar2=None, op0=ALU.is_ge), r=[b_tks], w=[b_msel])
                s_gate(0, g, b)
            for g in range(2):
                for par in range(2):
                    q_ = g * 2 + par
                    k.tr(ps[7][:, q_ * 4:(q_ + 1) * 4], msel[0:4, g, par:par + 256:2], identf[0:4, 0:4], r=[b_msel, b_identf], w=[b_ps[7]], sig=(q_ == 3))
            for g in range(2):
                for par in range(2):
                    q_ = g * 2 + par
                    k.v("dve", lambda: nc.vector.tensor_copy(out=Mexp[:, par, g * 16:(g + 1) * 16].rearrange("p (a t) -> p a t", a=4),
                                                              in_=ps[7][:, q_ * 4:(q_ + 1) * 4].unsqueeze(1).to_broadcast([128, 4, 4])), r=[b_ps[7]], w=[b_Mexp])

        if "s" in SB_:
            tl = []
            fst = {"v": True}
            k.v("pool", lambda: nc.gpsimd.memset(Zacc[:, :], 0.0), w=[b_Z])

            def sG(c):
                def f():
                    k.gather(Xk2[c % 3][:, :], sk_v, idx8[:, b, c:c + 1], r=[b_idx], w=[b_Xk[c % 3]])
                    k.gather(Xg2[c % 3][:, :], sv_v, idx8[:, b, c:c + 1], r=[b_idx], w=[b_Xg[c % 3]])
                return f

            def sT(c, hf):
                u = 2 * c + hf
                def f():
                    tpb, b_tpb = trbank((7, 6))
                    for s8 in range(8):
                        k.tr(tpb[:, s8 * 128:(s8 + 1) * 128], Xk[c % 3][:, hf * 8 + s8, :], identb[:, :], r=[b_Xk[c % 3], b_identb], w=[b_tpb], sig=(s8 == 7))
                    k.v("dve", lambda: nc.vector.tensor_copy(out=XT[u % 3][:, :, :], in_=tpb[:, :].rearrange("p (s n) -> p s n", s=8)), r=[b_tpb], w=[b_XT[u % 3]])
                return f

            def sS(c, hf):
                u = 2 * c + hf; psS = ps[u % 2]; b_S = b_ps[u % 2]
                def f():
                    for s8 in range(8):
                        rr = c * 16 + hf * 8 + s8
                        k.mm(psS[:, s8 * 32:(s8 + 1) * 32], XT[u % 3][:, s8, :], Qblk[:, b, :], start=True, stop=False, r=[b_XT[u % 3], b_Q], w=[b_S])
                        k.mm(psS[:, s8 * 32:(s8 + 1) * 32], s_lb[0:2, :], s_rbs[0:2, rr, :], start=False, stop=True, r=[b_sc_], w=[b_S], sig=(s8 == 7))
                return f

            def sE(c, hf):
                u = 2 * c + hf; pi = u % 2; par = 1 if c >= 4 else 0
                def f():
                    k.actf(PTf[pi][:, :], ps[u % 2][:, 0:256], AF.Exp, r=[b_ps[u % 2]], w=[b_PTf[pi]], scale=NSA_SCALE)
                    k.v("dve", lambda: nc.vector.tensor_tensor(out=PTq[pi][:, :].rearrange("p (s n) -> p s n", s=8), in0=PTf[pi][:, :].rearrange("p (s n) -> p s n", s=8),
                                                                in1=Mexp[:, par, :].unsqueeze(1).to_broadcast([128, 8, 32]), op=ALU.mult), r=[b_PTf[pi], b_Mexp], w=[b_PTq[pi]])
                    k.v("dve", lambda: nc.vector.tensor_reduce(out=tmpz[:, :], in_=PTq[pi][:, :].rearrange("p (s n) -> p n s", s=8), axis=AX.X, op=ALU.add), r=[b_PTq[pi]], w=[b_Z])
                    k.v("dve", lambda: nc.vector.tensor_tensor(out=Zacc[:, :], in0=Zacc[:, :], in1=tmpz[:, :], op=ALU.add), r=[b_Z], w=[b_Z])
                return f

            def sP(c, hf):
                u = 2 * c + hf; pi = u % 2
                def f():
                    for s8 in range(8):
                        for g in range(2):
                            k.mm(ps[2 + g][0:16, 0:128], PTq[pi][:, s8 * 32 + g * 16:s8 * 32 + (g + 1) * 16], Xg[c % 3][:, hf * 8 + s8, :], start=fst["v"], stop=False,
                                 r=[b_PTq[pi], b_Xg[c % 3]], w=[b_ps[2 + g]], sig=False)
                        fst["v"] = False
                return f
            for c in range(8):
                tl.append((4 * c - 2.2 if c >= 1 else -1, sG(c)))
                for hf in range(2):
                    u = 2 * c + hf
                    tl.append((2 * u, sT(c, hf)))
                    tl.append((2 * u + 3, sS(c, hf)))
                    tl.append((2 * u + 3.5, sE(c, hf)))
                    tl.append((2 * u + 5.6, sP(c, hf)))
            for _, f_ in sorted(tl, key=lambda e: e[0]):
                f_()
            si = cnt["s"] % 2; cnt["s"] += 1
            psS = ps[si]; b_S = b_ps[si]
            k.mm(psS[0:16, 0:32], smp_ksT[:, :], Qblk[:, b, :], start=True, stop=True, r=[b_smp, b_Q], w=[b_S])
            k.actf(nk_f[:, :], psS[0:16, 0:32], AF.Exp, r=[b_S], w=[b_nk], scale=NSA_SCALE)
            k.v("dve", lambda: nc.vector.tensor_tensor(out=nk_b[:, :], in0=nk_f[:, :], in1=s_en[:, b, :], op=ALU.mult), r=[b_nk, b_sc_], w=[b_nk])
            for g in range(2):
                acc = ps[2 + g]; b_acc = b_ps[2 + g]
                k.mm(acc[0:16, 0:128], nk_b[0:16, g * 16:(g + 1) * 16], smp_vs[0:16, 0:128], start=False, stop=True, r=[b_nk, b_smp], w=[b_acc])
                k.mm(ps[4 + g][0:16, 0:1], Zacc[:, g * 16:(g + 1) * 16], ones_f[:, 0:1], start=True, stop=False, r=[b_Z, b_sc_], w=[b_ps[4 + g]], sig=False)
                k.mm(ps[4 + g][0:16, 0:1], nk_b[0:16, g * 16:(g + 1) * 16], ones_c[0:16, 0:1], start=False, stop=True, r=[b_nk, b_sc_], w=[b_ps[4 + g]])
                k.v("dve", lambda: nc.vector.tensor_scalar(out=sm1[0:16, 0:1], in0=ps[4 + g][0:16, 0:1], scalar1=1e-30, scalar2=None, op0=ALU.max), r=[b_ps[4 + g]], w=[b_sm1])
                k.v("dve", lambda: nc.vector.reciprocal(out=sm1[0:16, 0:1], in_=sm1[0:16, 0:1]), r=[b_sm1], w=[b_sm1])
                k.v("dve", lambda: nc.vector.tensor_scalar(out=og[:, :], in0=acc[0:16, g * 64:(g + 1) * 64], scalar1=sm1[0:16, 0:1], scalar2=None, op0=ALU.mult), r=[b_acc, b_sm1], w=[b_og])
                s_gate(1, g, b)

        if "w" in SB_:
            k.dma(k.q_pool, Kw[:, :, :], win_k_in[b].rearrange("(kt p) f -> p kt f", p=128), w=[b_Kw])
            k.dma(k.q_pool, Vw[:, :, 0:128], win_v_in[b].rearrange("(kt p) f -> p kt f", p=128), w=[b_Kw])
            for kt in range(4):
                k.tr(psb[:, kt * 128:(kt + 1) * 128], Kw[:, kt, :], identb[:, :], r=[b_Kw, b_identb], w=[b_ps[7]], sig=(kt == 3))
            k.v("dve", lambda: nc.vector.tensor_copy(out=KwTs[:, :, :], in_=psb[:, 0:512].rearrange("p (s n) -> p s n", s=4)), r=[b_ps[7]], w=[b_Kw])
            si = cnt["s"] % 2; cnt["s"] += 1
            psS = ps[si]; b_S = b_ps[si]
            for kt in range(4):
                k.mm(psS[:, kt * 32:(kt + 1) * 32], KwTs[:, kt, :], Qblk[:, b, :], start=True, stop=True, r=[b_Kw, b_Q], w=[b_S], sig=(kt == 3))
            pi = cnt["pt"] % 2; cnt["pt"] += 1
            k.actf(PTf[pi][:, 0:128], psS[:, 0:128], AF.Exp, r=[b_S], w=[b_PTf[pi]], scale=NSA_SCALE)
            k.v("dve", lambda: nc.vector.tensor_tensor(out=PTq[pi][:, 0:128], in0=PTf[pi][:, 0:128], in1=s_ebw[:, :, :].rearrange("p a n -> p (a n)"), op=ALU.mult),
                r=[b_PTf[pi], b_sc_], w=[b_PTq[pi]])
            si = cnt["s"] % 2; cnt["s"] += 1
            psS2 = ps[si]; b_S2 = b_ps[si]
            k.mm(psS2[0:16, 0:32], smp_kwT[:, :], Qblk[:, b, :], start=True, stop=True, r=[b_smp, b_Q], w=[b_S2])
            k.actf(nk_f[:, :], psS2[0:16, 0:32], AF.Exp, r=[b_S2], w=[b_nk], scale=NSA_SCALE)
            k.v("dve", lambda: nc.vector.tensor_tensor(out=nk_b[:, :], in0=nk_f[:, :], in1=s_en[:, b, :], op=ALU.mult), r=[b_nk, b_sc_], w=[b_nk])
            for g in range(2):
                acc = ps[2 + g]; b_acc = b_ps[2 + g]
                for kt in range(4):
                    k.mm(acc[0:16, 0:129], PTq[pi][:, kt * 32 + g * 16:kt * 32 + (g + 1) * 16], Vw[:, kt, :], start=(kt == 0), stop=False, r=[b_PTq[pi], b_Kw], w=[b_acc], sig=False)
                k.mm(acc[0:16, 0:129], nk_b[0:16, g * 16:(g + 1) * 16], smp_vw[0:16, :], start=False, stop=True, r=[b_nk, b_smp], w=[b_acc])
                k.v("dve", lambda: nc.vector.tensor_scalar(out=sm1[0:16, 0:1], in0=acc[0:16, 128:129], scalar1=1e-30, scalar2=None, op0=ALU.max), r=[b_acc], w=[b_sm1])
                k.v("dve", lambda: nc.vector.reciprocal(out=sm1[0:16, 0:1], in_=sm1[0:16, 0:1]), r=[b_sm1], w=[b_sm1])
                k.v("dve", lambda: nc.vector.tensor_scalar(out=og[:, :], in0=acc[0:16, g * 64:(g + 1) * 64], scalar1=sm1[0:16, 0:1], scalar2=None, op0=ALU.mult), r=[b_acc, b_sm1], w=[b_og])
                s_gate(2, g, b)
    k.barrier()
    stS.close()
    if os.environ.get('KSTOP') == 'S':
        k.finish()
        return nc
    chunks5 = [(0, 512), (512, 512), (1024, 512), (1536, 512), (2048, TS)]
    stC = contextlib.ExitStack(); sbC = scoped(stC)
    h2T = sbC("h2T", [128, 8, TT], BF16); b_h2T = [Buf() for _ in range(NT)]
    stCm = contextlib.ExitStack(); sbCm = scoped(stCm)
    mTall = sbCm("mTall", [128, 8, TT], BF16); b_mT = [Buf() for _ in range(5)]
    stC1 = contextlib.ExitStack(); sbC1 = scoped(stC1)
    wpm_b = sbC1("wpm_b", [128, 4, D], BF16); b_wpm = Buf()
    wpn_b = sbC1("wpn_b", [128, 4, D], BF16); b_wpn = Buf()
    wG = [sbC1(f"wG{i}", [128, 8, 256], BF16) for i in range(2)]; b_wG = [Buf(), Buf()]
    stgG = [sbC1(f"stgG{i}", [128, 8, 256], F32) for i in range(2)]; b_stgG = [Buf(), Buf()]
    w_in_v = w_in.rearrange("(c p) f -> p c f", p=128)
    gsa = [sbC1(f"gsa{i}", [128, 512], F32) for i in range(2)]; b_gsa = [Buf(), Buf()]
    gsb = [sbC1(f"gsb{i}", [128, 512], F32) for i in range(2)]; b_gsb = [Buf(), Buf()]
    for c in range(4):
        k.dma(k.q_pool, wpm_b[:, c, :], w_pm[c * 128:(c + 1) * 128, :], w=[b_wpm])
        k.dma(k.q_pool, wpn_b[:, c, :], w_pn[c * 128:(c + 1) * 128, :], w=[b_wpn])
    it = 0
    for m in range(8):
        wi = m % 2
        k.dma(k.q_sp, stgG[wi][:, :, 0:128], w_in_v[:, :, C_GA + m * 128:C_GA + (m + 1) * 128], w=[b_stgG[wi]])
        k.dma(k.q_sp, stgG[wi][:, :, 128:256], w_in_v[:, :, C_GB + m * 128:C_GB + (m + 1) * 128], w=[b_stgG[wi]])
        k.v("pool", lambda: nc.gpsimd.tensor_tensor(out=wG[wi][:, :, :], in0=stgG[wi][:, :, :], in1=g1col[:, :].unsqueeze(2).to_broadcast([128, 8, 256]), op=ALU.mult),
            r=[b_stgG[wi], b_g1], w=[b_wG[wi]])
        for ci, (c0, cn) in enumerate(chunks5):
            hb = b_hT[4 * ci:4 * ci + 4] if ci < 4 else [b_hT[16]]
            i2 = it % 2; it += 1
            o_ = 4 * i2
            for c in range(8):
                k.mm(ps[o_ + 0][:, 0:cn], wG[wi][:, c, 0:128], hT[:, c, c0:c0 + cn], start=(c == 0), stop=(c == 7), r=[b_wG[wi]] + hb, w=[b_ps[o_ + 0]])
            for c in range(8):
                k.mm(ps[o_ + 1][:, 0:cn], wG[wi][:, c, 128:256], hT[:, c, c0:c0 + cn], start=(c == 0), stop=(c == 7), r=[b_wG[wi]] + hb, w=[b_ps[o_ + 1]])
            for c in range(4):
                k.mm(ps[o_ + 2][:, 0:cn], wpm_b[:, c, m * 128:(m + 1) * 128], o_mlaT[:, c, c0:c0 + cn], start=(c == 0), stop=(c == 3), r=[b_wpm, b_omla], w=[b_ps[o_ + 2]])
            for c in range(4):
                k.mm(ps[o_ + 3][:, 0:cn], wpn_b[:, c, m * 128:(m + 1) * 128], o_nsaT[:, c, c0:c0 + cn], start=(c == 0), stop=(c == 3), r=[b_wpn, b_onsa], w=[b_ps[o_ + 3]])
            k.actf(gsa[i2][:, 0:cn], ps[o_ + 0][:, 0:cn], AF.Sigmoid, r=[b_ps[o_ + 0]], w=[b_gsa[i2]])
            k.actf(gsb[i2][:, 0:cn], ps[o_ + 1][:, 0:cn], AF.Sigmoid, r=[b_ps[o_ + 1]], w=[b_gsb[i2]])
            k.v("dve", lambda: nc.vector.tensor_tensor(out=gsa[i2][:, 0:cn], in0=gsa[i2][:, 0:cn], in1=ps[o_ + 2][:, 0:cn], op=ALU.mult), r=[b_gsa[i2], b_ps[o_ + 2]], w=[b_gsa[i2]])
            k.v("dve", lambda: nc.vector.tensor_tensor(out=gsb[i2][:, 0:cn], in0=gsb[i2][:, 0:cn], in1=ps[o_ + 3][:, 0:cn], op=ALU.mult), r=[b_gsb[i2], b_ps[o_ + 3]], w=[b_gsb[i2]])
            k.v("dve", lambda: nc.vector.tensor_tensor(out=mTall[:, m, c0:c0 + cn], in0=gsa[i2][:, 0:cn], in1=gsb[i2][:, 0:cn], op=ALU.add), r=[b_gsa[i2], b_gsb[i2]], w=[b_mT[ci]])
    k.barrier()
    stC1.close()
    stC2 = contextlib.ExitStack(); sbC2 = scoped(stC2)
    wout_b = sbC2("wout_b", [128, 8, D], BF16); b_wout = Buf()
    xt2 = [sbC2(f"xt2_{i}", [128, D], F32) for i in range(2)]; b_xt2 = [Buf(), Buf()]
    xn2 = [sbC2(f"xn2_{i}", [128, D], BF16) for i in range(2)]; b_xn2 = [Buf(), Buf()]
    junk2 = sbC2("junk2", [128, D], BF16); b_junk2 = Buf()
    st2 = [sbC2(f"st2_{i}", [128, 4], F32) for i in range(2)]; b_st2 = [Buf(), Buf()]
    for c in range(8):
        k.dma(k.q_pool, wout_b[:, c, :], w_out[c * 128:(c + 1) * 128, :], w=[b_wout])

    def x1_tile(t):
        return x1all[:, t, :] if t < 16 else x1s[:, :]
    for t in range(NT):
        t0, n = tile_rows(t)
        s = t % 2
        ci = min(t // 4, 4)
        src = xp[t0:t0 + n, :] if t < 16 else xs[:, :]
        k.dma(k.q_sp, xt2[s][0:n, :], src, w=[b_xt2[s]])
        xd = x1_tile(t)
        for half in range(2):
            pb = 2 * (t % 2) + half
            for m in range(8):
                k.mm(ps[pb][0:n, :], mTall[:, m, t0:t0 + n], wout_b[:, m, half * 512:(half + 1) * 512], start=(m == 0), stop=(m == 7),
                     r=[b_mT[ci], b_wout], w=[b_ps[pb]])
            k.v("dve", lambda: nc.vector.tensor_tensor(out=xd[0:n, half * 512:(half + 1) * 512], in0=xt2[s][0:n, half * 512:(half + 1) * 512],
                                                        in1=ps[pb][0:n, :], op=ALU.add), r=[b_xt2[s], b_ps[pb]], w=[b_x1[t]])
        k.actf(junk2[0:n, :], xd[0:n, :], AF.Square, r=[b_x1[t]], w=[b_junk2, b_st2[s]], accum=st2[s][0:n, 0:1])
        k.actf(st2[s][0:n, 1:2], st2[s][0:n, 0:1], AF.Sqrt, r=[b_st2[s]], w=[b_st2[s]], scale=1.0 / D, bias=EPS)
        k.v("dve", lambda: nc.vector.reciprocal(out=st2[s][0:n, 2:3], in_=st2[s][0:n, 1:2]), r=[b_st2[s]], w=[b_st2[s]])
        k.v("dve", lambda: nc.vector.tensor_scalar(out=xn2[s][0:n, :], in0=xd[0:n, :], scalar1=st2[s][0:n, 2:3], scalar2=None, op0=ALU.mult),
            r=[b_x1[t], b_st2[s]], w=[b_xn2[s]])
        psbc = psbv2[t % 2]; b_psbc = b_ps[7 - (t % 2)]
        for c in range(8):
            k.tr(psbc[:, c * 128:c * 128 + n], xn2[s][0:n, c * 128:(c + 1) * 128], identb[0:n, 0:n], r=[b_xn2[s], b_identb], w=[b_psbc], sig=(c == 7))
        k.v("dve", lambda: nc.vector.tensor_copy(out=h2T[:, :, t0:t0 + n], in_=psbc[:, :].rearrange("p (c n) -> p c n", c=8)[:, :, 0:n]),
            r=[b_psbc], w=[b_h2T[t]])
    k.barrier()
    stC2.close()
    stCm.close()
    stD = contextlib.ExitStack(); sbD = scoped(stD)
    GW = 2
    NG = NFC // GW
    stgW = [sbD(f"stgW{i}", [128, 8, 128 * GW], F32) for i in range(2)]; b_stgW = [Buf(), Buf()]
    wg_b = [sbD(f"wg_b{i}", [128, 8, 128 * GW], BF16) for i in range(2)]; b_wg = [Buf(), Buf()]
    wu_b = [sbD(f"wu_b{i}", [128, 8, 128 * GW], BF16) for i in range(2)]; b_wu = [Buf(), Buf()]
    wd_b = [sbD(f"wd_b{i}", [128, GW, D], BF16) for i in range(2)]; b_wd = [Buf(), Buf()]
    gbuf = sbD("gbuf", [128, 2 + T], F32); b_gbuf = Buf()
    gbs = sbD("gbs", [128, 4, 6], F32); b_gbs = Buf()
    cv = [sbD(f"cv{i}", [128, 512], F32) for i in range(2)]; b_cv = [Buf(), Buf()]
    sg = [sbD(f"sg{i}", [128, 512], F32) for i in range(2)]; b_sg = [Buf(), Buf()]
    mTg = [sbD(f"mTg{i}", [128, GW, TT], BF16) for i in range(2)]; b_mTg = [Buf(), Buf()]
    histT = sbD("histT", [128, NFC, 8], F32); b_hist = Buf()
    cwT = sbD("cwT", [128, NFC, 4], F32); b_cwT = Buf()
    yo = [sbD(f"yo{i}", [128, D], F32) for i in range(2)]; b_yo = [Buf(), Buf()]
    st3 = [sbD(f"st3_{i}", [128, 4], F32) for i in range(2)]; b_st3 = [Buf(), Buf()]
    junk3 = sbD("junk3", [128, D], BF16); b_junk3 = Buf()
    with nc.allow_non_contiguous_dma(reason="tiny conv params, feature-major"):
        for fc in range(NFC):
            fsl = slice(fc * 128, (fc + 1) * 128)
            k.dma(k.q_sp, histT[:, fc, :], ffn_state[:, fsl].rearrange("e p -> p e"), w=[b_hist])
            k.dma(k.q_sp, cwT[:, fc, 0:3], conv_w[:, fsl].rearrange("j p -> p j"), w=[b_cwT])
            k.dma(k.q_sp, cwT[:, fc, 3:4], conv_b[fsl].rearrange("(p o) -> p o", o=1), w=[b_cwT])
    k.v("pool", lambda: nc.gpsimd.memset(gbuf[:, 0:2], 0.0), w=[b_gbuf])
    wgv = w_gate.rearrange("(c p) f -> p c f", p=128)
    wuv = w_up.rearrange("(c p) f -> p c f", p=128)
    wdv = w_down.rearrange("(c p) d -> p c d", p=128)
    cvi = 0
    with nc.allow_non_contiguous_dma(reason="conv state outputs are tiny feature-major slices"):
        for gi in range(NG):
            wi = gi % 2
            fs = slice(gi * 128 * GW, (gi + 1) * 128 * GW)
            for (wsrc, wdst, bw) in ((wgv, wg_b, b_wg), (wuv, wu_b, b_wu)):
                si_ = cvi % 2; cvi += 1
                k.dma(k.q_sp, stgW[si_][:, :, :], wsrc[:, :, fs], w=[b_stgW[si_]])
                for c in range(8):
                    k.actf(wdst[wi][:, c, :], stgW[si_][:, c, :], AF.Copy, r=[b_stgW[si_], b_g2], w=[bw[wi]], scale=g2col[:, c:c + 1])
            k.dma(k.q_pool, wd_b[wi][:, :, :], wdv[:, gi * GW:(gi + 1) * GW, :], w=[b_wd[wi]])
            for fl in range(GW):
                fc = gi * GW + fl
                w0 = cwT[:, fc, 0:1]; w1_ = cwT[:, fc, 1:2]; w2_ = cwT[:, fc, 2:3]; bc = cwT[:, fc, 3:4]
                for ci, (c0, cn) in enumerate(chunks5):
                    hb = b_h2T[4 * ci:4 * ci + 4] if ci < 4 else [b_h2T[16]]
                    i2 = (fc * 5 + ci) % 2
                    for c in range(8):
                        k.mm(ps[0 + i2][:, 0:cn], wg_b[wi][:, c, fl * 128:(fl + 1) * 128], h2T[:, c, c0:c0 + cn], start=(c == 0), stop=(c == 7), r=[b_wg[wi]] + hb, w=[b_ps[0 + i2]])
                    for c in range(8):
                        k.mm(ps[2 + i2][:, 0:cn], wu_b[wi][:, c, fl * 128:(fl + 1) * 128], h2T[:, c, c0:c0 + cn], start=(c == 0), stop=(c == 7), r=[b_wu[wi]] + hb, w=[b_ps[2 + i2]])
                    if ci < 4:
                        k.actf(gbuf[:, 2 + c0:2 + c0 + cn], ps[0 + i2][:, 0:cn], AF.Copy, r=[b_ps[0 + i2]], w=[b_gbuf])
                        gm2 = gbuf[:, c0:c0 + cn]; gm1 = gbuf[:, c0 + 1:c0 + 1 + cn]; g0 = gbuf[:, c0 + 2:c0 + 2 + cn]
                        cvt = cv[i2][:, 0:cn]; sgt = sg[i2][:, 0:cn]; mdst = mTg[wi][:, fl, c0:c0 + cn]; ups = ps[2 + i2][:, 0:cn]
                        rb = [b_gbuf]
                    else:
                        k.v("dve", lambda: nc.vector.tensor_copy(out=gbs[:, :, 0:2], in_=histT[:, fc, :].rearrange("p (b j) -> p b j", b=4)), r=[b_hist], w=[b_gbs])
                        k.actf(gbs[:, :, 2:6], ps[0 + i2][:, 0:TS].rearrange("p (b j) -> p b j", b=4), AF.Copy, r=[b_ps[0 + i2]], w=[b_gbs])
                        gm2 = gbs[:, :, 0:4]; gm1 = gbs[:, :, 1:5]; g0 = gbs[:, :, 2:6]
                        cvt = cv[i2][:, 0:TS].rearrange("p (b j) -> p b j", b=4); sgt = sg[i2][:, 0:TS].rearrange("p (b j) -> p b j", b=4)
                        mdst = mTg[wi][:, fl, c0:c0 + cn].rearrange("p (b j) -> p b j", b=4); ups = ps[2 + i2][:, 0:TS].rearrange("p (b j) -> p b j", b=4)
                        rb = [b_gbs]
                    k.v("pool", lambda: nc.gpsimd.tensor_scalar(out=cvt, in0=gm2, scalar1=w0, scalar2=bc, op0=ALU.mult, op1=ALU.add), r=rb + [b_cwT], w=[b_cv[i2]])
                    k.v("dve", lambda: nc.vector.scalar_tensor_tensor(out=cvt, in0=gm1, scalar=w1_, in1=cvt, op0=ALU.mult, op1=ALU.add), r=rb + [b_cwT, b_cv[i2]], w=[b_cv[i2]])
                    k.v("dve", lambda: nc.vector.scalar_tensor_tensor(out=cvt, in0=g0, scalar=w2_, in1=cvt, op0=ALU.mult, op1=ALU.add), r=rb + [b_cwT, b_cv[i2]], w=[b_cv[i2]])
                    k.actf(sgt, cvt, AF.Silu, r=[b_cv[i2]], w=[b_sg[i2]])
                    k.v("dve", lambda: nc.vector.tensor_tensor(out=mdst, in0=sgt, in1=ups, op=ALU.mult), r=[b_sg[i2], b_ps[2 + i2]], w=[b_mTg[wi]])
                    if ci == 3:
                        k.dma(k.q_pool, o_pconv[:, fc * 128:(fc + 1) * 128].rearrange("j f -> f j"), gbuf[:, T:T + 2], r=[b_gbuf])
                    if ci == 4:
                        for b in range(4):
                            k.dma(k.q_pool, o_sconv[b, :, fc * 128:(fc + 1) * 128].rearrange("j f -> f j"), gbs[:, b, 4:6], r=[b_gbs])
            for t in range(NT):
                t0, n = tile_rows(t)
                xd = x1_tile(t)
                for half in range(2):
                    pb = 4 + ((t * 2 + half) % 2)
                    for fl in range(GW):
                        k.mm(ps[pb][0:n, :], mTg[wi][:, fl, t0:t0 + n], wd_b[wi][:, fl, half * 512:(half + 1) * 512], start=(fl == 0), stop=(fl == GW - 1),
                             r=[b_mTg[wi], b_wd[wi]], w=[b_ps[pb]])
                    k.v("dve", lambda: nc.vector.tensor_tensor(out=xd[0:n, half * 512:(half + 1) * 512], in0=xd[0:n, half * 512:(half + 1) * 512],
                                                                in1=ps[pb][0:n, :], op=ALU.add), r=[b_x1[t], b_ps[pb]], w=[b_x1[t]])
    for t in range(NT):
        t0, n = tile_rows(t)
        s = t % 2
        xd = x1_tile(t)
        k.actf(junk3[0:n, :], xd[0:n, :], AF.Square, r=[b_x1[t]], w=[b_junk3, b_st3[s]], accum=st3[s][0:n, 0:1])
        k.actf(st3[s][0:n, 1:2], st3[s][0:n, 0:1], AF.Sqrt, r=[b_st3[s]], w=[b_st3[s]], scale=1.0 / D, bias=EPS)
        k.v("dve", lambda: nc.vector.reciprocal(out=st3[s][0:n, 2:3], in_=st3[s][0:n, 1:2]), r=[b_st3[s]], w=[b_st3[s]])
        k.v("dve", lambda: nc.vector.scalar_tensor_tensor(out=yo[s][0:n, :], in0=xd[0:n, :], scalar=st3[s][0:n, 2:3], in1=gf_b[0:n, :],
                                                           op0=ALU.mult, op1=ALU.mult), r=[b_x1[t], b_st3[s], b_gf], w=[b_yo[s]])
        if t < 16:
            k.dma(k.q_pool, o_yp[t0:t0 + n, :], yo[s][0:n, :], r=[b_yo[s]])
        else:
            k.dma(k.q_pool, o_ys[:, :], yo[s][0:n, :], r=[b_yo[s]])
    k.barrier()
    stD.close()
    stC.close()
    k.finish()
    return nc


def _inputs_for_core(c, inp):
    m = {
        "xp": np.ascontiguousarray(inp["x_prompt"][c]),
        "xs": np.ascontiguousarray(inp["x_sample"][4 * c:4 * c + 4].reshape(TS, D)),
        "w_in": np.ascontiguousarray(inp["w_in"][0]),
        "norm1_g": np.ascontiguousarray(inp["norm1_g"][0]),
        "q_norm_g": np.ascontiguousarray(inp["q_norm_g"][0]),
        "kv_norm_g": np.ascontiguousarray(inp["kv_norm_g"][0]),
        "w_uq": np.ascontiguousarray(inp["w_uq"][0].reshape(384, 768)),
        "ropecs": rope_tables(),
        "w_uk": np.ascontiguousarray(inp["w_uk"][0].reshape(256, 512)),
        "w_uv": np.ascontiguousarray(inp["w_uv"][0].reshape(256, 512)),
        "trimask": tri_masks(),
        "norm_f_g": np.ascontiguousarray(inp["norm_f_g"]), "norm2_g": np.ascontiguousarray(inp["norm2_g"][0]),
        "w_proj_mla": np.ascontiguousarray(inp["w_proj_mla"][0]), "w_proj_nsa": np.ascontiguousarray(inp["w_proj_nsa"][0]),
        "w_out": np.ascontiguousarray(inp["w_out"][0]), "w_gate": np.ascontiguousarray(inp["w_gate"][0]),
        "w_up": np.ascontiguousarray(inp["w_up"][0]), "w_down": np.ascontiguousarray(inp["w_down"][0]),
        "conv_w": np.ascontiguousarray(inp["conv_w"][0]), "conv_b": np.ascontiguousarray(inp["conv_b"][0]),
        "state_ffn_conv": np.ascontiguousarray(inp["state_ffn_conv"][0, 4 * c:4 * c + 4].reshape(8, DFF)),
        **nsa_consts(),
        **sample_consts(),
        "page_table": np.ascontiguousarray(inp["page_table"][4 * c:4 * c + 4]).astype(np.int32),
        "cache_mla_ckv": inp["cache_mla_ckv"].reshape(-1, 256), "cache_mla_krope": inp["cache_mla_krope"].reshape(-1, 32),
        "cache_nsa_cmp_k": inp["cache_nsa_cmp_k"].reshape(-1, 128), "cache_nsa_cmp_v": inp["cache_nsa_cmp_v"].reshape(-1, 128),
        "cache_nsa_slc_k": inp["cache_nsa_slc_k"].reshape(-1, 128), "cache_nsa_slc_v": inp["cache_nsa_slc_v"].reshape(-1, 128),
        "cmp_w1_k": np.ascontiguousarray(inp["cmp_w1_k"][0]), "cmp_w2_k": np.ascontiguousarray(inp["cmp_w2_k"][0]),
        "cmp_w1_v": np.ascontiguousarray(inp["cmp_w1_v"][0]), "cmp_w2_v": np.ascontiguousarray(inp["cmp_w2_v"][0]),
        "cmp_pos_k": np.ascontiguousarray(inp["cmp_pos_k"][0]), "cmp_pos_v": np.ascontiguousarray(inp["cmp_pos_v"][0]),
        "identf": np.eye(128, dtype=np.float32),
        "state_win_k": np.ascontiguousarray(inp["state_win_k"][0, 4 * c:4 * c + 4].reshape(4, 512, 128)),
        "state_win_v": np.ascontiguousarray(inp["state_win_v"][0, 4 * c:4 * c + 4].reshape(4, 512, 128)),
    }
    return m


def kernel(**inp):
    nc = build()
    in_maps = [_inputs_for_core(c, inp) for c in range(8)]
    res = run_bass_kernel_spmd(nc, in_maps, core_ids=list(range(8)))
    R = res.results

    def cat(nm, shape_per_core, default=None):
        outs = []
        for c in range(8):
            if nm in R[c]:
                outs.append(np.asarray(R[c][nm], dtype=np.float32).reshape(shape_per_core))
            else:
                outs.append(np.zeros(shape_per_core, np.float32))
        return outs

    y_prompt = np.stack(cat("yp", (T, D)), 0)
    y_sample = np.concatenate(cat("ys", (4, 4, D)), 0)
    outs = [y_prompt, y_sample]
    for nm, w in (("ckv", (256,)), ("krope", (32,)), ("cmp_k", (2, 64)), ("cmp_v", (2, 64)),
                  ("slc_k", (2, 64)), ("slc_v", (2, 64))):
        outs.append(np.stack(cat("p_" + nm, (T,) + w), 0)[None])
        outs.append(np.concatenate(cat("s_" + nm, (4, 4) + w), 0)[None])
    for nm in ("win_k", "win_v"):
        outs.append(np.stack(cat("p_" + nm, (512, 2, 64)), 0)[None])
        outs.append(np.concatenate(cat("s_" + nm, (4, 512, 2, 64)), 0)[None])
    outs.append(np.stack(cat("p_conv", (2, DFF)), 0)[None])
    outs.append(np.concatenate(cat("s_conv", (4, 2, DFF)), 0)[None])
    return tuple(outs)
```
